# Optimizing a Trainium2 kernel written in Bass

```python
import math
import jax, jax.numpy as jnp
from jax import lax
import numpy as np

D_MODEL = 1024
BATCH = 8
SEQ = 2048
DEPTH = 2

CHUNK = 64
Q_BLOCK = 128
N_MIXERS = 2
SB_HEADS = 16
SB_HEAD_DIM = D_MODEL // SB_HEADS
DIFF_HEADS = 8
DIFF_QK_DIM = D_MODEL // (2 * DIFF_HEADS)
DIFF_V_DIM = 2 * DIFF_QK_DIM
D_FF = 2816
ROPE_THETA = 10000.0
RMS_EPS = 1e-6
N_SB = (DEPTH + 1) // 2
N_DIFF = DEPTH // 2

kernel_name = "hybrid_stickbreak_diffattn_macaron"


def _rms_norm(x, g):
    xf = x.astype(jnp.float32)
    y = xf * lax.rsqrt(jnp.mean(xf * xf, axis=-1, keepdims=True) + RMS_EPS)
    return (y * g.astype(jnp.float32)).astype(x.dtype)


def _swiglu_ffn(h, w_in, w_out):
    gu = h @ w_in
    g, u = jnp.split(gu, 2, axis=-1)
    return (jax.nn.silu(g) * u) @ w_out


def _rope(x, pos):
    d = x.shape[-1]
    half = d // 2
    inv_freq = ROPE_THETA ** (-jnp.arange(half, dtype=jnp.float32) / half)
    ang = pos.astype(jnp.float32)[:, None] * inv_freq[None, :]
    cos, sin = jnp.cos(ang), jnp.sin(ang)
    xf = x.astype(jnp.float32)
    x1, x2 = xf[..., :half], xf[..., half:]
    out = jnp.concatenate([x1 * cos - x2 * sin, x2 * cos + x1 * sin], axis=-1)
    return out.astype(x.dtype)


def _stick_breaking_mixer(h, w_qkv, w_o):
    B, S, D = h.shape
    qkv = (h @ w_qkv).reshape(B, S, 3, SB_HEADS, SB_HEAD_DIM)
    qkv = jnp.transpose(qkv, (2, 0, 3, 1, 4))
    q, k, v = qkv[0], qkv[1], qkv[2]
    scale = 1.0 / math.sqrt(SB_HEAD_DIM)
    outs = []
    for blk in range(S // Q_BLOCK):
        q0 = blk * Q_BLOCK
        kend = q0 + Q_BLOCK
        qb = q[:, :, q0:kend]
        kb, vb = k[:, :, :kend], v[:, :, :kend]
        z = jnp.einsum('bhqd,bhkd->bhqk', qb, kb).astype(jnp.float32) * scale
        qpos = q0 + jnp.arange(Q_BLOCK)
        kpos = jnp.arange(kend)
        strict = kpos[None, :] < qpos[:, None]
        log_1m = jnp.where(strict, -jax.nn.softplus(z), 0.0)
        suffix = lax.cumsum(log_1m, axis=3, reverse=True) - log_1m
        log_a = jax.nn.log_sigmoid(z) + suffix
        a = jnp.where(strict, jnp.exp(log_a), 0.0)
        outs.append(jnp.einsum('bhqk,bhkd->bhqd', a, vb.astype(jnp.float32)))
    o = jnp.concatenate(outs, axis=2).astype(h.dtype)
    o = jnp.transpose(o, (0, 2, 1, 3)).reshape(B, S, D)
    return o @ w_o


def _diff_attention_mixer(h, w_qkv, w_o, lam_params, subln_g, layer_idx):
    B, S, D = h.shape
    lambda_init = 0.8 - 0.6 * math.exp(-0.3 * (layer_idx - 1))
    qk_w = DIFF_HEADS * 2 * DIFF_QK_DIM
    proj = h @ w_qkv
    q = proj[..., :qk_w].reshape(B, S, DIFF_HEADS, 2, DIFF_QK_DIM)
    k = proj[..., qk_w:2 * qk_w].reshape(B, S, DIFF_HEADS, 2, DIFF_QK_DIM)
    v = proj[..., 2 * qk_w:].reshape(B, S, DIFF_HEADS, DIFF_V_DIM)
    q = jnp.transpose(q, (3, 0, 2, 1, 4))
    k = jnp.transpose(k, (3, 0, 2, 1, 4))
    v = jnp.transpose(v, (0, 2, 1, 3))
    pos = jnp.arange(S)
    q1, q2 = _rope(q[0], pos), _rope(q[1], pos)
    k1, k2 = _rope(k[0], pos), _rope(k[1], pos)
    lp = lam_params.astype(jnp.float32)
    lam = jnp.exp(jnp.sum(lp[0] * lp[1])) - jnp.exp(jnp.sum(lp[2] * lp[3])) + lambda_init
    scale = 1.0 / math.sqrt(DIFF_QK_DIM)
    outs = []
    for blk in range(S // Q_BLOCK):
        q0 = blk * Q_BLOCK
        kend = q0 + Q_BLOCK
        qpos = q0 + jnp.arange(Q_BLOCK)
        kpos = jnp.arange(kend)
        mask = (kpos[None, :] // CHUNK) <= (qpos[:, None] // CHUNK)
        s1 = jnp.einsum('bhqd,bhkd->bhqk', q1[:, :, q0:kend], k1[:, :, :kend]).astype(jnp.float32) * scale
        s2 = jnp.einsum('bhqd,bhkd->bhqk', q2[:, :, q0:kend], k2[:, :, :kend]).astype(jnp.float32) * scale
        a1 = jax.nn.softmax(jnp.where(mask, s1, -jnp.inf), axis=-1)
        a2 = jax.nn.softmax(jnp.where(mask, s2, -jnp.inf), axis=-1)
        a = a1 - lam * a2
        outs.append(jnp.einsum('bhqk,bhkd->bhqd', a, v[:, :, :kend].astype(jnp.float32)))
    o = jnp.concatenate(outs, axis=2)
    o = o * lax.rsqrt(jnp.mean(o * o, axis=-1, keepdims=True) + RMS_EPS)
    o = o * subln_g.astype(jnp.float32) * (1.0 - lambda_init)
    o = jnp.transpose(o.astype(h.dtype), (0, 2, 1, 3)).reshape(B, S, DIFF_HEADS * DIFF_V_DIM)
    return o @ w_o


def setup_inputs(seed: int = 0) -> dict:
    key = jax.random.key(seed)
    ks = jax.random.split(key, 12)
    D, F = D_MODEL, D_FF
    x = jax.random.normal(ks[0], (BATCH, SEQ, D), jnp.float32)
    norm_gains = 1.0 + 0.02 * jax.random.normal(ks[1], (DEPTH, 3, D), jnp.float32)
    final_gain = 1.0 + 0.02 * jax.random.normal(ks[2], (D,), jnp.float32)
    ffn_w_in = jax.random.normal(ks[3], (DEPTH, 2, D, 2 * F), jnp.float32) * D ** -0.5
    ffn_w_out = jax.random.normal(ks[4], (DEPTH, 2, F, D), jnp.float32) * F ** -0.5
    sb_w_qkv = jax.random.normal(ks[5], (N_SB, D, 3 * D), jnp.float32) * D ** -0.5
    sb_w_o = jax.random.normal(ks[6], (N_SB, D, D), jnp.float32) * D ** -0.5
    diff_w_qkv = jax.random.normal(ks[7], (N_DIFF, D, 3 * D), jnp.float32) * D ** -0.5
    diff_w_o = jax.random.normal(ks[8], (N_DIFF, DIFF_HEADS * DIFF_V_DIM, D), jnp.float32) * D ** -0.5
    diff_lambda = 0.1 * jax.random.normal(ks[9], (N_DIFF, 4, DIFF_QK_DIM), jnp.float32)
    diff_subln = 1.0 + 0.02 * jax.random.normal(ks[10], (N_DIFF, DIFF_V_DIM), jnp.float32)
    return {"x": x, "norm_gains": norm_gains, "final_gain": final_gain,
            "ffn_w_in": ffn_w_in, "ffn_w_out": ffn_w_out,
            "sb_w_qkv": sb_w_qkv, "sb_w_o": sb_w_o,
            "diff_w_qkv": diff_w_qkv, "diff_w_o": diff_w_o,
            "diff_lambda": diff_lambda, "diff_subln": diff_subln}


def reference(x, norm_gains, final_gain, ffn_w_in, ffn_w_out, sb_w_qkv, sb_w_o,
              diff_w_qkv, diff_w_o, diff_lambda, diff_subln):
    h = x
    for i in range(DEPTH):
        h = h + 0.5 * _swiglu_ffn(_rms_norm(h, norm_gains[i, 0]), ffn_w_in[i, 0], ffn_w_out[i, 0])
        hn = _rms_norm(h, norm_gains[i, 1])
        j = i // N_MIXERS
        if i % N_MIXERS == 0:
            mix = _stick_breaking_mixer(hn, sb_w_qkv[j], sb_w_o[j])
        else:
            mix = _diff_attention_mixer(hn, diff_w_qkv[j], diff_w_o[j], diff_lambda[j],
                                        diff_subln[j], i + 1)
        h = h + mix
        h = h + 0.5 * _swiglu_ffn(_rms_norm(h, norm_gains[i, 2]), ffn_w_in[i, 1], ffn_w_out[i, 1])
    return _rms_norm(h, final_gain)
```

```python
import math
from contextlib import ExitStack

import numpy as np
import concourse.bass as bass
import concourse.mybir as mybir
from concourse.bass_utils import run_bass_kernel_spmd

F32 = mybir.dt.float32
BF16 = mybir.dt.bfloat16
AF = mybir.ActivationFunctionType
ALU = mybir.AluOpType

ENGS = ("pe", "act", "dve", "pool", "sp")

D_MODEL = 1024
SEQ = 2048
NCH = 8
D_FF = 2816
NF = 22
TG = 512
NG = 4
RMS_EPS = 1e-6
LAMBDA_INIT = 0.8 - 0.6 * math.exp(-0.3 * (2 - 1))
NSLOT = 6
MASKNEG = -30000.0


class Buf:
    __slots__ = ("name", "last_w", "readers")

    def __init__(self, name):
        self.name = name
        self.last_w = None
        self.readers = []


class Op:
    __slots__ = ("eng", "fn", "deps", "needed", "ticket", "dsem", "idx", "is_dma")

    def __init__(self, eng, fn, dsem=None):
        self.eng = eng
        self.fn = fn
        self.deps = []
        self.needed = False
        self.ticket = None
        self.dsem = dsem
        self.is_dma = dsem is not None
        self.idx = None


class Prog:
    def __init__(self):
        self.ops = {e: [] for e in ENGS}
        self.n = 0
        self.dma_keys = []
        self.last_dma = {}

    def op(self, eng, fn, reads=(), writes=(), dsem=None, extra_deps=()):
        o = Op(eng, fn, dsem)
        o.idx = self.n
        self.n += 1
        if dsem is not None:
            if dsem not in self.dma_keys:
                self.dma_keys.append(dsem)
        deps = {}
        for b in reads:
            if b.last_w is not None:
                deps[id(b.last_w)] = b.last_w
        for b in writes:
            if b.last_w is not None:
                deps[id(b.last_w)] = b.last_w
            for r in b.readers:
                deps[id(r)] = r
        for d in extra_deps:
            deps[id(d)] = d
        if dsem is not None and dsem in self.last_dma:
            d = self.last_dma[dsem]
            deps[id(d)] = d
        for d in deps.values():
            if d.eng == "pe" and eng == "pe" and not d.is_dma and not o.is_dma:
                continue
            o.deps.append(d)
            d.needed = True
        for b in reads:
            b.readers.append(o)
        for b in writes:
            b.last_w = o
            b.readers = []
        if dsem is not None:
            self.last_dma[dsem] = o
        self.ops[eng].append(o)
        return o

    def barrier(self):
        lasts = []
        for e in ENGS:
            for o in reversed(self.ops[e]):
                if not o.is_dma and o.fn is not None:
                    lasts.append(o)
                    break
        lasts.extend(self.last_dma.values())
        for e in ENGS:
            self.op(e, None, extra_deps=[d for d in lasts])

    def assign(self):
        cnt = {e: 0 for e in ENGS}
        dcnt = {k: 0 for k in self.dma_keys}
        allops = []
        for e in ENGS:
            allops.extend(self.ops[e])
        allops.sort(key=lambda o: o.idx)
        for o in allops:
            if o.is_dma:
                dcnt[o.dsem] += 16
                o.ticket = dcnt[o.dsem]
            elif o.needed and o.fn is not None:
                cnt[o.eng] += 1
                o.ticket = cnt[o.eng]

    def run_engine(self, eng, e, sems, dsems):
        waited = {}
        for o in self.ops[eng]:
            for d in o.deps:
                if d.ticket is None:
                    continue
                key = ("d", d.dsem) if d.is_dma else ("e", d.eng)
                if waited.get(key, 0) >= d.ticket:
                    continue
                waited[key] = d.ticket
                s = dsems[d.dsem] if d.is_dma else sems[d.eng]
                e.wait_ge(s, d.ticket)
            if o.fn is None:
                continue
            ins = o.fn(e)
            if o.is_dma:
                ins.then_inc(dsems[o.dsem], 16)
            elif o.needed:
                ins.then_inc(sems[o.eng], 1)


class WStream:
    def __init__(self, P, ring, nslot):
        self.P = P
        self.ring = ring
        self.nslot = nslot
        self.blocks = []
        self.issued = 0
        self.next = 0

    def plan(self, tag, src, n):
        self.blocks.append((tag, src, n))

    def _issue(self, k):
        tag, src, n = self.blocks[k]
        tile, buf = self.ring[k % self.nslot]
        s = k % self.nslot

        def fn(e, tile=tile, src=src, n=n):
            if n % 2 == 0 and n > 1024:
                return e.dma_start(out=tile[:, 0:n].rearrange("p (a b) -> p a b", a=2),
                                   in_=src.rearrange("p (a b) -> p a b", a=2))
            return e.dma_start(out=tile[:, 0:n], in_=src)
        self.P.op("pool", fn, writes=[buf], dsem=f"w{s}")

    def acquire(self, tag):
        k = self.next
        self.next += 1
        assert self.blocks[k][0] == tag, (self.blocks[k][0], tag)
        while self.issued < min(len(self.blocks), k + self.nslot - 1):
            self._issue(self.issued)
            self.issued += 1
        return self.ring[k % self.nslot]


class Builder:
    def __init__(self, stages):
        self.stages = stages
        nc = bass.Bass("TRN2", target_bir_lowering=False)
        self.nc = nc
        self.P = Prog()
        d = lambda name, shape, kind="ExternalInput": nc.dram_tensor(name, shape, F32, kind=kind).ap()
        self.d_x = d("xT", [128, NCH, SEQ])
        self.d_gains = d("gains", [128, 56])
        self.d_consts = d("consts", [128, 512])
        self.d_win = d("w_in", [4, NF, 128, 2048])
        self.d_wout = d("w_out", [4, NCH, 128, D_FF])
        self.d_wsb = d("w_sb", [16, 128, 2048])
        self.d_wdf = d("w_df", [24, 128, 2048])
        self.d_rope = d("rope", [128, 2, SEQ])
        self.d_lam = d("lam", [128, 256])
        self.d_subln = d("subln", [128, 1])
        self.d_out = d("outT", [128, NCH, SEQ], kind="ExternalOutput")

    def sb(self, name, shape, dt):
        return self.es.enter_context(self.nc.sbuf_tensor(name, shape, dt))

    def build(self):
        nc, P = self.nc, self.P
        with ExitStack() as es:
            self.es = es
            self.xT = self.sb("xT_sb", [128, NCH, SEQ], F32)
            self.HN = self.sb("hn_sb", [128, NCH * SEQ], BF16)
            self.R = self.sb("r_sb", [128, 16 * SEQ], BF16)
            self.ringt = [self.sb(f"ring{i}", [128, 2048], BF16) for i in range(NSLOT)]
            self.SCR = self.sb("scr", [128, 3072], F32)
            self.SCRB = self.sb("scrb", [128, 2048], BF16)
            self.cst = self.sb("cst", [128, 512], BF16)
            self.gains = self.sb("gains_sb", [128, 56], F32)
            self.lamt = self.sb("lam_sb", [128, 256], F32)
            self.small = self.sb("small_sb", [128, 8], F32)
            self.ropet = self.sb("rope_sb", [128, 2, TG], F32)
            self.psum = []
            for i in range(8):
                t = es.enter_context(nc.psum_tensor(f"ps{i}", [128, 512], F32))
                self.psum.append((t, Buf(f"ps{i}")))
            self.sems = {e: es.enter_context(nc.semaphore(f"s_{e}")) for e in ENGS}

            self.xb = [[Buf(f"x{c}_{g}") for g in range(NG)] for c in range(NCH)]
            self.ring = [(self.ringt[i], Buf(f"ring{i}")) for i in range(NSLOT)]
            self.scrb = [Buf(f"scr{i}") for i in range(6)]
            self.scrbb = [Buf(f"scrb{i}") for i in range(4)]
            self.cstb = Buf("cst")
            self.gainb = Buf("gains")
            self.lamb = Buf("lam")
            self.smallb = Buf("small")
            self.ropeb = Buf("rope")
            self.outb = [Buf(f"out{g}") for g in range(NG)]
            self.ws = WStream(P, self.ring, NSLOT)
            self.par = {}
            self.df_sched = [("A", 0), ("B", 0), ("C", 0)]
            self.sb_sched = [("A", 0), ("B1", 0), ("B2", 0), ("C", 0), ("D1", 0), ("D2", 0), ("E", 0)]

            self.plan_weights()
            self.emit_all()

            P.assign()
            dsems = {k: es.enter_context(nc.semaphore(f"d_{k}")) for k in P.dma_keys}
            sems = self.sems
            with nc.Block() as block:
                @block.sync
                def _(e):
                    P.run_engine("sp", e, sems, dsems)

                @block.gpsimd
                def _(e):
                    P.run_engine("pool", e, sems, dsems)

                @block.tensor
                def _(e):
                    P.run_engine("pe", e, sems, dsems)

                @block.scalar
                def _(e):
                    P.run_engine("act", e, sems, dsems)

                @block.vector
                def _(e):
                    P.run_engine("dve", e, sems, dsems)
        return nc

    def flip(self, key, n=2):
        v = self.par.get(key, 0)
        self.par[key] = (v + 1) % n
        return v

    def plan_weights(self):
        ws = self.ws
        for st in self.stages:
            kind = st[0]
            if kind == "ffn":
                widx = st[1]
                for (j0, j1) in ((0, 11), (11, 22)):
                    for j in range(j0, j1):
                        ws.plan(("win", widx, j), self.d_win[widx, j], 2048)
                    for c in range(NCH):
                        ws.plan(("wout", widx, j0, c), self.d_wout[widx, c, :, j0 * 128:j1 * 128], (j1 - j0) * 128)
            elif kind == "sb":
                for g in range(NG):
                    for blk in range(12):
                        ws.plan(("sb", blk, g), self.d_wsb[blk], 2048)
                    for blk in range(12, 16):
                        ws.plan(("sb", blk, g), self.d_wsb[blk], 2048)
            elif kind == "diff":
                for g in range(NG):
                    for blk in Builder.DF_ORDER:
                        ws.plan(("df", blk, g), self.d_wdf[blk], 2048)

    def emit_all(self):
        P = self.P
        P.op("pool", lambda e: e.dma_start(out=self.cst[:], in_=self.d_consts), writes=[self.cstb], dsem="cst")
        P.op("sp", lambda e: e.dma_start(out=self.gains[:], in_=self.d_gains), writes=[self.gainb], dsem="gains")
        for g in range(NG):
            P.op("sp", lambda e, g=g: e.dma_start(out=self.xT[:, :, g * TG:(g + 1) * TG],
                                                  in_=self.d_x[:, :, g * TG:(g + 1) * TG]),
                 writes=[self.xb[c][g] for c in range(NCH)], dsem=f"x{g}")
        first = True
        for st in self.stages:
            if not first:
                P.barrier()
            first = False
            if st[0] == "ffn":
                self.ffn(st[1], st[2])
            elif st[0] == "sb":
                self.sb_mixer(st[1])
            elif st[0] == "diff":
                self.diff_mixer(st[1])
            elif st[0] == "final":
                self.final_norm(st[1])
        P.barrier()
        for g in range(NG):
            P.op("sp", lambda e, g=g: e.dma_start(out=self.d_out[:, :, g * TG:(g + 1) * TG],
                                                  in_=self.xT[:, :, g * TG:(g + 1) * TG]),
                 reads=[self.xb[c][g] for c in range(NCH)], writes=[self.outb[g]], dsem=f"o{g}")
        P.op("sp", None, reads=self.outb)

    def norm_group(self, nidx, g, dst_fn, in_place=False):
        P = self.P
        tgs = slice(g * TG, (g + 1) * TG)
        ones = self.cst[:, 384:512]
        ps, psb = self.psum[6 + self.flip("norm")]
        xT = self.xT
        for c in range(NCH):
            k = self.flip("sq")
            sq = self.SCRB[:, (2 + k) * 512:(3 + k) * 512]
            sqb = self.scrbb[2 + k]
            P.op("act", lambda e, c=c, sq=sq: e.activation(out=sq, in_=xT[:, c, tgs], func=AF.Square),
                 reads=[self.xb[c][g]], writes=[sqb])
            P.op("pe", lambda e, c=c, sq=sq: e.matmul(ps[:], lhsT=ones, rhs=sq, start=(c == 0), stop=(c == NCH - 1)),
                 reads=[sqb, self.cstb], writes=[psb])
        lnt = self.SCR[:, 5 * 512:6 * 512]
        lnb = self.scrb[5]
        rs = self.SCR[:, 4 * 512:5 * 512]
        rsb = self.scrb[4]
        P.op("act", lambda e: e.activation(out=lnt, in_=ps[:], func=AF.Ln, scale=1.0 / D_MODEL, bias=RMS_EPS),
             reads=[psb], writes=[lnb])
        P.op("act", lambda e: e.activation(out=rs, in_=lnt, func=AF.Exp, scale=-0.5), reads=[lnb], writes=[rsb])
        for c in range(NCH):
            out_ap, obufs = dst_fn(c)
            gcol = self.gains[:, nidx * 8 + c:nidx * 8 + c + 1]
            rd = [self.xb[c][g], rsb, self.gainb]
            P.op("dve", lambda e, c=c, out_ap=out_ap, gcol=gcol: e.scalar_tensor_tensor(
                out=out_ap, in0=xT[:, c, tgs], scalar=gcol, in1=rs, op0=ALU.mult, op1=ALU.mult),
                reads=rd, writes=obufs)

    def ffn(self, widx, nidx):
        P = self.P
        xT = self.xT
        hn3 = self.HN[:].rearrange("p (c t) -> p c t", c=NCH)
        hnb = [[Buf(f"hn{c}_{g}") for g in range(NG)] for c in range(NCH)]
        actT = self.R[:, 0:11 * SEQ].rearrange("p (j t) -> p j t", j=11)
        actb = [[Buf(f"act{j}_{g}") for g in range(NG)] for j in range(11)]
        for g in range(NG):
            tgs = slice(g * TG, (g + 1) * TG)
            self.norm_group(nidx, g, lambda c, tgs=tgs, g=g: (hn3[:, c, tgs], [hnb[c][g]]))
        for (j0, j1) in ((0, 11), (11, 22)):
            nj = j1 - j0
            for j in range(j0, j1):
                slot, sbuf_ = self.ws.acquire(("win", widx, j))
                wv = slot[:, 0:2048].rearrange("p (g c m) -> p g c m", g=2, c=NCH)
                jj = j - j0
                for g in range(NG):
                    tgs = slice(g * TG, (g + 1) * TG)
                    par = self.flip("gu")
                    gps, gpb = self.psum[par * 2]
                    ups, upb = self.psum[par * 2 + 1]
                    for gu, (ps, pb) in enumerate(((gps, gpb), (ups, upb))):
                        for c in range(NCH):
                            P.op("pe", lambda e, ps=ps, gu=gu, c=c, wv=wv, tgs=tgs: e.matmul(
                                ps[:], lhsT=wv[:, gu, c, :], rhs=hn3[:, c, tgs], start=(c == 0), stop=(c == NCH - 1)),
                                reads=[sbuf_, hnb[c][g]], writes=[pb])
                    sg = self.SCR[:, par * 512:(par + 1) * 512]
                    sgb = self.scrb[par]
                    P.op("act", lambda e, sg=sg, gps=gps: e.activation(out=sg, in_=gps[:], func=AF.Silu),
                         reads=[gpb], writes=[sgb])
                    P.op("dve", lambda e, sg=sg, ups=ups, jj=jj, tgs=tgs: e.tensor_tensor(
                        out=actT[:, jj, tgs], in0=sg, in1=ups[:], op=ALU.mult),
                        reads=[sgb, upb], writes=[actb[jj][g]])
            for c in range(NCH):
                slot, sbuf_ = self.ws.acquire(("wout", widx, j0, c))
                wv = slot[:, 0:nj * 128].rearrange("p (j m) -> p j m", m=128)
                for g in range(NG):
                    tgs = slice(g * TG, (g + 1) * TG)
                    ps, pb = self.psum[4 + self.flip("y")]
                    for jj in range(nj):
                        P.op("pe", lambda e, ps=ps, jj=jj, wv=wv, tgs=tgs: e.matmul(
                            ps[:], lhsT=wv[:, jj, :], rhs=actT[:, jj, tgs], start=(jj == 0), stop=(jj == nj - 1)),
                            reads=[sbuf_, actb[jj][g]], writes=[pb])
                    P.op("dve", lambda e, ps=ps, c=c, tgs=tgs: e.scalar_tensor_tensor(
                        out=xT[:, c, tgs], in0=ps[:], scalar=0.5, in1=xT[:, c, tgs], op0=ALU.mult, op1=ALU.add),
                        reads=[pb, self.xb[c][g]], writes=[self.xb[c][g]])

    def final_norm(self, nidx):
        xT = self.xT
        for g in range(NG):
            tgs = slice(g * TG, (g + 1) * TG)
            self.norm_group(nidx, g, lambda c, tgs=tgs, g=g: (xT[:, c, tgs], [self.xb[c][g]]))

    def proj_fm(self, slot, sbuf_, half, src3, srcb, evac):
        P = self.P
        wv = slot[:, 0:2048].rearrange("p (c m) -> p c m", c=NCH)
        ps, pb = self.psum[6 + self.flip("norm")]
        for c in range(NCH):
            P.op("pe", lambda e, ps=ps, c=c, wv=wv, half=half: e.matmul(
                ps[:], lhsT=wv[:, c, half * 128:(half + 1) * 128], rhs=src3[:, c, :],
                start=(c == 0), stop=(c == NCH - 1)),
                reads=[sbuf_, srcb[c]], writes=[pb])
        evac(ps, pb)

    def sb_mixer(self, nidx):
        P = self.P
        xT = self.xT
        kT = self.R[:, 0:NCH * SEQ].rearrange("p (m t) -> p m t", m=NCH)
        vT = self.R[:, NCH * SEQ:16 * SEQ].rearrange("p (i f) -> p i f", i=16)
        kb = [[Buf(f"k{m}_{g}") for g in range(NG)] for m in range(NCH)]
        vb = [[Buf(f"v{i}_{nb}") for nb in range(4)] for i in range(16)]
        hn_g = self.HN[:, 0:4096].rearrange("p (c t) -> p c t", c=NCH)
        q_g = self.HN[:, 4096:8192].rearrange("p (c t) -> p c t", c=NCH)
        o_g = self.HN[:, 8192:12288].rearrange("p (c t) -> p c t", c=NCH)
        hngb = [Buf(f"hng{c}") for c in range(NCH)]
        qgb = [Buf(f"qg{c}") for c in range(NCH)]
        ogb = [Buf(f"og{c}") for c in range(NCH)]
        ident = self.cst[:, 0:128]
        tinc = self.cst[:, 128:256]
        maskneg = self.cst[:, 256:384]
        ones = self.cst[:, 384:512]
        cstb = self.cstb
        NE = 2
        e_t = [(self.SCR[:, k * 512:(k + 1) * 512], self.scrb[k]) for k in range(2)]
        ecs_t = [(self.SCR[:, (2 + k) * 512:(3 + k) * 512], self.scrb[2 + k]) for k in range(2)]
        sp_t = [(self.SCRB[:, k * 512:(k + 1) * 512], self.scrbb[k]) for k in range(2)]
        aT_t = [(self.HN[:, 12288 + k * 512:12288 + (k + 1) * 512], Buf(f"aT{k}")) for k in range(2)]
        R_t = [(self.HN[:, 13312 + k * 512:13312 + (k + 1) * 512], Buf(f"R{k}")) for k in range(2)]

        class T:
            pass

        for g in range(NG):
            tgs = slice(g * TG, (g + 1) * TG)
            self.norm_group(nidx, g, lambda c: (hn_g[:, c, :], [hngb[c]]))
            for kind in range(2):
                for nb in range(4):
                    slot, sbuf_ = self.ws.acquire(("sb", kind * 4 + nb, g))
                    for half in range(2):
                        m = nb * 2 + half
                        if kind == 0:
                            def evac(ps, pb, m=m):
                                P.op("act", lambda e: e.activation(out=q_g[:, m, :], in_=ps[:], func=AF.Copy, scale=0.125),
                                     reads=[pb], writes=[qgb[m]])
                        else:
                            def evac(ps, pb, m=m, tgs=tgs, g=g):
                                P.op("dve", lambda e: e.tensor_copy(out=kT[:, m, tgs], in_=ps[:]),
                                     reads=[pb], writes=[kb[m][g]])
                        self.proj_fm(slot, sbuf_, half, hn_g, hngb, evac)
            for nb in range(4):
                slot, sbuf_ = self.ws.acquire(("sb", 8 + nb, g))
                wv = slot[:, 0:2048].rearrange("p (c m) -> p c m", c=NCH)
                for tt in range(4):
                    ti = g * 4 + tt
                    ps, pb = self.psum[6 + self.flip("norm")]
                    for c in range(NCH):
                        P.op("pe", lambda e, ps=ps, c=c, wv=wv, tt=tt: e.matmul(
                            ps[:, 0:256], lhsT=hn_g[:, c, tt * 128:(tt + 1) * 128], rhs=wv[:, c, :],
                            start=(c == 0), stop=(c == NCH - 1)),
                            reads=[sbuf_, hngb[c]], writes=[pb])
                    P.op("dve", lambda e, ps=ps, ti=ti, nb=nb: e.tensor_copy(out=vT[:, ti, nb * 256:(nb + 1) * 256], in_=ps[:, 0:256]),
                         reads=[pb], writes=[vb[ti][nb]])
            tasks = []
            for m in range(NCH):
                for half in range(2):
                    tiles = list(range(4 * g + 3, -1, -1))
                    for idx, i in enumerate(tiles):
                        t = T()
                        t.m, t.half, t.h, t.p0, t.i, t.idx = m, half, 2 * m + half, 64 * half, i, idx
                        t.qoff = max(0, i - 4 * g) * 128
                        t.diag = i >= 4 * g
                        t.first = idx == 0
                        t.last = idx == len(tiles) - 1
                        t.nqoff = 0 if t.last else max(0, tiles[idx + 1] - 4 * g) * 128
                        t.n = len(tasks)
                        tasks.append(t)
            o_cur = {}

            def stA(t):
                t.z_ps, t.zb = self.psum[t.n % 2]
                qo = t.qoff
                P.op("pe", lambda e: e.matmul(
                    t.z_ps[:, qo:512], lhsT=kT[t.p0:t.p0 + 64, t.m, t.i * 128:(t.i + 1) * 128],
                    rhs=q_g[t.p0:t.p0 + 64, t.m, qo:512], start=True, stop=(not t.diag)),
                    reads=[kb[t.m][t.i // 4], qgb[t.m]], writes=[t.zb])
                if t.diag:
                    P.op("pe", lambda e: e.matmul(
                        t.z_ps[:, qo:qo + 128], lhsT=ident, rhs=maskneg, start=False, stop=True),
                        reads=[cstb], writes=[t.zb])

            def stB1(t):
                t.et, t.eb = e_t[t.n % NE]
                qo = t.qoff
                P.op("act", lambda e: e.activation(out=t.et[:, qo:512], in_=t.z_ps[:, qo:512], func=AF.Exp),
                     reads=[t.zb], writes=[t.eb])

            def stB2(t):
                t.spt, t.spb = sp_t[t.n % 2]
                qo = t.qoff
                P.op("act", lambda e: e.activation(out=t.spt[:, qo:512], in_=t.et[:, qo:512], func=AF.Ln, bias=1.0),
                     reads=[t.eb], writes=[t.spb])

            def stC(t):
                t.cs_ps, t.csb = self.psum[2 + t.n % 2]
                qo = t.qoff
                P.op("pe", lambda e: e.matmul(
                    t.cs_ps[:, qo:512], lhsT=tinc, rhs=t.spt[:, qo:512], start=True, stop=t.first),
                    reads=[t.spb, cstb], writes=[t.csb])
                if not t.first:
                    rt, rb = t.rprev
                    P.op("pe", lambda e: e.matmul(
                        t.cs_ps[:, qo:512], lhsT=ones, rhs=rt[:, qo:512], start=False, stop=True),
                        reads=[rb, cstb], writes=[t.csb])
                if not t.last:
                    rn, rnb = R_t[t.n % 2]
                    if t.first:
                        P.op("dve", lambda e: e.tensor_copy(out=rn[:, qo:512], in_=t.spt[:, qo:512]),
                             reads=[t.spb], writes=[rnb])
                    else:
                        rt, rb = t.rprev
                        P.op("dve", lambda e: e.tensor_tensor(
                            out=rn[:, qo:512], in0=rt[:, qo:512], in1=t.spt[:, qo:512], op=ALU.add),
                            reads=[t.spb, rb], writes=[rnb])
                    if t.nqoff < qo:
                        P.op("dve", lambda e: e.memset(rn[:, t.nqoff:qo], 0.0), reads=[], writes=[rnb])
                    tasks[t.n + 1].rprev = (rn, rnb)

            def stD1(t):
                t.ect, t.ecb = ecs_t[t.n % 2]
                qo = t.qoff
                P.op("act", lambda e: e.activation(out=t.ect[:, qo:512], in_=t.cs_ps[:, qo:512], func=AF.Exp, scale=-1.0),
                     reads=[t.csb], writes=[t.ecb])

            def stD2(t):
                t.at, t.ab = aT_t[t.n % 2]
                qo = t.qoff
                P.op("dve", lambda e: e.tensor_tensor(
                    out=t.at[:, qo:512], in0=t.et[:, qo:512], in1=t.ect[:, qo:512], op=ALU.mult),
                    reads=[t.eb, t.ecb], writes=[t.ab])

            def stE(t):
                if t.half == 0 and t.first:
                    o_cur[t.m] = self.psum[4 + self.flip("y")]
                o_ps, opb = o_cur[t.m]
                qo = t.qoff
                P.op("pe", lambda e: e.matmul(
                    o_ps[t.p0:t.p0 + 64, qo:512], lhsT=vT[:, t.i, t.h * 64:(t.h + 1) * 64], rhs=t.at[:, qo:512],
                    start=t.first, stop=t.last),
                    reads=[vb[t.i][t.h // 4], t.ab], writes=[opb])
                if t.half == 1 and t.last:
                    m = t.m
                    P.op("act", lambda e: e.activation(out=o_g[:, m, :], in_=o_ps[:], func=AF.Copy),
                         reads=[opb], writes=[ogb[m]])

            sched = self.sb_sched
            n = len(tasks)
            maxlag = max(l for (_, l) in sched)
            fns = {"A": stA, "B1": stB1, "B2": stB2, "C": stC, "D1": stD1, "D2": stD2, "E": stE}
            for s in range(n + maxlag):
                for (name, lag) in sched:
                    k = s - lag
                    if 0 <= k < n:
                        fns[name](tasks[k])
            for nb in range(4):
                slot, sbuf_ = self.ws.acquire(("sb", 12 + nb, g))
                for half in range(2):
                    c = nb * 2 + half

                    def evac(ps, pb, c=c, tgs=tgs, g=g):
                        P.op("dve", lambda e: e.tensor_tensor(
                            out=xT[:, c, tgs], in0=ps[:], in1=xT[:, c, tgs], op=ALU.add),
                            reads=[pb, self.xb[c][g]], writes=[self.xb[c][g]])
                    self.proj_fm(slot, sbuf_, half, o_g, ogb, evac)

    DF_ORDER = [0, 4, 1, 5, 2, 6, 3, 7, 8, 12, 9, 13, 10, 14, 11, 15, 16, 17, 18, 19, 20, 21, 22, 23]

    def diff_mixer(self, nidx):
        P = self.P
        xT = self.xT
        kT = self.R[:, 0:NCH * SEQ].rearrange("p (m t) -> p m t", m=NCH)
        vT = self.R[:, NCH * SEQ:16 * SEQ].rearrange("p (i f) -> p i f", i=16)
        kb = [[Buf(f"k{m}_{g}") for g in range(NG)] for m in range(NCH)]
        vb = [[Buf(f"v{i}_{nb}") for nb in range(4)] for i in range(16)]
        hn_g = self.HN[:, 0:4096].rearrange("p (c t) -> p c t", c=NCH)
        q_g = self.HN[:, 4096:8192].rearrange("p (c t) -> p c t", c=NCH)
        o_g = self.HN[:, 8192:12288].rearrange("p (c t) -> p c t", c=NCH)
        hngb = [Buf(f"hng{c}") for c in range(NCH)]
        qgb = [Buf(f"qg{c}") for c in range(NCH)]
        ogb = [Buf(f"og{c}") for c in range(NCH)]
        ones = self.cst[:, 384:512]
        cstb = self.cstb
        NP = 3
        p_t = [(self.HN[:, 12288 + k * 512:12288 + (k + 1) * 512], Buf(f"pT{k}")) for k in range(NP)]
        sqo, sqob = self.HN[:, 13824:14336], Buf("sqo")
        scr = lambda k: (self.SCR[:, k * 512:(k + 1) * 512], self.scrb[k])
        small, smallb = self.small, self.smallb
        ropet, ropeb = self.ropet, self.ropeb

        tmp, tmpb = scr(0)
        P.op("sp", lambda e: e.dma_start(out=self.lamt[:], in_=self.d_lam), writes=[self.lamb], dsem="lam")
        P.op("sp", lambda e: e.dma_start(out=small[:, 6:7], in_=self.d_subln), writes=[smallb], dsem="subln")
        for k in range(2):
            P.op("dve", lambda e, k=k: e.tensor_tensor(out=tmp[:, k * 64:(k + 1) * 64], in0=self.lamt[:, k * 128:k * 128 + 64],
                                                       in1=self.lamt[:, k * 128 + 64:k * 128 + 128], op=ALU.mult),
                 reads=[self.lamb], writes=[tmpb])
            P.op("dve", lambda e, k=k: e.reduce_sum(out=small[:, k:k + 1], in_=tmp[:, k * 64:(k + 1) * 64], axis=mybir.AxisListType.X),
                 reads=[tmpb], writes=[smallb])
        P.op("act", lambda e: e.activation(out=small[:, 2:4], in_=small[:, 0:2], func=AF.Exp), reads=[smallb], writes=[smallb])
        P.op("dve", lambda e: e.tensor_tensor(out=small[:, 4:5], in0=small[:, 3:4], in1=small[:, 2:3], op=ALU.subtract),
             reads=[smallb], writes=[smallb])
        P.op("dve", lambda e: e.tensor_scalar(out=small[:, 4:5], in0=small[:, 4:5], scalar1=-LAMBDA_INIT, scalar2=None, op0=ALU.add),
             reads=[smallb], writes=[smallb])
        P.op("dve", lambda e: e.tensor_scalar(out=small[:, 5:6], in0=small[:, 6:7], scalar1=(1.0 - LAMBDA_INIT), scalar2=None, op0=ALU.mult),
             reads=[smallb], writes=[smallb])
        neglam = small[:, 4:5]
        subg = small[:, 5:6]

        class T:
            pass

        for g in range(NG):
            tgs = slice(g * TG, (g + 1) * TG)
            self.norm_group(nidx, g, lambda c: (hn_g[:, c, :], [hngb[c]]))
            P.op("sp", lambda e, tgs=tgs: e.dma_start(out=ropet[:], in_=self.d_rope[:, :, tgs]), writes=[ropeb], dsem="rope")
            for kind in range(2):
                for nb in range(4):
                    slotA, sbA = self.ws.acquire(("df", kind * 8 + nb, g))
                    slotB, sbB = self.ws.acquire(("df", kind * 8 + 4 + nb, g))
                    for half in range(2):
                        m = nb * 2 + half
                        pair = self.flip("rp")
                        banks = (6, 7) if pair == 0 else (2, 3)
                        pss = []
                        for slot, sbuf_, bk in ((slotA, sbA, banks[0]), (slotB, sbB, banks[1])):
                            wv = slot[:, 0:2048].rearrange("p (c m) -> p c m", c=NCH)
                            ps, pb = self.psum[bk]
                            for c in range(NCH):
                                P.op("pe", lambda e, ps=ps, c=c, wv=wv, half=half: e.matmul(
                                    ps[:], lhsT=wv[:, c, half * 128:(half + 1) * 128], rhs=hn_g[:, c, :],
                                    start=(c == 0), stop=(c == NCH - 1)),
                                    reads=[sbuf_, hngb[c]], writes=[pb])
                            pss.append((ps, pb))
                        (t1, t1b), (t2, t2b) = scr(pair * 2), scr(pair * 2 + 1)
                        (ps1, pb1), (ps2, pb2) = pss
                        P.op("dve", lambda e, t1=t1, ps1=ps1: e.tensor_tensor(out=t1, in0=ps1[:], in1=ropet[:, 0, :], op=ALU.mult),
                             reads=[pb1, ropeb], writes=[t1b])
                        P.op("dve", lambda e, t2=t2, ps2=ps2: e.tensor_tensor(out=t2, in0=ps2[:], in1=ropet[:, 1, :], op=ALU.mult),
                             reads=[pb2, ropeb], writes=[t2b])
                        if kind == 0:
                            P.op("dve", lambda e, t1=t1, t2=t2, m=m: e.tensor_tensor(out=q_g[:, m, :], in0=t1, in1=t2, op=ALU.add),
                                 reads=[t1b, t2b], writes=[qgb[m]])
                        else:
                            P.op("dve", lambda e, t1=t1, t2=t2, m=m, tgs=tgs: e.tensor_tensor(out=kT[:, m, tgs], in0=t1, in1=t2, op=ALU.add),
                                 reads=[t1b, t2b], writes=[kb[m][g]])
            for nb in range(4):
                slot, sbuf_ = self.ws.acquire(("df", 16 + nb, g))
                wv = slot[:, 0:2048].rearrange("p (c m) -> p c m", c=NCH)
                for tt in range(4):
                    ti = g * 4 + tt
                    ps, pb = self.psum[6 + self.flip("norm")]
                    for c in range(NCH):
                        P.op("pe", lambda e, ps=ps, c=c, wv=wv, tt=tt: e.matmul(
                            ps[:, 0:256], lhsT=hn_g[:, c, tt * 128:(tt + 1) * 128], rhs=wv[:, c, :],
                            start=(c == 0), stop=(c == NCH - 1)),
                            reads=[sbuf_, hngb[c]], writes=[pb])
                    P.op("dve", lambda e, ps=ps, ti=ti, nb=nb: e.tensor_copy(out=vT[:, ti, nb * 256:(nb + 1) * 256], in_=ps[:, 0:256]),
                         reads=[pb], writes=[vb[ti][nb]])
            tasks = []
            for h in range(NCH):
                for mp in range(2):
                    ntl = 4 * g + 4
                    for i in range(ntl):
                        t = T()
                        t.h, t.mp, t.p0, t.i = h, mp, 64 * mp, i
                        t.qoff = max(0, i - 4 * g) * 128
                        t.diag = i >= 4 * g
                        t.first = i == 0
                        t.last = i == ntl - 1
                        t.n = len(tasks)
                        tasks.append(t)
            acc = {}

            def stA(t):
                t.s_ps, t.sb_ = self.psum[t.n % 2]
                qo = t.qoff
                P.op("pe", lambda e: e.matmul(
                    t.s_ps[:, qo:512], lhsT=kT[t.p0:t.p0 + 64, t.h, t.i * 128:(t.i + 1) * 128],
                    rhs=q_g[t.p0:t.p0 + 64, t.h, qo:512], start=True, stop=True),
                    reads=[kb[t.h][t.i // 4], qgb[t.h]], writes=[t.sb_])

            def stB(t):
                t.pt, t.ptb = p_t[t.n % NP]
                qo = t.qoff
                P.op("act", lambda e: e.activation(out=t.pt[:, qo:512], in_=t.s_ps[:, qo:512], func=AF.Exp, scale=0.125),
                     reads=[t.sb_], writes=[t.ptb])
                if t.diag:
                    P.op("dve", lambda e: e.memset(t.pt[64:128, qo:qo + 64], 0.0), reads=[], writes=[t.ptb])

            def stC(t):
                key = (t.h, t.mp)
                if t.first:
                    k = self.flip("acc")
                    acc[key] = (self.psum[2 + k], self.psum[4 + k])
                (den_ps, denb), (o_ps, opb) = acc[key]
                qo = t.qoff
                P.op("pe", lambda e: e.matmul(den_ps[:, qo:512], lhsT=ones, rhs=t.pt[:, qo:512], start=t.first, stop=t.last),
                     reads=[t.ptb, cstb], writes=[denb])
                P.op("pe", lambda e: e.matmul(o_ps[:, qo:512], lhsT=vT[:, t.i, t.h * 128:(t.h + 1) * 128], rhs=t.pt[:, qo:512],
                                              start=t.first, stop=t.last),
                     reads=[t.ptb, vb[t.i][t.h // 2]], writes=[opb])
                if t.last:
                    (lnd, lndb), (rden, rdenb) = scr(2), scr(3)
                    (o1n, o1nb), (od, odb) = scr(0), scr(1)
                    P.op("act", lambda e: e.activation(out=lnd, in_=den_ps[:], func=AF.Ln), reads=[denb], writes=[lndb])
                    P.op("act", lambda e: e.activation(out=rden, in_=lnd, func=AF.Exp, scale=-1.0), reads=[lndb], writes=[rdenb])
                    if t.mp == 0:
                        P.op("dve", lambda e: e.tensor_tensor(out=o1n, in0=o_ps[:], in1=rden, op=ALU.mult),
                             reads=[opb, rdenb], writes=[o1nb])
                    else:
                        h = t.h
                        P.op("dve", lambda e: e.tensor_tensor(out=od, in0=o_ps[:], in1=rden, op=ALU.mult),
                             reads=[opb, rdenb], writes=[odb])
                        P.op("dve", lambda e: e.scalar_tensor_tensor(out=od, in0=od, scalar=neglam, in1=o1n, op0=ALU.mult, op1=ALU.add),
                             reads=[odb, o1nb, smallb], writes=[odb])
                        P.op("act", lambda e: e.activation(out=sqo, in_=od, func=AF.Square), reads=[odb], writes=[sqob])
                        ss_ps, ssb = self.psum[6 + self.flip("norm")]
                        P.op("pe", lambda e: e.matmul(ss_ps[:], lhsT=ones, rhs=sqo, start=True, stop=True),
                             reads=[sqob, cstb], writes=[ssb])
                        (rs, rsb), (lnt, lnb) = scr(4), scr(5)
                        P.op("act", lambda e: e.activation(out=lnt, in_=ss_ps[:], func=AF.Ln, scale=1.0 / 128.0, bias=RMS_EPS),
                             reads=[ssb], writes=[lnb])
                        P.op("act", lambda e: e.activation(out=rs, in_=lnt, func=AF.Exp, scale=-0.5), reads=[lnb], writes=[rsb])
                        P.op("dve", lambda e: e.scalar_tensor_tensor(out=o_g[:, h, :], in0=od, scalar=subg, in1=rs, op0=ALU.mult, op1=ALU.mult),
                             reads=[odb, rsb, smallb], writes=[ogb[h]])

            sched = self.df_sched
            n = len(tasks)
            maxlag = max(l for (_, l) in sched)
            fns = {"A": stA, "B": stB, "C": stC}
            for s in range(n + maxlag):
                for (name, lag) in sched:
                    k = s - lag
                    if 0 <= k < n:
                        fns[name](tasks[k])
            dbg = getattr(self, "dbg", None)
            if dbg:
                srcs = {"o": (o_g, ogb), "q": (q_g, qgb), "hn": (hn_g, hngb)}
                if dbg == "k":
                    for m in range(NCH):
                        P.op("dve", lambda e, m=m, tgs=tgs: e.tensor_copy(out=xT[:, m, tgs], in_=kT[:, m, tgs]),
                             reads=[kb[m][g], self.xb[m][g]], writes=[self.xb[m][g]])
                elif dbg == "v":
                    for m in range(NCH):
                        P.op("dve", lambda e, m=m, tgs=tgs: e.tensor_copy(out=xT[:, m, tgs], in_=vT[:, g * 4 + m // 2, (m % 2) * 512:(m % 2) * 512 + 512]),
                             reads=[vb[g * 4 + m // 2][nb] for nb in range(4)] + [self.xb[m][g]], writes=[self.xb[m][g]])
                else:
                    s3, sb3 = srcs[dbg]
                    for m in range(NCH):
                        P.op("dve", lambda e, m=m, tgs=tgs, s3=s3: e.tensor_copy(out=xT[:, m, tgs], in_=s3[:, m, :]),
                             reads=[sb3[m], self.xb[m][g]], writes=[self.xb[m][g]])
                for nb in range(4):
                    self.ws.acquire(("df", 20 + nb, g))
                continue
            for nb in range(4):
                slot, sbuf_ = self.ws.acquire(("df", 20 + nb, g))
                for half in range(2):
                    c = nb * 2 + half

                    def evac(ps, pb, c=c, tgs=tgs, g=g):
                        P.op("dve", lambda e: e.tensor_tensor(
                            out=xT[:, c, tgs], in0=ps[:], in1=xT[:, c, tgs], op=ALU.add),
                            reads=[pb, self.xb[c][g]], writes=[self.xb[c][g]])
                    self.proj_fm(slot, sbuf_, half, o_g, ogb, evac)


def _consts():
    c = np.zeros((128, 512), np.float32)
    c[:, 0:128] = np.eye(128, dtype=np.float32)
    j = np.arange(128)[:, None]
    k = np.arange(128)[None, :]
    c[:, 128:256] = (j >= k).astype(np.float32)
    c[:, 256:384] = np.where(j >= k, MASKNEG, 0.0)
    c[:, 384:512] = 1.0
    return c


def _layout_weights(inputs):
    w_in = np.asarray(inputs["ffn_w_in"], np.float32).reshape(4, NCH, 128, 2, NF, 128)
    w_in = np.ascontiguousarray(w_in.transpose(0, 4, 2, 3, 1, 5)).reshape(4, NF, 128, 2048)
    w_out = np.asarray(inputs["ffn_w_out"], np.float32).reshape(4, NF, 128, NCH, 128)
    w_out = np.ascontiguousarray(w_out.transpose(0, 3, 2, 1, 4)).reshape(4, NCH, 128, D_FF)

    def colblocks(w, ncols):
        nb = ncols // 256
        a = w.reshape(NCH, 128, nb, 256).transpose(2, 1, 0, 3)
        return np.ascontiguousarray(a).reshape(nb, 128, 2048)
    sbq = np.asarray(inputs["sb_w_qkv"], np.float32)[0]
    sbo = np.asarray(inputs["sb_w_o"], np.float32)[0]
    w_sb = np.concatenate([colblocks(sbq, 3072), colblocks(sbo, 1024)], axis=0)
    dq = np.asarray(inputs["diff_w_qkv"], np.float32)[0]
    do = np.asarray(inputs["diff_w_o"], np.float32)[0]
    perm = np.arange(1024).reshape(16, 2, 32)[:, ::-1, :].reshape(1024)
    w_df = np.concatenate([colblocks(dq[:, 0:1024], 1024), colblocks(dq[:, 0:1024][:, perm], 1024),
                           colblocks(dq[:, 1024:2048], 1024), colblocks(dq[:, 1024:2048][:, perm], 1024),
                           colblocks(dq[:, 2048:3072], 1024), colblocks(do, 1024)], axis=0)
    return w_in, w_out, w_sb, w_df


def _rope_tables():
    p = np.arange(128)
    i = p % 32
    inv = 10000.0 ** (-(i.astype(np.float32)) / 32.0)
    ang = np.arange(SEQ, dtype=np.float32)[None, :] * inv[:, None].astype(np.float32)
    cos = np.cos(ang).astype(np.float32)
    sin = np.sin(ang).astype(np.float32)
    sgn = np.where((p % 64) < 32, -1.0, 1.0).astype(np.float32)[:, None]
    return np.ascontiguousarray(np.stack([cos, sin * sgn], axis=1))


FULL_STAGES = [("ffn", 0, 0), ("sb", 1), ("ffn", 1, 2), ("ffn", 2, 3), ("diff", 4), ("ffn", 3, 5), ("final", 6)]
_CACHE = {}


def _prepare(inputs):
    w_in, w_out, w_sb, w_df = _layout_weights(inputs)
    ng = np.asarray(inputs["norm_gains"], np.float32).reshape(6, NCH, 128)
    fg = np.asarray(inputs["final_gain"], np.float32).reshape(1, NCH, 128)
    gains = np.ascontiguousarray(np.concatenate([ng, fg], 0).transpose(2, 0, 1)).reshape(128, 56)
    lam = np.ascontiguousarray(np.broadcast_to(np.asarray(inputs["diff_lambda"], np.float32).reshape(1, 256), (128, 256)))
    subln = np.ascontiguousarray(np.asarray(inputs["diff_subln"], np.float32).reshape(128, 1))
    shared = {"gains": gains, "consts": _consts(), "w_in": w_in, "w_out": w_out, "w_sb": w_sb, "w_df": w_df,
              "rope": _rope_tables(), "lam": lam, "subln": subln}
    return shared


def _x_to_dev(xb):
    return np.ascontiguousarray(xb.T.reshape(NCH, 128, SEQ).transpose(1, 0, 2))


def _x_from_dev(y):
    return np.ascontiguousarray(y.transpose(1, 0, 2).reshape(D_MODEL, SEQ).T)


def run_stages(stages, xs, shared, core_ids=None):
    key = tuple(stages)
    if key not in _CACHE:
        _CACHE[key] = Builder(list(stages)).build()
    nc = _CACHE[key]
    n = len(xs)
    in_maps = [dict(shared, xT=_x_to_dev(np.asarray(x, np.float32))) for x in xs]
    res = run_bass_kernel_spmd(nc, in_maps, core_ids=list(range(n)))
    return [_x_from_dev(r["outT"]) for r in res.results]


def kernel(**inputs):
    shared = _prepare(inputs)
    x = np.asarray(inputs["x"], np.float32)
    outs = run_stages(FULL_STAGES, [x[b] for b in range(x.shape[0])], shared)
    return np.stack(outs, axis=0).astype(np.float32)
```

```python
import math
from contextlib import ExitStack

import numpy as np
import concourse.bass as bass
import concourse.mybir as mybir
from concourse.bass_utils import run_bass_kernel_spmd

F32 = mybir.dt.float32
BF16 = mybir.dt.bfloat16
AF = mybir.ActivationFunctionType
ALU = mybir.AluOpType

ENGS = ("pe", "act", "dve", "pool", "sp")

D_MODEL = 1024
SEQ = 2048
NCH = 8
D_FF = 2816
NF = 22
TG = 512
NG = 4
RMS_EPS = 1e-6
LAMBDA_INIT = 0.8 - 0.6 * math.exp(-0.3 * (2 - 1))
NSLOT = 6
MASKNEG = -30000.0


class Buf:
    __slots__ = ("name", "last_w", "readers")

    def __init__(self, name):
        self.name = name
        self.last_w = None
        self.readers = []


class Op:
    __slots__ = ("eng", "fn", "deps", "needed", "ticket", "dsem", "idx", "is_dma")

    def __init__(self, eng, fn, dsem=None):
        self.eng = eng
        self.fn = fn
        self.deps = []
        self.needed = False
        self.ticket = None
        self.dsem = dsem
        self.is_dma = dsem is not None
        self.idx = None


class Prog:
    def __init__(self):
        self.ops = {e: [] for e in ENGS}
        self.n = 0
        self.dma_keys = []
        self.last_dma = {}

    def op(self, eng, fn, reads=(), writes=(), dsem=None, extra_deps=()):
        o = Op(eng, fn, dsem)
        o.idx = self.n
        self.n += 1
        if dsem is not None:
            if dsem not in self.dma_keys:
                self.dma_keys.append(dsem)
        deps = {}
        for b in reads:
            if b.last_w is not None:
                deps[id(b.last_w)] = b.last_w
        for b in writes:
            if b.last_w is not None:
                deps[id(b.last_w)] = b.last_w
            for r in b.readers:
                deps[id(r)] = r
        for d in extra_deps:
            deps[id(d)] = d
        if dsem is not None and dsem in self.last_dma:
            d = self.last_dma[dsem]
            deps[id(d)] = d
        for d in deps.values():
            if d.eng == "pe" and eng == "pe" and not d.is_dma and not o.is_dma:
                continue
            o.deps.append(d)
            d.needed = True
        for b in reads:
            b.readers.append(o)
        for b in writes:
            b.last_w = o
            b.readers = []
        if dsem is not None:
            self.last_dma[dsem] = o
        self.ops[eng].append(o)
        return o

    def barrier(self):
        lasts = []
        for e in ENGS:
            for o in reversed(self.ops[e]):
                if not o.is_dma and o.fn is not None:
                    lasts.append(o)
                    break
        lasts.extend(self.last_dma.values())
        for e in ENGS:
            self.op(e, None, extra_deps=[d for d in lasts])

    def assign(self):
        cnt = {e: 0 for e in ENGS}
        dcnt = {k: 0 for k in self.dma_keys}
        allops = []
        for e in ENGS:
            allops.extend(self.ops[e])
        allops.sort(key=lambda o: o.idx)
        for o in allops:
            if o.is_dma:
                dcnt[o.dsem] += 16
                o.ticket = dcnt[o.dsem]
            elif o.needed and o.fn is not None:
                cnt[o.eng] += 1
                o.ticket = cnt[o.eng]

    def run_engine(self, eng, e, sems, dsems):
        waited = {}
        for o in self.ops[eng]:
            for d in o.deps:
                if d.ticket is None:
                    continue
                key = ("d", d.dsem) if d.is_dma else ("e", d.eng)
                if waited.get(key, 0) >= d.ticket:
                    continue
                waited[key] = d.ticket
                s = dsems[d.dsem] if d.is_dma else sems[d.eng]
                e.wait_ge(s, d.ticket)
            if o.fn is None:
                continue
            ins = o.fn(e)
            if o.is_dma:
                ins.then_inc(dsems[o.dsem], 16)
            elif o.needed:
                ins.then_inc(sems[o.eng], 1)


class WStream:
    def __init__(self, P, ring, nslot):
        self.P = P
        self.ring = ring
        self.nslot = nslot
        self.blocks = []
        self.issued = 0
        self.next = 0

    def plan(self, tag, src, n):
        self.blocks.append((tag, src, n))

    def _issue(self, k):
        tag, src, n = self.blocks[k]
        tile, buf = self.ring[k % self.nslot]
        s = k % self.nslot

        def fn(e, tile=tile, src=src, n=n):
            if n % 2 == 0 and n > 1024:
                return e.dma_start(out=tile[:, 0:n].rearrange("p (a b) -> p a b", a=2),
                                   in_=src.rearrange("p (a b) -> p a b", a=2))
            return e.dma_start(out=tile[:, 0:n], in_=src)
        self.P.op("pool", fn, writes=[buf], dsem=f"w{s}")

    def acquire(self, tag):
        k = self.next
        self.next += 1
        assert self.blocks[k][0] == tag, (self.blocks[k][0], tag)
        while self.issued < min(len(self.blocks), k + self.nslot - 1):
            self._issue(self.issued)
            self.issued += 1
        return self.ring[k % self.nslot]


class Builder:
    def __init__(self, stages):
        self.stages = stages
        nc = bass.Bass("TRN2", target_bir_lowering=False)
        self.nc = nc
        self.P = Prog()
        d = lambda name, shape, kind="ExternalInput": nc.dram_tensor(name, shape, F32, kind=kind).ap()
        self.d_x = d("xT", [128, NCH, SEQ])
        self.d_gains = d("gains", [128, 56])
        self.d_consts = d("consts", [128, 512])
        self.d_win = d("w_in", [4, NF, 128, 2048])
        self.d_wout = d("w_out", [4, NCH, 128, D_FF])
        self.d_wsb = d("w_sb", [16, 128, 2048])
        self.d_wdf = d("w_df", [24, 128, 2048])
        self.d_rope = d("rope", [128, 2, SEQ])
        self.d_lam = d("lam", [128, 256])
        self.d_subln = d("subln", [128, 1])
        self.d_out = d("outT", [128, NCH, SEQ], kind="ExternalOutput")

    def sb(self, name, shape, dt):
        return self.es.enter_context(self.nc.sbuf_tensor(name, shape, dt))

    def build(self):
        nc, P = self.nc, self.P
        with ExitStack() as es:
            self.es = es
            self.xT = self.sb("xT_sb", [128, NCH, SEQ], F32)
            self.HN = self.sb("hn_sb", [128, NCH * SEQ], BF16)
            self.R = self.sb("r_sb", [128, 16 * SEQ], BF16)
            self.ringt = [self.sb(f"ring{i}", [128, 2048], BF16) for i in range(NSLOT)]
            self.SCR = self.sb("scr", [128, 3072], F32)
            self.SCRB = self.sb("scrb", [128, 2048], BF16)
            self.cst = self.sb("cst", [128, 512], BF16)
            self.gains = self.sb("gains_sb", [128, 56], F32)
            self.lamt = self.sb("lam_sb", [128, 256], F32)
            self.small = self.sb("small_sb", [128, 8], F32)
            self.ropet = self.sb("rope_sb", [128, 2, TG], F32)
            self.psum = []
            for i in range(8):
                t = es.enter_context(nc.psum_tensor(f"ps{i}", [128, 512], F32))
                self.psum.append((t, Buf(f"ps{i}")))
            self.sems = {e: es.enter_context(nc.semaphore(f"s_{e}")) for e in ENGS}

            self.xb = [[Buf(f"x{c}_{g}") for g in range(NG)] for c in range(NCH)]
            self.ring = [(self.ringt[i], Buf(f"ring{i}")) for i in range(NSLOT)]
            self.scrb = [Buf(f"scr{i}") for i in range(6)]
            self.scrbb = [Buf(f"scrb{i}") for i in range(4)]
            self.cstb = Buf("cst")
            self.gainb = Buf("gains")
            self.lamb = Buf("lam")
            self.smallb = Buf("small")
            self.ropeb = Buf("rope")
            self.outb = [Buf(f"out{g}") for g in range(NG)]
            self.ws = WStream(P, self.ring, NSLOT)
            self.par = {}
            self.df_sched = [("A", 0), ("B", 1), ("C", 2)]
            self.sb_sched = [("A", 0), ("B1", 1), ("D1", 3), ("B2", 1), ("C", 2), ("D2", 3), ("E", 4)]

            self.plan_weights()
            self.emit_all()

            P.assign()
            dsems = {k: es.enter_context(nc.semaphore(f"d_{k}")) for k in P.dma_keys}
            sems = self.sems
            with nc.Block() as block:
                @block.sync
                def _(e):
                    P.run_engine("sp", e, sems, dsems)

                @block.gpsimd
                def _(e):
                    P.run_engine("pool", e, sems, dsems)

                @block.tensor
                def _(e):
                    P.run_engine("pe", e, sems, dsems)

                @block.scalar
                def _(e):
                    P.run_engine("act", e, sems, dsems)

                @block.vector
                def _(e):
                    P.run_engine("dve", e, sems, dsems)
        return nc

    def flip(self, key, n=2):
        v = self.par.get(key, 0)
        self.par[key] = (v + 1) % n
        return v

    def plan_weights(self):
        ws = self.ws
        for st in self.stages:
            kind = st[0]
            if kind == "ffn":
                widx = st[1]
                for (j0, j1) in ((0, 11), (11, 22)):
                    for j in range(j0, j1):
                        ws.plan(("win", widx, j), self.d_win[widx, j], 2048)
                    for c in range(NCH):
                        ws.plan(("wout", widx, j0, c), self.d_wout[widx, c, :, j0 * 128:j1 * 128], (j1 - j0) * 128)
            elif kind == "sb":
                for g in range(NG):
                    for blk in range(12):
                        ws.plan(("sb", blk, g), self.d_wsb[blk], 2048)
                    for blk in range(12, 16):
                        ws.plan(("sb", blk, g), self.d_wsb[blk], 2048)
            elif kind == "diff":
                for g in range(NG):
                    for blk in Builder.DF_ORDER:
                        ws.plan(("df", blk, g), self.d_wdf[blk], 2048)

    def emit_all(self):
        P = self.P
        P.op("pool", lambda e: e.dma_start(out=self.cst[:], in_=self.d_consts), writes=[self.cstb], dsem="cst")
        P.op("sp", lambda e: e.dma_start(out=self.gains[:], in_=self.d_gains), writes=[self.gainb], dsem="gains")
        for g in range(NG):
            P.op("sp", lambda e, g=g: e.dma_start(out=self.xT[:, :, g * TG:(g + 1) * TG],
                                                  in_=self.d_x[:, :, g * TG:(g + 1) * TG]),
                 writes=[self.xb[c][g] for c in range(NCH)], dsem=f"x{g}")
        first = True
        for st in self.stages:
            if not first:
                P.barrier()
            first = False
            if st[0] == "ffn":
                self.ffn(st[1], st[2])
            elif st[0] == "sb":
                self.sb_mixer(st[1])
            elif st[0] == "diff":
                self.diff_mixer(st[1])
            elif st[0] == "final":
                self.final_norm(st[1])
        P.barrier()
        for g in range(NG):
            P.op("sp", lambda e, g=g: e.dma_start(out=self.d_out[:, :, g * TG:(g + 1) * TG],
                                                  in_=self.xT[:, :, g * TG:(g + 1) * TG]),
                 reads=[self.xb[c][g] for c in range(NCH)], writes=[self.outb[g]], dsem=f"o{g}")
        P.op("sp", None, reads=self.outb)

    def norm_group(self, nidx, g, dst_fn, in_place=False):
        P = self.P
        tgs = slice(g * TG, (g + 1) * TG)
        ones = self.cst[:, 384:512]
        ps, psb = self.psum[6 + self.flip("norm")]
        xT = self.xT
        for c in range(NCH):
            k = self.flip("sq")
            sq = self.SCRB[:, (2 + k) * 512:(3 + k) * 512]
            sqb = self.scrbb[2 + k]
            P.op("act", lambda e, c=c, sq=sq: e.activation(out=sq, in_=xT[:, c, tgs], func=AF.Square),
                 reads=[self.xb[c][g]], writes=[sqb])
            P.op("pe", lambda e, c=c, sq=sq: e.matmul(ps[:], lhsT=ones, rhs=sq, start=(c == 0), stop=(c == NCH - 1)),
                 reads=[sqb, self.cstb], writes=[psb])
        lnt = self.SCR[:, 5 * 512:6 * 512]
        lnb = self.scrb[5]
        rs = self.SCR[:, 4 * 512:5 * 512]
        rsb = self.scrb[4]
        P.op("act", lambda e: e.activation(out=lnt, in_=ps[:], func=AF.Ln, scale=1.0 / D_MODEL, bias=RMS_EPS),
             reads=[psb], writes=[lnb])
        P.op("act", lambda e: e.activation(out=rs, in_=lnt, func=AF.Exp, scale=-0.5), reads=[lnb], writes=[rsb])
        for c in range(NCH):
            out_ap, obufs = dst_fn(c)
            gcol = self.gains[:, nidx * 8 + c:nidx * 8 + c + 1]
            rd = [self.xb[c][g], rsb, self.gainb]
            P.op("dve", lambda e, c=c, out_ap=out_ap, gcol=gcol: e.scalar_tensor_tensor(
                out=out_ap, in0=xT[:, c, tgs], scalar=gcol, in1=rs, op0=ALU.mult, op1=ALU.mult),
                reads=rd, writes=obufs)

    def ffn(self, widx, nidx):
        P = self.P
        xT = self.xT
        hn3 = self.HN[:].rearrange("p (c t) -> p c t", c=NCH)
        hnb = [[Buf(f"hn{c}_{g}") for g in range(NG)] for c in range(NCH)]
        actT = self.R[:, 0:11 * SEQ].rearrange("p (j t) -> p j t", j=11)
        actb = [[Buf(f"act{j}_{g}") for g in range(NG)] for j in range(11)]
        for g in range(NG):
            tgs = slice(g * TG, (g + 1) * TG)
            self.norm_group(nidx, g, lambda c, tgs=tgs, g=g: (hn3[:, c, tgs], [hnb[c][g]]))
        for (j0, j1) in ((0, 11), (11, 22)):
            nj = j1 - j0
            for j in range(j0, j1):
                slot, sbuf_ = self.ws.acquire(("win", widx, j))
                wv = slot[:, 0:2048].rearrange("p (g c m) -> p g c m", g=2, c=NCH)
                jj = j - j0
                for g in range(NG):
                    tgs = slice(g * TG, (g + 1) * TG)
                    par = self.flip("gu")
                    gps, gpb = self.psum[par * 2]
                    ups, upb = self.psum[par * 2 + 1]
                    for gu, (ps, pb) in enumerate(((gps, gpb), (ups, upb))):
                        for c in range(NCH):
                            P.op("pe", lambda e, ps=ps, gu=gu, c=c, wv=wv, tgs=tgs: e.matmul(
                                ps[:], lhsT=wv[:, gu, c, :], rhs=hn3[:, c, tgs], start=(c == 0), stop=(c == NCH - 1)),
                                reads=[sbuf_, hnb[c][g]], writes=[pb])
                    sg = self.SCR[:, par * 512:(par + 1) * 512]
                    sgb = self.scrb[par]
                    P.op("act", lambda e, sg=sg, gps=gps: e.activation(out=sg, in_=gps[:], func=AF.Silu),
                         reads=[gpb], writes=[sgb])
                    P.op("dve", lambda e, sg=sg, ups=ups, jj=jj, tgs=tgs: e.tensor_tensor(
                        out=actT[:, jj, tgs], in0=sg, in1=ups[:], op=ALU.mult),
                        reads=[sgb, upb], writes=[actb[jj][g]])
            for c in range(NCH):
                slot, sbuf_ = self.ws.acquire(("wout", widx, j0, c))
                wv = slot[:, 0:nj * 128].rearrange("p (j m) -> p j m", m=128)
                for g in range(NG):
                    tgs = slice(g * TG, (g + 1) * TG)
                    ps, pb = self.psum[4 + self.flip("y")]
                    for jj in range(nj):
                        P.op("pe", lambda e, ps=ps, jj=jj, wv=wv, tgs=tgs: e.matmul(
                            ps[:], lhsT=wv[:, jj, :], rhs=actT[:, jj, tgs], start=(jj == 0), stop=(jj == nj - 1)),
                            reads=[sbuf_, actb[jj][g]], writes=[pb])
                    P.op("dve", lambda e, ps=ps, c=c, tgs=tgs: e.scalar_tensor_tensor(
                        out=xT[:, c, tgs], in0=ps[:], scalar=0.5, in1=xT[:, c, tgs], op0=ALU.mult, op1=ALU.add),
                        reads=[pb, self.xb[c][g]], writes=[self.xb[c][g]])

    def final_norm(self, nidx):
        xT = self.xT
        for g in range(NG):
            tgs = slice(g * TG, (g + 1) * TG)
            self.norm_group(nidx, g, lambda c, tgs=tgs, g=g: (xT[:, c, tgs], [self.xb[c][g]]))

    def proj_fm(self, slot, sbuf_, half, src3, srcb, evac):
        P = self.P
        wv = slot[:, 0:2048].rearrange("p (c m) -> p c m", c=NCH)
        ps, pb = self.psum[6 + self.flip("norm")]
        for c in range(NCH):
            P.op("pe", lambda e, ps=ps, c=c, wv=wv, half=half: e.matmul(
                ps[:], lhsT=wv[:, c, half * 128:(half + 1) * 128], rhs=src3[:, c, :],
                start=(c == 0), stop=(c == NCH - 1)),
                reads=[sbuf_, srcb[c]], writes=[pb])
        evac(ps, pb)

    def sb_mixer(self, nidx):
        P = self.P
        xT = self.xT
        kT = self.R[:, 0:NCH * SEQ].rearrange("p (m t) -> p m t", m=NCH)
        vT = self.R[:, NCH * SEQ:16 * SEQ].rearrange("p (i f) -> p i f", i=16)
        kb = [[Buf(f"k{m}_{g}") for g in range(NG)] for m in range(NCH)]
        vb = [[Buf(f"v{i}_{nb}") for nb in range(4)] for i in range(16)]
        hn_g = self.HN[:, 0:4096].rearrange("p (c t) -> p c t", c=NCH)
        q_g = self.HN[:, 4096:8192].rearrange("p (c t) -> p c t", c=NCH)
        o_g = self.HN[:, 8192:12288].rearrange("p (c t) -> p c t", c=NCH)
        hngb = [Buf(f"hng{c}") for c in range(NCH)]
        qgb = [Buf(f"qg{c}") for c in range(NCH)]
        ogb = [Buf(f"og{c}") for c in range(NCH)]
        ident = self.cst[:, 0:128]
        tinc = self.cst[:, 128:256]
        maskneg = self.cst[:, 256:384]
        ones = self.cst[:, 384:512]
        cstb = self.cstb
        NE = 3
        e_t = [(self.SCR[:, k * 512:(k + 1) * 512], self.scrb[k]) for k in range(3)]
        ecs_t = [(self.SCR[:, (3 + k) * 512:(4 + k) * 512], self.scrb[3 + k]) for k in range(2)]
        sp_t = [(self.SCRB[:, k * 512:(k + 1) * 512], self.scrbb[k]) for k in range(2)]
        aT_t = [(self.HN[:, 12288 + k * 512:12288 + (k + 1) * 512], Buf(f"aT{k}")) for k in range(2)]
        R_t = [(self.HN[:, 13312 + k * 512:13312 + (k + 1) * 512], Buf(f"R{k}")) for k in range(2)]

        class T:
            pass

        for g in range(NG):
            tgs = slice(g * TG, (g + 1) * TG)
            self.norm_group(nidx, g, lambda c: (hn_g[:, c, :], [hngb[c]]))
            for kind in range(2):
                for nb in range(4):
                    slot, sbuf_ = self.ws.acquire(("sb", kind * 4 + nb, g))
                    for half in range(2):
                        m = nb * 2 + half
                        if kind == 0:
                            def evac(ps, pb, m=m):
                                P.op("act", lambda e: e.activation(out=q_g[:, m, :], in_=ps[:], func=AF.Copy, scale=0.125),
                                     reads=[pb], writes=[qgb[m]])
                        else:
                            def evac(ps, pb, m=m, tgs=tgs, g=g):
                                P.op("dve", lambda e: e.tensor_copy(out=kT[:, m, tgs], in_=ps[:]),
                                     reads=[pb], writes=[kb[m][g]])
                        self.proj_fm(slot, sbuf_, half, hn_g, hngb, evac)
            for nb in range(4):
                slot, sbuf_ = self.ws.acquire(("sb", 8 + nb, g))
                wv = slot[:, 0:2048].rearrange("p (c m) -> p c m", c=NCH)
                for tt in range(4):
                    ti = g * 4 + tt
                    ps, pb = self.psum[6 + self.flip("norm")]
                    for c in range(NCH):
                        P.op("pe", lambda e, ps=ps, c=c, wv=wv, tt=tt: e.matmul(
                            ps[:, 0:256], lhsT=hn_g[:, c, tt * 128:(tt + 1) * 128], rhs=wv[:, c, :],
                            start=(c == 0), stop=(c == NCH - 1)),
                            reads=[sbuf_, hngb[c]], writes=[pb])
                    P.op("dve", lambda e, ps=ps, ti=ti, nb=nb: e.tensor_copy(out=vT[:, ti, nb * 256:(nb + 1) * 256], in_=ps[:, 0:256]),
                         reads=[pb], writes=[vb[ti][nb]])
            tasks = []
            for m in range(NCH):
                for half in range(2):
                    tiles = list(range(4 * g + 3, -1, -1))
                    for idx, i in enumerate(tiles):
                        t = T()
                        t.m, t.half, t.h, t.p0, t.i, t.idx = m, half, 2 * m + half, 64 * half, i, idx
                        t.qoff = max(0, i - 4 * g) * 128
                        t.diag = i >= 4 * g
                        t.first = idx == 0
                        t.last = idx == len(tiles) - 1
                        t.nqoff = 0 if t.last else max(0, tiles[idx + 1] - 4 * g) * 128
                        t.n = len(tasks)
                        tasks.append(t)
            o_cur = {}

            def stA(t):
                t.z_ps, t.zb = self.psum[t.n % 2]
                qo = t.qoff
                P.op("pe", lambda e: e.matmul(
                    t.z_ps[:, qo:512], lhsT=kT[t.p0:t.p0 + 64, t.m, t.i * 128:(t.i + 1) * 128],
                    rhs=q_g[t.p0:t.p0 + 64, t.m, qo:512], start=True, stop=(not t.diag)),
                    reads=[kb[t.m][t.i // 4], qgb[t.m]], writes=[t.zb])
                if t.diag:
                    P.op("pe", lambda e: e.matmul(
                        t.z_ps[:, qo:qo + 128], lhsT=ident, rhs=maskneg, start=False, stop=True),
                        reads=[cstb], writes=[t.zb])

            def stB1(t):
                t.et, t.eb = e_t[t.n % NE]
                qo = t.qoff
                P.op("act", lambda e: e.activation(out=t.et[:, qo:512], in_=t.z_ps[:, qo:512], func=AF.Exp),
                     reads=[t.zb], writes=[t.eb])

            def stB2(t):
                t.spt, t.spb = sp_t[t.n % 2]
                qo = t.qoff
                P.op("act", lambda e: e.activation(out=t.spt[:, qo:512], in_=t.et[:, qo:512], func=AF.Ln, bias=1.0),
                     reads=[t.eb], writes=[t.spb])

            def stC(t):
                t.cs_ps, t.csb = self.psum[2 + t.n % 2]
                qo = t.qoff
                P.op("pe", lambda e: e.matmul(
                    t.cs_ps[:, qo:512], lhsT=tinc, rhs=t.spt[:, qo:512], start=True, stop=t.first),
                    reads=[t.spb, cstb], writes=[t.csb])
                if not t.first:
                    rt, rb = t.rprev
                    P.op("pe", lambda e: e.matmul(
                        t.cs_ps[:, qo:512], lhsT=ones, rhs=rt[:, qo:512], start=False, stop=True),
                        reads=[rb, cstb], writes=[t.csb])
                if not t.last:
                    rn, rnb = R_t[t.n % 2]
                    if t.first:
                        P.op("dve", lambda e: e.tensor_copy(out=rn[:, qo:512], in_=t.spt[:, qo:512]),
                             reads=[t.spb], writes=[rnb])
                    else:
                        rt, rb = t.rprev
                        P.op("dve", lambda e: e.tensor_tensor(
                            out=rn[:, qo:512], in0=rt[:, qo:512], in1=t.spt[:, qo:512], op=ALU.add),
                            reads=[t.spb, rb], writes=[rnb])
                    if t.nqoff < qo:
                        P.op("dve", lambda e: e.memset(rn[:, t.nqoff:qo], 0.0), reads=[], writes=[rnb])
                    tasks[t.n + 1].rprev = (rn, rnb)

            def stD1(t):
                t.ect, t.ecb = ecs_t[t.n % 2]
                qo = t.qoff
                P.op("act", lambda e: e.activation(out=t.ect[:, qo:512], in_=t.cs_ps[:, qo:512], func=AF.Exp, scale=-1.0),
                     reads=[t.csb], writes=[t.ecb])

            def stD2(t):
                t.at, t.ab = aT_t[t.n % 2]
                qo = t.qoff
                P.op("dve", lambda e: e.tensor_tensor(
                    out=t.at[:, qo:512], in0=t.et[:, qo:512], in1=t.ect[:, qo:512], op=ALU.mult),
                    reads=[t.eb, t.ecb], writes=[t.ab])

            def stE(t):
                if t.half == 0 and t.first:
                    o_cur[t.m] = self.psum[4 + self.flip("y")]
                o_ps, opb = o_cur[t.m]
                qo = t.qoff
                P.op("pe", lambda e: e.matmul(
                    o_ps[t.p0:t.p0 + 64, qo:512], lhsT=vT[:, t.i, t.h * 64:(t.h + 1) * 64], rhs=t.at[:, qo:512],
                    start=t.first, stop=t.last),
                    reads=[vb[t.i][t.h // 4], t.ab], writes=[opb])
                if t.half == 1 and t.last:
                    m = t.m
                    P.op("act", lambda e: e.activation(out=o_g[:, m, :], in_=o_ps[:], func=AF.Copy),
                         reads=[opb], writes=[ogb[m]])

            sched = self.sb_sched
            n = len(tasks)
            maxlag = max(l for (_, l) in sched)
            fns = {"A": stA, "B1": stB1, "B2": stB2, "C": stC, "D1": stD1, "D2": stD2, "E": stE}
            for s in range(n + maxlag):
                for (name, lag) in sched:
                    k = s - lag
                    if 0 <= k < n:
                        fns[name](tasks[k])
            for nb in range(4):
                slot, sbuf_ = self.ws.acquire(("sb", 12 + nb, g))
                for half in range(2):
                    c = nb * 2 + half

                    def evac(ps, pb, c=c, tgs=tgs, g=g):
                        P.op("dve", lambda e: e.tensor_tensor(
                            out=xT[:, c, tgs], in0=ps[:], in1=xT[:, c, tgs], op=ALU.add),
                            reads=[pb, self.xb[c][g]], writes=[self.xb[c][g]])
                    self.proj_fm(slot, sbuf_, half, o_g, ogb, evac)

    DF_ORDER = [0, 4, 1, 5, 2, 6, 3, 7, 8, 12, 9, 13, 10, 14, 11, 15, 16, 17, 18, 19, 20, 21, 22, 23]

    def diff_mixer(self, nidx):
        P = self.P
        xT = self.xT
        kT = self.R[:, 0:NCH * SEQ].rearrange("p (m t) -> p m t", m=NCH)
        vT = self.R[:, NCH * SEQ:16 * SEQ].rearrange("p (i f) -> p i f", i=16)
        kb = [[Buf(f"k{m}_{g}") for g in range(NG)] for m in range(NCH)]
        vb = [[Buf(f"v{i}_{nb}") for nb in range(4)] for i in range(16)]
        hn_g = self.HN[:, 0:4096].rearrange("p (c t) -> p c t", c=NCH)
        q_g = self.HN[:, 4096:8192].rearrange("p (c t) -> p c t", c=NCH)
        o_g = self.HN[:, 8192:12288].rearrange("p (c t) -> p c t", c=NCH)
        hngb = [Buf(f"hng{c}") for c in range(NCH)]
        qgb = [Buf(f"qg{c}") for c in range(NCH)]
        ogb = [Buf(f"og{c}") for c in range(NCH)]
        ones = self.cst[:, 384:512]
        cstb = self.cstb
        NP = 3
        p_t = [(self.HN[:, 12288 + k * 512:12288 + (k + 1) * 512], Buf(f"pT{k}")) for k in range(NP)]
        sqo, sqob = self.HN[:, 13824:14336], Buf("sqo")
        scr = lambda k: (self.SCR[:, k * 512:(k + 1) * 512], self.scrb[k])
        small, smallb = self.small, self.smallb
        ropet, ropeb = self.ropet, self.ropeb

        tmp, tmpb = scr(0)
        P.op("sp", lambda e: e.dma_start(out=self.lamt[:], in_=self.d_lam), writes=[self.lamb], dsem="lam")
        P.op("sp", lambda e: e.dma_start(out=small[:, 6:7], in_=self.d_subln), writes=[smallb], dsem="subln")
        for k in range(2):
            P.op("dve", lambda e, k=k: e.tensor_tensor(out=tmp[:, k * 64:(k + 1) * 64], in0=self.lamt[:, k * 128:k * 128 + 64],
                                                       in1=self.lamt[:, k * 128 + 64:k * 128 + 128], op=ALU.mult),
                 reads=[self.lamb], writes=[tmpb])
            P.op("dve", lambda e, k=k: e.reduce_sum(out=small[:, k:k + 1], in_=tmp[:, k * 64:(k + 1) * 64], axis=mybir.AxisListType.X),
                 reads=[tmpb], writes=[smallb])
        P.op("act", lambda e: e.activation(out=small[:, 2:4], in_=small[:, 0:2], func=AF.Exp), reads=[smallb], writes=[smallb])
        P.op("dve", lambda e: e.tensor_tensor(out=small[:, 4:5], in0=small[:, 3:4], in1=small[:, 2:3], op=ALU.subtract),
             reads=[smallb], writes=[smallb])
        P.op("dve", lambda e: e.tensor_scalar(out=small[:, 4:5], in0=small[:, 4:5], scalar1=-LAMBDA_INIT, scalar2=None, op0=ALU.add),
             reads=[smallb], writes=[smallb])
        P.op("dve", lambda e: e.tensor_scalar(out=small[:, 5:6], in0=small[:, 6:7], scalar1=(1.0 - LAMBDA_INIT), scalar2=None, op0=ALU.mult),
             reads=[smallb], writes=[smallb])
        neglam = small[:, 4:5]
        subg = small[:, 5:6]

        class T:
            pass

        for g in range(NG):
            tgs = slice(g * TG, (g + 1) * TG)
            self.norm_group(nidx, g, lambda c: (hn_g[:, c, :], [hngb[c]]))
            P.op("sp", lambda e, tgs=tgs: e.dma_start(out=ropet[:], in_=self.d_rope[:, :, tgs]), writes=[ropeb], dsem="rope")
            for kind in range(2):
                for nb in range(4):
                    slotA, sbA = self.ws.acquire(("df", kind * 8 + nb, g))
                    slotB, sbB = self.ws.acquire(("df", kind * 8 + 4 + nb, g))
                    for half in range(2):
                        m = nb * 2 + half
                        pair = self.flip("rp")
                        banks = (6, 7) if pair == 0 else (2, 3)
                        pss = []
                        for slot, sbuf_, bk in ((slotA, sbA, banks[0]), (slotB, sbB, banks[1])):
                            wv = slot[:, 0:2048].rearrange("p (c m) -> p c m", c=NCH)
                            ps, pb = self.psum[bk]
                            for c in range(NCH):
                                P.op("pe", lambda e, ps=ps, c=c, wv=wv, half=half: e.matmul(
                                    ps[:], lhsT=wv[:, c, half * 128:(half + 1) * 128], rhs=hn_g[:, c, :],
                                    start=(c == 0), stop=(c == NCH - 1)),
                                    reads=[sbuf_, hngb[c]], writes=[pb])
                            pss.append((ps, pb))
                        (t1, t1b), (t2, t2b) = scr(pair * 2), scr(pair * 2 + 1)
                        (ps1, pb1), (ps2, pb2) = pss
                        P.op("dve", lambda e, t1=t1, ps1=ps1: e.tensor_tensor(out=t1, in0=ps1[:], in1=ropet[:, 0, :], op=ALU.mult),
                             reads=[pb1, ropeb], writes=[t1b])
                        P.op("dve", lambda e, t2=t2, ps2=ps2: e.tensor_tensor(out=t2, in0=ps2[:], in1=ropet[:, 1, :], op=ALU.mult),
                             reads=[pb2, ropeb], writes=[t2b])
                        if kind == 0:
                            P.op("dve", lambda e, t1=t1, t2=t2, m=m: e.tensor_tensor(out=q_g[:, m, :], in0=t1, in1=t2, op=ALU.add),
                                 reads=[t1b, t2b], writes=[qgb[m]])
                        else:
                            P.op("dve", lambda e, t1=t1, t2=t2, m=m, tgs=tgs: e.tensor_tensor(out=kT[:, m, tgs], in0=t1, in1=t2, op=ALU.add),
                                 reads=[t1b, t2b], writes=[kb[m][g]])
            for nb in range(4):
                slot, sbuf_ = self.ws.acquire(("df", 16 + nb, g))
                wv = slot[:, 0:2048].rearrange("p (c m) -> p c m", c=NCH)
                for tt in range(4):
                    ti = g * 4 + tt
                    ps, pb = self.psum[6 + self.flip("norm")]
                    for c in range(NCH):
                        P.op("pe", lambda e, ps=ps, c=c, wv=wv, tt=tt: e.matmul(
                            ps[:, 0:256], lhsT=hn_g[:, c, tt * 128:(tt + 1) * 128], rhs=wv[:, c, :],
                            start=(c == 0), stop=(c == NCH - 1)),
                            reads=[sbuf_, hngb[c]], writes=[pb])
                    P.op("dve", lambda e, ps=ps, ti=ti, nb=nb: e.tensor_copy(out=vT[:, ti, nb * 256:(nb + 1) * 256], in_=ps[:, 0:256]),
                         reads=[pb], writes=[vb[ti][nb]])
            tasks = []
            for h in range(NCH):
                for mp in range(2):
                    ntl = 4 * g + 4
                    for i in range(ntl):
                        t = T()
                        t.h, t.mp, t.p0, t.i = h, mp, 64 * mp, i
                        t.qoff = max(0, i - 4 * g) * 128
                        t.diag = i >= 4 * g
                        t.first = i == 0
                        t.last = i == ntl - 1
                        t.n = len(tasks)
                        tasks.append(t)
            acc = {}

            def stA(t):
                t.s_ps, t.sb_ = self.psum[t.n % 2]
                qo = t.qoff
                P.op("pe", lambda e: e.matmul(
                    t.s_ps[:, qo:512], lhsT=kT[t.p0:t.p0 + 64, t.h, t.i * 128:(t.i + 1) * 128],
                    rhs=q_g[t.p0:t.p0 + 64, t.h, qo:512], start=True, stop=True),
                    reads=[kb[t.h][t.i // 4], qgb[t.h]], writes=[t.sb_])

            def stB(t):
                t.pt, t.ptb = p_t[t.n % NP]
                qo = t.qoff
                P.op("act", lambda e: e.activation(out=t.pt[:, qo:512], in_=t.s_ps[:, qo:512], func=AF.Exp, scale=0.125),
                     reads=[t.sb_], writes=[t.ptb])
                if t.diag:
                    P.op("dve", lambda e: e.memset(t.pt[64:128, qo:qo + 64], 0.0), reads=[], writes=[t.ptb])

            def stC(t):
                key = (t.h, t.mp)
                if t.first:
                    k = self.flip("acc")
                    acc[key] = (self.psum[2 + k], self.psum[4 + k])
                (den_ps, denb), (o_ps, opb) = acc[key]
                qo = t.qoff
                P.op("pe", lambda e: e.matmul(den_ps[:, qo:512], lhsT=ones, rhs=t.pt[:, qo:512], start=t.first, stop=t.last),
                     reads=[t.ptb, cstb], writes=[denb])
                P.op("pe", lambda e: e.matmul(o_ps[:, qo:512], lhsT=vT[:, t.i, t.h * 128:(t.h + 1) * 128], rhs=t.pt[:, qo:512],
                                              start=t.first, stop=t.last),
                     reads=[t.ptb, vb[t.i][t.h // 2]], writes=[opb])
                if t.last:
                    (lnd, lndb), (rden, rdenb) = scr(2), scr(3)
                    (o1n, o1nb), (od, odb) = scr(0), scr(1)
                    P.op("act", lambda e: e.activation(out=lnd, in_=den_ps[:], func=AF.Ln), reads=[denb], writes=[lndb])
                    P.op("act", lambda e: e.activation(out=rden, in_=lnd, func=AF.Exp, scale=-1.0), reads=[lndb], writes=[rdenb])
                    if t.mp == 0:
                        P.op("dve", lambda e: e.tensor_tensor(out=o1n, in0=o_ps[:], in1=rden, op=ALU.mult),
                             reads=[opb, rdenb], writes=[o1nb])
                    else:
                        h = t.h
                        P.op("dve", lambda e: e.tensor_tensor(out=od, in0=o_ps[:], in1=rden, op=ALU.mult),
                             reads=[opb, rdenb], writes=[odb])
                        P.op("dve", lambda e: e.scalar_tensor_tensor(out=od, in0=od, scalar=neglam, in1=o1n, op0=ALU.mult, op1=ALU.add),
                             reads=[odb, o1nb, smallb], writes=[odb])
                        P.op("act", lambda e: e.activation(out=sqo, in_=od, func=AF.Square), reads=[odb], writes=[sqob])
                        ss_ps, ssb = self.psum[6 + self.flip("norm")]
                        P.op("pe", lambda e: e.matmul(ss_ps[:], lhsT=ones, rhs=sqo, start=True, stop=True),
                             reads=[sqob, cstb], writes=[ssb])
                        (rs, rsb), (lnt, lnb) = scr(4), scr(5)
                        P.op("act", lambda e: e.activation(out=lnt, in_=ss_ps[:], func=AF.Ln, scale=1.0 / 128.0, bias=RMS_EPS),
                             reads=[ssb], writes=[lnb])
                        P.op("act", lambda e: e.activation(out=rs, in_=lnt, func=AF.Exp, scale=-0.5), reads=[lnb], writes=[rsb])
                        P.op("dve", lambda e: e.scalar_tensor_tensor(out=o_g[:, h, :], in0=od, scalar=subg, in1=rs, op0=ALU.mult, op1=ALU.mult),
                             reads=[odb, rsb, smallb], writes=[ogb[h]])

            sched = self.df_sched
            n = len(tasks)
            maxlag = max(l for (_, l) in sched)
            fns = {"A": stA, "B": stB, "C": stC}
            for s in range(n + maxlag):
                for (name, lag) in sched:
                    k = s - lag
                    if 0 <= k < n:
                        fns[name](tasks[k])
            dbg = getattr(self, "dbg", None)
            if dbg:
                srcs = {"o": (o_g, ogb), "q": (q_g, qgb), "hn": (hn_g, hngb)}
                if dbg == "k":
                    for m in range(NCH):
                        P.op("dve", lambda e, m=m, tgs=tgs: e.tensor_copy(out=xT[:, m, tgs], in_=kT[:, m, tgs]),
                             reads=[kb[m][g], self.xb[m][g]], writes=[self.xb[m][g]])
                elif dbg == "v":
                    for m in range(NCH):
                        P.op("dve", lambda e, m=m, tgs=tgs: e.tensor_copy(out=xT[:, m, tgs], in_=vT[:, g * 4 + m // 2, (m % 2) * 512:(m % 2) * 512 + 512]),
                             reads=[vb[g * 4 + m // 2][nb] for nb in range(4)] + [self.xb[m][g]], writes=[self.xb[m][g]])
                else:
                    s3, sb3 = srcs[dbg]
                    for m in range(NCH):
                        P.op("dve", lambda e, m=m, tgs=tgs, s3=s3: e.tensor_copy(out=xT[:, m, tgs], in_=s3[:, m, :]),
                             reads=[sb3[m], self.xb[m][g]], writes=[self.xb[m][g]])
                for nb in range(4):
                    self.ws.acquire(("df", 20 + nb, g))
                continue
            for nb in range(4):
                slot, sbuf_ = self.ws.acquire(("df", 20 + nb, g))
                for half in range(2):
                    c = nb * 2 + half

                    def evac(ps, pb, c=c, tgs=tgs, g=g):
                        P.op("dve", lambda e: e.tensor_tensor(
                            out=xT[:, c, tgs], in0=ps[:], in1=xT[:, c, tgs], op=ALU.add),
                            reads=[pb, self.xb[c][g]], writes=[self.xb[c][g]])
                    self.proj_fm(slot, sbuf_, half, o_g, ogb, evac)


def _consts():
    c = np.zeros((128, 512), np.float32)
    c[:, 0:128] = np.eye(128, dtype=np.float32)
    j = np.arange(128)[:, None]
    k = np.arange(128)[None, :]
    c[:, 128:256] = (j >= k).astype(np.float32)
    c[:, 256:384] = np.where(j >= k, MASKNEG, 0.0)
    c[:, 384:512] = 1.0
    return c


def _layout_weights(inputs):
    w_in = np.asarray(inputs["ffn_w_in"], np.float32).reshape(4, NCH, 128, 2, NF, 128)
    w_in = np.ascontiguousarray(w_in.transpose(0, 4, 2, 3, 1, 5)).reshape(4, NF, 128, 2048)
    w_out = np.asarray(inputs["ffn_w_out"], np.float32).reshape(4, NF, 128, NCH, 128)
    w_out = np.ascontiguousarray(w_out.transpose(0, 3, 2, 1, 4)).reshape(4, NCH, 128, D_FF)

    def colblocks(w, ncols):
        nb = ncols // 256
        a = w.reshape(NCH, 128, nb, 256).transpose(2, 1, 0, 3)
        return np.ascontiguousarray(a).reshape(nb, 128, 2048)
    sbq = np.asarray(inputs["sb_w_qkv"], np.float32)[0]
    sbo = np.asarray(inputs["sb_w_o"], np.float32)[0]
    w_sb = np.concatenate([colblocks(sbq, 3072), colblocks(sbo, 1024)], axis=0)
    dq = np.asarray(inputs["diff_w_qkv"], np.float32)[0]
    do = np.asarray(inputs["diff_w_o"], np.float32)[0]
    perm = np.arange(1024).reshape(16, 2, 32)[:, ::-1, :].reshape(1024)
    w_df = np.concatenate([colblocks(dq[:, 0:1024], 1024), colblocks(dq[:, 0:1024][:, perm], 1024),
                           colblocks(dq[:, 1024:2048], 1024), colblocks(dq[:, 1024:2048][:, perm], 1024),
                           colblocks(dq[:, 2048:3072], 1024), colblocks(do, 1024)], axis=0)
    return w_in, w_out, w_sb, w_df


def _rope_tables():
    p = np.arange(128)
    i = p % 32
    inv = 10000.0 ** (-(i.astype(np.float32)) / 32.0)
    ang = np.arange(SEQ, dtype=np.float32)[None, :] * inv[:, None].astype(np.float32)
    cos = np.cos(ang).astype(np.float32)
    sin = np.sin(ang).astype(np.float32)
    sgn = np.where((p % 64) < 32, -1.0, 1.0).astype(np.float32)[:, None]
    return np.ascontiguousarray(np.stack([cos, sin * sgn], axis=1))


FULL_STAGES = [("ffn", 0, 0), ("sb", 1), ("ffn", 1, 2), ("ffn", 2, 3), ("diff", 4), ("ffn", 3, 5), ("final", 6)]
_CACHE = {}


def _prepare(inputs):
    w_in, w_out, w_sb, w_df = _layout_weights(inputs)
    ng = np.asarray(inputs["norm_gains"], np.float32).reshape(6, NCH, 128)
    fg = np.asarray(inputs["final_gain"], np.float32).reshape(1, NCH, 128)
    gains = np.ascontiguousarray(np.concatenate([ng, fg], 0).transpose(2, 0, 1)).reshape(128, 56)
    lam = np.ascontiguousarray(np.broadcast_to(np.asarray(inputs["diff_lambda"], np.float32).reshape(1, 256), (128, 256)))
    subln = np.ascontiguousarray(np.asarray(inputs["diff_subln"], np.float32).reshape(128, 1))
    shared = {"gains": gains, "consts": _consts(), "w_in": w_in, "w_out": w_out, "w_sb": w_sb, "w_df": w_df,
              "rope": _rope_tables(), "lam": lam, "subln": subln}
    return shared


def _x_to_dev(xb):
    return np.ascontiguousarray(xb.T.reshape(NCH, 128, SEQ).transpose(1, 0, 2))


def _x_from_dev(y):
    return np.ascontiguousarray(y.transpose(1, 0, 2).reshape(D_MODEL, SEQ).T)


def run_stages(stages, xs, shared, core_ids=None):
    key = tuple(stages)
    if key not in _CACHE:
        _CACHE[key] = Builder(list(stages)).build()
    nc = _CACHE[key]
    n = len(xs)
    in_maps = [dict(shared, xT=_x_to_dev(np.asarray(x, np.float32))) for x in xs]
    res = run_bass_kernel_spmd(nc, in_maps, core_ids=list(range(n)))
    return [_x_from_dev(r["outT"]) for r in res.results]


def kernel(**inputs):
    shared = _prepare(inputs)
    x = np.asarray(inputs["x"], np.float32)
    outs = run_stages(FULL_STAGES, [x[b] for b in range(x.shape[0])], shared)
    return np.stack(outs, axis=0).astype(np.float32)
```

```python
import math
from contextlib import ExitStack

import numpy as np
import concourse.bass as bass
import concourse.mybir as mybir
from concourse.bass_utils import run_bass_kernel_spmd

F32 = mybir.dt.float32
BF16 = mybir.dt.bfloat16
AF = mybir.ActivationFunctionType
ALU = mybir.AluOpType

ENGS = ("pe", "act", "dve", "pool", "sp")

D_MODEL = 1024
SEQ = 2048
NCH = 8
D_FF = 2816
NF = 22
TG = 512
NG = 4
RMS_EPS = 1e-6
LAMBDA_INIT = 0.8 - 0.6 * math.exp(-0.3 * (2 - 1))
NSLOT = 5
MASKNEG = -30000.0


class Buf:
    __slots__ = ("name", "last_w", "readers")

    def __init__(self, name):
        self.name = name
        self.last_w = None
        self.readers = []


class Op:
    __slots__ = ("eng", "fn", "deps", "needed", "ticket", "dsem", "idx", "is_dma")

    def __init__(self, eng, fn, dsem=None):
        self.eng = eng
        self.fn = fn
        self.deps = []
        self.needed = False
        self.ticket = None
        self.dsem = dsem
        self.is_dma = dsem is not None
        self.idx = None


class Prog:
    def __init__(self):
        self.ops = {e: [] for e in ENGS}
        self.n = 0
        self.dma_keys = []
        self.last_dma = {}

    def op(self, eng, fn, reads=(), writes=(), dsem=None, extra_deps=()):
        o = Op(eng, fn, dsem)
        o.idx = self.n
        self.n += 1
        if dsem is not None:
            if dsem not in self.dma_keys:
                self.dma_keys.append(dsem)
        deps = {}
        for b in reads:
            if b.last_w is not None:
                deps[id(b.last_w)] = b.last_w
        for b in writes:
            if b.last_w is not None:
                deps[id(b.last_w)] = b.last_w
            for r in b.readers:
                deps[id(r)] = r
        for d in extra_deps:
            deps[id(d)] = d
        if dsem is not None and dsem in self.last_dma:
            d = self.last_dma[dsem]
            deps[id(d)] = d
        for d in deps.values():
            if d.eng == "pe" and eng == "pe" and not d.is_dma and not o.is_dma:
                continue
            o.deps.append(d)
            d.needed = True
        for b in reads:
            b.readers.append(o)
        for b in writes:
            b.last_w = o
            b.readers = []
        if dsem is not None:
            self.last_dma[dsem] = o
        self.ops[eng].append(o)
        return o

    def barrier(self):
        lasts = []
        for e in ENGS:
            for o in reversed(self.ops[e]):
                if not o.is_dma and o.fn is not None:
                    lasts.append(o)
                    break
        lasts.extend(self.last_dma.values())
        for e in ENGS:
            self.op(e, None, extra_deps=[d for d in lasts])

    def assign(self):
        cnt = {e: 0 for e in ENGS}
        dcnt = {k: 0 for k in self.dma_keys}
        allops = []
        for e in ENGS:
            allops.extend(self.ops[e])
        allops.sort(key=lambda o: o.idx)
        for o in allops:
            if o.is_dma:
                dcnt[o.dsem] += 16
                o.ticket = dcnt[o.dsem]
            elif o.needed and o.fn is not None:
                cnt[o.eng] += 1
                o.ticket = cnt[o.eng]

    def run_engine(self, eng, e, sems, dsems):
        waited = {}
        for o in self.ops[eng]:
            for d in o.deps:
                if d.ticket is None:
                    continue
                key = ("d", d.dsem) if d.is_dma else ("e", d.eng)
                if waited.get(key, 0) >= d.ticket:
                    continue
                waited[key] = d.ticket
                s = dsems[d.dsem] if d.is_dma else sems[d.eng]
                e.wait_ge(s, d.ticket)
            if o.fn is None:
                continue
            ins = o.fn(e)
            if o.is_dma:
                ins.then_inc(dsems[o.dsem], 16)
            elif o.needed:
                ins.then_inc(sems[o.eng], 1)


class WStream:
    def __init__(self, P, ring, nslot):
        self.P = P
        self.ring = ring
        self.nslot = nslot
        self.blocks = []
        self.issued = 0
        self.next = 0

    def plan(self, tag, src, n):
        self.blocks.append((tag, src, n))

    def _issue(self, k):
        tag, src, n = self.blocks[k]
        tile, buf = self.ring[k % self.nslot]
        s = k % self.nslot

        def fn(e, tile=tile, src=src, n=n):
            if n % 2 == 0 and n > 1024:
                return e.dma_start(out=tile[:, 0:n].rearrange("p (a b) -> p a b", a=2),
                                   in_=src.rearrange("p (a b) -> p a b", a=2))
            return e.dma_start(out=tile[:, 0:n], in_=src)
        self.P.op("pool", fn, writes=[buf], dsem=f"w{s}")

    def acquire(self, tag):
        k = self.next
        self.next += 1
        assert self.blocks[k][0] == tag, (self.blocks[k][0], tag)
        while self.issued < min(len(self.blocks), k + self.nslot - 1):
            self._issue(self.issued)
            self.issued += 1
        return self.ring[k % self.nslot]


class Builder:
    def __init__(self, stages):
        self.stages = stages
        nc = bass.Bass("TRN2", target_bir_lowering=False)
        self.nc = nc
        self.P = Prog()
        d = lambda name, shape, kind="ExternalInput": nc.dram_tensor(name, shape, F32, kind=kind).ap()
        self.d_x = d("xT", [128, NCH, SEQ])
        self.d_gains = d("gains", [128, 56])
        self.d_consts = d("consts", [128, 512])
        self.d_win = d("w_in", [4, NF, 128, 2048])
        self.d_wout = d("w_out", [4, NCH, 128, D_FF])
        self.d_wsb = d("w_sb", [16, 128, 2048])
        self.d_wdf = d("w_df", [24, 128, 2048])
        self.d_rope = d("rope", [128, 2, SEQ])
        self.d_lam = d("lam", [128, 256])
        self.d_subln = d("subln", [128, 1])
        self.d_out = d("outT", [128, NCH, SEQ], kind="ExternalOutput")

    def sb(self, name, shape, dt):
        return self.es.enter_context(self.nc.sbuf_tensor(name, shape, dt))

    def build(self):
        nc, P = self.nc, self.P
        with ExitStack() as es:
            self.es = es
            self.xT = self.sb("xT_sb", [128, NCH, SEQ], F32)
            self.HN = self.sb("hn_sb", [128, NCH * SEQ], BF16)
            self.R = self.sb("r_sb", [128, 16 * SEQ], BF16)
            self.ringt = [self.sb(f"ring{i}", [128, 2048], BF16) for i in range(NSLOT)]
            self.SCR = self.sb("scr", [128, 3072], F32)
            self.SCRB = self.sb("scrb", [128, 2048], BF16)
            self.SCRC = self.sb("scrc", [128, 2048], BF16)
            self.cst = self.sb("cst", [128, 512], BF16)
            self.gains = self.sb("gains_sb", [128, 56], F32)
            self.lamt = self.sb("lam_sb", [128, 256], F32)
            self.small = self.sb("small_sb", [128, 8], F32)
            self.ropet = self.sb("rope_sb", [128, 2, TG], F32)
            self.psum = []
            for i in range(8):
                t = es.enter_context(nc.psum_tensor(f"ps{i}", [128, 512], F32))
                self.psum.append((t, Buf(f"ps{i}")))
            self.sems = {e: es.enter_context(nc.semaphore(f"s_{e}")) for e in ENGS}

            self.xb = [[Buf(f"x{c}_{g}") for g in range(NG)] for c in range(NCH)]
            self.ring = [(self.ringt[i], Buf(f"ring{i}")) for i in range(NSLOT)]
            self.scrb = [Buf(f"scr{i}") for i in range(6)]
            self.scrbb = [Buf(f"scrb{i}") for i in range(4)]
            self.cstb = Buf("cst")
            self.gainb = Buf("gains")
            self.lamb = Buf("lam")
            self.smallb = Buf("small")
            self.ropeb = Buf("rope")
            self.outb = [Buf(f"out{g}") for g in range(NG)]
            self.ws = WStream(P, self.ring, NSLOT)
            self.par = {}
            self.df_sched = [("A", 0), ("B", 1), ("C", 2)]
            self.sb_sched = [("A", 0), ("B1", 1), ("D1", 3), ("B2", 1), ("C", 2), ("D2", 3), ("E", 4)]

            self.plan_weights()
            self.emit_all()

            P.assign()
            dsems = {k: es.enter_context(nc.semaphore(f"d_{k}")) for k in P.dma_keys}
            sems = self.sems
            with nc.Block() as block:
                @block.sync
                def _(e):
                    P.run_engine("sp", e, sems, dsems)

                @block.gpsimd
                def _(e):
                    P.run_engine("pool", e, sems, dsems)

                @block.tensor
                def _(e):
                    P.run_engine("pe", e, sems, dsems)

                @block.scalar
                def _(e):
                    P.run_engine("act", e, sems, dsems)

                @block.vector
                def _(e):
                    P.run_engine("dve", e, sems, dsems)
        return nc

    def flip(self, key, n=2):
        v = self.par.get(key, 0)
        self.par[key] = (v + 1) % n
        return v

    def plan_weights(self):
        ws = self.ws
        for st in self.stages:
            kind = st[0]
            if kind == "ffn":
                widx = st[1]
                for (j0, j1) in ((0, 11), (11, 22)):
                    for j in range(j0, j1):
                        ws.plan(("win", widx, j), self.d_win[widx, j], 2048)
                    for c in range(NCH):
                        ws.plan(("wout", widx, j0, c), self.d_wout[widx, c, :, j0 * 128:j1 * 128], (j1 - j0) * 128)
            elif kind == "sb":
                for g in range(NG):
                    for blk in range(12):
                        ws.plan(("sb", blk, g), self.d_wsb[blk], 2048)
                    for blk in range(12, 16):
                        ws.plan(("sb", blk, g), self.d_wsb[blk], 2048)
            elif kind == "diff":
                for g in range(NG):
                    for blk in Builder.DF_ORDER:
                        ws.plan(("df", blk, g), self.d_wdf[blk], 2048)

    def emit_all(self):
        P = self.P
        P.op("pool", lambda e: e.dma_start(out=self.cst[:], in_=self.d_consts), writes=[self.cstb], dsem="cst")
        P.op("sp", lambda e: e.dma_start(out=self.gains[:], in_=self.d_gains), writes=[self.gainb], dsem="gains")
        for g in range(NG):
            P.op("sp", lambda e, g=g: e.dma_start(out=self.xT[:, :, g * TG:(g + 1) * TG],
                                                  in_=self.d_x[:, :, g * TG:(g + 1) * TG]),
                 writes=[self.xb[c][g] for c in range(NCH)], dsem=f"x{g}")
        first = True
        for st in self.stages:
            if not first:
                P.barrier()
            first = False
            if st[0] == "ffn":
                self.ffn(st[1], st[2])
            elif st[0] == "sb":
                self.sb_mixer(st[1])
            elif st[0] == "diff":
                self.diff_mixer(st[1])
            elif st[0] == "final":
                self.final_norm(st[1])
        P.barrier()
        for g in range(NG):
            P.op("sp", lambda e, g=g: e.dma_start(out=self.d_out[:, :, g * TG:(g + 1) * TG],
                                                  in_=self.xT[:, :, g * TG:(g + 1) * TG]),
                 reads=[self.xb[c][g] for c in range(NCH)], writes=[self.outb[g]], dsem=f"o{g}")
        P.op("sp", None, reads=self.outb)

    def norm_group(self, nidx, g, dst_fn, in_place=False):
        P = self.P
        tgs = slice(g * TG, (g + 1) * TG)
        ones = self.cst[:, 384:512]
        ps, psb = self.psum[6 + self.flip("norm")]
        xT = self.xT
        for c in range(NCH):
            k = self.flip("sq")
            sq = self.SCRB[:, (2 + k) * 512:(3 + k) * 512]
            sqb = self.scrbb[2 + k]
            P.op("act", lambda e, c=c, sq=sq: e.activation(out=sq, in_=xT[:, c, tgs], func=AF.Square),
                 reads=[self.xb[c][g]], writes=[sqb])
            P.op("pe", lambda e, c=c, sq=sq: e.matmul(ps[:], lhsT=ones, rhs=sq, start=(c == 0), stop=(c == NCH - 1)),
                 reads=[sqb, self.cstb], writes=[psb])
        lnt = self.SCR[:, 5 * 512:6 * 512]
        lnb = self.scrb[5]
        rs = self.SCR[:, 4 * 512:5 * 512]
        rsb = self.scrb[4]
        P.op("act", lambda e: e.activation(out=lnt, in_=ps[:], func=AF.Ln, scale=1.0 / D_MODEL, bias=RMS_EPS),
             reads=[psb], writes=[lnb])
        P.op("act", lambda e: e.activation(out=rs, in_=lnt, func=AF.Exp, scale=-0.5), reads=[lnb], writes=[rsb])
        for c in range(NCH):
            out_ap, obufs = dst_fn(c)
            gcol = self.gains[:, nidx * 8 + c:nidx * 8 + c + 1]
            rd = [self.xb[c][g], rsb, self.gainb]
            P.op("dve", lambda e, c=c, out_ap=out_ap, gcol=gcol: e.scalar_tensor_tensor(
                out=out_ap, in0=xT[:, c, tgs], scalar=gcol, in1=rs, op0=ALU.mult, op1=ALU.mult),
                reads=rd, writes=obufs)

    def ffn(self, widx, nidx):
        P = self.P
        xT = self.xT
        hn3 = self.HN[:].rearrange("p (c t) -> p c t", c=NCH)
        hnb = [[Buf(f"hn{c}_{g}") for g in range(NG)] for c in range(NCH)]
        actT = self.R[:, 0:11 * SEQ].rearrange("p (j t) -> p j t", j=11)
        actb = [[Buf(f"act{j}_{g}") for g in range(NG)] for j in range(11)]
        for g in range(NG):
            tgs = slice(g * TG, (g + 1) * TG)
            self.norm_group(nidx, g, lambda c, tgs=tgs, g=g: (hn3[:, c, tgs], [hnb[c][g]]))
        for (j0, j1) in ((0, 11), (11, 22)):
            nj = j1 - j0
            for j in range(j0, j1):
                slot, sbuf_ = self.ws.acquire(("win", widx, j))
                wv = slot[:, 0:2048].rearrange("p (g c m) -> p g c m", g=2, c=NCH)
                jj = j - j0
                for g in range(NG):
                    tgs = slice(g * TG, (g + 1) * TG)
                    par = self.flip("gu")
                    gps, gpb = self.psum[par * 2]
                    ups, upb = self.psum[par * 2 + 1]
                    for gu, (ps, pb) in enumerate(((gps, gpb), (ups, upb))):
                        for c in range(NCH):
                            P.op("pe", lambda e, ps=ps, gu=gu, c=c, wv=wv, tgs=tgs: e.matmul(
                                ps[:], lhsT=wv[:, gu, c, :], rhs=hn3[:, c, tgs], start=(c == 0), stop=(c == NCH - 1)),
                                reads=[sbuf_, hnb[c][g]], writes=[pb])
                    sg = self.SCR[:, par * 512:(par + 1) * 512]
                    sgb = self.scrb[par]
                    P.op("act", lambda e, sg=sg, gps=gps: e.activation(out=sg, in_=gps[:], func=AF.Silu),
                         reads=[gpb], writes=[sgb])
                    P.op("dve", lambda e, sg=sg, ups=ups, jj=jj, tgs=tgs: e.tensor_tensor(
                        out=actT[:, jj, tgs], in0=sg, in1=ups[:], op=ALU.mult),
                        reads=[sgb, upb], writes=[actb[jj][g]])
            for c in range(NCH):
                slot, sbuf_ = self.ws.acquire(("wout", widx, j0, c))
                wv = slot[:, 0:nj * 128].rearrange("p (j m) -> p j m", m=128)
                for g in range(NG):
                    tgs = slice(g * TG, (g + 1) * TG)
                    ps, pb = self.psum[4 + self.flip("y")]
                    for jj in range(nj):
                        P.op("pe", lambda e, ps=ps, jj=jj, wv=wv, tgs=tgs: e.matmul(
                            ps[:], lhsT=wv[:, jj, :], rhs=actT[:, jj, tgs], start=(jj == 0), stop=(jj == nj - 1)),
                            reads=[sbuf_, actb[jj][g]], writes=[pb])
                    P.op("dve", lambda e, ps=ps, c=c, tgs=tgs: e.scalar_tensor_tensor(
                        out=xT[:, c, tgs], in0=ps[:], scalar=0.5, in1=xT[:, c, tgs], op0=ALU.mult, op1=ALU.add),
                        reads=[pb, self.xb[c][g]], writes=[self.xb[c][g]])

    def final_norm(self, nidx):
        xT = self.xT
        for g in range(NG):
            tgs = slice(g * TG, (g + 1) * TG)
            self.norm_group(nidx, g, lambda c, tgs=tgs, g=g: (xT[:, c, tgs], [self.xb[c][g]]))

    def proj_fm(self, slot, sbuf_, half, src3, srcb, evac):
        P = self.P
        wv = slot[:, 0:2048].rearrange("p (c m) -> p c m", c=NCH)
        ps, pb = self.psum[6 + self.flip("norm")]
        for c in range(NCH):
            P.op("pe", lambda e, ps=ps, c=c, wv=wv, half=half: e.matmul(
                ps[:], lhsT=wv[:, c, half * 128:(half + 1) * 128], rhs=src3[:, c, :],
                start=(c == 0), stop=(c == NCH - 1)),
                reads=[sbuf_, srcb[c]], writes=[pb])
        evac(ps, pb)

    def sb_mixer(self, nidx):
        P = self.P
        xT = self.xT
        kT = self.R[:, 0:NCH * SEQ].rearrange("p (m t) -> p m t", m=NCH)
        vT = self.R[:, NCH * SEQ:16 * SEQ].rearrange("p (i f) -> p i f", i=16)
        kb = [[Buf(f"k{m}_{g}") for g in range(NG)] for m in range(NCH)]
        vb = [[Buf(f"v{i}_{nb}") for nb in range(4)] for i in range(16)]
        hn_g = self.HN[:, 0:4096].rearrange("p (c t) -> p c t", c=NCH)
        q_m = [self.HN[:, 4096:8192].rearrange("p (c t) -> p c t", c=NCH),
               self.HN[:, 8192:12288].rearrange("p (c t) -> p c t", c=NCH)]
        o_g = self.HN[:, 12288:16384].rearrange("p (c t) -> p c t", c=NCH)
        hngb = [Buf(f"hng{c}") for c in range(NCH)]
        qgb = [[Buf(f"qg{hf}_{c}") for c in range(NCH)] for hf in range(2)]
        ogb = [Buf(f"og{c}") for c in range(NCH)]
        P.op("dve", lambda e: e.memset(q_m[0][64:128, :, :], 0.0), writes=qgb[0])
        P.op("dve", lambda e: e.memset(q_m[1][0:64, :, :], 0.0), writes=qgb[1])
        ident = self.cst[:, 0:128]
        tinc = self.cst[:, 128:256]
        maskneg = self.cst[:, 256:384]
        ones = self.cst[:, 384:512]
        cstb = self.cstb
        NE = 3
        e_t = [(self.SCR[:, k * 512:(k + 1) * 512], self.scrb[k]) for k in range(3)]
        ecs_t = [(self.SCR[:, (3 + k) * 512:(4 + k) * 512], self.scrb[3 + k]) for k in range(2)]
        sp_t = [(self.SCRB[:, k * 512:(k + 1) * 512], self.scrbb[k]) for k in range(2)]
        aT_t = [(self.SCRC[:, k * 512:(k + 1) * 512], Buf(f"aT{k}")) for k in range(2)]
        R_t = [(self.SCRC[:, 1024 + k * 512:1024 + (k + 1) * 512], Buf(f"R{k}")) for k in range(2)]

        class T:
            pass

        for g in range(NG):
            tgs = slice(g * TG, (g + 1) * TG)
            self.norm_group(nidx, g, lambda c: (hn_g[:, c, :], [hngb[c]]))
            for kind in range(2):
                for nb in range(4):
                    slot, sbuf_ = self.ws.acquire(("sb", kind * 4 + nb, g))
                    for half in range(2):
                        m = nb * 2 + half
                        if kind == 0:
                            def evac(ps, pb, m=m):
                                for hf in range(2):
                                    P.op("act", lambda e, hf=hf: e.activation(out=q_m[hf][hf * 64:(hf + 1) * 64, m, :],
                                                                              in_=ps[hf * 64:(hf + 1) * 64, :], func=AF.Copy, scale=0.125),
                                         reads=[pb], writes=[qgb[hf][m]])
                        else:
                            def evac(ps, pb, m=m, tgs=tgs, g=g):
                                P.op("dve", lambda e: e.tensor_copy(out=kT[:, m, tgs], in_=ps[:]),
                                     reads=[pb], writes=[kb[m][g]])
                        self.proj_fm(slot, sbuf_, half, hn_g, hngb, evac)
            for nb in range(4):
                slot, sbuf_ = self.ws.acquire(("sb", 8 + nb, g))
                wv = slot[:, 0:2048].rearrange("p (c m) -> p c m", c=NCH)
                for tt in range(4):
                    ti = g * 4 + tt
                    ps, pb = self.psum[6 + self.flip("norm")]
                    for c in range(NCH):
                        P.op("pe", lambda e, ps=ps, c=c, wv=wv, tt=tt: e.matmul(
                            ps[:, 0:256], lhsT=hn_g[:, c, tt * 128:(tt + 1) * 128], rhs=wv[:, c, :],
                            start=(c == 0), stop=(c == NCH - 1)),
                            reads=[sbuf_, hngb[c]], writes=[pb])
                    P.op("dve", lambda e, ps=ps, ti=ti, nb=nb: e.tensor_copy(out=vT[:, ti, nb * 256:(nb + 1) * 256], in_=ps[:, 0:256]),
                         reads=[pb], writes=[vb[ti][nb]])
            tasks = []
            for m in range(NCH):
                for half in range(2):
                    tiles = list(range(4 * g + 3, -1, -1))
                    for idx, i in enumerate(tiles):
                        t = T()
                        t.m, t.half, t.h, t.p0, t.i, t.idx = m, half, 2 * m + half, 64 * half, i, idx
                        t.qoff = max(0, i - 4 * g) * 128
                        t.diag = i >= 4 * g
                        t.first = idx == 0
                        t.last = idx == len(tiles) - 1
                        t.nqoff = 0 if t.last else max(0, tiles[idx + 1] - 4 * g) * 128
                        t.n = len(tasks)
                        tasks.append(t)
            o_cur = {}

            def stA(t):
                t.z_ps, t.zb = self.psum[t.n % 2]
                qo = t.qoff
                P.op("pe", lambda e: e.matmul(
                    t.z_ps[:, qo:512], lhsT=kT[:, t.m, t.i * 128:(t.i + 1) * 128],
                    rhs=q_m[t.half][:, t.m, qo:512], start=True, stop=(not t.diag)),
                    reads=[kb[t.m][t.i // 4], qgb[t.half][t.m]], writes=[t.zb])
                if t.diag:
                    P.op("pe", lambda e: e.matmul(
                        t.z_ps[:, qo:qo + 128], lhsT=ident, rhs=maskneg, start=False, stop=True),
                        reads=[cstb], writes=[t.zb])

            def stB1(t):
                t.et, t.eb = e_t[t.n % NE]
                qo = t.qoff
                P.op("act", lambda e: e.activation(out=t.et[:, qo:512], in_=t.z_ps[:, qo:512], func=AF.Exp),
                     reads=[t.zb], writes=[t.eb])

            def stB2(t):
                t.spt, t.spb = sp_t[t.n % 2]
                qo = t.qoff
                P.op("act", lambda e: e.activation(out=t.spt[:, qo:512], in_=t.et[:, qo:512], func=AF.Ln, bias=1.0),
                     reads=[t.eb], writes=[t.spb])

            def stC(t):
                t.cs_ps, t.csb = self.psum[2 + t.n % 2]
                qo = t.qoff
                P.op("pe", lambda e: e.matmul(
                    t.cs_ps[:, qo:512], lhsT=tinc, rhs=t.spt[:, qo:512], start=True, stop=t.first),
                    reads=[t.spb, cstb], writes=[t.csb])
                if not t.first:
                    rt, rb = t.rprev
                    P.op("pe", lambda e: e.matmul(
                        t.cs_ps[:, qo:512], lhsT=ones, rhs=rt[:, qo:512], start=False, stop=True),
                        reads=[rb, cstb], writes=[t.csb])
                if not t.last:
                    rn, rnb = R_t[t.n % 2]
                    if t.first:
                        P.op("dve", lambda e: e.tensor_copy(out=rn[:, qo:512], in_=t.spt[:, qo:512]),
                             reads=[t.spb], writes=[rnb])
                    else:
                        rt, rb = t.rprev
                        P.op("dve", lambda e: e.tensor_tensor(
                            out=rn[:, qo:512], in0=rt[:, qo:512], in1=t.spt[:, qo:512], op=ALU.add),
                            reads=[t.spb, rb], writes=[rnb])
                    if t.nqoff < qo:
                        P.op("dve", lambda e: e.memset(rn[:, t.nqoff:qo], 0.0), reads=[], writes=[rnb])
                    tasks[t.n + 1].rprev = (rn, rnb)

            def stD1(t):
                t.ect, t.ecb = ecs_t[t.n % 2]
                qo = t.qoff
                P.op("act", lambda e: e.activation(out=t.ect[:, qo:512], in_=t.cs_ps[:, qo:512], func=AF.Exp, scale=-1.0),
                     reads=[t.csb], writes=[t.ecb])

            def stD2(t):
                t.at, t.ab = aT_t[t.n % 2]
                qo = t.qoff
                P.op("dve", lambda e: e.tensor_tensor(
                    out=t.at[:, qo:512], in0=t.et[:, qo:512], in1=t.ect[:, qo:512], op=ALU.mult),
                    reads=[t.eb, t.ecb], writes=[t.ab])

            def stE(t):
                o_ps, opb = self.psum[4 + t.half]
                qo = t.qoff
                P.op("pe", lambda e: e.matmul(
                    o_ps[:, qo:512], lhsT=vT[:, t.i, t.m * 128:(t.m + 1) * 128], rhs=t.at[:, qo:512],
                    start=t.first, stop=t.last),
                    reads=[vb[t.i][t.m // 2], t.ab], writes=[opb])
                if t.last:
                    m, p0 = t.m, t.p0
                    P.op("act", lambda e: e.activation(out=o_g[p0:p0 + 64, m, :], in_=o_ps[p0:p0 + 64, :], func=AF.Copy),
                         reads=[opb], writes=[ogb[m]])

            sched = self.sb_sched
            n = len(tasks)
            maxlag = max(l for (_, l) in sched)
            fns = {"A": stA, "B1": stB1, "B2": stB2, "C": stC, "D1": stD1, "D2": stD2, "E": stE}
            for s in range(n + maxlag):
                for (name, lag) in sched:
                    k = s - lag
                    if 0 <= k < n:
                        fns[name](tasks[k])
            for nb in range(4):
                slot, sbuf_ = self.ws.acquire(("sb", 12 + nb, g))
                for half in range(2):
                    c = nb * 2 + half

                    def evac(ps, pb, c=c, tgs=tgs, g=g):
                        P.op("dve", lambda e: e.tensor_tensor(
                            out=xT[:, c, tgs], in0=ps[:], in1=xT[:, c, tgs], op=ALU.add),
                            reads=[pb, self.xb[c][g]], writes=[self.xb[c][g]])
                    self.proj_fm(slot, sbuf_, half, o_g, ogb, evac)

    DF_ORDER = [0, 4, 1, 5, 2, 6, 3, 7, 8, 12, 9, 13, 10, 14, 11, 15, 16, 17, 18, 19, 20, 21, 22, 23]

    def diff_mixer(self, nidx):
        P = self.P
        xT = self.xT
        kT = self.R[:, 0:NCH * SEQ].rearrange("p (m t) -> p m t", m=NCH)
        vT = self.R[:, NCH * SEQ:16 * SEQ].rearrange("p (i f) -> p i f", i=16)
        kb = [[Buf(f"k{m}_{g}") for g in range(NG)] for m in range(NCH)]
        vb = [[Buf(f"v{i}_{nb}") for nb in range(4)] for i in range(16)]
        hn_g = self.HN[:, 0:4096].rearrange("p (c t) -> p c t", c=NCH)
        q_m = [self.HN[:, 4096:8192].rearrange("p (c t) -> p c t", c=NCH),
               self.HN[:, 8192:12288].rearrange("p (c t) -> p c t", c=NCH)]
        o_g = self.HN[:, 12288:16384].rearrange("p (c t) -> p c t", c=NCH)
        hngb = [Buf(f"hng{c}") for c in range(NCH)]
        qgb = [[Buf(f"qg{hf}_{c}") for c in range(NCH)] for hf in range(2)]
        ogb = [Buf(f"og{c}") for c in range(NCH)]
        P.op("dve", lambda e: e.memset(q_m[0][64:128, :, :], 0.0), writes=qgb[0])
        P.op("dve", lambda e: e.memset(q_m[1][0:64, :, :], 0.0), writes=qgb[1])
        ones = self.cst[:, 384:512]
        cstb = self.cstb
        NP = 3
        p_t = [(self.SCRC[:, k * 512:(k + 1) * 512], Buf(f"pT{k}")) for k in range(NP)]
        sqo, sqob = self.SCRC[:, 1536:2048], Buf("sqo")
        scr = lambda k: (self.SCR[:, k * 512:(k + 1) * 512], self.scrb[k])
        small, smallb = self.small, self.smallb
        ropet, ropeb = self.ropet, self.ropeb

        tmp, tmpb = scr(0)
        P.op("sp", lambda e: e.dma_start(out=self.lamt[:], in_=self.d_lam), writes=[self.lamb], dsem="lam")
        P.op("sp", lambda e: e.dma_start(out=small[:, 6:7], in_=self.d_subln), writes=[smallb], dsem="subln")
        for k in range(2):
            P.op("dve", lambda e, k=k: e.tensor_tensor(out=tmp[:, k * 64:(k + 1) * 64], in0=self.lamt[:, k * 128:k * 128 + 64],
                                                       in1=self.lamt[:, k * 128 + 64:k * 128 + 128], op=ALU.mult),
                 reads=[self.lamb], writes=[tmpb])
            P.op("dve", lambda e, k=k: e.reduce_sum(out=small[:, k:k + 1], in_=tmp[:, k * 64:(k + 1) * 64], axis=mybir.AxisListType.X),
                 reads=[tmpb], writes=[smallb])
        P.op("act", lambda e: e.activation(out=small[:, 2:4], in_=small[:, 0:2], func=AF.Exp), reads=[smallb], writes=[smallb])
        P.op("dve", lambda e: e.tensor_tensor(out=small[:, 4:5], in0=small[:, 3:4], in1=small[:, 2:3], op=ALU.subtract),
             reads=[smallb], writes=[smallb])
        P.op("dve", lambda e: e.tensor_scalar(out=small[:, 4:5], in0=small[:, 4:5], scalar1=-LAMBDA_INIT, scalar2=None, op0=ALU.add),
             reads=[smallb], writes=[smallb])
        P.op("dve", lambda e: e.tensor_scalar(out=small[:, 5:6], in0=small[:, 6:7], scalar1=(1.0 - LAMBDA_INIT), scalar2=None, op0=ALU.mult),
             reads=[smallb], writes=[smallb])
        neglam = small[:, 4:5]
        subg = small[:, 5:6]

        class T:
            pass

        for g in range(NG):
            tgs = slice(g * TG, (g + 1) * TG)
            self.norm_group(nidx, g, lambda c: (hn_g[:, c, :], [hngb[c]]))
            P.op("sp", lambda e, tgs=tgs: e.dma_start(out=ropet[:], in_=self.d_rope[:, :, tgs]), writes=[ropeb], dsem="rope")
            for kind in range(2):
                for nb in range(4):
                    slotA, sbA = self.ws.acquire(("df", kind * 8 + nb, g))
                    slotB, sbB = self.ws.acquire(("df", kind * 8 + 4 + nb, g))
                    for half in range(2):
                        m = nb * 2 + half
                        pair = self.flip("rp")
                        banks = (6, 7) if pair == 0 else (2, 3)
                        pss = []
                        for slot, sbuf_, bk in ((slotA, sbA, banks[0]), (slotB, sbB, banks[1])):
                            wv = slot[:, 0:2048].rearrange("p (c m) -> p c m", c=NCH)
                            ps, pb = self.psum[bk]
                            for c in range(NCH):
                                P.op("pe", lambda e, ps=ps, c=c, wv=wv, half=half: e.matmul(
                                    ps[:], lhsT=wv[:, c, half * 128:(half + 1) * 128], rhs=hn_g[:, c, :],
                                    start=(c == 0), stop=(c == NCH - 1)),
                                    reads=[sbuf_, hngb[c]], writes=[pb])
                            pss.append((ps, pb))
                        (t1, t1b), (t2, t2b) = scr(pair * 2), scr(pair * 2 + 1)
                        (ps1, pb1), (ps2, pb2) = pss
                        P.op("dve", lambda e, t1=t1, ps1=ps1: e.tensor_tensor(out=t1, in0=ps1[:], in1=ropet[:, 0, :], op=ALU.mult),
                             reads=[pb1, ropeb], writes=[t1b])
                        P.op("dve", lambda e, t2=t2, ps2=ps2: e.tensor_tensor(out=t2, in0=ps2[:], in1=ropet[:, 1, :], op=ALU.mult),
                             reads=[pb2, ropeb], writes=[t2b])
                        if kind == 0:
                            for hf in range(2):
                                P.op("dve", lambda e, t1=t1, t2=t2, m=m, hf=hf: e.tensor_tensor(
                                    out=q_m[hf][hf * 64:(hf + 1) * 64, m, :], in0=t1[hf * 64:(hf + 1) * 64, :],
                                    in1=t2[hf * 64:(hf + 1) * 64, :], op=ALU.add),
                                    reads=[t1b, t2b], writes=[qgb[hf][m]])
                        else:
                            P.op("dve", lambda e, t1=t1, t2=t2, m=m, tgs=tgs: e.tensor_tensor(out=kT[:, m, tgs], in0=t1, in1=t2, op=ALU.add),
                                 reads=[t1b, t2b], writes=[kb[m][g]])
            for nb in range(4):
                slot, sbuf_ = self.ws.acquire(("df", 16 + nb, g))
                wv = slot[:, 0:2048].rearrange("p (c m) -> p c m", c=NCH)
                for tt in range(4):
                    ti = g * 4 + tt
                    ps, pb = self.psum[6 + self.flip("norm")]
                    for c in range(NCH):
                        P.op("pe", lambda e, ps=ps, c=c, wv=wv, tt=tt: e.matmul(
                            ps[:, 0:256], lhsT=hn_g[:, c, tt * 128:(tt + 1) * 128], rhs=wv[:, c, :],
                            start=(c == 0), stop=(c == NCH - 1)),
                            reads=[sbuf_, hngb[c]], writes=[pb])
                    P.op("dve", lambda e, ps=ps, ti=ti, nb=nb: e.tensor_copy(out=vT[:, ti, nb * 256:(nb + 1) * 256], in_=ps[:, 0:256]),
                         reads=[pb], writes=[vb[ti][nb]])
            tasks = []
            for h in range(NCH):
                for mp in range(2):
                    ntl = 4 * g + 4
                    for i in range(ntl):
                        t = T()
                        t.h, t.mp, t.p0, t.i = h, mp, 64 * mp, i
                        t.qoff = max(0, i - 4 * g) * 128
                        t.diag = i >= 4 * g
                        t.first = i == 0
                        t.last = i == ntl - 1
                        t.n = len(tasks)
                        tasks.append(t)
            acc = {}

            def stA(t):
                t.s_ps, t.sb_ = self.psum[t.n % 2]
                qo = t.qoff
                P.op("pe", lambda e: e.matmul(
                    t.s_ps[:, qo:512], lhsT=kT[:, t.h, t.i * 128:(t.i + 1) * 128],
                    rhs=q_m[t.mp][:, t.h, qo:512], start=True, stop=True),
                    reads=[kb[t.h][t.i // 4], qgb[t.mp][t.h]], writes=[t.sb_])

            def stB(t):
                t.pt, t.ptb = p_t[t.n % NP]
                qo = t.qoff
                P.op("act", lambda e: e.activation(out=t.pt[:, qo:512], in_=t.s_ps[:, qo:512], func=AF.Exp, scale=0.125),
                     reads=[t.sb_], writes=[t.ptb])
                if t.diag:
                    P.op("dve", lambda e: e.memset(t.pt[64:128, qo:qo + 64], 0.0), reads=[], writes=[t.ptb])

            def stC(t):
                key = (t.h, t.mp)
                if t.first:
                    k = self.flip("acc")
                    acc[key] = (self.psum[2 + k], self.psum[4 + k])
                (den_ps, denb), (o_ps, opb) = acc[key]
                qo = t.qoff
                P.op("pe", lambda e: e.matmul(den_ps[:, qo:512], lhsT=ones, rhs=t.pt[:, qo:512], start=t.first, stop=t.last),
                     reads=[t.ptb, cstb], writes=[denb])
                P.op("pe", lambda e: e.matmul(o_ps[:, qo:512], lhsT=vT[:, t.i, t.h * 128:(t.h + 1) * 128], rhs=t.pt[:, qo:512],
                                              start=t.first, stop=t.last),
                     reads=[t.ptb, vb[t.i][t.h // 2]], writes=[opb])
                if t.last:
                    (lnd, lndb), (rden, rdenb) = scr(2), scr(3)
                    (o1n, o1nb), (od, odb) = scr(0), scr(1)
                    P.op("act", lambda e: e.activation(out=lnd, in_=den_ps[:], func=AF.Ln), reads=[denb], writes=[lndb])
                    P.op("act", lambda e: e.activation(out=rden, in_=lnd, func=AF.Exp, scale=-1.0), reads=[lndb], writes=[rdenb])
                    if t.mp == 0:
                        P.op("dve", lambda e: e.tensor_tensor(out=o1n, in0=o_ps[:], in1=rden, op=ALU.mult),
                             reads=[opb, rdenb], writes=[o1nb])
                    else:
                        h = t.h
                        P.op("dve", lambda e: e.tensor_tensor(out=od, in0=o_ps[:], in1=rden, op=ALU.mult),
                             reads=[opb, rdenb], writes=[odb])
                        P.op("dve", lambda e: e.scalar_tensor_tensor(out=od, in0=od, scalar=neglam, in1=o1n, op0=ALU.mult, op1=ALU.add),
                             reads=[odb, o1nb, smallb], writes=[odb])
                        P.op("act", lambda e: e.activation(out=sqo, in_=od, func=AF.Square), reads=[odb], writes=[sqob])
                        ss_ps, ssb = self.psum[6 + self.flip("norm")]
                        P.op("pe", lambda e: e.matmul(ss_ps[:], lhsT=ones, rhs=sqo, start=True, stop=True),
                             reads=[sqob, cstb], writes=[ssb])
                        (rs, rsb), (lnt, lnb) = scr(4), scr(5)
                        P.op("act", lambda e: e.activation(out=lnt, in_=ss_ps[:], func=AF.Ln, scale=1.0 / 128.0, bias=RMS_EPS),
                             reads=[ssb], writes=[lnb])
                        P.op("act", lambda e: e.activation(out=rs, in_=lnt, func=AF.Exp, scale=-0.5), reads=[lnb], writes=[rsb])
                        P.op("dve", lambda e: e.scalar_tensor_tensor(out=o_g[:, h, :], in0=od, scalar=subg, in1=rs, op0=ALU.mult, op1=ALU.mult),
                             reads=[odb, rsb, smallb], writes=[ogb[h]])

            sched = self.df_sched
            n = len(tasks)
            maxlag = max(l for (_, l) in sched)
            fns = {"A": stA, "B": stB, "C": stC}
            for s in range(n + maxlag):
                for (name, lag) in sched:
                    k = s - lag
                    if 0 <= k < n:
                        fns[name](tasks[k])
            dbg = getattr(self, "dbg", None)
            if dbg:
                srcs = {"o": (o_g, ogb), "hn": (hn_g, hngb)}
                if dbg == "k":
                    for m in range(NCH):
                        P.op("dve", lambda e, m=m, tgs=tgs: e.tensor_copy(out=xT[:, m, tgs], in_=kT[:, m, tgs]),
                             reads=[kb[m][g], self.xb[m][g]], writes=[self.xb[m][g]])
                elif dbg == "v":
                    for m in range(NCH):
                        P.op("dve", lambda e, m=m, tgs=tgs: e.tensor_copy(out=xT[:, m, tgs], in_=vT[:, g * 4 + m // 2, (m % 2) * 512:(m % 2) * 512 + 512]),
                             reads=[vb[g * 4 + m // 2][nb] for nb in range(4)] + [self.xb[m][g]], writes=[self.xb[m][g]])
                else:
                    s3, sb3 = srcs[dbg]
                    for m in range(NCH):
                        P.op("dve", lambda e, m=m, tgs=tgs, s3=s3: e.tensor_copy(out=xT[:, m, tgs], in_=s3[:, m, :]),
                             reads=[sb3[m], self.xb[m][g]], writes=[self.xb[m][g]])
                for nb in range(4):
                    self.ws.acquire(("df", 20 + nb, g))
                continue
            for nb in range(4):
                slot, sbuf_ = self.ws.acquire(("df", 20 + nb, g))
                for half in range(2):
                    c = nb * 2 + half

                    def evac(ps, pb, c=c, tgs=tgs, g=g):
                        P.op("dve", lambda e: e.tensor_tensor(
                            out=xT[:, c, tgs], in0=ps[:], in1=xT[:, c, tgs], op=ALU.add),
                            reads=[pb, self.xb[c][g]], writes=[self.xb[c][g]])
                    self.proj_fm(slot, sbuf_, half, o_g, ogb, evac)


def _consts():
    c = np.zeros((128, 512), np.float32)
    c[:, 0:128] = np.eye(128, dtype=np.float32)
    j = np.arange(128)[:, None]
    k = np.arange(128)[None, :]
    c[:, 128:256] = (j >= k).astype(np.float32)
    c[:, 256:384] = np.where(j >= k, MASKNEG, 0.0)
    c[:, 384:512] = 1.0
    return c


def _layout_weights(inputs):
    w_in = np.asarray(inputs["ffn_w_in"], np.float32).reshape(4, NCH, 128, 2, NF, 128)
    w_in = np.ascontiguousarray(w_in.transpose(0, 4, 2, 3, 1, 5)).reshape(4, NF, 128, 2048)
    w_out = np.asarray(inputs["ffn_w_out"], np.float32).reshape(4, NF, 128, NCH, 128)
    w_out = np.ascontiguousarray(w_out.transpose(0, 3, 2, 1, 4)).reshape(4, NCH, 128, D_FF)

    def colblocks(w, ncols):
        nb = ncols // 256
        a = w.reshape(NCH, 128, nb, 256).transpose(2, 1, 0, 3)
        return np.ascontiguousarray(a).reshape(nb, 128, 2048)
    sbq = np.asarray(inputs["sb_w_qkv"], np.float32)[0]
    sbo = np.asarray(inputs["sb_w_o"], np.float32)[0]
    w_sb = np.concatenate([colblocks(sbq, 3072), colblocks(sbo, 1024)], axis=0)
    dq = np.asarray(inputs["diff_w_qkv"], np.float32)[0]
    do = np.asarray(inputs["diff_w_o"], np.float32)[0]
    perm = np.arange(1024).reshape(16, 2, 32)[:, ::-1, :].reshape(1024)
    w_df = np.concatenate([colblocks(dq[:, 0:1024], 1024), colblocks(dq[:, 0:1024][:, perm], 1024),
                           colblocks(dq[:, 1024:2048], 1024), colblocks(dq[:, 1024:2048][:, perm], 1024),
                           colblocks(dq[:, 2048:3072], 1024), colblocks(do, 1024)], axis=0)
    return w_in, w_out, w_sb, w_df


def _rope_tables():
    p = np.arange(128)
    i = p % 32
    inv = 10000.0 ** (-(i.astype(np.float32)) / 32.0)
    ang = np.arange(SEQ, dtype=np.float32)[None, :] * inv[:, None].astype(np.float32)
    cos = np.cos(ang).astype(np.float32)
    sin = np.sin(ang).astype(np.float32)
    sgn = np.where((p % 64) < 32, -1.0, 1.0).astype(np.float32)[:, None]
    return np.ascontiguousarray(np.stack([cos, sin * sgn], axis=1))


FULL_STAGES = [("ffn", 0, 0), ("sb", 1), ("ffn", 1, 2), ("ffn", 2, 3), ("diff", 4), ("ffn", 3, 5), ("final", 6)]
_CACHE = {}


def _prepare(inputs):
    w_in, w_out, w_sb, w_df = _layout_weights(inputs)
    ng = np.asarray(inputs["norm_gains"], np.float32).reshape(6, NCH, 128)
    fg = np.asarray(inputs["final_gain"], np.float32).reshape(1, NCH, 128)
    gains = np.ascontiguousarray(np.concatenate([ng, fg], 0).transpose(2, 0, 1)).reshape(128, 56)
    lam = np.ascontiguousarray(np.broadcast_to(np.asarray(inputs["diff_lambda"], np.float32).reshape(1, 256), (128, 256)))
    subln = np.ascontiguousarray(np.asarray(inputs["diff_subln"], np.float32).reshape(128, 1))
    shared = {"gains": gains, "consts": _consts(), "w_in": w_in, "w_out": w_out, "w_sb": w_sb, "w_df": w_df,
              "rope": _rope_tables(), "lam": lam, "subln": subln}
    return shared


def _x_to_dev(xb):
    return np.ascontiguousarray(xb.T.reshape(NCH, 128, SEQ).transpose(1, 0, 2))


def _x_from_dev(y):
    return np.ascontiguousarray(y.transpose(1, 0, 2).reshape(D_MODEL, SEQ).T)


def run_stages(stages, xs, shared, core_ids=None):
    key = tuple(stages)
    if key not in _CACHE:
        _CACHE[key] = Builder(list(stages)).build()
    nc = _CACHE[key]
    n = len(xs)
    in_maps = [dict(shared, xT=_x_to_dev(np.asarray(x, np.float32))) for x in xs]
    res = run_bass_kernel_spmd(nc, in_maps, core_ids=list(range(n)))
    return [_x_from_dev(r["outT"]) for r in res.results]


def kernel(**inputs):
    shared = _prepare(inputs)
    x = np.asarray(inputs["x"], np.float32)
    outs = run_stages(FULL_STAGES, [x[b] for b in range(x.shape[0])], shared)
    return np.stack(outs, axis=0).astype(np.float32)
```

```python
import math
from contextlib import ExitStack

import numpy as np
import concourse.bass as bass
import concourse.mybir as mybir
from concourse.bass_utils import run_bass_kernel_spmd

F32 = mybir.dt.float32
BF16 = mybir.dt.bfloat16
AF = mybir.ActivationFunctionType
ALU = mybir.AluOpType

ENGS = ("pe", "act", "dve", "pool", "sp")

D_MODEL = 1024
SEQ = 2048
NCH = 8
D_FF = 2816
NF = 22
TG = 512
NG = 4
RMS_EPS = 1e-6
LAMBDA_INIT = 0.8 - 0.6 * math.exp(-0.3 * (2 - 1))
NSLOT = 5
MASKNEG = -30000.0


class Buf:
    __slots__ = ("name", "last_w", "readers")

    def __init__(self, name):
        self.name = name
        self.last_w = None
        self.readers = []


class Op:
    __slots__ = ("eng", "fn", "deps", "needed", "ticket", "dsem", "idx", "is_dma")

    def __init__(self, eng, fn, dsem=None):
        self.eng = eng
        self.fn = fn
        self.deps = []
        self.needed = False
        self.ticket = None
        self.dsem = dsem
        self.is_dma = dsem is not None
        self.idx = None


class Prog:
    def __init__(self):
        self.ops = {e: [] for e in ENGS}
        self.n = 0
        self.dma_keys = []
        self.last_dma = {}

    def op(self, eng, fn, reads=(), writes=(), dsem=None, extra_deps=()):
        o = Op(eng, fn, dsem)
        o.idx = self.n
        self.n += 1
        if dsem is not None:
            if dsem not in self.dma_keys:
                self.dma_keys.append(dsem)
        deps = {}
        for b in reads:
            if b.last_w is not None:
                deps[id(b.last_w)] = b.last_w
        for b in writes:
            if b.last_w is not None:
                deps[id(b.last_w)] = b.last_w
            for r in b.readers:
                deps[id(r)] = r
        for d in extra_deps:
            deps[id(d)] = d
        if dsem is not None and dsem in self.last_dma:
            d = self.last_dma[dsem]
            deps[id(d)] = d
        for d in deps.values():
            if d.eng == "pe" and eng == "pe" and not d.is_dma and not o.is_dma:
                continue
            o.deps.append(d)
            d.needed = True
        for b in reads:
            b.readers.append(o)
        for b in writes:
            b.last_w = o
            b.readers = []
        if dsem is not None:
            self.last_dma[dsem] = o
        self.ops[eng].append(o)
        return o

    def barrier(self):
        lasts = []
        for e in ENGS:
            for o in reversed(self.ops[e]):
                if not o.is_dma and o.fn is not None:
                    lasts.append(o)
                    break
        lasts.extend(self.last_dma.values())
        for e in ENGS:
            self.op(e, None, extra_deps=[d for d in lasts])

    def assign(self):
        cnt = {e: 0 for e in ENGS}
        dcnt = {k: 0 for k in self.dma_keys}
        allops = []
        for e in ENGS:
            allops.extend(self.ops[e])
        allops.sort(key=lambda o: o.idx)
        for o in allops:
            if o.is_dma:
                dcnt[o.dsem] += 16
                o.ticket = dcnt[o.dsem]
            elif o.needed and o.fn is not None:
                cnt[o.eng] += 1
                o.ticket = cnt[o.eng]

    def run_engine(self, eng, e, sems, dsems):
        waited = {}
        for o in self.ops[eng]:
            for d in o.deps:
                if d.ticket is None:
                    continue
                key = ("d", d.dsem) if d.is_dma else ("e", d.eng)
                if waited.get(key, 0) >= d.ticket:
                    continue
                waited[key] = d.ticket
                s = dsems[d.dsem] if d.is_dma else sems[d.eng]
                e.wait_ge(s, d.ticket)
            if o.fn is None:
                continue
            ins = o.fn(e)
            if o.is_dma:
                ins.then_inc(dsems[o.dsem], 16)
            elif o.needed:
                ins.then_inc(sems[o.eng], 1)


class WStream:
    def __init__(self, P, ring, nslot):
        self.P = P
        self.ring = ring
        self.nslot = nslot
        self.blocks = []
        self.issued = 0
        self.next = 0

    def plan(self, tag, src, n):
        self.blocks.append((tag, src, n))

    def _issue(self, k):
        tag, src, n = self.blocks[k]
        tile, buf = self.ring[k % self.nslot]
        s = k % self.nslot

        def fn(e, tile=tile, src=src, n=n):
            if n % 2 == 0 and n > 1024:
                return e.dma_start(out=tile[:, 0:n].rearrange("p (a b) -> p a b", a=2),
                                   in_=src.rearrange("p (a b) -> p a b", a=2))
            return e.dma_start(out=tile[:, 0:n], in_=src)
        self.P.op("pool", fn, writes=[buf], dsem=f"w{s}")

    def acquire(self, tag):
        k = self.next
        self.next += 1
        assert self.blocks[k][0] == tag, (self.blocks[k][0], tag)
        while self.issued < min(len(self.blocks), k + self.nslot - 1):
            self._issue(self.issued)
            self.issued += 1
        return self.ring[k % self.nslot]


class Builder:
    def __init__(self, stages):
        self.stages = stages
        nc = bass.Bass("TRN2", target_bir_lowering=False)
        self.nc = nc
        self.P = Prog()
        d = lambda name, shape, kind="ExternalInput": nc.dram_tensor(name, shape, F32, kind=kind).ap()
        self.d_x = d("xT", [128, NCH, SEQ])
        self.d_gains = d("gains", [128, 56])
        self.d_consts = d("consts", [128, 512])
        self.d_win = d("w_in", [4, NF, 128, 2048])
        self.d_wout = d("w_out", [4, NCH, 128, D_FF])
        self.d_wsb = d("w_sb", [16, 128, 2048])
        self.d_wdf = d("w_df", [24, 128, 2048])
        self.d_rope = d("rope", [128, 2, SEQ])
        self.d_lam = d("lam", [128, 256])
        self.d_subln = d("subln", [128, 1])
        self.d_out = d("outT", [128, NCH, SEQ], kind="ExternalOutput")

    def sb(self, name, shape, dt):
        return self.es.enter_context(self.nc.sbuf_tensor(name, shape, dt))

    def build(self):
        nc, P = self.nc, self.P
        with ExitStack() as es:
            self.es = es
            self.xT = self.sb("xT_sb", [128, NCH, SEQ], F32)
            self.HN = self.sb("hn_sb", [128, NCH * SEQ], BF16)
            self.R = self.sb("r_sb", [128, 16 * SEQ], BF16)
            self.ringt = [self.sb(f"ring{i}", [128, 2048], BF16) for i in range(NSLOT)]
            self.SCR = self.sb("scr", [128, 3072], F32)
            self.SCRB = self.sb("scrb", [128, 2048], BF16)
            self.SCRC = self.sb("scrc", [128, 2048], BF16)
            self.cst = self.sb("cst", [128, 512], BF16)
            self.gains = self.sb("gains_sb", [128, 56], F32)
            self.lamt = self.sb("lam_sb", [128, 256], F32)
            self.small = self.sb("small_sb", [128, 8], F32)
            self.ropet = self.sb("rope_sb", [128, 2, TG], F32)
            self.psum = []
            for i in range(8):
                t = es.enter_context(nc.psum_tensor(f"ps{i}", [128, 512], F32))
                self.psum.append((t, Buf(f"ps{i}")))
            self.sems = {e: es.enter_context(nc.semaphore(f"s_{e}")) for e in ENGS}

            self.xb = [[Buf(f"x{c}_{g}") for g in range(NG)] for c in range(NCH)]
            self.ring = [(self.ringt[i], Buf(f"ring{i}")) for i in range(NSLOT)]
            self.scrb = [Buf(f"scr{i}") for i in range(6)]
            self.scrbb = [Buf(f"scrb{i}") for i in range(4)]
            self.cstb = Buf("cst")
            self.gainb = Buf("gains")
            self.lamb = Buf("lam")
            self.smallb = Buf("small")
            self.ropeb = Buf("rope")
            self.outb = [Buf(f"out{g}") for g in range(NG)]
            self.ws = WStream(P, self.ring, NSLOT)
            self.par = {}
            self.df_sched = [("A", 0), ("B", 1), ("C", 2)]
            self.sb_sched = [("A", 0), ("B1", 1), ("D1", 3), ("B2", 1), ("C", 2), ("D2", 3), ("E", 4)]

            self.plan_weights()
            self.emit_all()

            P.assign()
            dsems = {k: es.enter_context(nc.semaphore(f"d_{k}")) for k in P.dma_keys}
            sems = self.sems
            with nc.Block() as block:
                @block.sync
                def _(e):
                    P.run_engine("sp", e, sems, dsems)

                @block.gpsimd
                def _(e):
                    P.run_engine("pool", e, sems, dsems)

                @block.tensor
                def _(e):
                    P.run_engine("pe", e, sems, dsems)

                @block.scalar
                def _(e):
                    P.run_engine("act", e, sems, dsems)

                @block.vector
                def _(e):
                    P.run_engine("dve", e, sems, dsems)
        return nc

    def flip(self, key, n=2):
        v = self.par.get(key, 0)
        self.par[key] = (v + 1) % n
        return v

    def plan_weights(self):
        ws = self.ws
        for st in self.stages:
            kind = st[0]
            if kind == "ffn":
                widx = st[1]
                for (j0, j1) in ((0, 11), (11, 22)):
                    for j in range(j0, j1):
                        ws.plan(("win", widx, j), self.d_win[widx, j], 2048)
                    for c in range(NCH):
                        ws.plan(("wout", widx, j0, c), self.d_wout[widx, c, :, j0 * 128:j1 * 128], (j1 - j0) * 128)
            elif kind == "sb":
                for g in range(NG):
                    for blk in range(12):
                        ws.plan(("sb", blk, g), self.d_wsb[blk], 2048)
                    for blk in range(12, 16):
                        ws.plan(("sb", blk, g), self.d_wsb[blk], 2048)
            elif kind == "diff":
                for g in range(NG):
                    for blk in Builder.DF_ORDER:
                        ws.plan(("df", blk, g), self.d_wdf[blk], 2048)

    def emit_all(self):
        P = self.P
        P.op("pool", lambda e: e.dma_start(out=self.cst[:], in_=self.d_consts), writes=[self.cstb], dsem="cst")
        P.op("sp", lambda e: e.dma_start(out=self.gains[:], in_=self.d_gains), writes=[self.gainb], dsem="gains")
        for g in range(NG):
            P.op("sp", lambda e, g=g: e.dma_start(out=self.xT[:, :, g * TG:(g + 1) * TG],
                                                  in_=self.d_x[:, :, g * TG:(g + 1) * TG]),
                 writes=[self.xb[c][g] for c in range(NCH)], dsem=f"x{g}")
        first = True
        for st in self.stages:
            if not first:
                P.barrier()
            first = False
            if st[0] == "ffn":
                self.ffn(st[1], st[2])
            elif st[0] == "sb":
                self.sb_mixer(st[1])
            elif st[0] == "diff":
                self.diff_mixer(st[1])
            elif st[0] == "final":
                self.final_norm(st[1])
        P.barrier()
        for g in range(NG):
            P.op("sp", lambda e, g=g: e.dma_start(out=self.d_out[:, :, g * TG:(g + 1) * TG],
                                                  in_=self.xT[:, :, g * TG:(g + 1) * TG]),
                 reads=[self.xb[c][g] for c in range(NCH)], writes=[self.outb[g]], dsem=f"o{g}")
        P.op("sp", None, reads=self.outb)

    def norm_group(self, nidx, g, dst_fn, in_place=False):
        P = self.P
        tgs = slice(g * TG, (g + 1) * TG)
        ones = self.cst[:, 384:512]
        ps, psb = self.psum[6 + self.flip("norm")]
        xT = self.xT
        for c in range(NCH):
            k = self.flip("sq")
            sq = self.SCRB[:, (2 + k) * 512:(3 + k) * 512]
            sqb = self.scrbb[2 + k]
            P.op("act", lambda e, c=c, sq=sq: e.activation(out=sq, in_=xT[:, c, tgs], func=AF.Square),
                 reads=[self.xb[c][g]], writes=[sqb])
            P.op("pe", lambda e, c=c, sq=sq: e.matmul(ps[:], lhsT=ones, rhs=sq, start=(c == 0), stop=(c == NCH - 1)),
                 reads=[sqb, self.cstb], writes=[psb])
        lnt = self.SCR[:, 5 * 512:6 * 512]
        lnb = self.scrb[5]
        rs = self.SCR[:, 4 * 512:5 * 512]
        rsb = self.scrb[4]
        P.op("act", lambda e: e.activation(out=lnt, in_=ps[:], func=AF.Ln, scale=1.0 / D_MODEL, bias=RMS_EPS),
             reads=[psb], writes=[lnb])
        P.op("act", lambda e: e.activation(out=rs, in_=lnt, func=AF.Exp, scale=-0.5), reads=[lnb], writes=[rsb])
        for c in range(NCH):
            out_ap, obufs = dst_fn(c)
            gcol = self.gains[:, nidx * 8 + c:nidx * 8 + c + 1]
            rd = [self.xb[c][g], rsb, self.gainb]
            P.op("dve", lambda e, c=c, out_ap=out_ap, gcol=gcol: e.scalar_tensor_tensor(
                out=out_ap, in0=xT[:, c, tgs], scalar=gcol, in1=rs, op0=ALU.mult, op1=ALU.mult),
                reads=rd, writes=obufs)

    def ffn(self, widx, nidx):
        P = self.P
        xT = self.xT
        hn3 = self.HN[:].rearrange("p (c t) -> p c t", c=NCH)
        hnb = [[Buf(f"hn{c}_{g}") for g in range(NG)] for c in range(NCH)]
        actT = self.R[:, 0:11 * SEQ].rearrange("p (j t) -> p j t", j=11)
        actb = [[Buf(f"act{j}_{g}") for g in range(NG)] for j in range(11)]
        for g in range(NG):
            tgs = slice(g * TG, (g + 1) * TG)
            self.norm_group(nidx, g, lambda c, tgs=tgs, g=g: (hn3[:, c, tgs], [hnb[c][g]]))
        for (j0, j1) in ((0, 11), (11, 22)):
            nj = j1 - j0
            for j in range(j0, j1):
                slot, sbuf_ = self.ws.acquire(("win", widx, j))
                wv = slot[:, 0:2048].rearrange("p (g c m) -> p g c m", g=2, c=NCH)
                jj = j - j0
                for g in range(NG):
                    tgs = slice(g * TG, (g + 1) * TG)
                    par = self.flip("gu")
                    gps, gpb = self.psum[par * 2]
                    ups, upb = self.psum[par * 2 + 1]
                    for gu, (ps, pb) in enumerate(((gps, gpb), (ups, upb))):
                        for c in range(NCH):
                            P.op("pe", lambda e, ps=ps, gu=gu, c=c, wv=wv, tgs=tgs: e.matmul(
                                ps[:], lhsT=wv[:, gu, c, :], rhs=hn3[:, c, tgs], start=(c == 0), stop=(c == NCH - 1)),
                                reads=[sbuf_, hnb[c][g]], writes=[pb])
                    sg = self.SCR[:, par * 512:(par + 1) * 512]
                    sgb = self.scrb[par]
                    P.op("act", lambda e, sg=sg, gps=gps: e.activation(out=sg, in_=gps[:], func=AF.Silu),
                         reads=[gpb], writes=[sgb])
                    P.op("dve", lambda e, sg=sg, ups=ups, jj=jj, tgs=tgs: e.tensor_tensor(
                        out=actT[:, jj, tgs], in0=sg, in1=ups[:], op=ALU.mult),
                        reads=[sgb, upb], writes=[actb[jj][g]])
            for c in range(NCH):
                slot, sbuf_ = self.ws.acquire(("wout", widx, j0, c))
                wv = slot[:, 0:nj * 128].rearrange("p (j m) -> p j m", m=128)
                for g in range(NG):
                    tgs = slice(g * TG, (g + 1) * TG)
                    ps, pb = self.psum[4 + self.flip("y")]
                    for jj in range(nj):
                        P.op("pe", lambda e, ps=ps, jj=jj, wv=wv, tgs=tgs: e.matmul(
                            ps[:], lhsT=wv[:, jj, :], rhs=actT[:, jj, tgs], start=(jj == 0), stop=(jj == nj - 1)),
                            reads=[sbuf_, actb[jj][g]], writes=[pb])
                    P.op("dve", lambda e, ps=ps, c=c, tgs=tgs: e.scalar_tensor_tensor(
                        out=xT[:, c, tgs], in0=ps[:], scalar=0.5, in1=xT[:, c, tgs], op0=ALU.mult, op1=ALU.add),
                        reads=[pb, self.xb[c][g]], writes=[self.xb[c][g]])

    def final_norm(self, nidx):
        xT = self.xT
        for g in range(NG):
            tgs = slice(g * TG, (g + 1) * TG)
            self.norm_group(nidx, g, lambda c, tgs=tgs, g=g: (xT[:, c, tgs], [self.xb[c][g]]))

    def proj_fm(self, slot, sbuf_, half, src3, srcb, evac):
        P = self.P
        wv = slot[:, 0:2048].rearrange("p (c m) -> p c m", c=NCH)
        ps, pb = self.psum[6 + self.flip("norm")]
        for c in range(NCH):
            P.op("pe", lambda e, ps=ps, c=c, wv=wv, half=half: e.matmul(
                ps[:], lhsT=wv[:, c, half * 128:(half + 1) * 128], rhs=src3[:, c, :],
                start=(c == 0), stop=(c == NCH - 1)),
                reads=[sbuf_, srcb[c]], writes=[pb])
        evac(ps, pb)

    def sb_mixer(self, nidx):
        P = self.P
        xT = self.xT
        kT = self.R[:, 0:NCH * SEQ].rearrange("p (m t) -> p m t", m=NCH)
        vT = self.R[:, NCH * SEQ:16 * SEQ].rearrange("p (i f) -> p i f", i=16)
        kb = [[Buf(f"k{m}_{g}") for g in range(NG)] for m in range(NCH)]
        vb = [[Buf(f"v{i}_{nb}") for nb in range(4)] for i in range(16)]
        hn_g = self.HN[:, 0:4096].rearrange("p (c t) -> p c t", c=NCH)
        q_m = [self.HN[:, 4096:8192].rearrange("p (c t) -> p c t", c=NCH),
               self.HN[:, 8192:12288].rearrange("p (c t) -> p c t", c=NCH)]
        o_g = self.HN[:, 12288:16384].rearrange("p (c t) -> p c t", c=NCH)
        hngb = [Buf(f"hng{c}") for c in range(NCH)]
        qgb = [[Buf(f"qg{hf}_{c}") for c in range(NCH)] for hf in range(2)]
        ogb = [Buf(f"og{c}") for c in range(NCH)]
        P.op("dve", lambda e: e.memset(q_m[0][64:128, :, :], 0.0), writes=qgb[0])
        P.op("dve", lambda e: e.memset(q_m[1][0:64, :, :], 0.0), writes=qgb[1])
        ident = self.cst[:, 0:128]
        tinc = self.cst[:, 128:256]
        maskneg = self.cst[:, 256:384]
        ones = self.cst[:, 384:512]
        cstb = self.cstb
        NE = 4
        e_t = [(self.SCR[:, k * 512:(k + 1) * 512], self.scrb[k]) for k in range(4)]
        ecs_t = [(self.SCR[:, (4 + k) * 512:(5 + k) * 512], self.scrb[4 + k]) for k in range(2)]
        sp_t = [(self.SCRB[:, k * 512:(k + 1) * 512], self.scrbb[k]) for k in range(2)]
        aT_t = [(self.SCRC[:, k * 512:(k + 1) * 512], Buf(f"aT{k}")) for k in range(2)]
        R_t = [(self.SCRC[:, 1024 + k * 512:1024 + (k + 1) * 512], Buf(f"R{k}")) for k in range(2)]

        class T:
            pass

        for g in range(NG):
            tgs = slice(g * TG, (g + 1) * TG)
            self.norm_group(nidx, g, lambda c: (hn_g[:, c, :], [hngb[c]]))
            for kind in range(2):
                for nb in range(4):
                    slot, sbuf_ = self.ws.acquire(("sb", kind * 4 + nb, g))
                    for half in range(2):
                        m = nb * 2 + half
                        if kind == 0:
                            def evac(ps, pb, m=m):
                                for hf in range(2):
                                    P.op("act", lambda e, hf=hf: e.activation(out=q_m[hf][hf * 64:(hf + 1) * 64, m, :],
                                                                              in_=ps[hf * 64:(hf + 1) * 64, :], func=AF.Copy, scale=0.125),
                                         reads=[pb], writes=[qgb[hf][m]])
                        else:
                            def evac(ps, pb, m=m, tgs=tgs, g=g):
                                P.op("dve", lambda e: e.tensor_copy(out=kT[:, m, tgs], in_=ps[:]),
                                     reads=[pb], writes=[kb[m][g]])
                        self.proj_fm(slot, sbuf_, half, hn_g, hngb, evac)
            for nb in range(4):
                slot, sbuf_ = self.ws.acquire(("sb", 8 + nb, g))
                wv = slot[:, 0:2048].rearrange("p (c m) -> p c m", c=NCH)
                for tt in range(4):
                    ti = g * 4 + tt
                    ps, pb = self.psum[6 + self.flip("norm")]
                    for c in range(NCH):
                        P.op("pe", lambda e, ps=ps, c=c, wv=wv, tt=tt: e.matmul(
                            ps[:, 0:256], lhsT=hn_g[:, c, tt * 128:(tt + 1) * 128], rhs=wv[:, c, :],
                            start=(c == 0), stop=(c == NCH - 1)),
                            reads=[sbuf_, hngb[c]], writes=[pb])
                    P.op("dve", lambda e, ps=ps, ti=ti, nb=nb: e.tensor_copy(out=vT[:, ti, nb * 256:(nb + 1) * 256], in_=ps[:, 0:256]),
                         reads=[pb], writes=[vb[ti][nb]])
            tasks = []
            for m in range(NCH):
                for half in range(2):
                    tiles = list(range(4 * g + 3, -1, -1))
                    for idx, i in enumerate(tiles):
                        t = T()
                        t.m, t.half, t.h, t.p0, t.i, t.idx = m, half, 2 * m + half, 64 * half, i, idx
                        t.qoff = max(0, i - 4 * g) * 128
                        t.diag = i >= 4 * g
                        t.first = idx == 0
                        t.last = idx == len(tiles) - 1
                        t.nqoff = 0 if t.last else max(0, tiles[idx + 1] - 4 * g) * 128
                        t.n = len(tasks)
                        tasks.append(t)
            o_cur = {}

            def stA(t):
                t.z_ps, t.zb = self.psum[t.n % 2]
                qo = t.qoff
                P.op("pe", lambda e: e.matmul(
                    t.z_ps[:, qo:512], lhsT=kT[:, t.m, t.i * 128:(t.i + 1) * 128],
                    rhs=q_m[t.half][:, t.m, qo:512], start=True, stop=(not t.diag)),
                    reads=[kb[t.m][t.i // 4], qgb[t.half][t.m]], writes=[t.zb])
                if t.diag:
                    P.op("pe", lambda e: e.matmul(
                        t.z_ps[:, qo:qo + 128], lhsT=ident, rhs=maskneg, start=False, stop=True),
                        reads=[cstb], writes=[t.zb])
                for _f in range(getattr(self, "fill", 0)):
                    fps, fpb = self.psum[6 + (_f % 2)]
                    P.op("pe", lambda e, fps=fps: e.matmul(fps[:], lhsT=ones, rhs=self.cst[:, 0:512], start=True, stop=True),
                         reads=[cstb], writes=[fpb])

            def stB1(t):
                t.et, t.eb = e_t[t.n % NE]
                qo = t.qoff
                P.op("act", lambda e: e.activation(out=t.et[:, qo:512], in_=t.z_ps[:, qo:512], func=AF.Exp),
                     reads=[t.zb], writes=[t.eb])

            def stB2(t):
                t.spt, t.spb = sp_t[t.n % 2]
                qo = t.qoff
                P.op("act", lambda e: e.activation(out=t.spt[:, qo:512], in_=t.et[:, qo:512], func=AF.Ln, bias=1.0),
                     reads=[t.eb], writes=[t.spb])

            def stC(t):
                t.cs_ps, t.csb = self.psum[2 + t.n % 2]
                qo = t.qoff
                P.op("pe", lambda e: e.matmul(
                    t.cs_ps[:, qo:512], lhsT=tinc, rhs=t.spt[:, qo:512], start=True, stop=t.first),
                    reads=[t.spb, cstb], writes=[t.csb])
                if not t.first:
                    rt, rb = t.rprev
                    P.op("pe", lambda e: e.matmul(
                        t.cs_ps[:, qo:512], lhsT=ones, rhs=rt[:, qo:512], start=False, stop=True),
                        reads=[rb, cstb], writes=[t.csb])
                if not t.last:
                    rn, rnb = R_t[t.n % 2]
                    if t.first:
                        P.op("dve", lambda e: e.tensor_copy(out=rn[:, qo:512], in_=t.spt[:, qo:512]),
                             reads=[t.spb], writes=[rnb])
                    else:
                        rt, rb = t.rprev
                        P.op("dve", lambda e: e.tensor_tensor(
                            out=rn[:, qo:512], in0=rt[:, qo:512], in1=t.spt[:, qo:512], op=ALU.add),
                            reads=[t.spb, rb], writes=[rnb])
                    if t.nqoff < qo:
                        P.op("dve", lambda e: e.memset(rn[:, t.nqoff:qo], 0.0), reads=[], writes=[rnb])
                    tasks[t.n + 1].rprev = (rn, rnb)

            def stD1(t):
                t.ect, t.ecb = ecs_t[t.n % 2]
                qo = t.qoff
                P.op("act", lambda e: e.activation(out=t.ect[:, qo:512], in_=t.cs_ps[:, qo:512], func=AF.Exp, scale=-1.0),
                     reads=[t.csb], writes=[t.ecb])

            def stD2(t):
                t.at, t.ab = aT_t[t.n % 2]
                qo = t.qoff
                P.op("dve", lambda e: e.tensor_tensor(
                    out=t.at[:, qo:512], in0=t.et[:, qo:512], in1=t.ect[:, qo:512], op=ALU.mult),
                    reads=[t.eb, t.ecb], writes=[t.ab])

            def stE(t):
                o_ps, opb = self.psum[4 + t.half]
                qo = t.qoff
                P.op("pe", lambda e: e.matmul(
                    o_ps[:, qo:512], lhsT=vT[:, t.i, t.m * 128:(t.m + 1) * 128], rhs=t.at[:, qo:512],
                    start=t.first, stop=t.last),
                    reads=[vb[t.i][t.m // 2], t.ab], writes=[opb])
                if t.last:
                    m, p0 = t.m, t.p0
                    P.op("act", lambda e: e.activation(out=o_g[p0:p0 + 64, m, :], in_=o_ps[p0:p0 + 64, :], func=AF.Copy),
                         reads=[opb], writes=[ogb[m]])

            sched = self.sb_sched
            n = len(tasks)
            maxlag = max(l for (_, l) in sched)
            fns = {"A": stA, "B1": stB1, "B2": stB2, "C": stC, "D1": stD1, "D2": stD2, "E": stE}
            for s in range(n + maxlag):
                for (name, lag) in sched:
                    k = s - lag
                    if 0 <= k < n:
                        fns[name](tasks[k])
            for nb in range(4):
                slot, sbuf_ = self.ws.acquire(("sb", 12 + nb, g))
                for half in range(2):
                    c = nb * 2 + half

                    def evac(ps, pb, c=c, tgs=tgs, g=g):
                        P.op("dve", lambda e: e.tensor_tensor(
                            out=xT[:, c, tgs], in0=ps[:], in1=xT[:, c, tgs], op=ALU.add),
                            reads=[pb, self.xb[c][g]], writes=[self.xb[c][g]])
                    self.proj_fm(slot, sbuf_, half, o_g, ogb, evac)

    DF_ORDER = [0, 4, 1, 5, 2, 6, 3, 7, 8, 12, 9, 13, 10, 14, 11, 15, 16, 17, 18, 19, 20, 21, 22, 23]

    def diff_mixer(self, nidx):
        P = self.P
        xT = self.xT
        kT = self.R[:, 0:NCH * SEQ].rearrange("p (m t) -> p m t", m=NCH)
        vT = self.R[:, NCH * SEQ:16 * SEQ].rearrange("p (i f) -> p i f", i=16)
        kb = [[Buf(f"k{m}_{g}") for g in range(NG)] for m in range(NCH)]
        vb = [[Buf(f"v{i}_{nb}") for nb in range(4)] for i in range(16)]
        hn_g = self.HN[:, 0:4096].rearrange("p (c t) -> p c t", c=NCH)
        q_m = [self.HN[:, 4096:8192].rearrange("p (c t) -> p c t", c=NCH),
               self.HN[:, 8192:12288].rearrange("p (c t) -> p c t", c=NCH)]
        o_g = self.HN[:, 12288:16384].rearrange("p (c t) -> p c t", c=NCH)
        hngb = [Buf(f"hng{c}") for c in range(NCH)]
        qgb = [[Buf(f"qg{hf}_{c}") for c in range(NCH)] for hf in range(2)]
        ogb = [Buf(f"og{c}") for c in range(NCH)]
        P.op("dve", lambda e: e.memset(q_m[0][64:128, :, :], 0.0), writes=qgb[0])
        P.op("dve", lambda e: e.memset(q_m[1][0:64, :, :], 0.0), writes=qgb[1])
        ones = self.cst[:, 384:512]
        cstb = self.cstb
        NP = 3
        p_t = [(self.SCRC[:, k * 512:(k + 1) * 512], Buf(f"pT{k}")) for k in range(NP)]
        sqo, sqob = self.SCRC[:, 1536:2048], Buf("sqo")
        scr = lambda k: (self.SCR[:, k * 512:(k + 1) * 512], self.scrb[k])
        small, smallb = self.small, self.smallb
        ropet, ropeb = self.ropet, self.ropeb

        tmp, tmpb = scr(0)
        P.op("sp", lambda e: e.dma_start(out=self.lamt[:], in_=self.d_lam), writes=[self.lamb], dsem="lam")
        P.op("sp", lambda e: e.dma_start(out=small[:, 6:7], in_=self.d_subln), writes=[smallb], dsem="subln")
        for k in range(2):
            P.op("dve", lambda e, k=k: e.tensor_tensor(out=tmp[:, k * 64:(k + 1) * 64], in0=self.lamt[:, k * 128:k * 128 + 64],
                                                       in1=self.lamt[:, k * 128 + 64:k * 128 + 128], op=ALU.mult),
                 reads=[self.lamb], writes=[tmpb])
            P.op("dve", lambda e, k=k: e.reduce_sum(out=small[:, k:k + 1], in_=tmp[:, k * 64:(k + 1) * 64], axis=mybir.AxisListType.X),
                 reads=[tmpb], writes=[smallb])
        P.op("act", lambda e: e.activation(out=small[:, 2:4], in_=small[:, 0:2], func=AF.Exp), reads=[smallb], writes=[smallb])
        P.op("dve", lambda e: e.tensor_tensor(out=small[:, 4:5], in0=small[:, 3:4], in1=small[:, 2:3], op=ALU.subtract),
             reads=[smallb], writes=[smallb])
        P.op("dve", lambda e: e.tensor_scalar(out=small[:, 4:5], in0=small[:, 4:5], scalar1=-LAMBDA_INIT, scalar2=None, op0=ALU.add),
             reads=[smallb], writes=[smallb])
        P.op("dve", lambda e: e.tensor_scalar(out=small[:, 5:6], in0=small[:, 6:7], scalar1=(1.0 - LAMBDA_INIT), scalar2=None, op0=ALU.mult),
             reads=[smallb], writes=[smallb])
        neglam = small[:, 4:5]
        subg = small[:, 5:6]

        class T:
            pass

        for g in range(NG):
            tgs = slice(g * TG, (g + 1) * TG)
            self.norm_group(nidx, g, lambda c: (hn_g[:, c, :], [hngb[c]]))
            P.op("sp", lambda e, tgs=tgs: e.dma_start(out=ropet[:], in_=self.d_rope[:, :, tgs]), writes=[ropeb], dsem="rope")
            for kind in range(2):
                for nb in range(4):
                    slotA, sbA = self.ws.acquire(("df", kind * 8 + nb, g))
                    slotB, sbB = self.ws.acquire(("df", kind * 8 + 4 + nb, g))
                    for half in range(2):
                        m = nb * 2 + half
                        pair = self.flip("rp")
                        banks = (6, 7) if pair == 0 else (2, 3)
                        pss = []
                        for slot, sbuf_, bk in ((slotA, sbA, banks[0]), (slotB, sbB, banks[1])):
                            wv = slot[:, 0:2048].rearrange("p (c m) -> p c m", c=NCH)
                            ps, pb = self.psum[bk]
                            for c in range(NCH):
                                P.op("pe", lambda e, ps=ps, c=c, wv=wv, half=half: e.matmul(
                                    ps[:], lhsT=wv[:, c, half * 128:(half + 1) * 128], rhs=hn_g[:, c, :],
                                    start=(c == 0), stop=(c == NCH - 1)),
                                    reads=[sbuf_, hngb[c]], writes=[pb])
                            pss.append((ps, pb))
                        (t1, t1b), (t2, t2b) = scr(pair * 2), scr(pair * 2 + 1)
                        (ps1, pb1), (ps2, pb2) = pss
                        P.op("dve", lambda e, t1=t1, ps1=ps1: e.tensor_tensor(out=t1, in0=ps1[:], in1=ropet[:, 0, :], op=ALU.mult),
                             reads=[pb1, ropeb], writes=[t1b])
                        P.op("dve", lambda e, t2=t2, ps2=ps2: e.tensor_tensor(out=t2, in0=ps2[:], in1=ropet[:, 1, :], op=ALU.mult),
                             reads=[pb2, ropeb], writes=[t2b])
                        if kind == 0:
                            for hf in range(2):
                                P.op("dve", lambda e, t1=t1, t2=t2, m=m, hf=hf: e.tensor_tensor(
                                    out=q_m[hf][hf * 64:(hf + 1) * 64, m, :], in0=t1[hf * 64:(hf + 1) * 64, :],
                                    in1=t2[hf * 64:(hf + 1) * 64, :], op=ALU.add),
                                    reads=[t1b, t2b], writes=[qgb[hf][m]])
                        else:
                            P.op("dve", lambda e, t1=t1, t2=t2, m=m, tgs=tgs: e.tensor_tensor(out=kT[:, m, tgs], in0=t1, in1=t2, op=ALU.add),
                                 reads=[t1b, t2b], writes=[kb[m][g]])
            for nb in range(4):
                slot, sbuf_ = self.ws.acquire(("df", 16 + nb, g))
                wv = slot[:, 0:2048].rearrange("p (c m) -> p c m", c=NCH)
                for tt in range(4):
                    ti = g * 4 + tt
                    ps, pb = self.psum[6 + self.flip("norm")]
                    for c in range(NCH):
                        P.op("pe", lambda e, ps=ps, c=c, wv=wv, tt=tt: e.matmul(
                            ps[:, 0:256], lhsT=hn_g[:, c, tt * 128:(tt + 1) * 128], rhs=wv[:, c, :],
                            start=(c == 0), stop=(c == NCH - 1)),
                            reads=[sbuf_, hngb[c]], writes=[pb])
                    P.op("dve", lambda e, ps=ps, ti=ti, nb=nb: e.tensor_copy(out=vT[:, ti, nb * 256:(nb + 1) * 256], in_=ps[:, 0:256]),
                         reads=[pb], writes=[vb[ti][nb]])
            tasks = []
            for h in range(NCH):
                for mp in range(2):
                    ntl = 4 * g + 4
                    for i in range(ntl):
                        t = T()
                        t.h, t.mp, t.p0, t.i = h, mp, 64 * mp, i
                        t.qoff = max(0, i - 4 * g) * 128
                        t.diag = i >= 4 * g
                        t.first = i == 0
                        t.last = i == ntl - 1
                        t.n = len(tasks)
                        tasks.append(t)
            acc = {}

            def stA(t):
                t.s_ps, t.sb_ = self.psum[t.n % 2]
                qo = t.qoff
                P.op("pe", lambda e: e.matmul(
                    t.s_ps[:, qo:512], lhsT=kT[:, t.h, t.i * 128:(t.i + 1) * 128],
                    rhs=q_m[t.mp][:, t.h, qo:512], start=True, stop=True),
                    reads=[kb[t.h][t.i // 4], qgb[t.mp][t.h]], writes=[t.sb_])

            def stB(t):
                t.pt, t.ptb = p_t[t.n % NP]
                qo = t.qoff
                P.op("act", lambda e: e.activation(out=t.pt[:, qo:512], in_=t.s_ps[:, qo:512], func=AF.Exp, scale=0.125),
                     reads=[t.sb_], writes=[t.ptb])
                if t.diag:
                    P.op("dve", lambda e: e.memset(t.pt[64:128, qo:qo + 64], 0.0), reads=[], writes=[t.ptb])

            def stC(t):
                key = (t.h, t.mp)
                if t.first:
                    k = self.flip("acc")
                    acc[key] = (self.psum[2 + k], self.psum[4 + k])
                (den_ps, denb), (o_ps, opb) = acc[key]
                qo = t.qoff
                P.op("pe", lambda e: e.matmul(den_ps[:, qo:512], lhsT=ones, rhs=t.pt[:, qo:512], start=t.first, stop=t.last),
                     reads=[t.ptb, cstb], writes=[denb])
                P.op("pe", lambda e: e.matmul(o_ps[:, qo:512], lhsT=vT[:, t.i, t.h * 128:(t.h + 1) * 128], rhs=t.pt[:, qo:512],
                                              start=t.first, stop=t.last),
                     reads=[t.ptb, vb[t.i][t.h // 2]], writes=[opb])
                if t.last:
                    (lnd, lndb), (rden, rdenb) = scr(2), scr(3)
                    (o1n, o1nb), (od, odb) = scr(0), scr(1)
                    P.op("act", lambda e: e.activation(out=lnd, in_=den_ps[:], func=AF.Ln), reads=[denb], writes=[lndb])
                    P.op("act", lambda e: e.activation(out=rden, in_=lnd, func=AF.Exp, scale=-1.0), reads=[lndb], writes=[rdenb])
                    if t.mp == 0:
                        P.op("dve", lambda e: e.tensor_tensor(out=o1n, in0=o_ps[:], in1=rden, op=ALU.mult),
                             reads=[opb, rdenb], writes=[o1nb])
                    else:
                        h = t.h
                        P.op("dve", lambda e: e.tensor_tensor(out=od, in0=o_ps[:], in1=rden, op=ALU.mult),
                             reads=[opb, rdenb], writes=[odb])
                        P.op("dve", lambda e: e.scalar_tensor_tensor(out=od, in0=od, scalar=neglam, in1=o1n, op0=ALU.mult, op1=ALU.add),
                             reads=[odb, o1nb, smallb], writes=[odb])
                        P.op("act", lambda e: e.activation(out=sqo, in_=od, func=AF.Square), reads=[odb], writes=[sqob])
                        ss_ps, ssb = self.psum[6 + self.flip("norm")]
                        P.op("pe", lambda e: e.matmul(ss_ps[:], lhsT=ones, rhs=sqo, start=True, stop=True),
                             reads=[sqob, cstb], writes=[ssb])
                        (rs, rsb), (lnt, lnb) = scr(4), scr(5)
                        P.op("act", lambda e: e.activation(out=lnt, in_=ss_ps[:], func=AF.Ln, scale=1.0 / 128.0, bias=RMS_EPS),
                             reads=[ssb], writes=[lnb])
                        P.op("act", lambda e: e.activation(out=rs, in_=lnt, func=AF.Exp, scale=-0.5), reads=[lnb], writes=[rsb])
                        P.op("dve", lambda e: e.scalar_tensor_tensor(out=o_g[:, h, :], in0=od, scalar=subg, in1=rs, op0=ALU.mult, op1=ALU.mult),
                             reads=[odb, rsb, smallb], writes=[ogb[h]])

            sched = self.df_sched
            n = len(tasks)
            maxlag = max(l for (_, l) in sched)
            fns = {"A": stA, "B": stB, "C": stC}
            for s in range(n + maxlag):
                for (name, lag) in sched:
                    k = s - lag
                    if 0 <= k < n:
                        fns[name](tasks[k])
            dbg = getattr(self, "dbg", None)
            if dbg:
                srcs = {"o": (o_g, ogb), "hn": (hn_g, hngb)}
                if dbg == "k":
                    for m in range(NCH):
                        P.op("dve", lambda e, m=m, tgs=tgs: e.tensor_copy(out=xT[:, m, tgs], in_=kT[:, m, tgs]),
                             reads=[kb[m][g], self.xb[m][g]], writes=[self.xb[m][g]])
                elif dbg == "v":
                    for m in range(NCH):
                        P.op("dve", lambda e, m=m, tgs=tgs: e.tensor_copy(out=xT[:, m, tgs], in_=vT[:, g * 4 + m // 2, (m % 2) * 512:(m % 2) * 512 + 512]),
                             reads=[vb[g * 4 + m // 2][nb] for nb in range(4)] + [self.xb[m][g]], writes=[self.xb[m][g]])
                else:
                    s3, sb3 = srcs[dbg]
                    for m in range(NCH):
                        P.op("dve", lambda e, m=m, tgs=tgs, s3=s3: e.tensor_copy(out=xT[:, m, tgs], in_=s3[:, m, :]),
                             reads=[sb3[m], self.xb[m][g]], writes=[self.xb[m][g]])
                for nb in range(4):
                    self.ws.acquire(("df", 20 + nb, g))
                continue
            for nb in range(4):
                slot, sbuf_ = self.ws.acquire(("df", 20 + nb, g))
                for half in range(2):
                    c = nb * 2 + half

                    def evac(ps, pb, c=c, tgs=tgs, g=g):
                        P.op("dve", lambda e: e.tensor_tensor(
                            out=xT[:, c, tgs], in0=ps[:], in1=xT[:, c, tgs], op=ALU.add),
                            reads=[pb, self.xb[c][g]], writes=[self.xb[c][g]])
                    self.proj_fm(slot, sbuf_, half, o_g, ogb, evac)


def _consts():
    c = np.zeros((128, 512), np.float32)
    c[:, 0:128] = np.eye(128, dtype=np.float32)
    j = np.arange(128)[:, None]
    k = np.arange(128)[None, :]
    c[:, 128:256] = (j >= k).astype(np.float32)
    c[:, 256:384] = np.where(j >= k, MASKNEG, 0.0)
    c[:, 384:512] = 1.0
    return c


def _layout_weights(inputs):
    w_in = np.asarray(inputs["ffn_w_in"], np.float32).reshape(4, NCH, 128, 2, NF, 128)
    w_in = np.ascontiguousarray(w_in.transpose(0, 4, 2, 3, 1, 5)).reshape(4, NF, 128, 2048)
    w_out = np.asarray(inputs["ffn_w_out"], np.float32).reshape(4, NF, 128, NCH, 128)
    w_out = np.ascontiguousarray(w_out.transpose(0, 3, 2, 1, 4)).reshape(4, NCH, 128, D_FF)

    def colblocks(w, ncols):
        nb = ncols // 256
        a = w.reshape(NCH, 128, nb, 256).transpose(2, 1, 0, 3)
        return np.ascontiguousarray(a).reshape(nb, 128, 2048)
    sbq = np.asarray(inputs["sb_w_qkv"], np.float32)[0]
    sbo = np.asarray(inputs["sb_w_o"], np.float32)[0]
    w_sb = np.concatenate([colblocks(sbq, 3072), colblocks(sbo, 1024)], axis=0)
    dq = np.asarray(inputs["diff_w_qkv"], np.float32)[0]
    do = np.asarray(inputs["diff_w_o"], np.float32)[0]
    perm = np.arange(1024).reshape(16, 2, 32)[:, ::-1, :].reshape(1024)
    w_df = np.concatenate([colblocks(dq[:, 0:1024], 1024), colblocks(dq[:, 0:1024][:, perm], 1024),
                           colblocks(dq[:, 1024:2048], 1024), colblocks(dq[:, 1024:2048][:, perm], 1024),
                           colblocks(dq[:, 2048:3072], 1024), colblocks(do, 1024)], axis=0)
    return w_in, w_out, w_sb, w_df


def _rope_tables():
    p = np.arange(128)
    i = p % 32
    inv = 10000.0 ** (-(i.astype(np.float32)) / 32.0)
    ang = np.arange(SEQ, dtype=np.float32)[None, :] * inv[:, None].astype(np.float32)
    cos = np.cos(ang).astype(np.float32)
    sin = np.sin(ang).astype(np.float32)
    sgn = np.where((p % 64) < 32, -1.0, 1.0).astype(np.float32)[:, None]
    return np.ascontiguousarray(np.stack([cos, sin * sgn], axis=1))


FULL_STAGES = [("ffn", 0, 0), ("sb", 1), ("ffn", 1, 2), ("ffn", 2, 3), ("diff", 4), ("ffn", 3, 5), ("final", 6)]
_CACHE = {}
_RUN_KW = {}


def _prepare(inputs):
    w_in, w_out, w_sb, w_df = _layout_weights(inputs)
    ng = np.asarray(inputs["norm_gains"], np.float32).reshape(6, NCH, 128)
    fg = np.asarray(inputs["final_gain"], np.float32).reshape(1, NCH, 128)
    gains = np.ascontiguousarray(np.concatenate([ng, fg], 0).transpose(2, 0, 1)).reshape(128, 56)
    lam = np.ascontiguousarray(np.broadcast_to(np.asarray(inputs["diff_lambda"], np.float32).reshape(1, 256), (128, 256)))
    subln = np.ascontiguousarray(np.asarray(inputs["diff_subln"], np.float32).reshape(128, 1))
    shared = {"gains": gains, "consts": _consts(), "w_in": w_in, "w_out": w_out, "w_sb": w_sb, "w_df": w_df,
              "rope": _rope_tables(), "lam": lam, "subln": subln}
    return shared


def _x_to_dev(xb):
    return np.ascontiguousarray(xb.T.reshape(NCH, 128, SEQ).transpose(1, 0, 2))


def _x_from_dev(y):
    return np.ascontiguousarray(y.transpose(1, 0, 2).reshape(D_MODEL, SEQ).T)


def run_stages(stages, xs, shared, core_ids=None):
    key = tuple(stages)
    if key not in _CACHE:
        _CACHE[key] = Builder(list(stages)).build()
    nc = _CACHE[key]
    n = len(xs)
    in_maps = [dict(shared, xT=_x_to_dev(np.asarray(x, np.float32))) for x in xs]
    res = run_bass_kernel_spmd(nc, in_maps, core_ids=list(range(n)), **_RUN_KW)
    if res.exec_time_ns is not None:
        print("exec_time_ns", res.exec_time_ns)
    return [_x_from_dev(r["outT"]) for r in res.results]


def kernel(**inputs):
    shared = _prepare(inputs)
    x = np.asarray(inputs["x"], np.float32)
    outs = run_stages(FULL_STAGES, [x[b] for b in range(x.shape[0])], shared)
    return np.stack(outs, axis=0).astype(np.float32)
```

```python
import math
from contextlib import ExitStack

import numpy as np
import concourse.bass as bass
import concourse.mybir as mybir
from concourse.bass_utils import run_bass_kernel_spmd

F32 = mybir.dt.float32
BF16 = mybir.dt.bfloat16
AF = mybir.ActivationFunctionType
ALU = mybir.AluOpType

ENGS = ("pe", "act", "dve", "pool", "sp")

D_MODEL = 1024
SEQ = 2048
NCH = 8
D_FF = 2816
NF = 22
TG = 512
NG = 4
RMS_EPS = 1e-6
LAMBDA_INIT = 0.8 - 0.6 * math.exp(-0.3 * (2 - 1))
NSLOT = 5
MASKNEG = -30000.0


class Buf:
    __slots__ = ("name", "last_w", "readers")

    def __init__(self, name):
        self.name = name
        self.last_w = None
        self.readers = []


class Op:
    __slots__ = ("eng", "fn", "deps", "needed", "ticket", "dsem", "idx", "is_dma")

    def __init__(self, eng, fn, dsem=None):
        self.eng = eng
        self.fn = fn
        self.deps = []
        self.needed = False
        self.ticket = None
        self.dsem = dsem
        self.is_dma = dsem is not None
        self.idx = None


class Prog:
    def __init__(self):
        self.ops = {e: [] for e in ENGS}
        self.n = 0
        self.dma_keys = []
        self.last_dma = {}

    def op(self, eng, fn, reads=(), writes=(), dsem=None, extra_deps=()):
        o = Op(eng, fn, dsem)
        o.idx = self.n
        self.n += 1
        if dsem is not None:
            if dsem not in self.dma_keys:
                self.dma_keys.append(dsem)
        deps = {}
        for b in reads:
            if b.last_w is not None:
                deps[id(b.last_w)] = b.last_w
        for b in writes:
            if b.last_w is not None:
                deps[id(b.last_w)] = b.last_w
            for r in b.readers:
                deps[id(r)] = r
        for d in extra_deps:
            deps[id(d)] = d
        if dsem is not None and dsem in self.last_dma:
            d = self.last_dma[dsem]
            deps[id(d)] = d
        for d in deps.values():
            if d.eng == "pe" and eng == "pe" and not d.is_dma and not o.is_dma:
                continue
            o.deps.append(d)
            d.needed = True
        for b in reads:
            b.readers.append(o)
        for b in writes:
            b.last_w = o
            b.readers = []
        if dsem is not None:
            self.last_dma[dsem] = o
        self.ops[eng].append(o)
        return o

    def barrier(self):
        lasts = []
        for e in ENGS:
            for o in reversed(self.ops[e]):
                if not o.is_dma and o.fn is not None:
                    lasts.append(o)
                    break
        lasts.extend(self.last_dma.values())
        for e in ENGS:
            self.op(e, None, extra_deps=[d for d in lasts])

    def assign(self):
        cnt = {e: 0 for e in ENGS}
        dcnt = {k: 0 for k in self.dma_keys}
        allops = []
        for e in ENGS:
            allops.extend(self.ops[e])
        allops.sort(key=lambda o: o.idx)
        for o in allops:
            if o.is_dma:
                dcnt[o.dsem] += 16
                o.ticket = dcnt[o.dsem]
            elif o.needed and o.fn is not None:
                cnt[o.eng] += 1
                o.ticket = cnt[o.eng]

    def run_engine(self, eng, e, sems, dsems):
        waited = {}
        for o in self.ops[eng]:
            for d in o.deps:
                if d.ticket is None:
                    continue
                key = ("d", d.dsem) if d.is_dma else ("e", d.eng)
                if waited.get(key, 0) >= d.ticket:
                    continue
                waited[key] = d.ticket
                s = dsems[d.dsem] if d.is_dma else sems[d.eng]
                e.wait_ge(s, d.ticket)
            if o.fn is None:
                continue
            ins = o.fn(e)
            if o.is_dma:
                ins.then_inc(dsems[o.dsem], 16)
            elif o.needed:
                ins.then_inc(sems[o.eng], 1)


class WStream:
    def __init__(self, P, ring, nslot):
        self.P = P
        self.ring = ring
        self.nslot = nslot
        self.blocks = []
        self.issued = 0
        self.next = 0

    def plan(self, tag, src, n):
        self.blocks.append((tag, src, n))

    def _issue(self, k):
        tag, src, n = self.blocks[k]
        tile, buf = self.ring[k % self.nslot]
        s = k % self.nslot

        def fn(e, tile=tile, src=src, n=n):
            if n % 2 == 0 and n > 1024:
                return e.dma_start(out=tile[:, 0:n].rearrange("p (a b) -> p a b", a=2),
                                   in_=src.rearrange("p (a b) -> p a b", a=2))
            return e.dma_start(out=tile[:, 0:n], in_=src)
        self.P.op("pool", fn, writes=[buf], dsem=f"w{s}")

    def acquire(self, tag):
        k = self.next
        self.next += 1
        assert self.blocks[k][0] == tag, (self.blocks[k][0], tag)
        while self.issued < min(len(self.blocks), k + self.nslot - 1):
            self._issue(self.issued)
            self.issued += 1
        return self.ring[k % self.nslot]


class Builder:
    def __init__(self, stages):
        self.stages = stages
        nc = bass.Bass("TRN2", target_bir_lowering=False)
        self.nc = nc
        self.P = Prog()
        d = lambda name, shape, kind="ExternalInput": nc.dram_tensor(name, shape, F32, kind=kind).ap()
        self.d_x = d("xT", [128, NCH, SEQ])
        self.d_gains = d("gains", [128, 56])
        self.d_consts = d("consts", [128, 512])
        self.d_win = d("w_in", [4, NF, 128, 2048])
        self.d_wout = d("w_out", [4, NCH, 128, D_FF])
        self.d_wsb = d("w_sb", [16, 128, 2048])
        self.d_wdf = d("w_df", [24, 128, 2048])
        self.d_rope = d("rope", [128, 2, SEQ])
        self.d_lam = d("lam", [128, 256])
        self.d_subln = d("subln", [128, 1])
        self.d_out = d("outT", [128, NCH, SEQ], kind="ExternalOutput")

    def sb(self, name, shape, dt):
        return self.es.enter_context(self.nc.sbuf_tensor(name, shape, dt))

    def build(self):
        nc, P = self.nc, self.P
        with ExitStack() as es:
            self.es = es
            self.xT = self.sb("xT_sb", [128, NCH, SEQ], F32)
            self.HN = self.sb("hn_sb", [128, NCH * SEQ], BF16)
            self.R = self.sb("r_sb", [128, 16 * SEQ], BF16)
            self.ringt = [self.sb(f"ring{i}", [128, 2048], BF16) for i in range(NSLOT)]
            self.SCR = self.sb("scr", [128, 3072], F32)
            self.SCRB = self.sb("scrb", [128, 2048], BF16)
            self.SCRC = self.sb("scrc", [128, 2048], BF16)
            self.cst = self.sb("cst", [128, 512], BF16)
            self.gains = self.sb("gains_sb", [128, 56], F32)
            self.lamt = self.sb("lam_sb", [128, 256], F32)
            self.small = self.sb("small_sb", [128, 8], F32)
            self.ropet = self.sb("rope_sb", [128, 2, TG], F32)
            self.psum = []
            for i in range(8):
                t = es.enter_context(nc.psum_tensor(f"ps{i}", [128, 512], F32))
                self.psum.append((t, Buf(f"ps{i}")))
            self.sems = {e: es.enter_context(nc.semaphore(f"s_{e}")) for e in ENGS}

            self.xb = [[Buf(f"x{c}_{g}") for g in range(NG)] for c in range(NCH)]
            self.ring = [(self.ringt[i], Buf(f"ring{i}")) for i in range(NSLOT)]
            self.scrb = [Buf(f"scr{i}") for i in range(6)]
            self.scrbb = [Buf(f"scrb{i}") for i in range(4)]
            self.cstb = Buf("cst")
            self.gainb = Buf("gains")
            self.lamb = Buf("lam")
            self.smallb = Buf("small")
            self.ropeb = Buf("rope")
            self.outb = [Buf(f"out{g}") for g in range(NG)]
            self.ws = WStream(P, self.ring, NSLOT)
            self.par = {}
            self.df_sched = [("C", 4), ("A", 0), ("B", 2), ("F2", 6), ("F3", 7), ("F4", 8), ("F5", 9), ("F6", 10)]
            self.sb_sched = [("A", 0), ("B1", 1), ("D1", 3), ("B2", 1), ("C", 2), ("D2", 3), ("E", 4), ("F", 6)]

            self.plan_weights()
            self.emit_all()

            P.assign()
            dsems = {k: es.enter_context(nc.semaphore(f"d_{k}")) for k in P.dma_keys}
            sems = self.sems
            with nc.Block() as block:
                @block.sync
                def _(e):
                    P.run_engine("sp", e, sems, dsems)

                @block.gpsimd
                def _(e):
                    P.run_engine("pool", e, sems, dsems)

                @block.tensor
                def _(e):
                    P.run_engine("pe", e, sems, dsems)

                @block.scalar
                def _(e):
                    P.run_engine("act", e, sems, dsems)

                @block.vector
                def _(e):
                    P.run_engine("dve", e, sems, dsems)
        return nc

    def flip(self, key, n=2):
        v = self.par.get(key, 0)
        self.par[key] = (v + 1) % n
        return v

    def plan_weights(self):
        ws = self.ws
        for st in self.stages:
            kind = st[0]
            if kind == "ffn":
                widx = st[1]
                for (j0, j1) in ((0, 11), (11, 22)):
                    for j in range(j0, j1):
                        ws.plan(("win", widx, j), self.d_win[widx, j], 2048)
                    for c in range(NCH):
                        ws.plan(("wout", widx, j0, c), self.d_wout[widx, c, :, j0 * 128:j1 * 128], (j1 - j0) * 128)
            elif kind == "sb":
                for g in range(NG):
                    for blk in range(12):
                        ws.plan(("sb", blk, g), self.d_wsb[blk], 2048)
                    for blk in range(12, 16):
                        ws.plan(("sb", blk, g), self.d_wsb[blk], 2048)
            elif kind == "diff":
                for g in range(NG):
                    for blk in Builder.DF_ORDER:
                        ws.plan(("df", blk, g), self.d_wdf[blk], 2048)

    def emit_all(self):
        P = self.P
        P.op("pool", lambda e: e.dma_start(out=self.cst[:], in_=self.d_consts), writes=[self.cstb], dsem="cst")
        P.op("sp", lambda e: e.dma_start(out=self.gains[:], in_=self.d_gains), writes=[self.gainb], dsem="gains")
        for g in range(NG):
            P.op("sp", lambda e, g=g: e.dma_start(out=self.xT[:, :, g * TG:(g + 1) * TG],
                                                  in_=self.d_x[:, :, g * TG:(g + 1) * TG]),
                 writes=[self.xb[c][g] for c in range(NCH)], dsem=f"x{g}")
        first = True
        for st in self.stages:
            if not first:
                P.barrier()
            first = False
            if st[0] == "ffn":
                self.ffn(st[1], st[2])
            elif st[0] == "sb":
                self.sb_mixer(st[1])
            elif st[0] == "diff":
                self.diff_mixer(st[1])
            elif st[0] == "final":
                self.final_norm(st[1])
        P.barrier()
        for g in range(NG):
            P.op("sp", lambda e, g=g: e.dma_start(out=self.d_out[:, :, g * TG:(g + 1) * TG],
                                                  in_=self.xT[:, :, g * TG:(g + 1) * TG]),
                 reads=[self.xb[c][g] for c in range(NCH)], writes=[self.outb[g]], dsem=f"o{g}")
        P.op("sp", None, reads=self.outb)

    def norm_group(self, nidx, g, dst_fn, in_place=False):
        P = self.P
        tgs = slice(g * TG, (g + 1) * TG)
        ones = self.cst[:, 384:512]
        ps, psb = self.psum[6 + self.flip("norm")]
        xT = self.xT
        for c in range(NCH):
            k = self.flip("sq")
            sq = self.SCRB[:, (2 + k) * 512:(3 + k) * 512]
            sqb = self.scrbb[2 + k]
            P.op("act", lambda e, c=c, sq=sq: e.activation(out=sq, in_=xT[:, c, tgs], func=AF.Square),
                 reads=[self.xb[c][g]], writes=[sqb])
            P.op("pe", lambda e, c=c, sq=sq: e.matmul(ps[:], lhsT=ones, rhs=sq, start=(c == 0), stop=(c == NCH - 1)),
                 reads=[sqb, self.cstb], writes=[psb])
        lnt = self.SCR[:, 5 * 512:6 * 512]
        lnb = self.scrb[5]
        rs = self.SCR[:, 4 * 512:5 * 512]
        rsb = self.scrb[4]
        P.op("act", lambda e: e.activation(out=lnt, in_=ps[:], func=AF.Ln, scale=1.0 / D_MODEL, bias=RMS_EPS),
             reads=[psb], writes=[lnb])
        P.op("act", lambda e: e.activation(out=rs, in_=lnt, func=AF.Exp, scale=-0.5), reads=[lnb], writes=[rsb])
        for c in range(NCH):
            out_ap, obufs = dst_fn(c)
            gcol = self.gains[:, nidx * 8 + c:nidx * 8 + c + 1]
            rd = [self.xb[c][g], rsb, self.gainb]
            P.op("dve", lambda e, c=c, out_ap=out_ap, gcol=gcol: e.scalar_tensor_tensor(
                out=out_ap, in0=xT[:, c, tgs], scalar=gcol, in1=rs, op0=ALU.mult, op1=ALU.mult),
                reads=rd, writes=obufs)

    def ffn(self, widx, nidx):
        P = self.P
        xT = self.xT
        hn3 = self.HN[:].rearrange("p (c t) -> p c t", c=NCH)
        hnb = [[Buf(f"hn{c}_{g}") for g in range(NG)] for c in range(NCH)]
        actT = self.R[:, 0:11 * SEQ].rearrange("p (j t) -> p j t", j=11)
        actb = [[Buf(f"act{j}_{g}") for g in range(NG)] for j in range(11)]
        for g in range(NG):
            tgs = slice(g * TG, (g + 1) * TG)
            self.norm_group(nidx, g, lambda c, tgs=tgs, g=g: (hn3[:, c, tgs], [hnb[c][g]]))
        for (j0, j1) in ((0, 11), (11, 22)):
            nj = j1 - j0
            for j in range(j0, j1):
                slot, sbuf_ = self.ws.acquire(("win", widx, j))
                wv = slot[:, 0:2048].rearrange("p (g c m) -> p g c m", g=2, c=NCH)
                jj = j - j0
                for g in range(NG):
                    tgs = slice(g * TG, (g + 1) * TG)
                    par = self.flip("gu")
                    gps, gpb = self.psum[par * 2]
                    ups, upb = self.psum[par * 2 + 1]
                    for gu, (ps, pb) in enumerate(((gps, gpb), (ups, upb))):
                        for c in range(NCH):
                            P.op("pe", lambda e, ps=ps, gu=gu, c=c, wv=wv, tgs=tgs: e.matmul(
                                ps[:], lhsT=wv[:, gu, c, :], rhs=hn3[:, c, tgs], start=(c == 0), stop=(c == NCH - 1)),
                                reads=[sbuf_, hnb[c][g]], writes=[pb])
                    sg = self.SCR[:, par * 512:(par + 1) * 512]
                    sgb = self.scrb[par]
                    P.op("act", lambda e, sg=sg, gps=gps: e.activation(out=sg, in_=gps[:], func=AF.Silu),
                         reads=[gpb], writes=[sgb])
                    P.op("dve", lambda e, sg=sg, ups=ups, jj=jj, tgs=tgs: e.tensor_tensor(
                        out=actT[:, jj, tgs], in0=sg, in1=ups[:], op=ALU.mult),
                        reads=[sgb, upb], writes=[actb[jj][g]])
            for c in range(NCH):
                slot, sbuf_ = self.ws.acquire(("wout", widx, j0, c))
                wv = slot[:, 0:nj * 128].rearrange("p (j m) -> p j m", m=128)
                for g in range(NG):
                    tgs = slice(g * TG, (g + 1) * TG)
                    ps, pb = self.psum[4 + self.flip("y")]
                    for jj in range(nj):
                        P.op("pe", lambda e, ps=ps, jj=jj, wv=wv, tgs=tgs: e.matmul(
                            ps[:], lhsT=wv[:, jj, :], rhs=actT[:, jj, tgs], start=(jj == 0), stop=(jj == nj - 1)),
                            reads=[sbuf_, actb[jj][g]], writes=[pb])
                    P.op("dve", lambda e, ps=ps, c=c, tgs=tgs: e.scalar_tensor_tensor(
                        out=xT[:, c, tgs], in0=ps[:], scalar=0.5, in1=xT[:, c, tgs], op0=ALU.mult, op1=ALU.add),
                        reads=[pb, self.xb[c][g]], writes=[self.xb[c][g]])

    def final_norm(self, nidx):
        xT = self.xT
        for g in range(NG):
            tgs = slice(g * TG, (g + 1) * TG)
            self.norm_group(nidx, g, lambda c, tgs=tgs, g=g: (xT[:, c, tgs], [self.xb[c][g]]))

    def proj_fm(self, slot, sbuf_, half, src3, srcb, evac):
        P = self.P
        wv = slot[:, 0:2048].rearrange("p (c m) -> p c m", c=NCH)
        ps, pb = self.psum[6 + self.flip("norm")]
        for c in range(NCH):
            P.op("pe", lambda e, ps=ps, c=c, wv=wv, half=half: e.matmul(
                ps[:], lhsT=wv[:, c, half * 128:(half + 1) * 128], rhs=src3[:, c, :],
                start=(c == 0), stop=(c == NCH - 1)),
                reads=[sbuf_, srcb[c]], writes=[pb])
        evac(ps, pb)

    def sb_mixer(self, nidx):
        P = self.P
        xT = self.xT
        kT = self.R[:, 0:NCH * SEQ].rearrange("p (m t) -> p m t", m=NCH)
        vT = self.R[:, NCH * SEQ:16 * SEQ].rearrange("p (i f) -> p i f", i=16)
        kb = [[Buf(f"k{m}_{g}") for g in range(NG)] for m in range(NCH)]
        vb = [[Buf(f"v{i}_{nb}") for nb in range(4)] for i in range(16)]
        hn_g = self.HN[:, 0:4096].rearrange("p (c t) -> p c t", c=NCH)
        q_m = [self.HN[:, 4096:8192].rearrange("p (c t) -> p c t", c=NCH),
               self.HN[:, 8192:12288].rearrange("p (c t) -> p c t", c=NCH)]
        o_g = self.HN[:, 12288:16384].rearrange("p (c t) -> p c t", c=NCH)
        hngb = [Buf(f"hng{c}") for c in range(NCH)]
        qgb = [[Buf(f"qg{hf}_{c}") for c in range(NCH)] for hf in range(2)]
        ogb = [Buf(f"og{c}") for c in range(NCH)]
        P.op("dve", lambda e: e.memset(q_m[0][64:128, :, :], 0.0), writes=qgb[0])
        P.op("dve", lambda e: e.memset(q_m[1][0:64, :, :], 0.0), writes=qgb[1])
        ident = self.cst[:, 0:128]
        tinc = self.cst[:, 128:256]
        maskneg = self.cst[:, 256:384]
        ones = self.cst[:, 384:512]
        cstb = self.cstb
        NE = 4
        e_t = [(self.SCR[:, k * 512:(k + 1) * 512], self.scrb[k]) for k in range(4)]
        ecs_t = [(self.SCR[:, (4 + k) * 512:(5 + k) * 512], self.scrb[4 + k]) for k in range(2)]
        sp_t = [(self.SCRB[:, k * 512:(k + 1) * 512], self.scrbb[k]) for k in range(2)]
        aT_t = [(self.SCRC[:, k * 512:(k + 1) * 512], Buf(f"aT{k}")) for k in range(2)]
        R_t = [(self.SCRC[:, 1024 + k * 512:1024 + (k + 1) * 512], Buf(f"R{k}")) for k in range(2)]

        class T:
            pass

        for g in range(NG):
            tgs = slice(g * TG, (g + 1) * TG)
            self.norm_group(nidx, g, lambda c: (hn_g[:, c, :], [hngb[c]]))
            for kind in range(2):
                for nb in range(4):
                    slot, sbuf_ = self.ws.acquire(("sb", kind * 4 + nb, g))
                    for half in range(2):
                        m = nb * 2 + half
                        if kind == 0:
                            def evac(ps, pb, m=m):
                                for hf in range(2):
                                    P.op("act", lambda e, hf=hf: e.activation(out=q_m[hf][hf * 64:(hf + 1) * 64, m, :],
                                                                              in_=ps[hf * 64:(hf + 1) * 64, :], func=AF.Copy, scale=0.125),
                                         reads=[pb], writes=[qgb[hf][m]])
                        else:
                            def evac(ps, pb, m=m, tgs=tgs, g=g):
                                P.op("dve", lambda e: e.tensor_copy(out=kT[:, m, tgs], in_=ps[:]),
                                     reads=[pb], writes=[kb[m][g]])
                        self.proj_fm(slot, sbuf_, half, hn_g, hngb, evac)
            for nb in range(4):
                slot, sbuf_ = self.ws.acquire(("sb", 8 + nb, g))
                wv = slot[:, 0:2048].rearrange("p (c m) -> p c m", c=NCH)
                for tt in range(4):
                    ti = g * 4 + tt
                    ps, pb = self.psum[6 + self.flip("norm")]
                    for c in range(NCH):
                        P.op("pe", lambda e, ps=ps, c=c, wv=wv, tt=tt: e.matmul(
                            ps[:, 0:256], lhsT=hn_g[:, c, tt * 128:(tt + 1) * 128], rhs=wv[:, c, :],
                            start=(c == 0), stop=(c == NCH - 1)),
                            reads=[sbuf_, hngb[c]], writes=[pb])
                    P.op("dve", lambda e, ps=ps, ti=ti, nb=nb: e.tensor_copy(out=vT[:, ti, nb * 256:(nb + 1) * 256], in_=ps[:, 0:256]),
                         reads=[pb], writes=[vb[ti][nb]])
            tasks = []
            for m in range(NCH):
                for half in range(2):
                    tiles = list(range(4 * g + 3, -1, -1))
                    for idx, i in enumerate(tiles):
                        t = T()
                        t.m, t.half, t.h, t.p0, t.i, t.idx = m, half, 2 * m + half, 64 * half, i, idx
                        t.qoff = max(0, i - 4 * g) * 128
                        t.diag = i >= 4 * g
                        t.first = idx == 0
                        t.last = idx == len(tiles) - 1
                        t.nqoff = 0 if t.last else max(0, tiles[idx + 1] - 4 * g) * 128
                        t.n = len(tasks)
                        tasks.append(t)
            o_cur = {}

            def stA(t):
                t.z_ps, t.zb = self.psum[t.n % 2]
                qo = t.qoff
                P.op("pe", lambda e: e.matmul(
                    t.z_ps[:, qo:512], lhsT=kT[:, t.m, t.i * 128:(t.i + 1) * 128],
                    rhs=q_m[t.half][:, t.m, qo:512], start=True, stop=(not t.diag)),
                    reads=[kb[t.m][t.i // 4], qgb[t.half][t.m]], writes=[t.zb])
                if t.diag:
                    P.op("pe", lambda e: e.matmul(
                        t.z_ps[:, qo:qo + 128], lhsT=ident, rhs=maskneg, start=False, stop=True),
                        reads=[cstb], writes=[t.zb])
                for _f in range(getattr(self, "fill", 0)):
                    fps, fpb = self.psum[6 + (_f % 2)]
                    P.op("pe", lambda e, fps=fps: e.matmul(fps[:], lhsT=ones, rhs=self.cst[:, 0:512], start=True, stop=True),
                         reads=[cstb], writes=[fpb])

            def stB1(t):
                t.et, t.eb = e_t[t.n % NE]
                qo = t.qoff
                P.op("act", lambda e: e.activation(out=t.et[:, qo:512], in_=t.z_ps[:, qo:512], func=AF.Exp),
                     reads=[t.zb], writes=[t.eb])

            def stB2(t):
                t.spt, t.spb = sp_t[t.n % 2]
                qo = t.qoff
                P.op("act", lambda e: e.activation(out=t.spt[:, qo:512], in_=t.et[:, qo:512], func=AF.Ln, bias=1.0),
                     reads=[t.eb], writes=[t.spb])

            def stC(t):
                t.cs_ps, t.csb = self.psum[2 + t.n % 2]
                qo = t.qoff
                P.op("pe", lambda e: e.matmul(
                    t.cs_ps[:, qo:512], lhsT=tinc, rhs=t.spt[:, qo:512], start=True, stop=t.first),
                    reads=[t.spb, cstb], writes=[t.csb])
                if not t.first:
                    rt, rb = t.rprev
                    P.op("pe", lambda e: e.matmul(
                        t.cs_ps[:, qo:512], lhsT=ones, rhs=rt[:, qo:512], start=False, stop=True),
                        reads=[rb, cstb], writes=[t.csb])
                if not t.last:
                    rn, rnb = R_t[t.n % 2]
                    if t.first:
                        P.op("dve", lambda e: e.tensor_copy(out=rn[:, qo:512], in_=t.spt[:, qo:512]),
                             reads=[t.spb], writes=[rnb])
                    else:
                        rt, rb = t.rprev
                        P.op("dve", lambda e: e.tensor_tensor(
                            out=rn[:, qo:512], in0=rt[:, qo:512], in1=t.spt[:, qo:512], op=ALU.add),
                            reads=[t.spb, rb], writes=[rnb])
                    if t.nqoff < qo:
                        P.op("dve", lambda e: e.memset(rn[:, t.nqoff:qo], 0.0), reads=[], writes=[rnb])
                    tasks[t.n + 1].rprev = (rn, rnb)

            def stD1(t):
                t.ect, t.ecb = ecs_t[t.n % 2]
                qo = t.qoff
                P.op("act", lambda e: e.activation(out=t.ect[:, qo:512], in_=t.cs_ps[:, qo:512], func=AF.Exp, scale=-1.0),
                     reads=[t.csb], writes=[t.ecb])

            def stD2(t):
                t.at, t.ab = aT_t[t.n % 2]
                qo = t.qoff
                P.op("dve", lambda e: e.tensor_tensor(
                    out=t.at[:, qo:512], in0=t.et[:, qo:512], in1=t.ect[:, qo:512], op=ALU.mult),
                    reads=[t.eb, t.ecb], writes=[t.ab])

            def stE(t):
                o_ps, opb = self.psum[4 + t.half]
                qo = t.qoff
                P.op("pe", lambda e: e.matmul(
                    o_ps[:, qo:512], lhsT=vT[:, t.i, t.m * 128:(t.m + 1) * 128], rhs=t.at[:, qo:512],
                    start=t.first, stop=t.last),
                    reads=[vb[t.i][t.m // 2], t.ab], writes=[opb])

            def stF(t):
                if t.last:
                    o_ps, opb = self.psum[4 + t.half]
                    m, p0 = t.m, t.p0
                    P.op("act", lambda e: e.activation(out=o_g[p0:p0 + 64, m, :], in_=o_ps[p0:p0 + 64, :], func=AF.Copy),
                         reads=[opb], writes=[ogb[m]])

            sched = self.sb_sched
            n = len(tasks)
            maxlag = max(l for (_, l) in sched)
            fns = {"A": stA, "B1": stB1, "B2": stB2, "C": stC, "D1": stD1, "D2": stD2, "E": stE, "F": stF}
            for s in range(n + maxlag):
                for (name, lag) in sched:
                    k = s - lag
                    if 0 <= k < n:
                        fns[name](tasks[k])
            for nb in range(4):
                slot, sbuf_ = self.ws.acquire(("sb", 12 + nb, g))
                for half in range(2):
                    c = nb * 2 + half

                    def evac(ps, pb, c=c, tgs=tgs, g=g):
                        P.op("dve", lambda e: e.tensor_tensor(
                            out=xT[:, c, tgs], in0=ps[:], in1=xT[:, c, tgs], op=ALU.add),
                            reads=[pb, self.xb[c][g]], writes=[self.xb[c][g]])
                    self.proj_fm(slot, sbuf_, half, o_g, ogb, evac)

    DF_ORDER = [0, 4, 1, 5, 2, 6, 3, 7, 8, 12, 9, 13, 10, 14, 11, 15, 16, 17, 18, 19, 20, 21, 22, 23]

    def diff_mixer(self, nidx):
        P = self.P
        xT = self.xT
        kT = self.R[:, 0:NCH * SEQ].rearrange("p (m t) -> p m t", m=NCH)
        vT = self.R[:, NCH * SEQ:16 * SEQ].rearrange("p (i f) -> p i f", i=16)
        kb = [[Buf(f"k{m}_{g}") for g in range(NG)] for m in range(NCH)]
        vb = [[Buf(f"v{i}_{nb}") for nb in range(4)] for i in range(16)]
        hn_g = self.HN[:, 0:4096].rearrange("p (c t) -> p c t", c=NCH)
        q_m = [self.HN[:, 4096:8192].rearrange("p (c t) -> p c t", c=NCH),
               self.HN[:, 8192:12288].rearrange("p (c t) -> p c t", c=NCH)]
        o_g = self.HN[:, 12288:16384].rearrange("p (c t) -> p c t", c=NCH)
        hngb = [Buf(f"hng{c}") for c in range(NCH)]
        qgb = [[Buf(f"qg{hf}_{c}") for c in range(NCH)] for hf in range(2)]
        ogb = [Buf(f"og{c}") for c in range(NCH)]
        P.op("dve", lambda e: e.memset(q_m[0][64:128, :, :], 0.0), writes=qgb[0])
        P.op("dve", lambda e: e.memset(q_m[1][0:64, :, :], 0.0), writes=qgb[1])
        ones = self.cst[:, 384:512]
        cstb = self.cstb
        NP = 4
        p_t = [(self.SCRC[:, k * 512:(k + 1) * 512], Buf(f"pT{k}")) for k in range(NP)]
        sqo, sqob = self.SCRB[:, 0:512], self.scrbb[0]
        scr = lambda k: (self.SCR[:, k * 512:(k + 1) * 512], self.scrb[k])
        small, smallb = self.small, self.smallb
        ropet, ropeb = self.ropet, self.ropeb

        tmp, tmpb = scr(0)
        P.op("sp", lambda e: e.dma_start(out=self.lamt[:], in_=self.d_lam), writes=[self.lamb], dsem="lam")
        P.op("sp", lambda e: e.dma_start(out=small[:, 6:7], in_=self.d_subln), writes=[smallb], dsem="subln")
        for k in range(2):
            P.op("dve", lambda e, k=k: e.tensor_tensor(out=tmp[:, k * 64:(k + 1) * 64], in0=self.lamt[:, k * 128:k * 128 + 64],
                                                       in1=self.lamt[:, k * 128 + 64:k * 128 + 128], op=ALU.mult),
                 reads=[self.lamb], writes=[tmpb])
            P.op("dve", lambda e, k=k: e.reduce_sum(out=small[:, k:k + 1], in_=tmp[:, k * 64:(k + 1) * 64], axis=mybir.AxisListType.X),
                 reads=[tmpb], writes=[smallb])
        P.op("act", lambda e: e.activation(out=small[:, 2:4], in_=small[:, 0:2], func=AF.Exp), reads=[smallb], writes=[smallb])
        P.op("dve", lambda e: e.tensor_tensor(out=small[:, 4:5], in0=small[:, 3:4], in1=small[:, 2:3], op=ALU.subtract),
             reads=[smallb], writes=[smallb])
        P.op("dve", lambda e: e.tensor_scalar(out=small[:, 4:5], in0=small[:, 4:5], scalar1=-LAMBDA_INIT, scalar2=None, op0=ALU.add),
             reads=[smallb], writes=[smallb])
        P.op("dve", lambda e: e.tensor_scalar(out=small[:, 5:6], in0=small[:, 6:7], scalar1=(1.0 - LAMBDA_INIT), scalar2=None, op0=ALU.mult),
             reads=[smallb], writes=[smallb])
        neglam = small[:, 4:5]
        subg = small[:, 5:6]

        class T:
            pass

        for g in range(NG):
            tgs = slice(g * TG, (g + 1) * TG)
            self.norm_group(nidx, g, lambda c: (hn_g[:, c, :], [hngb[c]]))
            P.op("sp", lambda e, tgs=tgs: e.dma_start(out=ropet[:], in_=self.d_rope[:, :, tgs]), writes=[ropeb], dsem="rope")
            for kind in range(2):
                for nb in range(4):
                    slotA, sbA = self.ws.acquire(("df", kind * 8 + nb, g))
                    slotB, sbB = self.ws.acquire(("df", kind * 8 + 4 + nb, g))
                    for half in range(2):
                        m = nb * 2 + half
                        pair = self.flip("rp")
                        banks = (6, 7) if pair == 0 else (2, 3)
                        pss = []
                        for slot, sbuf_, bk in ((slotA, sbA, banks[0]), (slotB, sbB, banks[1])):
                            wv = slot[:, 0:2048].rearrange("p (c m) -> p c m", c=NCH)
                            ps, pb = self.psum[bk]
                            for c in range(NCH):
                                P.op("pe", lambda e, ps=ps, c=c, wv=wv, half=half: e.matmul(
                                    ps[:], lhsT=wv[:, c, half * 128:(half + 1) * 128], rhs=hn_g[:, c, :],
                                    start=(c == 0), stop=(c == NCH - 1)),
                                    reads=[sbuf_, hngb[c]], writes=[pb])
                            pss.append((ps, pb))
                        (t1, t1b), (t2, t2b) = scr(pair * 2), scr(pair * 2 + 1)
                        (ps1, pb1), (ps2, pb2) = pss
                        P.op("dve", lambda e, t1=t1, ps1=ps1: e.tensor_tensor(out=t1, in0=ps1[:], in1=ropet[:, 0, :], op=ALU.mult),
                             reads=[pb1, ropeb], writes=[t1b])
                        P.op("dve", lambda e, t2=t2, ps2=ps2: e.tensor_tensor(out=t2, in0=ps2[:], in1=ropet[:, 1, :], op=ALU.mult),
                             reads=[pb2, ropeb], writes=[t2b])
                        if kind == 0:
                            for hf in range(2):
                                P.op("dve", lambda e, t1=t1, t2=t2, m=m, hf=hf: e.tensor_tensor(
                                    out=q_m[hf][hf * 64:(hf + 1) * 64, m, :], in0=t1[hf * 64:(hf + 1) * 64, :],
                                    in1=t2[hf * 64:(hf + 1) * 64, :], op=ALU.add),
                                    reads=[t1b, t2b], writes=[qgb[hf][m]])
                        else:
                            P.op("dve", lambda e, t1=t1, t2=t2, m=m, tgs=tgs: e.tensor_tensor(out=kT[:, m, tgs], in0=t1, in1=t2, op=ALU.add),
                                 reads=[t1b, t2b], writes=[kb[m][g]])
            for nb in range(4):
                slot, sbuf_ = self.ws.acquire(("df", 16 + nb, g))
                wv = slot[:, 0:2048].rearrange("p (c m) -> p c m", c=NCH)
                for tt in range(4):
                    ti = g * 4 + tt
                    ps, pb = self.psum[6 + self.flip("norm")]
                    for c in range(NCH):
                        P.op("pe", lambda e, ps=ps, c=c, wv=wv, tt=tt: e.matmul(
                            ps[:, 0:256], lhsT=hn_g[:, c, tt * 128:(tt + 1) * 128], rhs=wv[:, c, :],
                            start=(c == 0), stop=(c == NCH - 1)),
                            reads=[sbuf_, hngb[c]], writes=[pb])
                    P.op("dve", lambda e, ps=ps, ti=ti, nb=nb: e.tensor_copy(out=vT[:, ti, nb * 256:(nb + 1) * 256], in_=ps[:, 0:256]),
                         reads=[pb], writes=[vb[ti][nb]])
            tasks = []
            for h in range(NCH):
                for mp in range(2):
                    ntl = 4 * g + 4
                    for i in range(ntl):
                        t = T()
                        t.h, t.mp, t.p0, t.i = h, mp, 64 * mp, i
                        t.qoff = max(0, i - 4 * g) * 128
                        t.diag = i >= 4 * g
                        t.first = i == 0
                        t.last = i == ntl - 1
                        t.n = len(tasks)
                        t.par = (h * 2 + mp) % 2
                        tasks.append(t)

            def stA(t):
                t.s_ps, t.sb_ = self.psum[(0, 1, 6)[t.n % 3]]
                qo = t.qoff
                P.op("pe", lambda e: e.matmul(
                    t.s_ps[:, qo:512], lhsT=kT[:, t.h, t.i * 128:(t.i + 1) * 128],
                    rhs=q_m[t.mp][:, t.h, qo:512], start=True, stop=True),
                    reads=[kb[t.h][t.i // 4], qgb[t.mp][t.h]], writes=[t.sb_])

            def stB(t):
                t.pt, t.ptb = p_t[t.n % NP]
                qo = t.qoff
                P.op("act", lambda e: e.activation(out=t.pt[:, qo:512], in_=t.s_ps[:, qo:512], func=AF.Exp, scale=0.125),
                     reads=[t.sb_], writes=[t.ptb])
                if t.diag:
                    P.op("dve", lambda e: e.memset(t.pt[64:128, qo:qo + 64], 0.0), reads=[], writes=[t.ptb])

            def stC(t):
                o_ps, opb = self.psum[4 + t.par]
                den_ps, denb = self.psum[2 + t.par]
                qo = t.qoff
                P.op("pe", lambda e: e.matmul(den_ps[:, qo:512], lhsT=ones, rhs=t.pt[:, qo:512], start=t.first, stop=t.last),
                     reads=[t.ptb, cstb], writes=[denb])
                P.op("pe", lambda e: e.matmul(o_ps[:, qo:512], lhsT=vT[:, t.i, t.h * 128:(t.h + 1) * 128], rhs=t.pt[:, qo:512],
                                              start=t.first, stop=t.last),
                     reads=[t.ptb, vb[t.i][t.h // 2]], writes=[opb])

            def stF2(t):
                if not t.last:
                    return
                den_ps, denb = self.psum[2 + t.par]
                rden, rdenb = scr(2)
                P.op("act", lambda e: e.activation(out=rden, in_=den_ps[:], func=AF.Ln), reads=[denb], writes=[rdenb])
                P.op("act", lambda e: e.activation(out=rden, in_=rden, func=AF.Exp, scale=-1.0), reads=[rdenb], writes=[rdenb])

            def stF3(t):
                if not t.last:
                    return
                o_ps, opb = self.psum[4 + t.par]
                (rden, rdenb), (o1n, o1nb), (od, odb) = scr(2), scr(0), scr(1)
                if t.mp == 0:
                    P.op("dve", lambda e: e.tensor_tensor(out=o1n, in0=o_ps[:], in1=rden, op=ALU.mult),
                         reads=[opb, rdenb], writes=[o1nb])
                else:
                    P.op("dve", lambda e: e.tensor_tensor(out=od, in0=o_ps[:], in1=rden, op=ALU.mult),
                         reads=[opb, rdenb], writes=[odb])
                    P.op("dve", lambda e: e.scalar_tensor_tensor(out=od, in0=od, scalar=neglam, in1=o1n, op0=ALU.mult, op1=ALU.add),
                         reads=[odb, o1nb, smallb], writes=[odb])

            def stF4(t):
                if not (t.last and t.mp == 1):
                    return
                od, odb = scr(1)
                P.op("act", lambda e: e.activation(out=sqo, in_=od, func=AF.Square), reads=[odb], writes=[sqob])
                t.ss = self.psum[7]
                P.op("pe", lambda e: e.matmul(t.ss[0][:], lhsT=ones, rhs=sqo, start=True, stop=True),
                     reads=[sqob, cstb], writes=[t.ss[1]])

            def stF5(t):
                if not (t.last and t.mp == 1):
                    return
                rs, rsb = scr(4)
                P.op("act", lambda e: e.activation(out=rs, in_=t.ss[0][:], func=AF.Ln, scale=1.0 / 128.0, bias=RMS_EPS),
                     reads=[t.ss[1]], writes=[rsb])
                P.op("act", lambda e: e.activation(out=rs, in_=rs, func=AF.Exp, scale=-0.5), reads=[rsb], writes=[rsb])

            def stF6(t):
                if not (t.last and t.mp == 1):
                    return
                (od, odb), (rs, rsb) = scr(1), scr(4)
                h = t.h
                P.op("dve", lambda e: e.scalar_tensor_tensor(out=o_g[:, h, :], in0=od, scalar=subg, in1=rs, op0=ALU.mult, op1=ALU.mult),
                     reads=[odb, rsb, smallb], writes=[ogb[h]])

            sched = self.df_sched
            n = len(tasks)
            maxlag = max(l for (_, l) in sched)
            fns = {"A": stA, "B": stB, "C": stC, "F2": stF2, "F3": stF3, "F4": stF4, "F5": stF5, "F6": stF6}
            for s in range(n + maxlag):
                for (name, lag) in sched:
                    k = s - lag
                    if 0 <= k < n:
                        fns[name](tasks[k])
            dbg = getattr(self, "dbg", None)
            if dbg:
                srcs = {"o": (o_g, ogb), "hn": (hn_g, hngb)}
                if dbg == "k":
                    for m in range(NCH):
                        P.op("dve", lambda e, m=m, tgs=tgs: e.tensor_copy(out=xT[:, m, tgs], in_=kT[:, m, tgs]),
                             reads=[kb[m][g], self.xb[m][g]], writes=[self.xb[m][g]])
                elif dbg == "v":
                    for m in range(NCH):
                        P.op("dve", lambda e, m=m, tgs=tgs: e.tensor_copy(out=xT[:, m, tgs], in_=vT[:, g * 4 + m // 2, (m % 2) * 512:(m % 2) * 512 + 512]),
                             reads=[vb[g * 4 + m // 2][nb] for nb in range(4)] + [self.xb[m][g]], writes=[self.xb[m][g]])
                else:
                    s3, sb3 = srcs[dbg]
                    for m in range(NCH):
                        P.op("dve", lambda e, m=m, tgs=tgs, s3=s3: e.tensor_copy(out=xT[:, m, tgs], in_=s3[:, m, :]),
                             reads=[sb3[m], self.xb[m][g]], writes=[self.xb[m][g]])
                for nb in range(4):
                    self.ws.acquire(("df", 20 + nb, g))
                continue
            for nb in range(4):
                slot, sbuf_ = self.ws.acquire(("df", 20 + nb, g))
                for half in range(2):
                    c = nb * 2 + half

                    def evac(ps, pb, c=c, tgs=tgs, g=g):
                        P.op("dve", lambda e: e.tensor_tensor(
                            out=xT[:, c, tgs], in0=ps[:], in1=xT[:, c, tgs], op=ALU.add),
                            reads=[pb, self.xb[c][g]], writes=[self.xb[c][g]])
                    self.proj_fm(slot, sbuf_, half, o_g, ogb, evac)


def _consts():
    c = np.zeros((128, 512), np.float32)
    c[:, 0:128] = np.eye(128, dtype=np.float32)
    j = np.arange(128)[:, None]
    k = np.arange(128)[None, :]
    c[:, 128:256] = (j >= k).astype(np.float32)
    c[:, 256:384] = np.where(j >= k, MASKNEG, 0.0)
    c[:, 384:512] = 1.0
    return c


def _layout_weights(inputs):
    w_in = np.asarray(inputs["ffn_w_in"], np.float32).reshape(4, NCH, 128, 2, NF, 128)
    w_in = np.ascontiguousarray(w_in.transpose(0, 4, 2, 3, 1, 5)).reshape(4, NF, 128, 2048)
    w_out = np.asarray(inputs["ffn_w_out"], np.float32).reshape(4, NF, 128, NCH, 128)
    w_out = np.ascontiguousarray(w_out.transpose(0, 3, 2, 1, 4)).reshape(4, NCH, 128, D_FF)

    def colblocks(w, ncols):
        nb = ncols // 256
        a = w.reshape(NCH, 128, nb, 256).transpose(2, 1, 0, 3)
        return np.ascontiguousarray(a).reshape(nb, 128, 2048)
    sbq = np.asarray(inputs["sb_w_qkv"], np.float32)[0]
    sbo = np.asarray(inputs["sb_w_o"], np.float32)[0]
    w_sb = np.concatenate([colblocks(sbq, 3072), colblocks(sbo, 1024)], axis=0)
    dq = np.asarray(inputs["diff_w_qkv"], np.float32)[0]
    do = np.asarray(inputs["diff_w_o"], np.float32)[0]
    perm = np.arange(1024).reshape(16, 2, 32)[:, ::-1, :].reshape(1024)
    w_df = np.concatenate([colblocks(dq[:, 0:1024], 1024), colblocks(dq[:, 0:1024][:, perm], 1024),
                           colblocks(dq[:, 1024:2048], 1024), colblocks(dq[:, 1024:2048][:, perm], 1024),
                           colblocks(dq[:, 2048:3072], 1024), colblocks(do, 1024)], axis=0)
    return w_in, w_out, w_sb, w_df


def _rope_tables():
    p = np.arange(128)
    i = p % 32
    inv = 10000.0 ** (-(i.astype(np.float32)) / 32.0)
    ang = np.arange(SEQ, dtype=np.float32)[None, :] * inv[:, None].astype(np.float32)
    cos = np.cos(ang).astype(np.float32)
    sin = np.sin(ang).astype(np.float32)
    sgn = np.where((p % 64) < 32, -1.0, 1.0).astype(np.float32)[:, None]
    return np.ascontiguousarray(np.stack([cos, sin * sgn], axis=1))


FULL_STAGES = [("ffn", 0, 0), ("sb", 1), ("ffn", 1, 2), ("ffn", 2, 3), ("diff", 4), ("ffn", 3, 5), ("final", 6)]
_CACHE = {}
_RUN_KW = {}


def _prepare(inputs):
    w_in, w_out, w_sb, w_df = _layout_weights(inputs)
    ng = np.asarray(inputs["norm_gains"], np.float32).reshape(6, NCH, 128)
    fg = np.asarray(inputs["final_gain"], np.float32).reshape(1, NCH, 128)
    gains = np.ascontiguousarray(np.concatenate([ng, fg], 0).transpose(2, 0, 1)).reshape(128, 56)
    lam = np.ascontiguousarray(np.broadcast_to(np.asarray(inputs["diff_lambda"], np.float32).reshape(1, 256), (128, 256)))
    subln = np.ascontiguousarray(np.asarray(inputs["diff_subln"], np.float32).reshape(128, 1))
    shared = {"gains": gains, "consts": _consts(), "w_in": w_in, "w_out": w_out, "w_sb": w_sb, "w_df": w_df,
              "rope": _rope_tables(), "lam": lam, "subln": subln}
    return shared


def _x_to_dev(xb):
    return np.ascontiguousarray(xb.T.reshape(NCH, 128, SEQ).transpose(1, 0, 2))


def _x_from_dev(y):
    return np.ascontiguousarray(y.transpose(1, 0, 2).reshape(D_MODEL, SEQ).T)


def run_stages(stages, xs, shared, core_ids=None):
    key = tuple(stages)
    if key not in _CACHE:
        _CACHE[key] = Builder(list(stages)).build()
    nc = _CACHE[key]
    n = len(xs)
    in_maps = [dict(shared, xT=_x_to_dev(np.asarray(x, np.float32))) for x in xs]
    res = run_bass_kernel_spmd(nc, in_maps, core_ids=list(range(n)), **_RUN_KW)
    if res.exec_time_ns is not None:
        print("exec_time_ns", res.exec_time_ns)
    return [_x_from_dev(r["outT"]) for r in res.results]


def kernel(**inputs):
    shared = _prepare(inputs)
    x = np.asarray(inputs["x"], np.float32)
    outs = run_stages(FULL_STAGES, [x[b] for b in range(x.shape[0])], shared)
    return np.stack(outs, axis=0).astype(np.float32)
```

```python
import math
from contextlib import ExitStack

import numpy as np
import concourse.bass as bass
import concourse.mybir as mybir
from concourse.bass_utils import run_bass_kernel_spmd

F32 = mybir.dt.float32
BF16 = mybir.dt.bfloat16
AF = mybir.ActivationFunctionType
ALU = mybir.AluOpType

ENGS = ("pe", "act", "dve", "pool", "sp")

D_MODEL = 1024
SEQ = 2048
NCH = 8
D_FF = 2816
NF = 22
TG = 512
NG = 4
RMS_EPS = 1e-6
LAMBDA_INIT = 0.8 - 0.6 * math.exp(-0.3 * (2 - 1))
NSLOT = 5
MASKNEG = -30000.0


class Buf:
    __slots__ = ("name", "last_w", "readers")

    def __init__(self, name):
        self.name = name
        self.last_w = None
        self.readers = []


class Op:
    __slots__ = ("eng", "fn", "deps", "needed", "ticket", "dsem", "idx", "is_dma")

    def __init__(self, eng, fn, dsem=None):
        self.eng = eng
        self.fn = fn
        self.deps = []
        self.needed = False
        self.ticket = None
        self.dsem = dsem
        self.is_dma = dsem is not None
        self.idx = None


class Prog:
    def __init__(self):
        self.ops = {e: [] for e in ENGS}
        self.n = 0
        self.dma_keys = []
        self.last_dma = {}

    def op(self, eng, fn, reads=(), writes=(), dsem=None, extra_deps=()):
        o = Op(eng, fn, dsem)
        o.idx = self.n
        self.n += 1
        if dsem is not None:
            if dsem not in self.dma_keys:
                self.dma_keys.append(dsem)
        deps = {}
        for b in reads:
            if b.last_w is not None:
                deps[id(b.last_w)] = b.last_w
        for b in writes:
            if b.last_w is not None:
                deps[id(b.last_w)] = b.last_w
            for r in b.readers:
                deps[id(r)] = r
        for d in extra_deps:
            deps[id(d)] = d
        if dsem is not None and dsem in self.last_dma:
            d = self.last_dma[dsem]
            deps[id(d)] = d
        for d in deps.values():
            if d.eng == "pe" and eng == "pe" and not d.is_dma and not o.is_dma:
                continue
            o.deps.append(d)
            d.needed = True
        for b in reads:
            b.readers.append(o)
        for b in writes:
            b.last_w = o
            b.readers = []
        if dsem is not None:
            self.last_dma[dsem] = o
        self.ops[eng].append(o)
        return o

    def barrier(self):
        lasts = []
        for e in ENGS:
            for o in reversed(self.ops[e]):
                if not o.is_dma and o.fn is not None:
                    lasts.append(o)
                    break
        lasts.extend(self.last_dma.values())
        for e in ENGS:
            self.op(e, None, extra_deps=[d for d in lasts])

    def assign(self):
        cnt = {e: 0 for e in ENGS}
        dcnt = {k: 0 for k in self.dma_keys}
        allops = []
        for e in ENGS:
            allops.extend(self.ops[e])
        allops.sort(key=lambda o: o.idx)
        for o in allops:
            if o.is_dma:
                dcnt[o.dsem] += 16
                o.ticket = dcnt[o.dsem]
            elif o.needed and o.fn is not None:
                cnt[o.eng] += 1
                o.ticket = cnt[o.eng]

    def run_engine(self, eng, e, sems, dsems):
        waited = {}
        for o in self.ops[eng]:
            for d in o.deps:
                if d.ticket is None:
                    continue
                key = ("d", d.dsem) if d.is_dma else ("e", d.eng)
                if waited.get(key, 0) >= d.ticket:
                    continue
                waited[key] = d.ticket
                s = dsems[d.dsem] if d.is_dma else sems[d.eng]
                e.wait_ge(s, d.ticket)
            if o.fn is None:
                continue
            ins = o.fn(e)
            if o.is_dma:
                ins.then_inc(dsems[o.dsem], 16)
            elif o.needed:
                ins.then_inc(sems[o.eng], 1)


class WStream:
    def __init__(self, P, ring, nslot):
        self.P = P
        self.ring = ring
        self.nslot = nslot
        self.blocks = []
        self.issued = 0
        self.next = 0

    def plan(self, tag, src, n):
        self.blocks.append((tag, src, n))

    def _issue(self, k):
        tag, src, n = self.blocks[k]
        tile, buf = self.ring[k % self.nslot]
        s = k % self.nslot

        def fn(e, tile=tile, src=src, n=n):
            if n % 2 == 0 and n > 1024:
                return e.dma_start(out=tile[:, 0:n].rearrange("p (a b) -> p a b", a=2),
                                   in_=src.rearrange("p (a b) -> p a b", a=2))
            return e.dma_start(out=tile[:, 0:n], in_=src)
        self.P.op("pool", fn, writes=[buf], dsem=f"w{s}")

    def acquire(self, tag):
        k = self.next
        self.next += 1
        assert self.blocks[k][0] == tag, (self.blocks[k][0], tag)
        while self.issued < min(len(self.blocks), k + self.nslot - 1):
            self._issue(self.issued)
            self.issued += 1
        return self.ring[k % self.nslot]


class Builder:
    def __init__(self, stages):
        self.stages = stages
        nc = bass.Bass("TRN2", target_bir_lowering=False)
        self.nc = nc
        self.P = Prog()
        d = lambda name, shape, kind="ExternalInput": nc.dram_tensor(name, shape, F32, kind=kind).ap()
        self.d_x = d("xT", [128, NCH, SEQ])
        self.d_gains = d("gains", [128, 56])
        self.d_consts = d("consts", [128, 512])
        self.d_win = d("w_in", [4, NF, 128, 2048])
        self.d_wout = d("w_out", [4, NCH, 128, D_FF])
        self.d_wsb = d("w_sb", [16, 128, 2048])
        self.d_wdf = d("w_df", [24, 128, 2048])
        self.d_rope = d("rope", [128, 2, SEQ])
        self.d_lam = d("lam", [128, 256])
        self.d_subln = d("subln", [128, 1])
        self.d_out = d("outT", [128, NCH, SEQ], kind="ExternalOutput")

    def sb(self, name, shape, dt):
        return self.es.enter_context(self.nc.sbuf_tensor(name, shape, dt))

    def build(self):
        nc, P = self.nc, self.P
        with ExitStack() as es:
            self.es = es
            self.xT = self.sb("xT_sb", [128, NCH, SEQ], F32)
            self.HN = self.sb("hn_sb", [128, NCH * SEQ], BF16)
            self.R = self.sb("r_sb", [128, 16 * SEQ], BF16)
            self.ringt = [self.sb(f"ring{i}", [128, 2048], BF16) for i in range(NSLOT)]
            self.SCR = self.sb("scr", [128, 3072], F32)
            self.SCRB = self.sb("scrb", [128, 2048], BF16)
            self.SCRC = self.sb("scrc", [128, 2048], BF16)
            self.cst = self.sb("cst", [128, 512], BF16)
            self.gains = self.sb("gains_sb", [128, 56], F32)
            self.lamt = self.sb("lam_sb", [128, 256], F32)
            self.small = self.sb("small_sb", [128, 8], F32)
            self.ropet = self.sb("rope_sb", [128, 2, TG], F32)
            self.psum = []
            for i in range(8):
                t = es.enter_context(nc.psum_tensor(f"ps{i}", [128, 512], F32))
                self.psum.append((t, Buf(f"ps{i}")))
            self.sems = {e: es.enter_context(nc.semaphore(f"s_{e}")) for e in ENGS}

            self.xb = [[Buf(f"x{c}_{g}") for g in range(NG)] for c in range(NCH)]
            self.ring = [(self.ringt[i], Buf(f"ring{i}")) for i in range(NSLOT)]
            self.scrb = [Buf(f"scr{i}") for i in range(6)]
            self.scrbb = [Buf(f"scrb{i}") for i in range(4)]
            self.cstb = Buf("cst")
            self.gainb = Buf("gains")
            self.lamb = Buf("lam")
            self.smallb = Buf("small")
            self.ropeb = Buf("rope")
            self.outb = [Buf(f"out{g}") for g in range(NG)]
            self.ws = WStream(P, self.ring, NSLOT)
            self.par = {}
            self.region_prev = {}
            self.region_cur = {}
            self.df_sched = [("C", 4), ("A", 0), ("B", 2), ("F2", 6), ("F3", 7), ("F4", 8), ("F5", 9), ("F6", 10)]
            self.sb_sched = [("A", 0), ("B1", 1), ("D1", 3), ("B2", 1), ("C", 2), ("D2", 3), ("E", 4), ("F", 6)]

            self.plan_weights()
            self.emit_all()

            P.assign()
            dsems = {k: es.enter_context(nc.semaphore(f"d_{k}")) for k in P.dma_keys}
            sems = self.sems
            with nc.Block() as block:
                @block.sync
                def _(e):
                    P.run_engine("sp", e, sems, dsems)

                @block.gpsimd
                def _(e):
                    P.run_engine("pool", e, sems, dsems)

                @block.tensor
                def _(e):
                    P.run_engine("pe", e, sems, dsems)

                @block.scalar
                def _(e):
                    P.run_engine("act", e, sems, dsems)

                @block.vector
                def _(e):
                    P.run_engine("dve", e, sems, dsems)
        return nc

    def nb(self, name, region):
        b = Buf(name)
        b.readers = list(self.region_prev.get(region, []))
        self.region_cur.setdefault(region, []).append(b)
        return b

    def stage_end(self):
        for region, bufs in self.region_cur.items():
            last = {}
            for b in bufs:
                for o in ([b.last_w] if b.last_w is not None else []) + list(b.readers):
                    key = ("d", o.dsem) if o.is_dma else ("e", o.eng)
                    if key not in last or last[key].idx < o.idx:
                        last[key] = o
            self.region_prev[region] = list(last.values())
        self.region_cur = {}

    def flip(self, key, n=2):
        v = self.par.get(key, 0)
        self.par[key] = (v + 1) % n
        return v

    def plan_weights(self):
        ws = self.ws
        for st in self.stages:
            kind = st[0]
            if kind == "ffn":
                widx = st[1]
                for (j0, j1) in ((0, 11), (11, 22)):
                    for j in range(j0, j1):
                        ws.plan(("win", widx, j), self.d_win[widx, j], 2048)
                    for c in range(NCH):
                        ws.plan(("wout", widx, j0, c), self.d_wout[widx, c, :, j0 * 128:j1 * 128], (j1 - j0) * 128)
            elif kind == "sb":
                for g in range(NG):
                    for blk in range(12):
                        ws.plan(("sb", blk, g), self.d_wsb[blk], 2048)
                    for blk in range(12, 16):
                        ws.plan(("sb", blk, g), self.d_wsb[blk], 2048)
            elif kind == "diff":
                for g in range(NG):
                    for blk in Builder.DF_ORDER:
                        ws.plan(("df", blk, g), self.d_wdf[blk], 2048)

    def emit_all(self):
        P = self.P
        P.op("pool", lambda e: e.dma_start(out=self.cst[:], in_=self.d_consts), writes=[self.cstb], dsem="cst")
        P.op("sp", lambda e: e.dma_start(out=self.gains[:], in_=self.d_gains), writes=[self.gainb], dsem="gains")
        for g in range(NG):
            P.op("sp", lambda e, g=g: e.dma_start(out=self.xT[:, :, g * TG:(g + 1) * TG],
                                                  in_=self.d_x[:, :, g * TG:(g + 1) * TG]),
                 writes=[self.xb[c][g] for c in range(NCH)], dsem=f"x{g}")
        for st in self.stages:
            if st[0] == "ffn":
                self.ffn(st[1], st[2])
            elif st[0] == "sb":
                self.sb_mixer(st[1])
            elif st[0] == "diff":
                self.diff_mixer(st[1])
            elif st[0] == "final":
                self.final_norm(st[1])
            self.stage_end()
        for g in range(NG):
            P.op("sp", lambda e, g=g: e.dma_start(out=self.d_out[:, :, g * TG:(g + 1) * TG],
                                                  in_=self.xT[:, :, g * TG:(g + 1) * TG]),
                 reads=[self.xb[c][g] for c in range(NCH)], writes=[self.outb[g]], dsem=f"o{g}")
        P.op("sp", None, reads=self.outb)

    def norm_group(self, nidx, g, dst_fn, in_place=False):
        P = self.P
        tgs = slice(g * TG, (g + 1) * TG)
        ones = self.cst[:, 384:512]
        ps, psb = self.psum[6 + self.flip("norm")]
        xT = self.xT
        for c in range(NCH):
            k = self.flip("sq")
            sq = self.SCRB[:, (2 + k) * 512:(3 + k) * 512]
            sqb = self.scrbb[2 + k]
            P.op("act", lambda e, c=c, sq=sq: e.activation(out=sq, in_=xT[:, c, tgs], func=AF.Square),
                 reads=[self.xb[c][g]], writes=[sqb])
            P.op("pe", lambda e, c=c, sq=sq: e.matmul(ps[:], lhsT=ones, rhs=sq, start=(c == 0), stop=(c == NCH - 1)),
                 reads=[sqb, self.cstb], writes=[psb])
        lnt = self.SCR[:, 5 * 512:6 * 512]
        lnb = self.scrb[5]
        rs = self.SCR[:, 4 * 512:5 * 512]
        rsb = self.scrb[4]
        P.op("act", lambda e: e.activation(out=lnt, in_=ps[:], func=AF.Ln, scale=1.0 / D_MODEL, bias=RMS_EPS),
             reads=[psb], writes=[lnb])
        P.op("act", lambda e: e.activation(out=rs, in_=lnt, func=AF.Exp, scale=-0.5), reads=[lnb], writes=[rsb])
        for c in range(NCH):
            out_ap, obufs = dst_fn(c)
            gcol = self.gains[:, nidx * 8 + c:nidx * 8 + c + 1]
            rd = [self.xb[c][g], rsb, self.gainb]
            P.op("dve", lambda e, c=c, out_ap=out_ap, gcol=gcol: e.scalar_tensor_tensor(
                out=out_ap, in0=xT[:, c, tgs], scalar=gcol, in1=rs, op0=ALU.mult, op1=ALU.mult),
                reads=rd, writes=obufs)

    def ffn(self, widx, nidx):
        P = self.P
        xT = self.xT
        hn3 = self.HN[:].rearrange("p (c t) -> p c t", c=NCH)
        hnb = [[self.nb(f"hn{c}_{g}", "HN") for g in range(NG)] for c in range(NCH)]
        actT = self.R[:, 0:11 * SEQ].rearrange("p (j t) -> p j t", j=11)
        actb = [[self.nb(f"act{j}_{g}", "R") for g in range(NG)] for j in range(11)]
        for g in range(NG):
            tgs = slice(g * TG, (g + 1) * TG)
            self.norm_group(nidx, g, lambda c, tgs=tgs, g=g: (hn3[:, c, tgs], [hnb[c][g]]))
        for (j0, j1) in ((0, 11), (11, 22)):
            nj = j1 - j0
            for j in range(j0, j1):
                slot, sbuf_ = self.ws.acquire(("win", widx, j))
                wv = slot[:, 0:2048].rearrange("p (g c m) -> p g c m", g=2, c=NCH)
                jj = j - j0
                for g in range(NG):
                    tgs = slice(g * TG, (g + 1) * TG)
                    par = self.flip("gu")
                    gps, gpb = self.psum[par * 2]
                    ups, upb = self.psum[par * 2 + 1]
                    for gu, (ps, pb) in enumerate(((gps, gpb), (ups, upb))):
                        for c in range(NCH):
                            P.op("pe", lambda e, ps=ps, gu=gu, c=c, wv=wv, tgs=tgs: e.matmul(
                                ps[:], lhsT=wv[:, gu, c, :], rhs=hn3[:, c, tgs], start=(c == 0), stop=(c == NCH - 1)),
                                reads=[sbuf_, hnb[c][g]], writes=[pb])
                    sg = self.SCR[:, par * 512:(par + 1) * 512]
                    sgb = self.scrb[par]
                    P.op("act", lambda e, sg=sg, gps=gps: e.activation(out=sg, in_=gps[:], func=AF.Silu),
                         reads=[gpb], writes=[sgb])
                    P.op("dve", lambda e, sg=sg, ups=ups, jj=jj, tgs=tgs: e.tensor_tensor(
                        out=actT[:, jj, tgs], in0=sg, in1=ups[:], op=ALU.mult),
                        reads=[sgb, upb], writes=[actb[jj][g]])
            for c in range(NCH):
                slot, sbuf_ = self.ws.acquire(("wout", widx, j0, c))
                wv = slot[:, 0:nj * 128].rearrange("p (j m) -> p j m", m=128)
                for g in range(NG):
                    tgs = slice(g * TG, (g + 1) * TG)
                    ps, pb = self.psum[4 + self.flip("y")]
                    for jj in range(nj):
                        P.op("pe", lambda e, ps=ps, jj=jj, wv=wv, tgs=tgs: e.matmul(
                            ps[:], lhsT=wv[:, jj, :], rhs=actT[:, jj, tgs], start=(jj == 0), stop=(jj == nj - 1)),
                            reads=[sbuf_, actb[jj][g]], writes=[pb])
                    P.op("dve", lambda e, ps=ps, c=c, tgs=tgs: e.scalar_tensor_tensor(
                        out=xT[:, c, tgs], in0=ps[:], scalar=0.5, in1=xT[:, c, tgs], op0=ALU.mult, op1=ALU.add),
                        reads=[pb, self.xb[c][g]], writes=[self.xb[c][g]])

    def final_norm(self, nidx):
        xT = self.xT
        for g in range(NG):
            tgs = slice(g * TG, (g + 1) * TG)
            self.norm_group(nidx, g, lambda c, tgs=tgs, g=g: (xT[:, c, tgs], [self.xb[c][g]]))

    def proj_fm(self, slot, sbuf_, half, src3, srcb, evac):
        P = self.P
        wv = slot[:, 0:2048].rearrange("p (c m) -> p c m", c=NCH)
        ps, pb = self.psum[6 + self.flip("norm")]
        for c in range(NCH):
            P.op("pe", lambda e, ps=ps, c=c, wv=wv, half=half: e.matmul(
                ps[:], lhsT=wv[:, c, half * 128:(half + 1) * 128], rhs=src3[:, c, :],
                start=(c == 0), stop=(c == NCH - 1)),
                reads=[sbuf_, srcb[c]], writes=[pb])
        evac(ps, pb)

    def sb_mixer(self, nidx):
        P = self.P
        xT = self.xT
        kT = self.R[:, 0:NCH * SEQ].rearrange("p (m t) -> p m t", m=NCH)
        vT = self.R[:, NCH * SEQ:16 * SEQ].rearrange("p (i f) -> p i f", i=16)
        kb = [[self.nb(f"k{m}_{g}", "R") for g in range(NG)] for m in range(NCH)]
        vb = [[self.nb(f"v{i}_{nb}", "R") for nb in range(4)] for i in range(16)]
        hn_g = self.HN[:, 0:4096].rearrange("p (c t) -> p c t", c=NCH)
        q_m = [self.HN[:, 4096:8192].rearrange("p (c t) -> p c t", c=NCH),
               self.HN[:, 8192:12288].rearrange("p (c t) -> p c t", c=NCH)]
        o_g = self.HN[:, 12288:16384].rearrange("p (c t) -> p c t", c=NCH)
        hngb = [self.nb(f"hng{c}", "HN") for c in range(NCH)]
        qgb = [[self.nb(f"qg{hf}_{c}", "HN") for c in range(NCH)] for hf in range(2)]
        ogb = [self.nb(f"og{c}", "HN") for c in range(NCH)]
        P.op("dve", lambda e: e.memset(q_m[0][64:128, :, :], 0.0), writes=qgb[0])
        P.op("dve", lambda e: e.memset(q_m[1][0:64, :, :], 0.0), writes=qgb[1])
        ident = self.cst[:, 0:128]
        tinc = self.cst[:, 128:256]
        maskneg = self.cst[:, 256:384]
        ones = self.cst[:, 384:512]
        cstb = self.cstb
        NE = 4
        e_t = [(self.SCR[:, k * 512:(k + 1) * 512], self.scrb[k]) for k in range(4)]
        ecs_t = [(self.SCR[:, (4 + k) * 512:(5 + k) * 512], self.scrb[4 + k]) for k in range(2)]
        sp_t = [(self.SCRB[:, k * 512:(k + 1) * 512], self.scrbb[k]) for k in range(2)]
        aT_t = [(self.SCRC[:, k * 512:(k + 1) * 512], self.nb(f"aT{k}", "SCRC")) for k in range(2)]
        R_t = [(self.SCRC[:, 1024 + k * 512:1024 + (k + 1) * 512], self.nb(f"R{k}", "SCRC")) for k in range(2)]

        class T:
            pass

        for g in range(NG):
            tgs = slice(g * TG, (g + 1) * TG)
            self.norm_group(nidx, g, lambda c: (hn_g[:, c, :], [hngb[c]]))
            for kind in range(2):
                for nb in range(4):
                    slot, sbuf_ = self.ws.acquire(("sb", kind * 4 + nb, g))
                    for half in range(2):
                        m = nb * 2 + half
                        if kind == 0:
                            def evac(ps, pb, m=m):
                                for hf in range(2):
                                    P.op("act", lambda e, hf=hf: e.activation(out=q_m[hf][hf * 64:(hf + 1) * 64, m, :],
                                                                              in_=ps[hf * 64:(hf + 1) * 64, :], func=AF.Copy, scale=0.125),
                                         reads=[pb], writes=[qgb[hf][m]])
                        else:
                            def evac(ps, pb, m=m, tgs=tgs, g=g):
                                P.op("dve", lambda e: e.tensor_copy(out=kT[:, m, tgs], in_=ps[:]),
                                     reads=[pb], writes=[kb[m][g]])
                        self.proj_fm(slot, sbuf_, half, hn_g, hngb, evac)
            for nb in range(4):
                slot, sbuf_ = self.ws.acquire(("sb", 8 + nb, g))
                wv = slot[:, 0:2048].rearrange("p (c m) -> p c m", c=NCH)
                for tt in range(4):
                    ti = g * 4 + tt
                    ps, pb = self.psum[6 + self.flip("norm")]
                    for c in range(NCH):
                        P.op("pe", lambda e, ps=ps, c=c, wv=wv, tt=tt: e.matmul(
                            ps[:, 0:256], lhsT=hn_g[:, c, tt * 128:(tt + 1) * 128], rhs=wv[:, c, :],
                            start=(c == 0), stop=(c == NCH - 1)),
                            reads=[sbuf_, hngb[c]], writes=[pb])
                    P.op("dve", lambda e, ps=ps, ti=ti, nb=nb: e.tensor_copy(out=vT[:, ti, nb * 256:(nb + 1) * 256], in_=ps[:, 0:256]),
                         reads=[pb], writes=[vb[ti][nb]])
            tasks = []
            for m in range(NCH):
                for half in range(2):
                    tiles = list(range(4 * g + 3, -1, -1))
                    for idx, i in enumerate(tiles):
                        t = T()
                        t.m, t.half, t.h, t.p0, t.i, t.idx = m, half, 2 * m + half, 64 * half, i, idx
                        t.qoff = max(0, i - 4 * g) * 128
                        t.diag = i >= 4 * g
                        t.first = idx == 0
                        t.last = idx == len(tiles) - 1
                        t.nqoff = 0 if t.last else max(0, tiles[idx + 1] - 4 * g) * 128
                        t.n = len(tasks)
                        tasks.append(t)
            o_cur = {}

            def stA(t):
                t.z_ps, t.zb = self.psum[t.n % 2]
                qo = t.qoff
                P.op("pe", lambda e: e.matmul(
                    t.z_ps[:, qo:512], lhsT=kT[:, t.m, t.i * 128:(t.i + 1) * 128],
                    rhs=q_m[t.half][:, t.m, qo:512], start=True, stop=(not t.diag)),
                    reads=[kb[t.m][t.i // 4], qgb[t.half][t.m]], writes=[t.zb])
                if t.diag:
                    P.op("pe", lambda e: e.matmul(
                        t.z_ps[:, qo:qo + 128], lhsT=ident, rhs=maskneg, start=False, stop=True),
                        reads=[cstb], writes=[t.zb])
                for _f in range(getattr(self, "fill", 0)):
                    fps, fpb = self.psum[6 + (_f % 2)]
                    P.op("pe", lambda e, fps=fps: e.matmul(fps[:], lhsT=ones, rhs=self.cst[:, 0:512], start=True, stop=True),
                         reads=[cstb], writes=[fpb])

            def stB1(t):
                t.et, t.eb = e_t[t.n % NE]
                qo = t.qoff
                P.op("act", lambda e: e.activation(out=t.et[:, qo:512], in_=t.z_ps[:, qo:512], func=AF.Exp),
                     reads=[t.zb], writes=[t.eb])

            def stB2(t):
                t.spt, t.spb = sp_t[t.n % 2]
                qo = t.qoff
                P.op("act", lambda e: e.activation(out=t.spt[:, qo:512], in_=t.et[:, qo:512], func=AF.Ln, bias=1.0),
                     reads=[t.eb], writes=[t.spb])

            def stC(t):
                t.cs_ps, t.csb = self.psum[2 + t.n % 2]
                qo = t.qoff
                P.op("pe", lambda e: e.matmul(
                    t.cs_ps[:, qo:512], lhsT=tinc, rhs=t.spt[:, qo:512], start=True, stop=t.first),
                    reads=[t.spb, cstb], writes=[t.csb])
                if not t.first:
                    rt, rb = t.rprev
                    P.op("pe", lambda e: e.matmul(
                        t.cs_ps[:, qo:512], lhsT=ones, rhs=rt[:, qo:512], start=False, stop=True),
                        reads=[rb, cstb], writes=[t.csb])
                if not t.last:
                    rn, rnb = R_t[t.n % 2]
                    if t.first:
                        P.op("dve", lambda e: e.tensor_copy(out=rn[:, qo:512], in_=t.spt[:, qo:512]),
                             reads=[t.spb], writes=[rnb])
                    else:
                        rt, rb = t.rprev
                        P.op("dve", lambda e: e.tensor_tensor(
                            out=rn[:, qo:512], in0=rt[:, qo:512], in1=t.spt[:, qo:512], op=ALU.add),
                            reads=[t.spb, rb], writes=[rnb])
                    if t.nqoff < qo:
                        P.op("dve", lambda e: e.memset(rn[:, t.nqoff:qo], 0.0), reads=[], writes=[rnb])
                    tasks[t.n + 1].rprev = (rn, rnb)

            def stD1(t):
                t.ect, t.ecb = ecs_t[t.n % 2]
                qo = t.qoff
                P.op("act", lambda e: e.activation(out=t.ect[:, qo:512], in_=t.cs_ps[:, qo:512], func=AF.Exp, scale=-1.0),
                     reads=[t.csb], writes=[t.ecb])

            def stD2(t):
                t.at, t.ab = aT_t[t.n % 2]
                qo = t.qoff
                P.op("dve", lambda e: e.tensor_tensor(
                    out=t.at[:, qo:512], in0=t.et[:, qo:512], in1=t.ect[:, qo:512], op=ALU.mult),
                    reads=[t.eb, t.ecb], writes=[t.ab])

            def stE(t):
                o_ps, opb = self.psum[4 + t.half]
                qo = t.qoff
                P.op("pe", lambda e: e.matmul(
                    o_ps[:, qo:512], lhsT=vT[:, t.i, t.m * 128:(t.m + 1) * 128], rhs=t.at[:, qo:512],
                    start=t.first, stop=t.last),
                    reads=[vb[t.i][t.m // 2], t.ab], writes=[opb])

            def stF(t):
                if t.last:
                    o_ps, opb = self.psum[4 + t.half]
                    m, p0 = t.m, t.p0
                    P.op("act", lambda e: e.activation(out=o_g[p0:p0 + 64, m, :], in_=o_ps[p0:p0 + 64, :], func=AF.Copy),
                         reads=[opb], writes=[ogb[m]])

            sched = self.sb_sched
            n = len(tasks)
            maxlag = max(l for (_, l) in sched)
            fns = {"A": stA, "B1": stB1, "B2": stB2, "C": stC, "D1": stD1, "D2": stD2, "E": stE, "F": stF}
            for s in range(n + maxlag):
                for (name, lag) in sched:
                    k = s - lag
                    if 0 <= k < n:
                        fns[name](tasks[k])
            for nb in range(4):
                slot, sbuf_ = self.ws.acquire(("sb", 12 + nb, g))
                for half in range(2):
                    c = nb * 2 + half

                    def evac(ps, pb, c=c, tgs=tgs, g=g):
                        P.op("dve", lambda e: e.tensor_tensor(
                            out=xT[:, c, tgs], in0=ps[:], in1=xT[:, c, tgs], op=ALU.add),
                            reads=[pb, self.xb[c][g]], writes=[self.xb[c][g]])
                    self.proj_fm(slot, sbuf_, half, o_g, ogb, evac)

    DF_ORDER = [0, 4, 1, 5, 2, 6, 3, 7, 8, 12, 9, 13, 10, 14, 11, 15, 16, 17, 18, 19, 20, 21, 22, 23]

    def diff_mixer(self, nidx):
        P = self.P
        xT = self.xT
        kT = self.R[:, 0:NCH * SEQ].rearrange("p (m t) -> p m t", m=NCH)
        vT = self.R[:, NCH * SEQ:16 * SEQ].rearrange("p (i f) -> p i f", i=16)
        kb = [[self.nb(f"k{m}_{g}", "R") for g in range(NG)] for m in range(NCH)]
        vb = [[self.nb(f"v{i}_{nb}", "R") for nb in range(4)] for i in range(16)]
        hn_g = self.HN[:, 0:4096].rearrange("p (c t) -> p c t", c=NCH)
        q_m = [self.HN[:, 4096:8192].rearrange("p (c t) -> p c t", c=NCH),
               self.HN[:, 8192:12288].rearrange("p (c t) -> p c t", c=NCH)]
        o_g = self.HN[:, 12288:16384].rearrange("p (c t) -> p c t", c=NCH)
        hngb = [self.nb(f"hng{c}", "HN") for c in range(NCH)]
        qgb = [[self.nb(f"qg{hf}_{c}", "HN") for c in range(NCH)] for hf in range(2)]
        ogb = [self.nb(f"og{c}", "HN") for c in range(NCH)]
        P.op("dve", lambda e: e.memset(q_m[0][64:128, :, :], 0.0), writes=qgb[0])
        P.op("dve", lambda e: e.memset(q_m[1][0:64, :, :], 0.0), writes=qgb[1])
        ones = self.cst[:, 384:512]
        cstb = self.cstb
        NP = 4
        p_t = [(self.SCRC[:, k * 512:(k + 1) * 512], self.nb(f"pT{k}", "SCRC")) for k in range(NP)]
        sqo, sqob = self.SCRB[:, 0:512], self.scrbb[0]
        scr = lambda k: (self.SCR[:, k * 512:(k + 1) * 512], self.scrb[k])
        small, smallb = self.small, self.smallb
        ropet, ropeb = self.ropet, self.ropeb

        tmp, tmpb = scr(0)
        P.op("sp", lambda e: e.dma_start(out=self.lamt[:], in_=self.d_lam), writes=[self.lamb], dsem="lam")
        P.op("sp", lambda e: e.dma_start(out=small[:, 6:7], in_=self.d_subln), writes=[smallb], dsem="subln")
        for k in range(2):
            P.op("dve", lambda e, k=k: e.tensor_tensor(out=tmp[:, k * 64:(k + 1) * 64], in0=self.lamt[:, k * 128:k * 128 + 64],
                                                       in1=self.lamt[:, k * 128 + 64:k * 128 + 128], op=ALU.mult),
                 reads=[self.lamb], writes=[tmpb])
            P.op("dve", lambda e, k=k: e.reduce_sum(out=small[:, k:k + 1], in_=tmp[:, k * 64:(k + 1) * 64], axis=mybir.AxisListType.X),
                 reads=[tmpb], writes=[smallb])
        P.op("act", lambda e: e.activation(out=small[:, 2:4], in_=small[:, 0:2], func=AF.Exp), reads=[smallb], writes=[smallb])
        P.op("dve", lambda e: e.tensor_tensor(out=small[:, 4:5], in0=small[:, 3:4], in1=small[:, 2:3], op=ALU.subtract),
             reads=[smallb], writes=[smallb])
        P.op("dve", lambda e: e.tensor_scalar(out=small[:, 4:5], in0=small[:, 4:5], scalar1=-LAMBDA_INIT, scalar2=None, op0=ALU.add),
             reads=[smallb], writes=[smallb])
        P.op("dve", lambda e: e.tensor_scalar(out=small[:, 5:6], in0=small[:, 6:7], scalar1=(1.0 - LAMBDA_INIT), scalar2=None, op0=ALU.mult),
             reads=[smallb], writes=[smallb])
        neglam = small[:, 4:5]
        subg = small[:, 5:6]

        class T:
            pass

        for g in range(NG):
            tgs = slice(g * TG, (g + 1) * TG)
            self.norm_group(nidx, g, lambda c: (hn_g[:, c, :], [hngb[c]]))
            P.op("sp", lambda e, tgs=tgs: e.dma_start(out=ropet[:], in_=self.d_rope[:, :, tgs]), writes=[ropeb], dsem="rope")
            for kind in range(2):
                for nb in range(4):
                    slotA, sbA = self.ws.acquire(("df", kind * 8 + nb, g))
                    slotB, sbB = self.ws.acquire(("df", kind * 8 + 4 + nb, g))
                    for half in range(2):
                        m = nb * 2 + half
                        pair = self.flip("rp")
                        banks = (6, 7) if pair == 0 else (2, 3)
                        pss = []
                        for slot, sbuf_, bk in ((slotA, sbA, banks[0]), (slotB, sbB, banks[1])):
                            wv = slot[:, 0:2048].rearrange("p (c m) -> p c m", c=NCH)
                            ps, pb = self.psum[bk]
                            for c in range(NCH):
                                P.op("pe", lambda e, ps=ps, c=c, wv=wv, half=half: e.matmul(
                                    ps[:], lhsT=wv[:, c, half * 128:(half + 1) * 128], rhs=hn_g[:, c, :],
                                    start=(c == 0), stop=(c == NCH - 1)),
                                    reads=[sbuf_, hngb[c]], writes=[pb])
                            pss.append((ps, pb))
                        (t1, t1b), (t2, t2b) = scr(pair * 2), scr(pair * 2 + 1)
                        (ps1, pb1), (ps2, pb2) = pss
                        P.op("dve", lambda e, t1=t1, ps1=ps1: e.tensor_tensor(out=t1, in0=ps1[:], in1=ropet[:, 0, :], op=ALU.mult),
                             reads=[pb1, ropeb], writes=[t1b])
                        P.op("dve", lambda e, t2=t2, ps2=ps2: e.tensor_tensor(out=t2, in0=ps2[:], in1=ropet[:, 1, :], op=ALU.mult),
                             reads=[pb2, ropeb], writes=[t2b])
                        if kind == 0:
                            for hf in range(2):
                                P.op("dve", lambda e, t1=t1, t2=t2, m=m, hf=hf: e.tensor_tensor(
                                    out=q_m[hf][hf * 64:(hf + 1) * 64, m, :], in0=t1[hf * 64:(hf + 1) * 64, :],
                                    in1=t2[hf * 64:(hf + 1) * 64, :], op=ALU.add),
                                    reads=[t1b, t2b], writes=[qgb[hf][m]])
                        else:
                            P.op("dve", lambda e, t1=t1, t2=t2, m=m, tgs=tgs: e.tensor_tensor(out=kT[:, m, tgs], in0=t1, in1=t2, op=ALU.add),
                                 reads=[t1b, t2b], writes=[kb[m][g]])
            for nb in range(4):
                slot, sbuf_ = self.ws.acquire(("df", 16 + nb, g))
                wv = slot[:, 0:2048].rearrange("p (c m) -> p c m", c=NCH)
                for tt in range(4):
                    ti = g * 4 + tt
                    ps, pb = self.psum[6 + self.flip("norm")]
                    for c in range(NCH):
                        P.op("pe", lambda e, ps=ps, c=c, wv=wv, tt=tt: e.matmul(
                            ps[:, 0:256], lhsT=hn_g[:, c, tt * 128:(tt + 1) * 128], rhs=wv[:, c, :],
                            start=(c == 0), stop=(c == NCH - 1)),
                            reads=[sbuf_, hngb[c]], writes=[pb])
                    P.op("dve", lambda e, ps=ps, ti=ti, nb=nb: e.tensor_copy(out=vT[:, ti, nb * 256:(nb + 1) * 256], in_=ps[:, 0:256]),
                         reads=[pb], writes=[vb[ti][nb]])
            tasks = []
            for h in range(NCH):
                for mp in range(2):
                    ntl = 4 * g + 4
                    for i in range(ntl):
                        t = T()
                        t.h, t.mp, t.p0, t.i = h, mp, 64 * mp, i
                        t.qoff = max(0, i - 4 * g) * 128
                        t.diag = i >= 4 * g
                        t.first = i == 0
                        t.last = i == ntl - 1
                        t.n = len(tasks)
                        t.par = (h * 2 + mp) % 2
                        tasks.append(t)

            def stA(t):
                t.s_ps, t.sb_ = self.psum[(0, 1, 6)[t.n % 3]]
                qo = t.qoff
                P.op("pe", lambda e: e.matmul(
                    t.s_ps[:, qo:512], lhsT=kT[:, t.h, t.i * 128:(t.i + 1) * 128],
                    rhs=q_m[t.mp][:, t.h, qo:512], start=True, stop=True),
                    reads=[kb[t.h][t.i // 4], qgb[t.mp][t.h]], writes=[t.sb_])

            def stB(t):
                t.pt, t.ptb = p_t[t.n % NP]
                qo = t.qoff
                P.op("act", lambda e: e.activation(out=t.pt[:, qo:512], in_=t.s_ps[:, qo:512], func=AF.Exp, scale=0.125),
                     reads=[t.sb_], writes=[t.ptb])
                if t.diag:
                    P.op("dve", lambda e: e.memset(t.pt[64:128, qo:qo + 64], 0.0), reads=[], writes=[t.ptb])

            def stC(t):
                o_ps, opb = self.psum[4 + t.par]
                den_ps, denb = self.psum[2 + t.par]
                qo = t.qoff
                P.op("pe", lambda e: e.matmul(den_ps[:, qo:512], lhsT=ones, rhs=t.pt[:, qo:512], start=t.first, stop=t.last),
                     reads=[t.ptb, cstb], writes=[denb])
                P.op("pe", lambda e: e.matmul(o_ps[:, qo:512], lhsT=vT[:, t.i, t.h * 128:(t.h + 1) * 128], rhs=t.pt[:, qo:512],
                                              start=t.first, stop=t.last),
                     reads=[t.ptb, vb[t.i][t.h // 2]], writes=[opb])

            def stF2(t):
                if not t.last:
                    return
                den_ps, denb = self.psum[2 + t.par]
                rden, rdenb = scr(2)
                P.op("act", lambda e: e.activation(out=rden, in_=den_ps[:], func=AF.Ln), reads=[denb], writes=[rdenb])
                P.op("act", lambda e: e.activation(out=rden, in_=rden, func=AF.Exp, scale=-1.0), reads=[rdenb], writes=[rdenb])

            def stF3(t):
                if not t.last:
                    return
                o_ps, opb = self.psum[4 + t.par]
                (rden, rdenb), (o1n, o1nb), (od, odb) = scr(2), scr(0), scr(1)
                if t.mp == 0:
                    P.op("dve", lambda e: e.tensor_tensor(out=o1n, in0=o_ps[:], in1=rden, op=ALU.mult),
                         reads=[opb, rdenb], writes=[o1nb])
                else:
                    P.op("dve", lambda e: e.tensor_tensor(out=od, in0=o_ps[:], in1=rden, op=ALU.mult),
                         reads=[opb, rdenb], writes=[odb])
                    P.op("dve", lambda e: e.scalar_tensor_tensor(out=od, in0=od, scalar=neglam, in1=o1n, op0=ALU.mult, op1=ALU.add),
                         reads=[odb, o1nb, smallb], writes=[odb])

            def stF4(t):
                if not (t.last and t.mp == 1):
                    return
                od, odb = scr(1)
                P.op("act", lambda e: e.activation(out=sqo, in_=od, func=AF.Square), reads=[odb], writes=[sqob])
                t.ss = self.psum[7]
                P.op("pe", lambda e: e.matmul(t.ss[0][:], lhsT=ones, rhs=sqo, start=True, stop=True),
                     reads=[sqob, cstb], writes=[t.ss[1]])

            def stF5(t):
                if not (t.last and t.mp == 1):
                    return
                rs, rsb = scr(4)
                P.op("act", lambda e: e.activation(out=rs, in_=t.ss[0][:], func=AF.Ln, scale=1.0 / 128.0, bias=RMS_EPS),
                     reads=[t.ss[1]], writes=[rsb])
                P.op("act", lambda e: e.activation(out=rs, in_=rs, func=AF.Exp, scale=-0.5), reads=[rsb], writes=[rsb])

            def stF6(t):
                if not (t.last and t.mp == 1):
                    return
                (od, odb), (rs, rsb) = scr(1), scr(4)
                h = t.h
                P.op("dve", lambda e: e.scalar_tensor_tensor(out=o_g[:, h, :], in0=od, scalar=subg, in1=rs, op0=ALU.mult, op1=ALU.mult),
                     reads=[odb, rsb, smallb], writes=[ogb[h]])

            sched = self.df_sched
            n = len(tasks)
            maxlag = max(l for (_, l) in sched)
            fns = {"A": stA, "B": stB, "C": stC, "F2": stF2, "F3": stF3, "F4": stF4, "F5": stF5, "F6": stF6}
            for s in range(n + maxlag):
                for (name, lag) in sched:
                    k = s - lag
                    if 0 <= k < n:
                        fns[name](tasks[k])
            dbg = getattr(self, "dbg", None)
            if dbg:
                srcs = {"o": (o_g, ogb), "hn": (hn_g, hngb)}
                if dbg == "k":
                    for m in range(NCH):
                        P.op("dve", lambda e, m=m, tgs=tgs: e.tensor_copy(out=xT[:, m, tgs], in_=kT[:, m, tgs]),
                             reads=[kb[m][g], self.xb[m][g]], writes=[self.xb[m][g]])
                elif dbg == "v":
                    for m in range(NCH):
                        P.op("dve", lambda e, m=m, tgs=tgs: e.tensor_copy(out=xT[:, m, tgs], in_=vT[:, g * 4 + m // 2, (m % 2) * 512:(m % 2) * 512 + 512]),
                             reads=[vb[g * 4 + m // 2][nb] for nb in range(4)] + [self.xb[m][g]], writes=[self.xb[m][g]])
                else:
                    s3, sb3 = srcs[dbg]
                    for m in range(NCH):
                        P.op("dve", lambda e, m=m, tgs=tgs, s3=s3: e.tensor_copy(out=xT[:, m, tgs], in_=s3[:, m, :]),
                             reads=[sb3[m], self.xb[m][g]], writes=[self.xb[m][g]])
                for nb in range(4):
                    self.ws.acquire(("df", 20 + nb, g))
                continue
            for nb in range(4):
                slot, sbuf_ = self.ws.acquire(("df", 20 + nb, g))
                for half in range(2):
                    c = nb * 2 + half

                    def evac(ps, pb, c=c, tgs=tgs, g=g):
                        P.op("dve", lambda e: e.tensor_tensor(
                            out=xT[:, c, tgs], in0=ps[:], in1=xT[:, c, tgs], op=ALU.add),
                            reads=[pb, self.xb[c][g]], writes=[self.xb[c][g]])
                    self.proj_fm(slot, sbuf_, half, o_g, ogb, evac)


def _consts():
    c = np.zeros((128, 512), np.float32)
    c[:, 0:128] = np.eye(128, dtype=np.float32)
    j = np.arange(128)[:, None]
    k = np.arange(128)[None, :]
    c[:, 128:256] = (j >= k).astype(np.float32)
    c[:, 256:384] = np.where(j >= k, MASKNEG, 0.0)
    c[:, 384:512] = 1.0
    return c


def _layout_weights(inputs):
    w_in = np.asarray(inputs["ffn_w_in"], np.float32).reshape(4, NCH, 128, 2, NF, 128)
    w_in = np.ascontiguousarray(w_in.transpose(0, 4, 2, 3, 1, 5)).reshape(4, NF, 128, 2048)
    w_out = np.asarray(inputs["ffn_w_out"], np.float32).reshape(4, NF, 128, NCH, 128)
    w_out = np.ascontiguousarray(w_out.transpose(0, 3, 2, 1, 4)).reshape(4, NCH, 128, D_FF)

    def colblocks(w, ncols):
        nb = ncols // 256
        a = w.reshape(NCH, 128, nb, 256).transpose(2, 1, 0, 3)
        return np.ascontiguousarray(a).reshape(nb, 128, 2048)
    sbq = np.asarray(inputs["sb_w_qkv"], np.float32)[0]
    sbo = np.asarray(inputs["sb_w_o"], np.float32)[0]
    w_sb = np.concatenate([colblocks(sbq, 3072), colblocks(sbo, 1024)], axis=0)
    dq = np.asarray(inputs["diff_w_qkv"], np.float32)[0]
    do = np.asarray(inputs["diff_w_o"], np.float32)[0]
    perm = np.arange(1024).reshape(16, 2, 32)[:, ::-1, :].reshape(1024)
    w_df = np.concatenate([colblocks(dq[:, 0:1024], 1024), colblocks(dq[:, 0:1024][:, perm], 1024),
                           colblocks(dq[:, 1024:2048], 1024), colblocks(dq[:, 1024:2048][:, perm], 1024),
                           colblocks(dq[:, 2048:3072], 1024), colblocks(do, 1024)], axis=0)
    return w_in, w_out, w_sb, w_df


def _rope_tables():
    p = np.arange(128)
    i = p % 32
    inv = 10000.0 ** (-(i.astype(np.float32)) / 32.0)
    ang = np.arange(SEQ, dtype=np.float32)[None, :] * inv[:, None].astype(np.float32)
    cos = np.cos(ang).astype(np.float32)
    sin = np.sin(ang).astype(np.float32)
    sgn = np.where((p % 64) < 32, -1.0, 1.0).astype(np.float32)[:, None]
    return np.ascontiguousarray(np.stack([cos, sin * sgn], axis=1))


FULL_STAGES = [("ffn", 0, 0), ("sb", 1), ("ffn", 1, 2), ("ffn", 2, 3), ("diff", 4), ("ffn", 3, 5), ("final", 6)]
_CACHE = {}
_RUN_KW = {}


def _prepare(inputs):
    w_in, w_out, w_sb, w_df = _layout_weights(inputs)
    ng = np.asarray(inputs["norm_gains"], np.float32).reshape(6, NCH, 128)
    fg = np.asarray(inputs["final_gain"], np.float32).reshape(1, NCH, 128)
    gains = np.ascontiguousarray(np.concatenate([ng, fg], 0).transpose(2, 0, 1)).reshape(128, 56)
    lam = np.ascontiguousarray(np.broadcast_to(np.asarray(inputs["diff_lambda"], np.float32).reshape(1, 256), (128, 256)))
    subln = np.ascontiguousarray(np.asarray(inputs["diff_subln"], np.float32).reshape(128, 1))
    shared = {"gains": gains, "consts": _consts(), "w_in": w_in, "w_out": w_out, "w_sb": w_sb, "w_df": w_df,
              "rope": _rope_tables(), "lam": lam, "subln": subln}
    return shared


def _x_to_dev(xb):
    return np.ascontiguousarray(xb.T.reshape(NCH, 128, SEQ).transpose(1, 0, 2))


def _x_from_dev(y):
    return np.ascontiguousarray(y.transpose(1, 0, 2).reshape(D_MODEL, SEQ).T)


def run_stages(stages, xs, shared, core_ids=None):
    key = tuple(stages)
    if key not in _CACHE:
        _CACHE[key] = Builder(list(stages)).build()
    nc = _CACHE[key]
    n = len(xs)
    in_maps = [dict(shared, xT=_x_to_dev(np.asarray(x, np.float32))) for x in xs]
    res = run_bass_kernel_spmd(nc, in_maps, core_ids=list(range(n)), **_RUN_KW)
    if res.exec_time_ns is not None:
        print("exec_time_ns", res.exec_time_ns)
    return [_x_from_dev(r["outT"]) for r in res.results]


def kernel(**inputs):
    shared = _prepare(inputs)
    x = np.asarray(inputs["x"], np.float32)
    outs = run_stages(FULL_STAGES, [x[b] for b in range(x.shape[0])], shared)
    return np.stack(outs, axis=0).astype(np.float32)
```

```python
import math
from contextlib import ExitStack

import numpy as np
import concourse.bass as bass
import concourse.mybir as mybir
from concourse.bass_utils import run_bass_kernel_spmd

F32 = mybir.dt.float32
BF16 = mybir.dt.bfloat16
AF = mybir.ActivationFunctionType
ALU = mybir.AluOpType

ENGS = ("pe", "act", "dve", "pool", "sp")

D_MODEL = 1024
SEQ = 2048
NCH = 8
D_FF = 2816
NF = 22
TG = 512
NG = 4
RMS_EPS = 1e-6
LAMBDA_INIT = 0.8 - 0.6 * math.exp(-0.3 * (2 - 1))
NSLOT = 5
MASKNEG = -30000.0


class Buf:
    __slots__ = ("name", "last_w", "readers")

    def __init__(self, name):
        self.name = name
        self.last_w = None
        self.readers = []


class Op:
    __slots__ = ("eng", "fn", "deps", "needed", "ticket", "dsem", "idx", "is_dma")

    def __init__(self, eng, fn, dsem=None):
        self.eng = eng
        self.fn = fn
        self.deps = []
        self.needed = False
        self.ticket = None
        self.dsem = dsem
        self.is_dma = dsem is not None
        self.idx = None


class Prog:
    def __init__(self):
        self.ops = {e: [] for e in ENGS}
        self.n = 0
        self.dma_keys = []
        self.last_dma = {}

    def op(self, eng, fn, reads=(), writes=(), dsem=None, extra_deps=()):
        o = Op(eng, fn, dsem)
        o.idx = self.n
        self.n += 1
        if dsem is not None:
            if dsem not in self.dma_keys:
                self.dma_keys.append(dsem)
        deps = {}
        for b in reads:
            if b.last_w is not None:
                deps[id(b.last_w)] = b.last_w
        for b in writes:
            if b.last_w is not None:
                deps[id(b.last_w)] = b.last_w
            for r in b.readers:
                deps[id(r)] = r
        for d in extra_deps:
            deps[id(d)] = d
        if dsem is not None and dsem in self.last_dma:
            d = self.last_dma[dsem]
            deps[id(d)] = d
        for d in deps.values():
            if d.eng == "pe" and eng == "pe" and not d.is_dma and not o.is_dma:
                continue
            o.deps.append(d)
            d.needed = True
        for b in reads:
            b.readers.append(o)
        for b in writes:
            b.last_w = o
            b.readers = []
        if dsem is not None:
            self.last_dma[dsem] = o
        self.ops[eng].append(o)
        return o

    def barrier(self):
        lasts = []
        for e in ENGS:
            for o in reversed(self.ops[e]):
                if not o.is_dma and o.fn is not None:
                    lasts.append(o)
                    break
        lasts.extend(self.last_dma.values())
        for e in ENGS:
            self.op(e, None, extra_deps=[d for d in lasts])

    def assign(self):
        cnt = {e: 0 for e in ENGS}
        dcnt = {k: 0 for k in self.dma_keys}
        allops = []
        for e in ENGS:
            allops.extend(self.ops[e])
        allops.sort(key=lambda o: o.idx)
        for o in allops:
            if o.is_dma:
                dcnt[o.dsem] += 16
                o.ticket = dcnt[o.dsem]
            elif o.needed and o.fn is not None:
                cnt[o.eng] += 1
                o.ticket = cnt[o.eng]

    def run_engine(self, eng, e, sems, dsems):
        waited = {}
        for o in self.ops[eng]:
            for d in o.deps:
                if d.ticket is None:
                    continue
                key = ("d", d.dsem) if d.is_dma else ("e", d.eng)
                if waited.get(key, 0) >= d.ticket:
                    continue
                waited[key] = d.ticket
                s = dsems[d.dsem] if d.is_dma else sems[d.eng]
                e.wait_ge(s, d.ticket)
            if o.fn is None:
                continue
            ins = o.fn(e)
            if o.is_dma:
                ins.then_inc(dsems[o.dsem], 16)
            elif o.needed:
                ins.then_inc(sems[o.eng], 1)


class WStream:
    def __init__(self, P, ring, nslot):
        self.P = P
        self.ring = ring
        self.nslot = nslot
        self.blocks = []
        self.issued = 0
        self.next = 0

    def plan(self, tag, src, n):
        self.blocks.append((tag, src, n))

    def _issue(self, k):
        tag, src, n = self.blocks[k]
        tile, buf = self.ring[k % self.nslot]
        s = k % self.nslot

        def fn(e, tile=tile, src=src, n=n):
            if n % 2 == 0 and n > 1024:
                return e.dma_start(out=tile[:, 0:n].rearrange("p (a b) -> p a b", a=2),
                                   in_=src.rearrange("p (a b) -> p a b", a=2))
            return e.dma_start(out=tile[:, 0:n], in_=src)
        self.P.op("pool", fn, writes=[buf], dsem=f"w{s}")

    def acquire(self, tag):
        k = self.next
        self.next += 1
        assert self.blocks[k][0] == tag, (self.blocks[k][0], tag)
        while self.issued < min(len(self.blocks), k + self.nslot - 1):
            self._issue(self.issued)
            self.issued += 1
        return self.ring[k % self.nslot]


class Builder:
    def __init__(self, stages):
        self.stages = stages
        nc = bass.Bass("TRN2", target_bir_lowering=False)
        self.nc = nc
        self.P = Prog()
        d = lambda name, shape, kind="ExternalInput": nc.dram_tensor(name, shape, F32, kind=kind).ap()
        self.d_x = d("xT", [128, NCH, SEQ])
        self.d_gains = d("gains", [128, 56])
        self.d_consts = d("consts", [128, 640])
        self.d_win = d("w_in", [4, NF, 128, 2048])
        self.d_wout = d("w_out", [4, NCH, 128, D_FF])
        self.d_wsb = d("w_sb", [16, 128, 2048])
        self.d_wdf = d("w_df", [16, 128, 2048])
        self.d_rope = d("rope", [128, 2, SEQ])
        self.d_lam = d("lam", [128, 256])
        self.d_subln = d("subln", [128, 1])
        self.d_out = d("outT", [128, NCH, SEQ], kind="ExternalOutput")

    def sb(self, name, shape, dt):
        return self.es.enter_context(self.nc.sbuf_tensor(name, shape, dt))

    def build(self):
        nc, P = self.nc, self.P
        with ExitStack() as es:
            self.es = es
            self.xT = self.sb("xT_sb", [128, NCH, SEQ], F32)
            self.HN = self.sb("hn_sb", [128, NCH * SEQ], BF16)
            self.R = self.sb("r_sb", [128, 16 * SEQ], BF16)
            self.ringt = [self.sb(f"ring{i}", [128, 2048], BF16) for i in range(NSLOT)]
            self.SCR = self.sb("scr", [128, 3072], F32)
            self.SCRB = self.sb("scrb", [128, 2048], BF16)
            self.SCRC = self.sb("scrc", [128, 2048], BF16)
            self.cst = self.sb("cst", [128, 640], BF16)
            self.gains = self.sb("gains_sb", [128, 56], F32)
            self.lamt = self.sb("lam_sb", [128, 256], F32)
            self.small = self.sb("small_sb", [128, 8], F32)
            self.ropet = self.sb("rope_sb", [128, 2, TG], F32)
            self.psum = []
            for i in range(8):
                t = es.enter_context(nc.psum_tensor(f"ps{i}", [128, 512], F32))
                self.psum.append((t, Buf(f"ps{i}")))
            self.sems = {e: es.enter_context(nc.semaphore(f"s_{e}")) for e in ENGS}

            self.xb = [[Buf(f"x{c}_{g}") for g in range(NG)] for c in range(NCH)]
            self.ring = [(self.ringt[i], Buf(f"ring{i}")) for i in range(NSLOT)]
            self.scrb = [Buf(f"scr{i}") for i in range(6)]
            self.scrbb = [Buf(f"scrb{i}") for i in range(4)]
            self.cstb = Buf("cst")
            self.gainb = Buf("gains")
            self.lamb = Buf("lam")
            self.smallb = Buf("small")
            self.ropeb = Buf("rope")
            self.outb = [Buf(f"out{g}") for g in range(NG)]
            self.ws = WStream(P, self.ring, NSLOT)
            self.par = {}
            self.region_prev = {}
            self.region_cur = {}
            self.df_sched = [("C", 4), ("A", 0), ("B", 2), ("F2", 6), ("F3", 7), ("F4", 8), ("F5", 9), ("F6", 10)]
            self.sb_sched = [("A", 0), ("B1", 1), ("D1", 3), ("B2", 1), ("C", 2), ("D2", 3), ("E", 4), ("F", 6)]

            self.plan_weights()
            self.emit_all()

            P.assign()
            dsems = {k: es.enter_context(nc.semaphore(f"d_{k}")) for k in P.dma_keys}
            sems = self.sems
            with nc.Block() as block:
                @block.sync
                def _(e):
                    P.run_engine("sp", e, sems, dsems)

                @block.gpsimd
                def _(e):
                    P.run_engine("pool", e, sems, dsems)

                @block.tensor
                def _(e):
                    P.run_engine("pe", e, sems, dsems)

                @block.scalar
                def _(e):
                    P.run_engine("act", e, sems, dsems)

                @block.vector
                def _(e):
                    P.run_engine("dve", e, sems, dsems)
        return nc

    def nb(self, name, region):
        b = Buf(name)
        b.readers = list(self.region_prev.get(region, []))
        self.region_cur.setdefault(region, []).append(b)
        return b

    def stage_end(self):
        for region, bufs in self.region_cur.items():
            last = {}
            for b in bufs:
                for o in ([b.last_w] if b.last_w is not None else []) + list(b.readers):
                    key = ("d", o.dsem) if o.is_dma else ("e", o.eng)
                    if key not in last or last[key].idx < o.idx:
                        last[key] = o
            self.region_prev[region] = list(last.values())
        self.region_cur = {}

    def flip(self, key, n=2):
        v = self.par.get(key, 0)
        self.par[key] = (v + 1) % n
        return v

    def plan_weights(self):
        ws = self.ws
        for st in self.stages:
            kind = st[0]
            if kind == "ffn":
                widx = st[1]
                for (j0, j1) in ((0, 11), (11, 22)):
                    for j in range(j0, j1):
                        ws.plan(("win", widx, j), self.d_win[widx, j], 2048)
                    for c in range(NCH):
                        ws.plan(("wout", widx, j0, c), self.d_wout[widx, c, :, j0 * 128:j1 * 128], (j1 - j0) * 128)
            elif kind == "sb":
                for g in range(NG):
                    for blk in range(12):
                        ws.plan(("sb", blk, g), self.d_wsb[blk], 2048)
                    for blk in range(12, 16):
                        ws.plan(("sb", blk, g), self.d_wsb[blk], 2048)
            elif kind == "diff":
                for g in range(NG):
                    for blk in Builder.DF_ORDER:
                        ws.plan(("df", blk, g), self.d_wdf[blk], 2048)

    def emit_all(self):
        P = self.P
        P.op("pool", lambda e: e.dma_start(out=self.cst[:], in_=self.d_consts), writes=[self.cstb], dsem="cst")
        P.op("sp", lambda e: e.dma_start(out=self.gains[:], in_=self.d_gains), writes=[self.gainb], dsem="gains")
        for g in range(NG):
            P.op("sp", lambda e, g=g: e.dma_start(out=self.xT[:, :, g * TG:(g + 1) * TG],
                                                  in_=self.d_x[:, :, g * TG:(g + 1) * TG]),
                 writes=[self.xb[c][g] for c in range(NCH)], dsem=f"x{g}")
        for st in self.stages:
            if st[0] == "ffn":
                self.ffn(st[1], st[2])
            elif st[0] == "sb":
                self.sb_mixer(st[1])
            elif st[0] == "diff":
                self.diff_mixer(st[1])
            elif st[0] == "final":
                self.final_norm(st[1])
            self.stage_end()
        for g in range(NG):
            P.op("sp", lambda e, g=g: e.dma_start(out=self.d_out[:, :, g * TG:(g + 1) * TG],
                                                  in_=self.xT[:, :, g * TG:(g + 1) * TG]),
                 reads=[self.xb[c][g] for c in range(NCH)], writes=[self.outb[g]], dsem=f"o{g}")
        P.op("sp", None, reads=self.outb)

    def norm_group(self, nidx, g, dst_fn, in_place=False):
        P = self.P
        tgs = slice(g * TG, (g + 1) * TG)
        ones = self.cst[:, 384:512]
        ps, psb = self.psum[6 + self.flip("norm")]
        xT = self.xT
        for c in range(NCH):
            k = self.flip("sq")
            sq = self.SCRB[:, (2 + k) * 512:(3 + k) * 512]
            sqb = self.scrbb[2 + k]
            P.op("act", lambda e, c=c, sq=sq: e.activation(out=sq, in_=xT[:, c, tgs], func=AF.Square),
                 reads=[self.xb[c][g]], writes=[sqb])
            P.op("pe", lambda e, c=c, sq=sq: e.matmul(ps[:], lhsT=ones, rhs=sq, start=(c == 0), stop=(c == NCH - 1)),
                 reads=[sqb, self.cstb], writes=[psb])
        lnt = self.SCR[:, 5 * 512:6 * 512]
        lnb = self.scrb[5]
        rs = self.SCR[:, 4 * 512:5 * 512]
        rsb = self.scrb[4]
        P.op("act", lambda e: e.activation(out=lnt, in_=ps[:], func=AF.Ln, scale=1.0 / D_MODEL, bias=RMS_EPS),
             reads=[psb], writes=[lnb])
        P.op("act", lambda e: e.activation(out=rs, in_=lnt, func=AF.Exp, scale=-0.5), reads=[lnb], writes=[rsb])
        for c in range(NCH):
            out_ap, obufs = dst_fn(c)
            gcol = self.gains[:, nidx * 8 + c:nidx * 8 + c + 1]
            rd = [self.xb[c][g], rsb, self.gainb]
            P.op("dve", lambda e, c=c, out_ap=out_ap, gcol=gcol: e.scalar_tensor_tensor(
                out=out_ap, in0=xT[:, c, tgs], scalar=gcol, in1=rs, op0=ALU.mult, op1=ALU.mult),
                reads=rd, writes=obufs)

    def ffn(self, widx, nidx):
        P = self.P
        xT = self.xT
        hn3 = self.HN[:].rearrange("p (c t) -> p c t", c=NCH)
        hnb = [[self.nb(f"hn{c}_{g}", "HN") for g in range(NG)] for c in range(NCH)]
        actT = self.R[:, 0:11 * SEQ].rearrange("p (j t) -> p j t", j=11)
        actb = [[self.nb(f"act{j}_{g}", "R") for g in range(NG)] for j in range(11)]
        for g in range(NG):
            tgs = slice(g * TG, (g + 1) * TG)
            self.norm_group(nidx, g, lambda c, tgs=tgs, g=g: (hn3[:, c, tgs], [hnb[c][g]]))
        for (j0, j1) in ((0, 11), (11, 22)):
            nj = j1 - j0
            for j in range(j0, j1):
                slot, sbuf_ = self.ws.acquire(("win", widx, j))
                wv = slot[:, 0:2048].rearrange("p (g c m) -> p g c m", g=2, c=NCH)
                jj = j - j0
                for g in range(NG):
                    tgs = slice(g * TG, (g + 1) * TG)
                    par = self.flip("gu")
                    gps, gpb = self.psum[par * 2]
                    ups, upb = self.psum[par * 2 + 1]
                    for gu, (ps, pb) in enumerate(((gps, gpb), (ups, upb))):
                        for c in range(NCH):
                            P.op("pe", lambda e, ps=ps, gu=gu, c=c, wv=wv, tgs=tgs: e.matmul(
                                ps[:], lhsT=wv[:, gu, c, :], rhs=hn3[:, c, tgs], start=(c == 0), stop=(c == NCH - 1)),
                                reads=[sbuf_, hnb[c][g]], writes=[pb])
                    sg = self.SCR[:, par * 512:(par + 1) * 512]
                    sgb = self.scrb[par]
                    P.op("act", lambda e, sg=sg, gps=gps: e.activation(out=sg, in_=gps[:], func=AF.Silu),
                         reads=[gpb], writes=[sgb])
                    P.op("dve", lambda e, sg=sg, ups=ups, jj=jj, tgs=tgs: e.tensor_tensor(
                        out=actT[:, jj, tgs], in0=sg, in1=ups[:], op=ALU.mult),
                        reads=[sgb, upb], writes=[actb[jj][g]])
            for c in range(NCH):
                slot, sbuf_ = self.ws.acquire(("wout", widx, j0, c))
                wv = slot[:, 0:nj * 128].rearrange("p (j m) -> p j m", m=128)
                for g in range(NG):
                    tgs = slice(g * TG, (g + 1) * TG)
                    ps, pb = self.psum[4 + self.flip("y")]
                    for jj in range(nj):
                        P.op("pe", lambda e, ps=ps, jj=jj, wv=wv, tgs=tgs: e.matmul(
                            ps[:], lhsT=wv[:, jj, :], rhs=actT[:, jj, tgs], start=(jj == 0), stop=(jj == nj - 1)),
                            reads=[sbuf_, actb[jj][g]], writes=[pb])
                    P.op("dve", lambda e, ps=ps, c=c, tgs=tgs: e.scalar_tensor_tensor(
                        out=xT[:, c, tgs], in0=ps[:], scalar=0.5, in1=xT[:, c, tgs], op0=ALU.mult, op1=ALU.add),
                        reads=[pb, self.xb[c][g]], writes=[self.xb[c][g]])

    def final_norm(self, nidx):
        xT = self.xT
        for g in range(NG):
            tgs = slice(g * TG, (g + 1) * TG)
            self.norm_group(nidx, g, lambda c, tgs=tgs, g=g: (xT[:, c, tgs], [self.xb[c][g]]))

    def proj_fm(self, slot, sbuf_, half, src3, srcb, evac):
        P = self.P
        wv = slot[:, 0:2048].rearrange("p (c m) -> p c m", c=NCH)
        ps, pb = self.psum[6 + self.flip("norm")]
        for c in range(NCH):
            P.op("pe", lambda e, ps=ps, c=c, wv=wv, half=half: e.matmul(
                ps[:], lhsT=wv[:, c, half * 128:(half + 1) * 128], rhs=src3[:, c, :],
                start=(c == 0), stop=(c == NCH - 1)),
                reads=[sbuf_, srcb[c]], writes=[pb])
        evac(ps, pb)

    def sb_mixer(self, nidx):
        P = self.P
        xT = self.xT
        kT = self.R[:, 0:NCH * SEQ].rearrange("p (m t) -> p m t", m=NCH)
        vT = self.R[:, NCH * SEQ:16 * SEQ].rearrange("p (i f) -> p i f", i=16)
        kb = [[self.nb(f"k{m}_{g}", "R") for g in range(NG)] for m in range(NCH)]
        vb = [[self.nb(f"v{i}_{nb}", "R") for nb in range(4)] for i in range(16)]
        hn_g = self.HN[:, 0:4096].rearrange("p (c t) -> p c t", c=NCH)
        q_m = [self.HN[:, 4096:8192].rearrange("p (c t) -> p c t", c=NCH),
               self.HN[:, 8192:12288].rearrange("p (c t) -> p c t", c=NCH)]
        o_g = self.HN[:, 12288:16384].rearrange("p (c t) -> p c t", c=NCH)
        hngb = [self.nb(f"hng{c}", "HN") for c in range(NCH)]
        qgb = [[self.nb(f"qg{hf}_{c}", "HN") for c in range(NCH)] for hf in range(2)]
        ogb = [self.nb(f"og{c}", "HN") for c in range(NCH)]
        P.op("dve", lambda e: e.memset(q_m[0][64:128, :, :], 0.0), writes=qgb[0])
        P.op("dve", lambda e: e.memset(q_m[1][0:64, :, :], 0.0), writes=qgb[1])
        ident = self.cst[:, 0:128]
        tinc = self.cst[:, 128:256]
        maskneg = self.cst[:, 256:384]
        ones = self.cst[:, 384:512]
        cstb = self.cstb
        NE = 4
        e_t = [(self.SCR[:, k * 512:(k + 1) * 512], self.scrb[k]) for k in range(4)]
        ecs_t = [(self.SCR[:, (4 + k) * 512:(5 + k) * 512], self.scrb[4 + k]) for k in range(2)]
        sp_t = [(self.SCRB[:, k * 512:(k + 1) * 512], self.scrbb[k]) for k in range(2)]
        aT_t = [(self.SCRC[:, k * 512:(k + 1) * 512], self.nb(f"aT{k}", "SCRC")) for k in range(2)]
        R_t = [(self.SCRC[:, 1024 + k * 512:1024 + (k + 1) * 512], self.nb(f"R{k}", "SCRC")) for k in range(2)]

        class T:
            pass

        for g in range(NG):
            tgs = slice(g * TG, (g + 1) * TG)
            self.norm_group(nidx, g, lambda c: (hn_g[:, c, :], [hngb[c]]))
            for kind in range(2):
                for nb in range(4):
                    slot, sbuf_ = self.ws.acquire(("sb", kind * 4 + nb, g))
                    for half in range(2):
                        m = nb * 2 + half
                        if kind == 0:
                            def evac(ps, pb, m=m):
                                for hf in range(2):
                                    P.op("dve", lambda e, hf=hf: e.tensor_scalar(out=q_m[hf][hf * 64:(hf + 1) * 64, m, :],
                                                                                 in0=ps[hf * 64:(hf + 1) * 64, :], scalar1=0.125, scalar2=None, op0=ALU.mult),
                                         reads=[pb], writes=[qgb[hf][m]])
                        else:
                            def evac(ps, pb, m=m, tgs=tgs, g=g):
                                P.op("dve", lambda e: e.tensor_copy(out=kT[:, m, tgs], in_=ps[:]),
                                     reads=[pb], writes=[kb[m][g]])
                        self.proj_fm(slot, sbuf_, half, hn_g, hngb, evac)
            for nb in range(4):
                slot, sbuf_ = self.ws.acquire(("sb", 8 + nb, g))
                wv = slot[:, 0:2048].rearrange("p (c m) -> p c m", c=NCH)
                for tt in range(4):
                    ti = g * 4 + tt
                    ps, pb = self.psum[6 + self.flip("norm")]
                    for c in range(NCH):
                        P.op("pe", lambda e, ps=ps, c=c, wv=wv, tt=tt: e.matmul(
                            ps[:, 0:256], lhsT=hn_g[:, c, tt * 128:(tt + 1) * 128], rhs=wv[:, c, :],
                            start=(c == 0), stop=(c == NCH - 1)),
                            reads=[sbuf_, hngb[c]], writes=[pb])
                    P.op("dve", lambda e, ps=ps, ti=ti, nb=nb: e.tensor_copy(out=vT[:, ti, nb * 256:(nb + 1) * 256], in_=ps[:, 0:256]),
                         reads=[pb], writes=[vb[ti][nb]])
            tasks = []
            for m in range(NCH):
                for half in range(2):
                    tiles = list(range(4 * g + 3, -1, -1))
                    for idx, i in enumerate(tiles):
                        t = T()
                        t.m, t.half, t.h, t.p0, t.i, t.idx = m, half, 2 * m + half, 64 * half, i, idx
                        t.qoff = max(0, i - 4 * g) * 128
                        t.diag = i >= 4 * g
                        t.first = idx == 0
                        t.last = idx == len(tiles) - 1
                        t.nqoff = 0 if t.last else max(0, tiles[idx + 1] - 4 * g) * 128
                        t.n = len(tasks)
                        tasks.append(t)
            o_cur = {}

            def stA(t):
                t.z_ps, t.zb = self.psum[t.n % 2]
                qo = t.qoff
                P.op("pe", lambda e: e.matmul(
                    t.z_ps[:, qo:512], lhsT=kT[:, t.m, t.i * 128:(t.i + 1) * 128],
                    rhs=q_m[t.half][:, t.m, qo:512], start=True, stop=(not t.diag)),
                    reads=[kb[t.m][t.i // 4], qgb[t.half][t.m]], writes=[t.zb])
                if t.diag:
                    P.op("pe", lambda e: e.matmul(
                        t.z_ps[:, qo:qo + 128], lhsT=ident, rhs=maskneg, start=False, stop=True),
                        reads=[cstb], writes=[t.zb])
                for _f in range(getattr(self, "fill", 0)):
                    fps, fpb = self.psum[6 + (_f % 2)]
                    P.op("pe", lambda e, fps=fps: e.matmul(fps[:], lhsT=ones, rhs=self.cst[:, 0:512], start=True, stop=True),
                         reads=[cstb], writes=[fpb])

            def stB1(t):
                t.et, t.eb = e_t[t.n % NE]
                qo = t.qoff
                P.op("act", lambda e: e.activation(out=t.et[:, qo:512], in_=t.z_ps[:, qo:512], func=AF.Exp),
                     reads=[t.zb], writes=[t.eb])

            def stB2(t):
                t.spt, t.spb = sp_t[t.n % 2]
                qo = t.qoff
                P.op("act", lambda e: e.activation(out=t.spt[:, qo:512], in_=t.et[:, qo:512], func=AF.Ln, bias=1.0),
                     reads=[t.eb], writes=[t.spb])

            def stC(t):
                t.cs_ps, t.csb = self.psum[2 + t.n % 2]
                qo = t.qoff
                P.op("pe", lambda e: e.matmul(
                    t.cs_ps[:, qo:512], lhsT=tinc, rhs=t.spt[:, qo:512], start=True, stop=t.first),
                    reads=[t.spb, cstb], writes=[t.csb])
                if not t.first:
                    rt, rb = t.rprev
                    P.op("pe", lambda e: e.matmul(
                        t.cs_ps[:, qo:512], lhsT=ones, rhs=rt[:, qo:512], start=False, stop=True),
                        reads=[rb, cstb], writes=[t.csb])
                if not t.last:
                    rn, rnb = R_t[t.n % 2]
                    if t.first:
                        P.op("dve", lambda e: e.tensor_copy(out=rn[:, qo:512], in_=t.spt[:, qo:512]),
                             reads=[t.spb], writes=[rnb])
                    else:
                        rt, rb = t.rprev
                        P.op("dve", lambda e: e.tensor_tensor(
                            out=rn[:, qo:512], in0=rt[:, qo:512], in1=t.spt[:, qo:512], op=ALU.add),
                            reads=[t.spb, rb], writes=[rnb])
                    if t.nqoff < qo:
                        P.op("dve", lambda e: e.memset(rn[:, t.nqoff:qo], 0.0), reads=[], writes=[rnb])
                    tasks[t.n + 1].rprev = (rn, rnb)

            def stD1(t):
                t.ect, t.ecb = ecs_t[t.n % 2]
                qo = t.qoff
                P.op("act", lambda e: e.activation(out=t.ect[:, qo:512], in_=t.cs_ps[:, qo:512], func=AF.Exp, scale=-1.0),
                     reads=[t.csb], writes=[t.ecb])

            def stD2(t):
                t.at, t.ab = aT_t[t.n % 2]
                qo = t.qoff
                P.op("dve", lambda e: e.tensor_tensor(
                    out=t.at[:, qo:512], in0=t.et[:, qo:512], in1=t.ect[:, qo:512], op=ALU.mult),
                    reads=[t.eb, t.ecb], writes=[t.ab])

            def stE(t):
                o_ps, opb = self.psum[4 + t.half]
                qo = t.qoff
                P.op("pe", lambda e: e.matmul(
                    o_ps[:, qo:512], lhsT=vT[:, t.i, t.m * 128:(t.m + 1) * 128], rhs=t.at[:, qo:512],
                    start=t.first, stop=t.last),
                    reads=[vb[t.i][t.m // 2], t.ab], writes=[opb])

            def stF(t):
                if t.last:
                    o_ps, opb = self.psum[4 + t.half]
                    m, p0 = t.m, t.p0
                    P.op("dve", lambda e: e.tensor_copy(out=o_g[p0:p0 + 64, m, :], in_=o_ps[p0:p0 + 64, :]),
                         reads=[opb], writes=[ogb[m]])

            sched = self.sb_sched
            n = len(tasks)
            maxlag = max(l for (_, l) in sched)
            fns = {"A": stA, "B1": stB1, "B2": stB2, "C": stC, "D1": stD1, "D2": stD2, "E": stE, "F": stF}
            for s in range(n + maxlag):
                for (name, lag) in sched:
                    k = s - lag
                    if 0 <= k < n:
                        fns[name](tasks[k])
            for nb in range(4):
                slot, sbuf_ = self.ws.acquire(("sb", 12 + nb, g))
                for half in range(2):
                    c = nb * 2 + half

                    def evac(ps, pb, c=c, tgs=tgs, g=g):
                        P.op("dve", lambda e: e.tensor_tensor(
                            out=xT[:, c, tgs], in0=ps[:], in1=xT[:, c, tgs], op=ALU.add),
                            reads=[pb, self.xb[c][g]], writes=[self.xb[c][g]])
                    self.proj_fm(slot, sbuf_, half, o_g, ogb, evac)

    DF_ORDER = list(range(16))

    def diff_mixer(self, nidx):
        P = self.P
        xT = self.xT
        kT = self.R[:, 0:NCH * SEQ].rearrange("p (m t) -> p m t", m=NCH)
        vT = self.R[:, NCH * SEQ:16 * SEQ].rearrange("p (i f) -> p i f", i=16)
        kb = [[self.nb(f"k{m}_{g}", "R") for g in range(NG)] for m in range(NCH)]
        vb = [[self.nb(f"v{i}_{nb}", "R") for nb in range(4)] for i in range(16)]
        hn_g = self.HN[:, 0:4096].rearrange("p (c t) -> p c t", c=NCH)
        q_m = [self.HN[:, 4096:8192].rearrange("p (c t) -> p c t", c=NCH),
               self.HN[:, 8192:12288].rearrange("p (c t) -> p c t", c=NCH)]
        o_g = self.HN[:, 12288:16384].rearrange("p (c t) -> p c t", c=NCH)
        hngb = [self.nb(f"hng{c}", "HN") for c in range(NCH)]
        qgb = [[self.nb(f"qg{hf}_{c}", "HN") for c in range(NCH)] for hf in range(2)]
        ogb = [self.nb(f"og{c}", "HN") for c in range(NCH)]
        P.op("dve", lambda e: e.memset(q_m[0][64:128, :, :], 0.0), writes=qgb[0])
        P.op("dve", lambda e: e.memset(q_m[1][0:64, :, :], 0.0), writes=qgb[1])
        ones = self.cst[:, 384:512]
        permm = self.cst[:, 512:640]
        cstb = self.cstb
        NP = 4
        p_t = [(self.SCRC[:, k * 512:(k + 1) * 512], self.nb(f"pT{k}", "SCRC")) for k in range(NP)]
        sqo, sqob = self.SCRB[:, 0:512], self.scrbb[0]
        scr = lambda k: (self.SCR[:, k * 512:(k + 1) * 512], self.scrb[k])
        small, smallb = self.small, self.smallb
        ropet, ropeb = self.ropet, self.ropeb

        tmp, tmpb = scr(0)
        P.op("sp", lambda e: e.dma_start(out=self.lamt[:], in_=self.d_lam), writes=[self.lamb], dsem="lam")
        P.op("sp", lambda e: e.dma_start(out=small[:, 6:7], in_=self.d_subln), writes=[smallb], dsem="subln")
        for k in range(2):
            P.op("dve", lambda e, k=k: e.tensor_tensor(out=tmp[:, k * 64:(k + 1) * 64], in0=self.lamt[:, k * 128:k * 128 + 64],
                                                       in1=self.lamt[:, k * 128 + 64:k * 128 + 128], op=ALU.mult),
                 reads=[self.lamb], writes=[tmpb])
            P.op("dve", lambda e, k=k: e.reduce_sum(out=small[:, k:k + 1], in_=tmp[:, k * 64:(k + 1) * 64], axis=mybir.AxisListType.X),
                 reads=[tmpb], writes=[smallb])
        P.op("act", lambda e: e.activation(out=small[:, 2:4], in_=small[:, 0:2], func=AF.Exp), reads=[smallb], writes=[smallb])
        P.op("dve", lambda e: e.tensor_tensor(out=small[:, 4:5], in0=small[:, 3:4], in1=small[:, 2:3], op=ALU.subtract),
             reads=[smallb], writes=[smallb])
        P.op("dve", lambda e: e.tensor_scalar(out=small[:, 4:5], in0=small[:, 4:5], scalar1=-LAMBDA_INIT, scalar2=None, op0=ALU.add),
             reads=[smallb], writes=[smallb])
        P.op("dve", lambda e: e.tensor_scalar(out=small[:, 5:6], in0=small[:, 6:7], scalar1=(1.0 - LAMBDA_INIT), scalar2=None, op0=ALU.mult),
             reads=[smallb], writes=[smallb])
        neglam = small[:, 4:5]
        subg = small[:, 5:6]

        class T:
            pass

        for g in range(NG):
            tgs = slice(g * TG, (g + 1) * TG)
            self.norm_group(nidx, g, lambda c: (hn_g[:, c, :], [hngb[c]]))
            P.op("sp", lambda e, tgs=tgs: e.dma_start(out=ropet[:], in_=self.d_rope[:, :, tgs]), writes=[ropeb], dsem="rope")
            chunks = []
            for kind in range(2):
                for nb in range(4):
                    for half in range(2):
                        chunks.append((kind, nb, half))
            slots = {}

            def rp_first(ci):
                kind, nb, half = chunks[ci]
                if half == 0:
                    slots[kind, nb] = self.ws.acquire(("df", kind * 4 + nb, g))
                slot, sbuf_ = slots[kind, nb]
                pair = ci % 2
                banks = (6, 7) if pair == 0 else (2, 3)
                wv = slot[:, 0:2048].rearrange("p (c m) -> p c m", c=NCH)
                ps1, pb1 = self.psum[banks[0]]
                for c in range(NCH):
                    P.op("pe", lambda e, c=c: e.matmul(
                        ps1[:], lhsT=wv[:, c, half * 128:(half + 1) * 128], rhs=hn_g[:, c, :],
                        start=(c == 0), stop=(c == NCH - 1)),
                        reads=[sbuf_, hngb[c]], writes=[pb1])
                qb, qbb = self.SCRB[:, pair * 512:(pair + 1) * 512], self.scrbb[pair]
                P.op("act", lambda e: e.activation(out=qb, in_=ps1[:], func=AF.Copy), reads=[pb1], writes=[qbb])

            def rp_second(ci):
                kind, nb, half = chunks[ci]
                m = nb * 2 + half
                pair = ci % 2
                banks = (6, 7) if pair == 0 else (2, 3)
                ps1, pb1 = self.psum[banks[0]]
                ps2, pb2 = self.psum[banks[1]]
                qb, qbb = self.SCRB[:, pair * 512:(pair + 1) * 512], self.scrbb[pair]
                P.op("pe", lambda e: e.matmul(ps2[:], lhsT=permm, rhs=qb, start=True, stop=True),
                     reads=[qbb, cstb], writes=[pb2])
                (t1, t1b), (t2, t2b) = scr(pair * 2), scr(pair * 2 + 1)
                P.op("dve", lambda e: e.tensor_tensor(out=t1, in0=ps1[:], in1=ropet[:, 0, :], op=ALU.mult),
                     reads=[pb1, ropeb, qbb], writes=[t1b])
                P.op("dve", lambda e: e.tensor_tensor(out=t2, in0=ps2[:], in1=ropet[:, 1, :], op=ALU.mult),
                     reads=[pb2, ropeb], writes=[t2b])
                if kind == 0:
                    for hf in range(2):
                        P.op("dve", lambda e, hf=hf: e.tensor_tensor(
                            out=q_m[hf][hf * 64:(hf + 1) * 64, m, :], in0=t1[hf * 64:(hf + 1) * 64, :],
                            in1=t2[hf * 64:(hf + 1) * 64, :], op=ALU.add),
                            reads=[t1b, t2b], writes=[qgb[hf][m]])
                else:
                    tg_l = tgs
                    P.op("dve", lambda e: e.tensor_tensor(out=kT[:, m, tg_l], in0=t1, in1=t2, op=ALU.add),
                         reads=[t1b, t2b], writes=[kb[m][g]])

            for ci in range(len(chunks) + 1):
                if ci < len(chunks):
                    rp_first(ci)
                if ci >= 1:
                    rp_second(ci - 1)
            for nb in range(4):
                slot, sbuf_ = self.ws.acquire(("df", 8 + nb, g))
                wv = slot[:, 0:2048].rearrange("p (c m) -> p c m", c=NCH)
                for tt in range(4):
                    ti = g * 4 + tt
                    ps, pb = self.psum[6 + self.flip("norm")]
                    for c in range(NCH):
                        P.op("pe", lambda e, ps=ps, c=c, wv=wv, tt=tt: e.matmul(
                            ps[:, 0:256], lhsT=hn_g[:, c, tt * 128:(tt + 1) * 128], rhs=wv[:, c, :],
                            start=(c == 0), stop=(c == NCH - 1)),
                            reads=[sbuf_, hngb[c]], writes=[pb])
                    P.op("dve", lambda e, ps=ps, ti=ti, nb=nb: e.tensor_copy(out=vT[:, ti, nb * 256:(nb + 1) * 256], in_=ps[:, 0:256]),
                         reads=[pb], writes=[vb[ti][nb]])
            tasks = []
            for h in range(NCH):
                for mp in range(2):
                    ntl = 4 * g + 4
                    for i in range(ntl):
                        t = T()
                        t.h, t.mp, t.p0, t.i = h, mp, 64 * mp, i
                        t.qoff = max(0, i - 4 * g) * 128
                        t.diag = i >= 4 * g
                        t.first = i == 0
                        t.last = i == ntl - 1
                        t.n = len(tasks)
                        t.par = (h * 2 + mp) % 2
                        tasks.append(t)

            def stA(t):
                t.s_ps, t.sb_ = self.psum[(0, 1, 6)[t.n % 3]]
                qo = t.qoff
                P.op("pe", lambda e: e.matmul(
                    t.s_ps[:, qo:512], lhsT=kT[:, t.h, t.i * 128:(t.i + 1) * 128],
                    rhs=q_m[t.mp][:, t.h, qo:512], start=True, stop=True),
                    reads=[kb[t.h][t.i // 4], qgb[t.mp][t.h]], writes=[t.sb_])

            def stB(t):
                t.pt, t.ptb = p_t[t.n % NP]
                qo = t.qoff
                P.op("act", lambda e: e.activation(out=t.pt[:, qo:512], in_=t.s_ps[:, qo:512], func=AF.Exp, scale=0.125),
                     reads=[t.sb_], writes=[t.ptb])
                if t.diag:
                    P.op("dve", lambda e: e.memset(t.pt[64:128, qo:qo + 64], 0.0), reads=[], writes=[t.ptb])

            def stC(t):
                o_ps, opb = self.psum[4 + t.par]
                den_ps, denb = self.psum[2 + t.par]
                qo = t.qoff
                P.op("pe", lambda e: e.matmul(den_ps[:, qo:512], lhsT=ones, rhs=t.pt[:, qo:512], start=t.first, stop=t.last),
                     reads=[t.ptb, cstb], writes=[denb])
                P.op("pe", lambda e: e.matmul(o_ps[:, qo:512], lhsT=vT[:, t.i, t.h * 128:(t.h + 1) * 128], rhs=t.pt[:, qo:512],
                                              start=t.first, stop=t.last),
                     reads=[t.ptb, vb[t.i][t.h // 2]], writes=[opb])

            def stF2(t):
                if not t.last:
                    return
                den_ps, denb = self.psum[2 + t.par]
                rden, rdenb = scr(2)
                P.op("act", lambda e: e.activation(out=rden, in_=den_ps[:], func=AF.Ln), reads=[denb], writes=[rdenb])
                P.op("act", lambda e: e.activation(out=rden, in_=rden, func=AF.Exp, scale=-1.0), reads=[rdenb], writes=[rdenb])

            def stF3(t):
                if not t.last:
                    return
                o_ps, opb = self.psum[4 + t.par]
                (rden, rdenb), (o1n, o1nb), (od, odb) = scr(2), scr(0), scr(1)
                if t.mp == 0:
                    P.op("dve", lambda e: e.tensor_tensor(out=o1n, in0=o_ps[:], in1=rden, op=ALU.mult),
                         reads=[opb, rdenb], writes=[o1nb])
                else:
                    P.op("dve", lambda e: e.tensor_tensor(out=od, in0=o_ps[:], in1=rden, op=ALU.mult),
                         reads=[opb, rdenb], writes=[odb])
                    P.op("dve", lambda e: e.scalar_tensor_tensor(out=od, in0=od, scalar=neglam, in1=o1n, op0=ALU.mult, op1=ALU.add),
                         reads=[odb, o1nb, smallb], writes=[odb])

            def stF4(t):
                if not (t.last and t.mp == 1):
                    return
                od, odb = scr(1)
                P.op("act", lambda e: e.activation(out=sqo, in_=od, func=AF.Square), reads=[odb], writes=[sqob])
                t.ss = self.psum[7]
                P.op("pe", lambda e: e.matmul(t.ss[0][:], lhsT=ones, rhs=sqo, start=True, stop=True),
                     reads=[sqob, cstb], writes=[t.ss[1]])

            def stF5(t):
                if not (t.last and t.mp == 1):
                    return
                rs, rsb = scr(4)
                P.op("act", lambda e: e.activation(out=rs, in_=t.ss[0][:], func=AF.Ln, scale=1.0 / 128.0, bias=RMS_EPS),
                     reads=[t.ss[1]], writes=[rsb])
                P.op("act", lambda e: e.activation(out=rs, in_=rs, func=AF.Exp, scale=-0.5), reads=[rsb], writes=[rsb])

            def stF6(t):
                if not (t.last and t.mp == 1):
                    return
                (od, odb), (rs, rsb) = scr(1), scr(4)
                h = t.h
                P.op("dve", lambda e: e.scalar_tensor_tensor(out=o_g[:, h, :], in0=od, scalar=subg, in1=rs, op0=ALU.mult, op1=ALU.mult),
                     reads=[odb, rsb, smallb], writes=[ogb[h]])

            sched = self.df_sched
            n = len(tasks)
            maxlag = max(l for (_, l) in sched)
            fns = {"A": stA, "B": stB, "C": stC, "F2": stF2, "F3": stF3, "F4": stF4, "F5": stF5, "F6": stF6}
            for s in range(n + maxlag):
                for (name, lag) in sched:
                    k = s - lag
                    if 0 <= k < n:
                        fns[name](tasks[k])
            dbg = getattr(self, "dbg", None)
            if dbg:
                srcs = {"o": (o_g, ogb), "hn": (hn_g, hngb)}
                if dbg == "k":
                    for m in range(NCH):
                        P.op("dve", lambda e, m=m, tgs=tgs: e.tensor_copy(out=xT[:, m, tgs], in_=kT[:, m, tgs]),
                             reads=[kb[m][g], self.xb[m][g]], writes=[self.xb[m][g]])
                elif dbg == "v":
                    for m in range(NCH):
                        P.op("dve", lambda e, m=m, tgs=tgs: e.tensor_copy(out=xT[:, m, tgs], in_=vT[:, g * 4 + m // 2, (m % 2) * 512:(m % 2) * 512 + 512]),
                             reads=[vb[g * 4 + m // 2][nb] for nb in range(4)] + [self.xb[m][g]], writes=[self.xb[m][g]])
                else:
                    s3, sb3 = srcs[dbg]
                    for m in range(NCH):
                        P.op("dve", lambda e, m=m, tgs=tgs, s3=s3: e.tensor_copy(out=xT[:, m, tgs], in_=s3[:, m, :]),
                             reads=[sb3[m], self.xb[m][g]], writes=[self.xb[m][g]])
                for nb in range(4):
                    self.ws.acquire(("df", 12 + nb, g))
                continue
            for nb in range(4):
                slot, sbuf_ = self.ws.acquire(("df", 12 + nb, g))
                for half in range(2):
                    c = nb * 2 + half

                    def evac(ps, pb, c=c, tgs=tgs, g=g):
                        P.op("dve", lambda e: e.tensor_tensor(
                            out=xT[:, c, tgs], in0=ps[:], in1=xT[:, c, tgs], op=ALU.add),
                            reads=[pb, self.xb[c][g]], writes=[self.xb[c][g]])
                    self.proj_fm(slot, sbuf_, half, o_g, ogb, evac)


def _consts():
    c = np.zeros((128, 640), np.float32)
    c[:, 0:128] = np.eye(128, dtype=np.float32)
    j = np.arange(128)[:, None]
    k = np.arange(128)[None, :]
    c[:, 128:256] = (j >= k).astype(np.float32)
    c[:, 256:384] = np.where(j >= k, MASKNEG, 0.0)
    c[:, 384:512] = 1.0
    p = np.arange(128)
    c[(p // 64) * 64 + (p % 64 + 32) % 64, 512 + p] = 1.0
    return c


def _layout_weights(inputs):
    w_in = np.asarray(inputs["ffn_w_in"], np.float32).reshape(4, NCH, 128, 2, NF, 128)
    w_in = np.ascontiguousarray(w_in.transpose(0, 4, 2, 3, 1, 5)).reshape(4, NF, 128, 2048)
    w_out = np.asarray(inputs["ffn_w_out"], np.float32).reshape(4, NF, 128, NCH, 128)
    w_out = np.ascontiguousarray(w_out.transpose(0, 3, 2, 1, 4)).reshape(4, NCH, 128, D_FF)

    def colblocks(w, ncols):
        nb = ncols // 256
        a = w.reshape(NCH, 128, nb, 256).transpose(2, 1, 0, 3)
        return np.ascontiguousarray(a).reshape(nb, 128, 2048)
    sbq = np.asarray(inputs["sb_w_qkv"], np.float32)[0]
    sbo = np.asarray(inputs["sb_w_o"], np.float32)[0]
    w_sb = np.concatenate([colblocks(sbq, 3072), colblocks(sbo, 1024)], axis=0)
    dq = np.asarray(inputs["diff_w_qkv"], np.float32)[0]
    do = np.asarray(inputs["diff_w_o"], np.float32)[0]
    w_df = np.concatenate([colblocks(dq, 3072), colblocks(do, 1024)], axis=0)
    return w_in, w_out, w_sb, w_df


def _rope_tables():
    p = np.arange(128)
    i = p % 32
    inv = 10000.0 ** (-(i.astype(np.float32)) / 32.0)
    ang = np.arange(SEQ, dtype=np.float32)[None, :] * inv[:, None].astype(np.float32)
    cos = np.cos(ang).astype(np.float32)
    sin = np.sin(ang).astype(np.float32)
    sgn = np.where((p % 64) < 32, -1.0, 1.0).astype(np.float32)[:, None]
    return np.ascontiguousarray(np.stack([cos, sin * sgn], axis=1))


FULL_STAGES = [("ffn", 0, 0), ("sb", 1), ("ffn", 1, 2), ("ffn", 2, 3), ("diff", 4), ("ffn", 3, 5), ("final", 6)]
_CACHE = {}
_RUN_KW = {}


def _prepare(inputs):
    w_in, w_out, w_sb, w_df = _layout_weights(inputs)
    ng = np.asarray(inputs["norm_gains"], np.float32).reshape(6, NCH, 128)
    fg = np.asarray(inputs["final_gain"], np.float32).reshape(1, NCH, 128)
    gains = np.ascontiguousarray(np.concatenate([ng, fg], 0).transpose(2, 0, 1)).reshape(128, 56)
    lam = np.ascontiguousarray(np.broadcast_to(np.asarray(inputs["diff_lambda"], np.float32).reshape(1, 256), (128, 256)))
    subln = np.ascontiguousarray(np.asarray(inputs["diff_subln"], np.float32).reshape(128, 1))
    shared = {"gains": gains, "consts": _consts(), "w_in": w_in, "w_out": w_out, "w_sb": w_sb, "w_df": w_df,
              "rope": _rope_tables(), "lam": lam, "subln": subln}
    return shared


def _x_to_dev(xb):
    return np.ascontiguousarray(xb.T.reshape(NCH, 128, SEQ).transpose(1, 0, 2))


def _x_from_dev(y):
    return np.ascontiguousarray(y.transpose(1, 0, 2).reshape(D_MODEL, SEQ).T)


def run_stages(stages, xs, shared, core_ids=None):
    key = tuple(stages)
    if key not in _CACHE:
        _CACHE[key] = Builder(list(stages)).build()
    nc = _CACHE[key]
    n = len(xs)
    in_maps = [dict(shared, xT=_x_to_dev(np.asarray(x, np.float32))) for x in xs]
    res = run_bass_kernel_spmd(nc, in_maps, core_ids=list(range(n)), **_RUN_KW)
    if res.exec_time_ns is not None:
        print("exec_time_ns", res.exec_time_ns)
    return [_x_from_dev(r["outT"]) for r in res.results]


def kernel(**inputs):
    shared = _prepare(inputs)
    x = np.asarray(inputs["x"], np.float32)
    outs = run_stages(FULL_STAGES, [x[b] for b in range(x.shape[0])], shared)
    return np.stack(outs, axis=0).astype(np.float32)
```

```python
import math
from contextlib import ExitStack

import numpy as np
import concourse.bass as bass
import concourse.mybir as mybir
from concourse.bass_utils import run_bass_kernel_spmd

F32 = mybir.dt.float32
BF16 = mybir.dt.bfloat16
AF = mybir.ActivationFunctionType
ALU = mybir.AluOpType

ENGS = ("pe", "act", "dve", "pool", "sp")

D_MODEL = 1024
SEQ = 2048
NCH = 8
D_FF = 2816
NF = 22
TG = 512
NG = 4
RMS_EPS = 1e-6
LAMBDA_INIT = 0.8 - 0.6 * math.exp(-0.3 * (2 - 1))
NSLOT = 5
MASKNEG = -30000.0


class Buf:
    __slots__ = ("name", "last_w", "readers")

    def __init__(self, name):
        self.name = name
        self.last_w = None
        self.readers = []


class Op:
    __slots__ = ("eng", "fn", "deps", "needed", "ticket", "dsem", "idx", "is_dma")

    def __init__(self, eng, fn, dsem=None):
        self.eng = eng
        self.fn = fn
        self.deps = []
        self.needed = False
        self.ticket = None
        self.dsem = dsem
        self.is_dma = dsem is not None
        self.idx = None


class Prog:
    def __init__(self):
        self.ops = {e: [] for e in ENGS}
        self.n = 0
        self.dma_keys = []
        self.last_dma = {}

    def op(self, eng, fn, reads=(), writes=(), dsem=None, extra_deps=()):
        o = Op(eng, fn, dsem)
        o.idx = self.n
        self.n += 1
        if dsem is not None:
            if dsem not in self.dma_keys:
                self.dma_keys.append(dsem)
        deps = {}
        for b in reads:
            if b.last_w is not None:
                deps[id(b.last_w)] = b.last_w
        for b in writes:
            if b.last_w is not None:
                deps[id(b.last_w)] = b.last_w
            for r in b.readers:
                deps[id(r)] = r
        for d in extra_deps:
            deps[id(d)] = d
        if dsem is not None and dsem in self.last_dma:
            d = self.last_dma[dsem]
            deps[id(d)] = d
        for d in deps.values():
            if d.eng == "pe" and eng == "pe" and not d.is_dma and not o.is_dma:
                continue
            o.deps.append(d)
            d.needed = True
        for b in reads:
            b.readers.append(o)
        for b in writes:
            b.last_w = o
            b.readers = []
        if dsem is not None:
            self.last_dma[dsem] = o
        self.ops[eng].append(o)
        return o

    def barrier(self):
        lasts = []
        for e in ENGS:
            for o in reversed(self.ops[e]):
                if not o.is_dma and o.fn is not None:
                    lasts.append(o)
                    break
        lasts.extend(self.last_dma.values())
        for e in ENGS:
            self.op(e, None, extra_deps=[d for d in lasts])

    def assign(self):
        cnt = {e: 0 for e in ENGS}
        dcnt = {k: 0 for k in self.dma_keys}
        allops = []
        for e in ENGS:
            allops.extend(self.ops[e])
        allops.sort(key=lambda o: o.idx)
        for o in allops:
            if o.is_dma:
                dcnt[o.dsem] += 16
                o.ticket = dcnt[o.dsem]
            elif o.needed and o.fn is not None:
                cnt[o.eng] += 1
                o.ticket = cnt[o.eng]

    def run_engine(self, eng, e, sems, dsems):
        waited = {}
        for o in self.ops[eng]:
            for d in o.deps:
                if d.ticket is None:
                    continue
                key = ("d", d.dsem) if d.is_dma else ("e", d.eng)
                if waited.get(key, 0) >= d.ticket:
                    continue
                waited[key] = d.ticket
                s = dsems[d.dsem] if d.is_dma else sems[d.eng]
                e.wait_ge(s, d.ticket)
            if o.fn is None:
                continue
            ins = o.fn(e)
            if o.is_dma:
                ins.then_inc(dsems[o.dsem], 16)
            elif o.needed:
                ins.then_inc(sems[o.eng], 1)


class WStream:
    def __init__(self, P, ring, nslot):
        self.P = P
        self.ring = ring
        self.nslot = nslot
        self.blocks = []
        self.issued = 0
        self.next = 0

    def plan(self, tag, src, n):
        self.blocks.append((tag, src, n))

    def _issue(self, k):
        tag, src, n = self.blocks[k]
        tile, buf = self.ring[k % self.nslot]
        s = k % self.nslot

        def fn(e, tile=tile, src=src, n=n):
            if n % 2 == 0 and n > 1024:
                return e.dma_start(out=tile[:, 0:n].rearrange("p (a b) -> p a b", a=2),
                                   in_=src.rearrange("p (a b) -> p a b", a=2))
            return e.dma_start(out=tile[:, 0:n], in_=src)
        self.P.op("pool", fn, writes=[buf], dsem=f"w{s}")

    def acquire(self, tag):
        k = self.next
        self.next += 1
        assert self.blocks[k][0] == tag, (self.blocks[k][0], tag)
        while self.issued < min(len(self.blocks), k + self.nslot - 1):
            self._issue(self.issued)
            self.issued += 1
        return self.ring[k % self.nslot]


class Builder:
    def __init__(self, stages):
        self.stages = stages
        nc = bass.Bass("TRN2", target_bir_lowering=False)
        self.nc = nc
        self.P = Prog()
        d = lambda name, shape, kind="ExternalInput": nc.dram_tensor(name, shape, F32, kind=kind).ap()
        self.d_x = d("xT", [128, NCH, SEQ])
        self.d_gains = d("gains", [128, 56])
        self.d_consts = d("consts", [128, 640])
        self.d_win = d("w_in", [4, NF, 128, 2048])
        self.d_wout = d("w_out", [4, NCH, 128, D_FF])
        self.d_wsb = d("w_sb", [16, 128, 2048])
        self.d_wdf = d("w_df", [16, 128, 2048])
        self.d_rope = d("rope", [128, 2, SEQ])
        self.d_lam = d("lam", [128, 256])
        self.d_subln = d("subln", [128, 1])
        self.d_out = d("outT", [128, NCH, SEQ], kind="ExternalOutput")

    def sb(self, name, shape, dt):
        return self.es.enter_context(self.nc.sbuf_tensor(name, shape, dt))

    def build(self):
        nc, P = self.nc, self.P
        with ExitStack() as es:
            self.es = es
            self.xT = self.sb("xT_sb", [128, NCH, SEQ], F32)
            self.HN = self.sb("hn_sb", [128, NCH * SEQ], BF16)
            self.R = self.sb("r_sb", [128, 16 * SEQ], BF16)
            self.ringt = [self.sb(f"ring{i}", [128, 2048], BF16) for i in range(NSLOT)]
            self.SCR = self.sb("scr", [128, 3072], F32)
            self.SCRB = self.sb("scrb", [128, 2048], BF16)
            self.SCRC = self.sb("scrc", [128, 2048], BF16)
            self.cst = self.sb("cst", [128, 640], BF16)
            self.gains = self.sb("gains_sb", [128, 56], F32)
            self.lamt = self.sb("lam_sb", [128, 256], F32)
            self.small = self.sb("small_sb", [128, 8], F32)
            self.ropet = self.sb("rope_sb", [128, 2, TG], F32)
            self.psum = []
            for i in range(8):
                t = es.enter_context(nc.psum_tensor(f"ps{i}", [128, 512], F32))
                self.psum.append((t, Buf(f"ps{i}")))
            self.sems = {e: es.enter_context(nc.semaphore(f"s_{e}")) for e in ENGS}

            self.xb = [[Buf(f"x{c}_{g}") for g in range(NG)] for c in range(NCH)]
            self.ring = [(self.ringt[i], Buf(f"ring{i}")) for i in range(NSLOT)]
            self.scrb = [Buf(f"scr{i}") for i in range(6)]
            self.scrbb = [Buf(f"scrb{i}") for i in range(4)]
            self.cstb = Buf("cst")
            self.gainb = Buf("gains")
            self.lamb = Buf("lam")
            self.smallb = Buf("small")
            self.ropeb = Buf("rope")
            self.outb = [Buf(f"out{g}") for g in range(NG)]
            self.ws = WStream(P, self.ring, NSLOT)
            self.par = {}
            self.region_prev = {}
            self.region_cur = {}
            self.df_sched = [("C", 4), ("A", 0), ("B", 2), ("F2", 6), ("F3", 7), ("F4", 9), ("F4b", 10), ("F5", 11), ("F6", 12)]
            self.sb_sched = [("A", 0), ("B1", 1), ("D1", 3), ("B2", 1), ("C", 2), ("D2", 3), ("E", 4), ("F", 6)]

            self.plan_weights()
            self.emit_all()

            P.assign()
            dsems = {k: es.enter_context(nc.semaphore(f"d_{k}")) for k in P.dma_keys}
            sems = self.sems
            with nc.Block() as block:
                @block.sync
                def _(e):
                    P.run_engine("sp", e, sems, dsems)

                @block.gpsimd
                def _(e):
                    P.run_engine("pool", e, sems, dsems)

                @block.tensor
                def _(e):
                    P.run_engine("pe", e, sems, dsems)

                @block.scalar
                def _(e):
                    P.run_engine("act", e, sems, dsems)

                @block.vector
                def _(e):
                    P.run_engine("dve", e, sems, dsems)
        return nc

    def nb(self, name, region):
        b = Buf(name)
        b.readers = list(self.region_prev.get(region, []))
        self.region_cur.setdefault(region, []).append(b)
        return b

    def stage_end(self):
        for region, bufs in self.region_cur.items():
            last = {}
            for b in bufs:
                for o in ([b.last_w] if b.last_w is not None else []) + list(b.readers):
                    key = ("d", o.dsem) if o.is_dma else ("e", o.eng)
                    if key not in last or last[key].idx < o.idx:
                        last[key] = o
            self.region_prev[region] = list(last.values())
        self.region_cur = {}

    def flip(self, key, n=2):
        v = self.par.get(key, 0)
        self.par[key] = (v + 1) % n
        return v

    def plan_weights(self):
        ws = self.ws
        for st in self.stages:
            kind = st[0]
            if kind == "ffn":
                widx = st[1]
                for (j0, j1) in ((0, 11), (11, 22)):
                    for j in range(j0, j1):
                        ws.plan(("win", widx, j), self.d_win[widx, j], 2048)
                    for c in range(NCH):
                        ws.plan(("wout", widx, j0, c), self.d_wout[widx, c, :, j0 * 128:j1 * 128], (j1 - j0) * 128)
            elif kind == "sb":
                for blk in range(4, 12):
                    ws.plan(("sb", blk, 0), self.d_wsb[blk], 2048)
                for g in range(NG):
                    for blk in range(0, 4):
                        ws.plan(("sb", blk, g), self.d_wsb[blk], 2048)
                    if g + 1 < NG:
                        for blk in range(4, 12):
                            ws.plan(("sb", blk, g + 1), self.d_wsb[blk], 2048)
                    for blk in range(12, 16):
                        ws.plan(("sb", blk, g), self.d_wsb[blk], 2048)
            elif kind == "diff":
                for g in range(NG):
                    for blk in Builder.DF_ORDER:
                        ws.plan(("df", blk, g), self.d_wdf[blk], 2048)

    def emit_all(self):
        P = self.P
        P.op("pool", lambda e: e.dma_start(out=self.cst[:], in_=self.d_consts), writes=[self.cstb], dsem="cst")
        P.op("sp", lambda e: e.dma_start(out=self.gains[:], in_=self.d_gains), writes=[self.gainb], dsem="gains")
        for g in range(NG):
            P.op("sp", lambda e, g=g: e.dma_start(out=self.xT[:, :, g * TG:(g + 1) * TG],
                                                  in_=self.d_x[:, :, g * TG:(g + 1) * TG]),
                 writes=[self.xb[c][g] for c in range(NCH)], dsem=f"x{g}")
        for st in self.stages:
            if st[0] == "ffn":
                self.ffn(st[1], st[2])
            elif st[0] == "sb":
                self.sb_mixer(st[1])
            elif st[0] == "diff":
                self.diff_mixer(st[1])
            elif st[0] == "final":
                self.final_norm(st[1])
            self.stage_end()
        for g in range(NG):
            P.op("sp", lambda e, g=g: e.dma_start(out=self.d_out[:, :, g * TG:(g + 1) * TG],
                                                  in_=self.xT[:, :, g * TG:(g + 1) * TG]),
                 reads=[self.xb[c][g] for c in range(NCH)], writes=[self.outb[g]], dsem=f"o{g}")
        P.op("sp", None, reads=self.outb)

    def norm_parts(self, nidx, g, dst_fn, scratch=None):
        P = self.P
        tgs = slice(g * TG, (g + 1) * TG)
        ones = self.cst[:, 384:512]
        xT = self.xT
        st = {}

        def p0():
            ps, psb = self.psum[6 + self.flip("norm")]
            st["ps"] = (ps, psb)
            for c in range(NCH):
                k = self.flip("sq")
                sq = self.SCRB[:, (2 + k) * 512:(3 + k) * 512]
                sqb = self.scrbb[2 + k]
                P.op("act", lambda e, c=c, sq=sq: e.activation(out=sq, in_=xT[:, c, tgs], func=AF.Square),
                     reads=[self.xb[c][g]], writes=[sqb])
                P.op("pe", lambda e, c=c, sq=sq: e.matmul(ps[:], lhsT=ones, rhs=sq, start=(c == 0), stop=(c == NCH - 1)),
                     reads=[sqb, self.cstb], writes=[psb])
        if scratch is None:
            rs, rsb = self.SCR[:, 5 * 512:6 * 512], self.scrb[5]
        else:
            rs, rsb = scratch

        def p1():
            ps, psb = st["ps"]
            P.op("act", lambda e: e.activation(out=rs, in_=ps[:], func=AF.Ln, scale=1.0 / D_MODEL, bias=RMS_EPS),
                 reads=[psb], writes=[rsb])
            P.op("act", lambda e: e.activation(out=rs, in_=rs, func=AF.Exp, scale=-0.5), reads=[rsb], writes=[rsb])

        def p2(c0):
            for c in range(c0, c0 + 2):
                out_ap, obufs = dst_fn(c)
                gcol = self.gains[:, nidx * 8 + c:nidx * 8 + c + 1]
                rd = [self.xb[c][g], rsb, self.gainb]
                P.op("dve", lambda e, c=c, out_ap=out_ap, gcol=gcol: e.scalar_tensor_tensor(
                    out=out_ap, in0=xT[:, c, tgs], scalar=gcol, in1=rs, op0=ALU.mult, op1=ALU.mult),
                    reads=rd, writes=obufs)
        return [p0, p1] + [(lambda c0=c0: p2(c0)) for c0 in range(0, NCH, 2)]

    def norm_group(self, nidx, g, dst_fn):
        for p in self.norm_parts(nidx, g, dst_fn):
            p()

    def ffn(self, widx, nidx):
        P = self.P
        xT = self.xT
        hn3 = self.HN[:].rearrange("p (c t) -> p c t", c=NCH)
        hnb = [[self.nb(f"hn{c}_{g}", "HN") for g in range(NG)] for c in range(NCH)]
        actT = self.R[:, 0:11 * SEQ].rearrange("p (j t) -> p j t", j=11)
        actb = [[self.nb(f"act{j}_{g}", "R") for g in range(NG)] for j in range(11)]
        for g in range(NG):
            tgs = slice(g * TG, (g + 1) * TG)
            self.norm_group(nidx, g, lambda c, tgs=tgs, g=g: (hn3[:, c, tgs], [hnb[c][g]]))
        for (j0, j1) in ((0, 11), (11, 22)):
            nj = j1 - j0
            for j in range(j0, j1):
                slot, sbuf_ = self.ws.acquire(("win", widx, j))
                wv = slot[:, 0:2048].rearrange("p (g c m) -> p g c m", g=2, c=NCH)
                jj = j - j0
                for g in range(NG):
                    tgs = slice(g * TG, (g + 1) * TG)
                    par = self.flip("gu")
                    gps, gpb = self.psum[par * 2]
                    ups, upb = self.psum[par * 2 + 1]
                    for gu, (ps, pb) in enumerate(((gps, gpb), (ups, upb))):
                        for c in range(NCH):
                            P.op("pe", lambda e, ps=ps, gu=gu, c=c, wv=wv, tgs=tgs: e.matmul(
                                ps[:], lhsT=wv[:, gu, c, :], rhs=hn3[:, c, tgs], start=(c == 0), stop=(c == NCH - 1)),
                                reads=[sbuf_, hnb[c][g]], writes=[pb])
                    sg = self.SCR[:, par * 512:(par + 1) * 512]
                    sgb = self.scrb[par]
                    P.op("act", lambda e, sg=sg, gps=gps: e.activation(out=sg, in_=gps[:], func=AF.Silu),
                         reads=[gpb], writes=[sgb])
                    P.op("dve", lambda e, sg=sg, ups=ups, jj=jj, tgs=tgs: e.tensor_tensor(
                        out=actT[:, jj, tgs], in0=sg, in1=ups[:], op=ALU.mult),
                        reads=[sgb, upb], writes=[actb[jj][g]])
            for c in range(NCH):
                slot, sbuf_ = self.ws.acquire(("wout", widx, j0, c))
                wv = slot[:, 0:nj * 128].rearrange("p (j m) -> p j m", m=128)
                for g in range(NG):
                    tgs = slice(g * TG, (g + 1) * TG)
                    ps, pb = self.psum[4 + self.flip("y")]
                    for jj in range(nj):
                        P.op("pe", lambda e, ps=ps, jj=jj, wv=wv, tgs=tgs: e.matmul(
                            ps[:], lhsT=wv[:, jj, :], rhs=actT[:, jj, tgs], start=(jj == 0), stop=(jj == nj - 1)),
                            reads=[sbuf_, actb[jj][g]], writes=[pb])
                    P.op("dve", lambda e, ps=ps, c=c, tgs=tgs: e.scalar_tensor_tensor(
                        out=xT[:, c, tgs], in0=ps[:], scalar=0.5, in1=xT[:, c, tgs], op0=ALU.mult, op1=ALU.add),
                        reads=[pb, self.xb[c][g]], writes=[self.xb[c][g]])

    def final_norm(self, nidx):
        xT = self.xT
        for g in range(NG):
            tgs = slice(g * TG, (g + 1) * TG)
            self.norm_group(nidx, g, lambda c, tgs=tgs, g=g: (xT[:, c, tgs], [self.xb[c][g]]))

    def proj_fm(self, slot, sbuf_, half, src3, srcb, evac):
        P = self.P
        wv = slot[:, 0:2048].rearrange("p (c m) -> p c m", c=NCH)
        ps, pb = self.psum[6 + self.flip("norm")]
        for c in range(NCH):
            P.op("pe", lambda e, ps=ps, c=c, wv=wv, half=half: e.matmul(
                ps[:], lhsT=wv[:, c, half * 128:(half + 1) * 128], rhs=src3[:, c, :],
                start=(c == 0), stop=(c == NCH - 1)),
                reads=[sbuf_, srcb[c]], writes=[pb])
        evac(ps, pb)

    def sb_mixer(self, nidx):
        P = self.P
        xT = self.xT
        kT = self.R[:, 0:NCH * SEQ].rearrange("p (m t) -> p m t", m=NCH)
        vT = self.R[:, NCH * SEQ:16 * SEQ].rearrange("p (i f) -> p i f", i=16)
        kb = [[self.nb(f"k{m}_{g}", "R") for g in range(NG)] for m in range(NCH)]
        vb = [[self.nb(f"v{i}_{nb}", "R") for nb in range(4)] for i in range(16)]
        hn_g = self.HN[:, 0:4096].rearrange("p (c t) -> p c t", c=NCH)
        q_m = [self.HN[:, 4096:8192].rearrange("p (c t) -> p c t", c=NCH),
               self.HN[:, 8192:12288].rearrange("p (c t) -> p c t", c=NCH)]
        o_g = self.HN[:, 12288:16384].rearrange("p (c t) -> p c t", c=NCH)
        hngb = [self.nb(f"hng{c}", "HN") for c in range(NCH)]
        qgb = [[self.nb(f"qg{hf}_{c}", "HN") for c in range(NCH)] for hf in range(2)]
        ogb = [self.nb(f"og{c}", "HN") for c in range(NCH)]
        P.op("dve", lambda e: e.memset(q_m[0][64:128, :, :], 0.0), writes=qgb[0])
        P.op("dve", lambda e: e.memset(q_m[1][0:64, :, :], 0.0), writes=qgb[1])
        ident = self.cst[:, 0:128]
        tinc = self.cst[:, 128:256]
        maskneg = self.cst[:, 256:384]
        ones = self.cst[:, 384:512]
        cstb = self.cstb
        NE = 4
        e_t = [(self.SCR[:, k * 512:(k + 1) * 512], self.scrb[k]) for k in range(4)]
        ecs_t = [(self.SCR[:, (4 + k) * 512:(5 + k) * 512], self.scrb[4 + k]) for k in range(2)]
        sp_t = [(self.SCRB[:, k * 512:(k + 1) * 512], self.scrbb[k]) for k in range(2)]
        aT_t = [(self.SCRC[:, k * 512:(k + 1) * 512], self.nb(f"aT{k}", "SCRC")) for k in range(2)]
        R_t = [(self.SCRC[:, 1024 + k * 512:1024 + (k + 1) * 512], self.nb(f"R{k}", "SCRC")) for k in range(2)]

        class T:
            pass

        def kv_timeline(g):
            tgs = slice(g * TG, (g + 1) * TG)
            st = {}
            tl = []
            nparts = self.norm_parts(nidx, g, lambda c: (hn_g[:, c, :], [hngb[c]]),
                                     scratch=(self.ropet[:, 0, :], self.ropeb))
            for j, p in enumerate(nparts):
                tl.append((2 * j if j < 2 else 3 + j, p))
            t0 = 3 + len(nparts) + 1
            units = []
            for nb in range(4):
                for half in range(2):
                    units.append(("k", nb, half))
            for nb in range(4):
                for tt in range(4):
                    units.append(("v", nb, tt))
            for ui, (kind, nb, sub) in enumerate(units):
                ctx = {}
                start = t0 + 3 * ui

                def mm(j, kind=kind, nb=nb, sub=sub, ctx=ctx):
                    if j == 0:
                        if sub == 0:
                            st[kind, nb] = self.ws.acquire(("sb", (4 if kind == "k" else 8) + nb, g))
                        ctx["ps"] = self.psum[6 + self.flip("norm")]
                    slot, sbuf_ = st[kind, nb]
                    wv = slot[:, 0:2048].rearrange("p (c m) -> p c m", c=NCH)
                    ps, pb = ctx["ps"]
                    for c in (2 * j, 2 * j + 1):
                        if kind == "k":
                            P.op("pe", lambda e, c=c: e.matmul(
                                ps[:], lhsT=wv[:, c, sub * 128:(sub + 1) * 128], rhs=hn_g[:, c, :],
                                start=(c == 0), stop=(c == NCH - 1)),
                                reads=[sbuf_, hngb[c]], writes=[pb])
                        else:
                            P.op("pe", lambda e, c=c: e.matmul(
                                ps[:, 0:256], lhsT=hn_g[:, c, sub * 128:(sub + 1) * 128], rhs=wv[:, c, :],
                                start=(c == 0), stop=(c == NCH - 1)),
                                reads=[sbuf_, hngb[c]], writes=[pb])

                def ev(kind=kind, nb=nb, sub=sub, ctx=ctx):
                    ps, pb = ctx["ps"]
                    if kind == "k":
                        m = nb * 2 + sub
                        P.op("dve", lambda e: e.tensor_copy(out=kT[:, m, tgs], in_=ps[:]),
                             reads=[pb], writes=[kb[m][g]])
                    else:
                        ti = g * 4 + sub
                        P.op("dve", lambda e: e.tensor_copy(out=vT[:, ti, nb * 256:(nb + 1) * 256], in_=ps[:, 0:256]),
                             reads=[pb], writes=[vb[ti][nb]])
                for j in range(4):
                    tl.append((start + j, (lambda j=j, mm=mm: mm(j))))
                tl.append((start + 5, ev))
            tl = [(off, i, fn) for i, (off, fn) in enumerate(tl)]
            tl.sort(key=lambda x: (x[0], x[1]))
            return tl

        for (_, _, fn) in kv_timeline(0):
            fn()
        for g in range(NG):
            tgs = slice(g * TG, (g + 1) * TG)
            for nb in range(4):
                slot, sbuf_ = self.ws.acquire(("sb", nb, g))
                for half in range(2):
                    m = nb * 2 + half

                    def evac(ps, pb, m=m):
                        for hf in range(2):
                            P.op("dve", lambda e, hf=hf: e.tensor_scalar(out=q_m[hf][hf * 64:(hf + 1) * 64, m, :],
                                                                         in0=ps[hf * 64:(hf + 1) * 64, :], scalar1=0.125, scalar2=None, op0=ALU.mult),
                                 reads=[pb], writes=[qgb[hf][m]])
                    self.proj_fm(slot, sbuf_, half, hn_g, hngb, evac)
            side = kv_timeline(g + 1) if g + 1 < NG else []
            tasks = []
            for m in range(NCH):
                for half in range(2):
                    tiles = list(range(4 * g + 3, -1, -1))
                    for idx, i in enumerate(tiles):
                        t = T()
                        t.m, t.half, t.h, t.p0, t.i, t.idx = m, half, 2 * m + half, 64 * half, i, idx
                        t.qoff = max(0, i - 4 * g) * 128
                        t.diag = i >= 4 * g
                        t.first = idx == 0
                        t.last = idx == len(tiles) - 1
                        t.nqoff = 0 if t.last else max(0, tiles[idx + 1] - 4 * g) * 128
                        t.pqoff = t.qoff if idx == 0 else max(0, tiles[idx - 1] - 4 * g) * 128
                        t.n = len(tasks)
                        tasks.append(t)
            o_cur = {}

            def stA(t):
                t.z_ps, t.zb = self.psum[t.n % 2]
                qo = t.qoff
                P.op("pe", lambda e: e.matmul(
                    t.z_ps[:, qo:512], lhsT=kT[:, t.m, t.i * 128:(t.i + 1) * 128],
                    rhs=q_m[t.half][:, t.m, qo:512], start=True, stop=(not t.diag)),
                    reads=[kb[t.m][t.i // 4], qgb[t.half][t.m]], writes=[t.zb])
                if t.diag:
                    P.op("pe", lambda e: e.matmul(
                        t.z_ps[:, qo:qo + 128], lhsT=ident, rhs=maskneg, start=False, stop=True),
                        reads=[cstb], writes=[t.zb])
                for _f in range(getattr(self, "fill", 0)):
                    fps, fpb = self.psum[6 + (_f % 2)]
                    P.op("pe", lambda e, fps=fps: e.matmul(fps[:], lhsT=ones, rhs=self.cst[:, 0:512], start=True, stop=True),
                         reads=[cstb], writes=[fpb])

            def stB1(t):
                t.et, t.eb = e_t[t.n % NE]
                qo = t.qoff
                P.op("act", lambda e: e.activation(out=t.et[:, qo:512], in_=t.z_ps[:, qo:512], func=AF.Exp),
                     reads=[t.zb], writes=[t.eb])

            def stB2(t):
                t.spt, t.spb = sp_t[t.n % 2]
                qo = t.qoff
                P.op("act", lambda e: e.activation(out=t.spt[:, qo:512], in_=t.et[:, qo:512], func=AF.Ln, bias=1.0),
                     reads=[t.eb], writes=[t.spb])

            def stC(t):
                t.cs_ps, t.csb = self.psum[2 + t.n % 2]
                qo = t.qoff
                P.op("pe", lambda e: e.matmul(
                    t.cs_ps[:, qo:512], lhsT=tinc, rhs=t.spt[:, qo:512], start=True, stop=t.first),
                    reads=[t.spb, cstb], writes=[t.csb])
                if not t.first:
                    rt, rb = t.rprev
                    P.op("pe", lambda e: e.matmul(
                        t.cs_ps[:, qo:512], lhsT=ones, rhs=rt[:, qo:512], start=False, stop=True),
                        reads=[rb, cstb], writes=[t.csb])
                if not t.last:
                    rn, rnb = R_t[t.n % 2]
                    if t.first:
                        P.op("dve", lambda e: e.tensor_copy(out=rn[:, qo:512], in_=t.spt[:, qo:512]),
                             reads=[t.spb], writes=[rnb])
                    else:
                        rt, rb = t.rprev
                        P.op("dve", lambda e: e.tensor_tensor(
                            out=rn[:, qo:512], in0=rt[:, qo:512], in1=t.spt[:, qo:512], op=ALU.add),
                            reads=[t.spb, rb], writes=[rnb])
                    if t.nqoff < qo:
                        P.op("dve", lambda e: e.memset(rn[:, t.nqoff:qo], 0.0), reads=[], writes=[rnb])
                    tasks[t.n + 1].rprev = (rn, rnb)

            def stD1(t):
                t.ect, t.ecb = ecs_t[t.n % 2]
                qo = t.qoff
                P.op("act", lambda e: e.activation(out=t.ect[:, qo:512], in_=t.cs_ps[:, qo:512], func=AF.Exp, scale=-1.0),
                     reads=[t.csb], writes=[t.ecb])

            def stD2(t):
                t.at, t.ab = aT_t[t.n % 2]
                qo = t.qoff
                P.op("dve", lambda e: e.tensor_tensor(
                    out=t.at[:, qo:512], in0=t.et[:, qo:512], in1=t.ect[:, qo:512], op=ALU.mult),
                    reads=[t.eb, t.ecb], writes=[t.ab])

            def stE(t):
                o_ps, opb = self.psum[4 + t.half]
                qo = t.qoff
                pq = t.pqoff
                rngs = [(qo, 512)] if (t.first or pq == qo) else [(pq, 512), (qo, pq)]
                for ri, (c0, c1) in enumerate(rngs):
                    P.op("pe", lambda e, c0=c0, c1=c1: e.matmul(
                        o_ps[:, c0:c1], lhsT=vT[:, t.i, t.m * 128:(t.m + 1) * 128], rhs=t.at[:, c0:c1],
                        start=t.first, stop=(t.last and ri == len(rngs) - 1)),
                        reads=[vb[t.i][t.m // 2], t.ab], writes=[opb])

            def stF(t):
                if t.last:
                    o_ps, opb = self.psum[4 + t.half]
                    m, p0 = t.m, t.p0
                    P.op("dve", lambda e: e.tensor_copy(out=o_g[p0:p0 + 64, m, :], in_=o_ps[p0:p0 + 64, :]),
                         reads=[opb], writes=[ogb[m]])

            sched = self.sb_sched
            n = len(tasks)
            maxlag = max(l for (_, l) in sched)
            fns = {"A": stA, "B1": stB1, "B2": stB2, "C": stC, "D1": stD1, "D2": stD2, "E": stE, "F": stF}
            for s in range(n + maxlag):
                for (name, lag) in sched:
                    k = s - lag
                    if 0 <= k < n:
                        fns[name](tasks[k])
                while side and side[0][0] + 6 <= s:
                    side.pop(0)[2]()
            while side:
                side.pop(0)[2]()
            for nb in range(4):
                slot, sbuf_ = self.ws.acquire(("sb", 12 + nb, g))
                for half in range(2):
                    c = nb * 2 + half

                    def evac(ps, pb, c=c, tgs=tgs, g=g):
                        P.op("dve", lambda e: e.tensor_tensor(
                            out=xT[:, c, tgs], in0=ps[:], in1=xT[:, c, tgs], op=ALU.add),
                            reads=[pb, self.xb[c][g]], writes=[self.xb[c][g]])
                    self.proj_fm(slot, sbuf_, half, o_g, ogb, evac)

    DF_ORDER = list(range(16))

    def diff_mixer(self, nidx):
        P = self.P
        xT = self.xT
        kT = self.R[:, 0:NCH * SEQ].rearrange("p (m t) -> p m t", m=NCH)
        vT = self.R[:, NCH * SEQ:16 * SEQ].rearrange("p (i f) -> p i f", i=16)
        kb = [[self.nb(f"k{m}_{g}", "R") for g in range(NG)] for m in range(NCH)]
        vb = [[self.nb(f"v{i}_{nb}", "R") for nb in range(4)] for i in range(16)]
        hn_g = self.HN[:, 0:4096].rearrange("p (c t) -> p c t", c=NCH)
        q_m = [self.HN[:, 4096:8192].rearrange("p (c t) -> p c t", c=NCH),
               self.HN[:, 8192:12288].rearrange("p (c t) -> p c t", c=NCH)]
        o_g = self.HN[:, 12288:16384].rearrange("p (c t) -> p c t", c=NCH)
        hngb = [self.nb(f"hng{c}", "HN") for c in range(NCH)]
        qgb = [[self.nb(f"qg{hf}_{c}", "HN") for c in range(NCH)] for hf in range(2)]
        ogb = [self.nb(f"og{c}", "HN") for c in range(NCH)]
        P.op("dve", lambda e: e.memset(q_m[0][64:128, :, :], 0.0), writes=qgb[0])
        P.op("dve", lambda e: e.memset(q_m[1][0:64, :, :], 0.0), writes=qgb[1])
        ones = self.cst[:, 384:512]
        permm = self.cst[:, 512:640]
        cstb = self.cstb
        NP = 4
        p_t = [(self.SCRC[:, k * 512:(k + 1) * 512], self.nb(f"pT{k}", "SCRC")) for k in range(NP)]
        sqo, sqob = self.SCRB[:, 0:512], self.scrbb[0]
        scr = lambda k: (self.SCR[:, k * 512:(k + 1) * 512], self.scrb[k])
        small, smallb = self.small, self.smallb
        ropet, ropeb = self.ropet, self.ropeb

        tmp, tmpb = scr(0)
        P.op("sp", lambda e: e.dma_start(out=self.lamt[:], in_=self.d_lam), writes=[self.lamb], dsem="lam")
        P.op("sp", lambda e: e.dma_start(out=small[:, 6:7], in_=self.d_subln), writes=[smallb], dsem="subln")
        for k in range(2):
            P.op("dve", lambda e, k=k: e.tensor_tensor(out=tmp[:, k * 64:(k + 1) * 64], in0=self.lamt[:, k * 128:k * 128 + 64],
                                                       in1=self.lamt[:, k * 128 + 64:k * 128 + 128], op=ALU.mult),
                 reads=[self.lamb], writes=[tmpb])
            P.op("dve", lambda e, k=k: e.reduce_sum(out=small[:, k:k + 1], in_=tmp[:, k * 64:(k + 1) * 64], axis=mybir.AxisListType.X),
                 reads=[tmpb], writes=[smallb])
        P.op("act", lambda e: e.activation(out=small[:, 2:4], in_=small[:, 0:2], func=AF.Exp), reads=[smallb], writes=[smallb])
        P.op("dve", lambda e: e.tensor_tensor(out=small[:, 4:5], in0=small[:, 3:4], in1=small[:, 2:3], op=ALU.subtract),
             reads=[smallb], writes=[smallb])
        P.op("dve", lambda e: e.tensor_scalar(out=small[:, 4:5], in0=small[:, 4:5], scalar1=-LAMBDA_INIT, scalar2=None, op0=ALU.add),
             reads=[smallb], writes=[smallb])
        P.op("dve", lambda e: e.tensor_scalar(out=small[:, 5:6], in0=small[:, 6:7], scalar1=(1.0 - LAMBDA_INIT), scalar2=None, op0=ALU.mult),
             reads=[smallb], writes=[smallb])
        neglam = small[:, 4:5]
        subg = small[:, 5:6]

        class T:
            pass

        for g in range(NG):
            tgs = slice(g * TG, (g + 1) * TG)
            self.norm_group(nidx, g, lambda c: (hn_g[:, c, :], [hngb[c]]))
            P.op("sp", lambda e, tgs=tgs: e.dma_start(out=ropet[:], in_=self.d_rope[:, :, tgs]), writes=[ropeb], dsem="rope")
            chunks = []
            for kind in range(2):
                for nb in range(4):
                    for half in range(2):
                        chunks.append((kind, nb, half))
            slots = {}

            def rp_first(ci):
                kind, nb, half = chunks[ci]
                if half == 0:
                    slots[kind, nb] = self.ws.acquire(("df", kind * 4 + nb, g))
                slot, sbuf_ = slots[kind, nb]
                pair = ci % 2
                banks = (6, 7) if pair == 0 else (2, 3)
                wv = slot[:, 0:2048].rearrange("p (c m) -> p c m", c=NCH)
                ps1, pb1 = self.psum[banks[0]]
                for c in range(NCH):
                    P.op("pe", lambda e, c=c: e.matmul(
                        ps1[:], lhsT=wv[:, c, half * 128:(half + 1) * 128], rhs=hn_g[:, c, :],
                        start=(c == 0), stop=(c == NCH - 1)),
                        reads=[sbuf_, hngb[c]], writes=[pb1])
                qb, qbb = self.SCRB[:, pair * 512:(pair + 1) * 512], self.scrbb[pair]
                P.op("act", lambda e: e.activation(out=qb, in_=ps1[:], func=AF.Copy), reads=[pb1], writes=[qbb])

            def rp_second(ci):
                kind, nb, half = chunks[ci]
                m = nb * 2 + half
                pair = ci % 2
                banks = (6, 7) if pair == 0 else (2, 3)
                ps1, pb1 = self.psum[banks[0]]
                ps2, pb2 = self.psum[banks[1]]
                qb, qbb = self.SCRB[:, pair * 512:(pair + 1) * 512], self.scrbb[pair]
                P.op("pe", lambda e: e.matmul(ps2[:], lhsT=permm, rhs=qb, start=True, stop=True),
                     reads=[qbb, cstb], writes=[pb2])
                (t1, t1b), (t2, t2b) = scr(pair * 2), scr(pair * 2 + 1)
                P.op("dve", lambda e: e.tensor_tensor(out=t1, in0=ps1[:], in1=ropet[:, 0, :], op=ALU.mult),
                     reads=[pb1, ropeb, qbb], writes=[t1b])
                P.op("dve", lambda e: e.tensor_tensor(out=t2, in0=ps2[:], in1=ropet[:, 1, :], op=ALU.mult),
                     reads=[pb2, ropeb], writes=[t2b])
                if kind == 0:
                    for hf in range(2):
                        P.op("dve", lambda e, hf=hf: e.tensor_tensor(
                            out=q_m[hf][hf * 64:(hf + 1) * 64, m, :], in0=t1[hf * 64:(hf + 1) * 64, :],
                            in1=t2[hf * 64:(hf + 1) * 64, :], op=ALU.add),
                            reads=[t1b, t2b], writes=[qgb[hf][m]])
                else:
                    tg_l = tgs
                    P.op("dve", lambda e: e.tensor_tensor(out=kT[:, m, tg_l], in0=t1, in1=t2, op=ALU.add),
                         reads=[t1b, t2b], writes=[kb[m][g]])

            for ci in range(len(chunks) + 1):
                if ci < len(chunks):
                    rp_first(ci)
                if ci >= 1:
                    rp_second(ci - 1)
            for nb in range(4):
                slot, sbuf_ = self.ws.acquire(("df", 8 + nb, g))
                wv = slot[:, 0:2048].rearrange("p (c m) -> p c m", c=NCH)
                for tt in range(4):
                    ti = g * 4 + tt
                    ps, pb = self.psum[6 + self.flip("norm")]
                    for c in range(NCH):
                        P.op("pe", lambda e, ps=ps, c=c, wv=wv, tt=tt: e.matmul(
                            ps[:, 0:256], lhsT=hn_g[:, c, tt * 128:(tt + 1) * 128], rhs=wv[:, c, :],
                            start=(c == 0), stop=(c == NCH - 1)),
                            reads=[sbuf_, hngb[c]], writes=[pb])
                    P.op("dve", lambda e, ps=ps, ti=ti, nb=nb: e.tensor_copy(out=vT[:, ti, nb * 256:(nb + 1) * 256], in_=ps[:, 0:256]),
                         reads=[pb], writes=[vb[ti][nb]])
            tasks = []
            for h in range(NCH):
                for mp in range(2):
                    ntl = 4 * g + 4
                    for i in range(ntl):
                        t = T()
                        t.h, t.mp, t.p0, t.i = h, mp, 64 * mp, i
                        t.qoff = max(0, i - 4 * g) * 128
                        t.diag = i >= 4 * g
                        t.first = i == 0
                        t.last = i == ntl - 1
                        t.n = len(tasks)
                        t.par = (h * 2 + mp) % 2
                        tasks.append(t)

            def stA(t):
                t.s_ps, t.sb_ = self.psum[(0, 1, 6)[t.n % 3]]
                qo = t.qoff
                P.op("pe", lambda e: e.matmul(
                    t.s_ps[:, qo:512], lhsT=kT[:, t.h, t.i * 128:(t.i + 1) * 128],
                    rhs=q_m[t.mp][:, t.h, qo:512], start=True, stop=True),
                    reads=[kb[t.h][t.i // 4], qgb[t.mp][t.h]], writes=[t.sb_])

            def stB(t):
                t.pt, t.ptb = p_t[t.n % NP]
                qo = t.qoff
                P.op("act", lambda e: e.activation(out=t.pt[:, qo:512], in_=t.s_ps[:, qo:512], func=AF.Exp, scale=0.125),
                     reads=[t.sb_], writes=[t.ptb])
                if t.diag:
                    P.op("dve", lambda e: e.memset(t.pt[64:128, qo:qo + 64], 0.0), reads=[], writes=[t.ptb])

            def stC(t):
                o_ps, opb = self.psum[4 + t.par]
                den_ps, denb = self.psum[2 + t.par]
                qo = t.qoff
                P.op("pe", lambda e: e.matmul(den_ps[:, qo:512], lhsT=ones, rhs=t.pt[:, qo:512], start=t.first, stop=t.last),
                     reads=[t.ptb, cstb], writes=[denb])
                P.op("pe", lambda e: e.matmul(o_ps[:, qo:512], lhsT=vT[:, t.i, t.h * 128:(t.h + 1) * 128], rhs=t.pt[:, qo:512],
                                              start=t.first, stop=t.last),
                     reads=[t.ptb, vb[t.i][t.h // 2]], writes=[opb])

            def stF2(t):
                if not t.last:
                    return
                den_ps, denb = self.psum[2 + t.par]
                rden, rdenb = scr(2)
                P.op("act", lambda e: e.activation(out=rden, in_=den_ps[:], func=AF.Ln), reads=[denb], writes=[rdenb])
                P.op("act", lambda e: e.activation(out=rden, in_=rden, func=AF.Exp, scale=-1.0), reads=[rdenb], writes=[rdenb])

            def stF3(t):
                if not t.last:
                    return
                o_ps, opb = self.psum[4 + t.par]
                (rden, rdenb), (o1n, o1nb), (od, odb) = scr(2), scr(0), scr(1)
                if t.mp == 0:
                    P.op("dve", lambda e: e.tensor_tensor(out=o1n, in0=o_ps[:], in1=rden, op=ALU.mult),
                         reads=[opb, rdenb], writes=[o1nb])
                else:
                    P.op("dve", lambda e: e.tensor_tensor(out=od, in0=o_ps[:], in1=rden, op=ALU.mult),
                         reads=[opb, rdenb], writes=[odb])
                    P.op("dve", lambda e: e.scalar_tensor_tensor(out=od, in0=od, scalar=neglam, in1=o1n, op0=ALU.mult, op1=ALU.add),
                         reads=[odb, o1nb, smallb], writes=[odb])

            def stF4(t):
                if not (t.last and t.mp == 1):
                    return
                od, odb = scr(1)
                P.op("act", lambda e: e.activation(out=sqo, in_=od, func=AF.Square), reads=[odb], writes=[sqob])

            def stF4b(t):
                if not (t.last and t.mp == 1):
                    return
                t.ss = self.psum[7]
                P.op("pe", lambda e: e.matmul(t.ss[0][:], lhsT=ones, rhs=sqo, start=True, stop=True),
                     reads=[sqob, cstb], writes=[t.ss[1]])

            def stF5(t):
                if not (t.last and t.mp == 1):
                    return
                rs, rsb = scr(4)
                P.op("act", lambda e: e.activation(out=rs, in_=t.ss[0][:], func=AF.Ln, scale=1.0 / 128.0, bias=RMS_EPS),
                     reads=[t.ss[1]], writes=[rsb])
                P.op("act", lambda e: e.activation(out=rs, in_=rs, func=AF.Exp, scale=-0.5), reads=[rsb], writes=[rsb])

            def stF6(t):
                if not (t.last and t.mp == 1):
                    return
                (od, odb), (rs, rsb) = scr(1), scr(4)
                h = t.h
                P.op("dve", lambda e: e.scalar_tensor_tensor(out=o_g[:, h, :], in0=od, scalar=subg, in1=rs, op0=ALU.mult, op1=ALU.mult),
                     reads=[odb, rsb, smallb], writes=[ogb[h]])

            sched = self.df_sched
            n = len(tasks)
            maxlag = max(l for (_, l) in sched)
            fns = {"A": stA, "B": stB, "C": stC, "F2": stF2, "F3": stF3, "F4": stF4, "F4b": stF4b, "F5": stF5, "F6": stF6}
            for s in range(n + maxlag):
                for (name, lag) in sched:
                    k = s - lag
                    if 0 <= k < n:
                        fns[name](tasks[k])
            dbg = getattr(self, "dbg", None)
            if dbg:
                srcs = {"o": (o_g, ogb), "hn": (hn_g, hngb)}
                if dbg == "k":
                    for m in range(NCH):
                        P.op("dve", lambda e, m=m, tgs=tgs: e.tensor_copy(out=xT[:, m, tgs], in_=kT[:, m, tgs]),
                             reads=[kb[m][g], self.xb[m][g]], writes=[self.xb[m][g]])
                elif dbg == "v":
                    for m in range(NCH):
                        P.op("dve", lambda e, m=m, tgs=tgs: e.tensor_copy(out=xT[:, m, tgs], in_=vT[:, g * 4 + m // 2, (m % 2) * 512:(m % 2) * 512 + 512]),
                             reads=[vb[g * 4 + m // 2][nb] for nb in range(4)] + [self.xb[m][g]], writes=[self.xb[m][g]])
                else:
                    s3, sb3 = srcs[dbg]
                    for m in range(NCH):
                        P.op("dve", lambda e, m=m, tgs=tgs, s3=s3: e.tensor_copy(out=xT[:, m, tgs], in_=s3[:, m, :]),
                             reads=[sb3[m], self.xb[m][g]], writes=[self.xb[m][g]])
                for nb in range(4):
                    self.ws.acquire(("df", 12 + nb, g))
                continue
            for nb in range(4):
                slot, sbuf_ = self.ws.acquire(("df", 12 + nb, g))
                for half in range(2):
                    c = nb * 2 + half

                    def evac(ps, pb, c=c, tgs=tgs, g=g):
                        P.op("dve", lambda e: e.tensor_tensor(
                            out=xT[:, c, tgs], in0=ps[:], in1=xT[:, c, tgs], op=ALU.add),
                            reads=[pb, self.xb[c][g]], writes=[self.xb[c][g]])
                    self.proj_fm(slot, sbuf_, half, o_g, ogb, evac)


def _consts():
    c = np.zeros((128, 640), np.float32)
    c[:, 0:128] = np.eye(128, dtype=np.float32)
    j = np.arange(128)[:, None]
    k = np.arange(128)[None, :]
    c[:, 128:256] = (j >= k).astype(np.float32)
    c[:, 256:384] = np.where(j >= k, MASKNEG, 0.0)
    c[:, 384:512] = 1.0
    p = np.arange(128)
    c[(p // 64) * 64 + (p % 64 + 32) % 64, 512 + p] = 1.0
    return c


def _layout_weights(inputs):
    w_in = np.asarray(inputs["ffn_w_in"], np.float32).reshape(4, NCH, 128, 2, NF, 128)
    w_in = np.ascontiguousarray(w_in.transpose(0, 4, 2, 3, 1, 5)).reshape(4, NF, 128, 2048)
    w_out = np.asarray(inputs["ffn_w_out"], np.float32).reshape(4, NF, 128, NCH, 128)
    w_out = np.ascontiguousarray(w_out.transpose(0, 3, 2, 1, 4)).reshape(4, NCH, 128, D_FF)

    def colblocks(w, ncols):
        nb = ncols // 256
        a = w.reshape(NCH, 128, nb, 256).transpose(2, 1, 0, 3)
        return np.ascontiguousarray(a).reshape(nb, 128, 2048)
    sbq = np.asarray(inputs["sb_w_qkv"], np.float32)[0]
    sbo = np.asarray(inputs["sb_w_o"], np.float32)[0]
    w_sb = np.concatenate([colblocks(sbq, 3072), colblocks(sbo, 1024)], axis=0)
    dq = np.asarray(inputs["diff_w_qkv"], np.float32)[0]
    do = np.asarray(inputs["diff_w_o"], np.float32)[0]
    w_df = np.concatenate([colblocks(dq, 3072), colblocks(do, 1024)], axis=0)
    return w_in, w_out, w_sb, w_df


def _rope_tables():
    p = np.arange(128)
    i = p % 32
    inv = 10000.0 ** (-(i.astype(np.float32)) / 32.0)
    ang = np.arange(SEQ, dtype=np.float32)[None, :] * inv[:, None].astype(np.float32)
    cos = np.cos(ang).astype(np.float32)
    sin = np.sin(ang).astype(np.float32)
    sgn = np.where((p % 64) < 32, -1.0, 1.0).astype(np.float32)[:, None]
    return np.ascontiguousarray(np.stack([cos, sin * sgn], axis=1))


FULL_STAGES = [("ffn", 0, 0), ("sb", 1), ("ffn", 1, 2), ("ffn", 2, 3), ("diff", 4), ("ffn", 3, 5), ("final", 6)]
_CACHE = {}
_RUN_KW = {}


def _prepare(inputs):
    w_in, w_out, w_sb, w_df = _layout_weights(inputs)
    ng = np.asarray(inputs["norm_gains"], np.float32).reshape(6, NCH, 128)
    fg = np.asarray(inputs["final_gain"], np.float32).reshape(1, NCH, 128)
    gains = np.ascontiguousarray(np.concatenate([ng, fg], 0).transpose(2, 0, 1)).reshape(128, 56)
    lam = np.ascontiguousarray(np.broadcast_to(np.asarray(inputs["diff_lambda"], np.float32).reshape(1, 256), (128, 256)))
    subln = np.ascontiguousarray(np.asarray(inputs["diff_subln"], np.float32).reshape(128, 1))
    shared = {"gains": gains, "consts": _consts(), "w_in": w_in, "w_out": w_out, "w_sb": w_sb, "w_df": w_df,
              "rope": _rope_tables(), "lam": lam, "subln": subln}
    return shared


def _x_to_dev(xb):
    return np.ascontiguousarray(xb.T.reshape(NCH, 128, SEQ).transpose(1, 0, 2))


def _x_from_dev(y):
    return np.ascontiguousarray(y.transpose(1, 0, 2).reshape(D_MODEL, SEQ).T)


def run_stages(stages, xs, shared, core_ids=None):
    key = tuple(stages)
    if key not in _CACHE:
        _CACHE[key] = Builder(list(stages)).build()
    nc = _CACHE[key]
    n = len(xs)
    in_maps = [dict(shared, xT=_x_to_dev(np.asarray(x, np.float32))) for x in xs]
    res = run_bass_kernel_spmd(nc, in_maps, core_ids=list(range(n)), **_RUN_KW)
    if res.exec_time_ns is not None:
        print("exec_time_ns", res.exec_time_ns)
    return [_x_from_dev(r["outT"]) for r in res.results]


def kernel(**inputs):
    shared = _prepare(inputs)
    x = np.asarray(inputs["x"], np.float32)
    outs = run_stages(FULL_STAGES, [x[b] for b in range(x.shape[0])], shared)
    return np.stack(outs, axis=0).astype(np.float32)
```

```python
import math
from contextlib import ExitStack

import numpy as np
import concourse.bass as bass
import concourse.mybir as mybir
from concourse.bass_utils import run_bass_kernel_spmd

F32 = mybir.dt.float32
BF16 = mybir.dt.bfloat16
AF = mybir.ActivationFunctionType
ALU = mybir.AluOpType

ENGS = ("pe", "act", "dve", "pool", "sp")

D_MODEL = 1024
SEQ = 2048
NCH = 8
D_FF = 2816
NF = 22
TG = 512
NG = 4
RMS_EPS = 1e-6
LAMBDA_INIT = 0.8 - 0.6 * math.exp(-0.3 * (2 - 1))
NSLOT = 5
MASKNEG = -30000.0


class Buf:
    __slots__ = ("name", "last_w", "readers")

    def __init__(self, name):
        self.name = name
        self.last_w = None
        self.readers = []


class Op:
    __slots__ = ("eng", "fn", "deps", "needed", "ticket", "dsem", "idx", "is_dma")

    def __init__(self, eng, fn, dsem=None):
        self.eng = eng
        self.fn = fn
        self.deps = []
        self.needed = False
        self.ticket = None
        self.dsem = dsem
        self.is_dma = dsem is not None
        self.idx = None


class Prog:
    def __init__(self):
        self.ops = {e: [] for e in ENGS}
        self.n = 0
        self.dma_keys = []
        self.last_dma = {}

    def op(self, eng, fn, reads=(), writes=(), dsem=None, extra_deps=()):
        o = Op(eng, fn, dsem)
        o.idx = self.n
        self.n += 1
        if dsem is not None:
            if dsem not in self.dma_keys:
                self.dma_keys.append(dsem)
        deps = {}
        for b in reads:
            if b.last_w is not None:
                deps[id(b.last_w)] = b.last_w
        for b in writes:
            if b.last_w is not None:
                deps[id(b.last_w)] = b.last_w
            for r in b.readers:
                deps[id(r)] = r
        for d in extra_deps:
            deps[id(d)] = d
        if dsem is not None and dsem in self.last_dma:
            d = self.last_dma[dsem]
            deps[id(d)] = d
        for d in deps.values():
            if d.eng == "pe" and eng == "pe" and not d.is_dma and not o.is_dma:
                continue
            o.deps.append(d)
            d.needed = True
        for b in reads:
            b.readers.append(o)
        for b in writes:
            b.last_w = o
            b.readers = []
        if dsem is not None:
            self.last_dma[dsem] = o
        self.ops[eng].append(o)
        return o

    def barrier(self):
        lasts = []
        for e in ENGS:
            for o in reversed(self.ops[e]):
                if not o.is_dma and o.fn is not None:
                    lasts.append(o)
                    break
        lasts.extend(self.last_dma.values())
        for e in ENGS:
            self.op(e, None, extra_deps=[d for d in lasts])

    def assign(self):
        cnt = {e: 0 for e in ENGS}
        dcnt = {k: 0 for k in self.dma_keys}
        allops = []
        for e in ENGS:
            allops.extend(self.ops[e])
        allops.sort(key=lambda o: o.idx)
        for o in allops:
            if o.is_dma:
                dcnt[o.dsem] += 16
                o.ticket = dcnt[o.dsem]
            elif o.needed and o.fn is not None:
                cnt[o.eng] += 1
                o.ticket = cnt[o.eng]

    def run_engine(self, eng, e, sems, dsems):
        waited = {}
        for o in self.ops[eng]:
            for d in o.deps:
                if d.ticket is None:
                    continue
                key = ("d", d.dsem) if d.is_dma else ("e", d.eng)
                if waited.get(key, 0) >= d.ticket:
                    continue
                waited[key] = d.ticket
                s = dsems[d.dsem] if d.is_dma else sems[d.eng]
                e.wait_ge(s, d.ticket)
            if o.fn is None:
                continue
            ins = o.fn(e)
            if o.is_dma:
                ins.then_inc(dsems[o.dsem], 16)
            elif o.needed:
                ins.then_inc(sems[o.eng], 1)


class WStream:
    def __init__(self, P, ring, nslot):
        self.P = P
        self.ring = ring
        self.nslot = nslot
        self.blocks = []
        self.issued = 0
        self.next = 0

    def plan(self, tag, src, n):
        self.blocks.append((tag, src, n))

    def _issue(self, k):
        tag, src, n = self.blocks[k]
        tile, buf = self.ring[k % self.nslot]
        s = k % self.nslot

        def fn(e, tile=tile, src=src, n=n):
            if n % 2 == 0 and n > 1024:
                return e.dma_start(out=tile[:, 0:n].rearrange("p (a b) -> p a b", a=2),
                                   in_=src.rearrange("p (a b) -> p a b", a=2))
            return e.dma_start(out=tile[:, 0:n], in_=src)
        self.P.op("pool", fn, writes=[buf], dsem=f"w{s}")

    def acquire(self, tag):
        k = self.next
        self.next += 1
        assert self.blocks[k][0] == tag, (self.blocks[k][0], tag)
        while self.issued < min(len(self.blocks), k + self.nslot - 1):
            self._issue(self.issued)
            self.issued += 1
        return self.ring[k % self.nslot]


class Builder:
    def __init__(self, stages):
        self.stages = stages
        nc = bass.Bass("TRN2", target_bir_lowering=False)
        self.nc = nc
        self.P = Prog()
        d = lambda name, shape, kind="ExternalInput": nc.dram_tensor(name, shape, F32, kind=kind).ap()
        self.d_x = d("xT", [128, NCH, SEQ])
        self.d_gains = d("gains", [128, 56])
        self.d_consts = d("consts", [128, 768])
        self.d_win = d("w_in", [4, NF, 128, 2048])
        self.d_wout = d("w_out", [4, NCH, 128, D_FF])
        self.d_wsb = d("w_sb", [16, 128, 2048])
        self.d_wdf = d("w_df", [16, 128, 2048])
        self.d_rope = d("rope", [128, 2, SEQ])
        self.d_lam = d("lam", [128, 256])
        self.d_subln = d("subln", [128, 1])
        self.d_out = d("outT", [128, NCH, SEQ], kind="ExternalOutput")

    def sb(self, name, shape, dt):
        return self.es.enter_context(self.nc.sbuf_tensor(name, shape, dt))

    def build(self):
        nc, P = self.nc, self.P
        with ExitStack() as es:
            self.es = es
            self.xT = self.sb("xT_sb", [128, NCH, SEQ], F32)
            self.HN = self.sb("hn_sb", [128, NCH * SEQ], BF16)
            self.R = self.sb("r_sb", [128, 16 * SEQ], BF16)
            self.ringt = [self.sb(f"ring{i}", [128, 2048], BF16) for i in range(NSLOT)]
            self.SCR = self.sb("scr", [128, 3072], F32)
            self.SCRB = self.sb("scrb", [128, 2048], BF16)
            self.SCRC = self.sb("scrc", [128, 2048], BF16)
            self.cst = self.sb("cst", [128, 768], BF16)
            self.gains = self.sb("gains_sb", [128, 56], F32)
            self.lamt = self.sb("lam_sb", [128, 256], F32)
            self.small = self.sb("small_sb", [128, 8], F32)
            self.ropet = self.sb("rope_sb", [128, 2, TG], F32)
            self.psum = []
            for i in range(8):
                t = es.enter_context(nc.psum_tensor(f"ps{i}", [128, 512], F32))
                self.psum.append((t, Buf(f"ps{i}")))
            self.sems = {e: es.enter_context(nc.semaphore(f"s_{e}")) for e in ENGS}

            self.xb = [[Buf(f"x{c}_{g}") for g in range(NG)] for c in range(NCH)]
            self.ring = [(self.ringt[i], Buf(f"ring{i}")) for i in range(NSLOT)]
            self.scrb = [Buf(f"scr{i}") for i in range(6)]
            self.scrbb = [Buf(f"scrb{i}") for i in range(4)]
            self.cstb = Buf("cst")
            self.gainb = Buf("gains")
            self.lamb = Buf("lam")
            self.smallb = Buf("small")
            self.ropeb = Buf("rope")
            self.outb = [Buf(f"out{g}") for g in range(NG)]
            self.ws = WStream(P, self.ring, NSLOT)
            self.par = {}
            self.region_prev = {}
            self.region_cur = {}
            self.df_sched = [("C", 4), ("A", 0), ("B", 2), ("F2", 6), ("F3", 7), ("F4", 9), ("F4b", 10), ("F5", 11), ("F6", 12)]
            self.sb_sched = [("A", 0), ("B1", 1), ("D1", 3), ("B2", 1), ("C", 2), ("D2", 3), ("E", 4), ("F", 6)]

            self.plan_weights()
            self.emit_all()

            P.assign()
            dsems = {k: es.enter_context(nc.semaphore(f"d_{k}")) for k in P.dma_keys}
            sems = self.sems
            with nc.Block() as block:
                @block.sync
                def _(e):
                    P.run_engine("sp", e, sems, dsems)

                @block.gpsimd
                def _(e):
                    P.run_engine("pool", e, sems, dsems)

                @block.tensor
                def _(e):
                    P.run_engine("pe", e, sems, dsems)

                @block.scalar
                def _(e):
                    P.run_engine("act", e, sems, dsems)

                @block.vector
                def _(e):
                    P.run_engine("dve", e, sems, dsems)
        return nc

    def nb(self, name, region):
        b = Buf(name)
        b.readers = list(self.region_prev.get(region, []))
        self.region_cur.setdefault(region, []).append(b)
        return b

    def stage_end(self):
        for region, bufs in self.region_cur.items():
            last = {}
            for b in bufs:
                for o in ([b.last_w] if b.last_w is not None else []) + list(b.readers):
                    key = ("d", o.dsem) if o.is_dma else ("e", o.eng)
                    if key not in last or last[key].idx < o.idx:
                        last[key] = o
            self.region_prev[region] = list(last.values())
        self.region_cur = {}

    def flip(self, key, n=2):
        v = self.par.get(key, 0)
        self.par[key] = (v + 1) % n
        return v

    def plan_weights(self):
        ws = self.ws
        for st in self.stages:
            kind = st[0]
            if kind == "ffn":
                widx = st[1]
                for (j0, j1) in ((0, 11), (11, 22)):
                    for j in range(j0, j1):
                        ws.plan(("win", widx, j), self.d_win[widx, j], 2048)
                    for c in range(NCH):
                        ws.plan(("wout", widx, j0, c), self.d_wout[widx, c, :, j0 * 128:j1 * 128], (j1 - j0) * 128)
            elif kind == "sb":
                for blk in range(4, 12):
                    ws.plan(("sb", blk, 0), self.d_wsb[blk], 2048)
                for g in range(NG):
                    for blk in range(0, 4):
                        ws.plan(("sb", blk, g), self.d_wsb[blk], 2048)
                    if g + 1 < NG:
                        for blk in range(4, 12):
                            ws.plan(("sb", blk, g + 1), self.d_wsb[blk], 2048)
                    for blk in range(12, 16):
                        ws.plan(("sb", blk, g), self.d_wsb[blk], 2048)
            elif kind == "diff":
                for g in range(NG):
                    for blk in Builder.DF_ORDER:
                        ws.plan(("df", blk, g), self.d_wdf[blk], 2048)

    def emit_all(self):
        P = self.P
        P.op("pool", lambda e: e.dma_start(out=self.cst[:], in_=self.d_consts), writes=[self.cstb], dsem="cst")
        P.op("sp", lambda e: e.dma_start(out=self.gains[:], in_=self.d_gains), writes=[self.gainb], dsem="gains")
        for g in range(NG):
            P.op("sp", lambda e, g=g: e.dma_start(out=self.xT[:, :, g * TG:(g + 1) * TG],
                                                  in_=self.d_x[:, :, g * TG:(g + 1) * TG]),
                 writes=[self.xb[c][g] for c in range(NCH)], dsem=f"x{g}")
        for st in self.stages:
            if st[0] == "ffn":
                self.ffn(st[1], st[2])
            elif st[0] == "sb":
                self.sb_mixer(st[1])
            elif st[0] == "diff":
                self.diff_mixer(st[1])
            elif st[0] == "final":
                self.final_norm(st[1])
            self.stage_end()
        for g in range(NG):
            P.op("sp", lambda e, g=g: e.dma_start(out=self.d_out[:, :, g * TG:(g + 1) * TG],
                                                  in_=self.xT[:, :, g * TG:(g + 1) * TG]),
                 reads=[self.xb[c][g] for c in range(NCH)], writes=[self.outb[g]], dsem=f"o{g}")
        P.op("sp", None, reads=self.outb)

    def norm_parts(self, nidx, g, dst_fn, scratch=None):
        P = self.P
        tgs = slice(g * TG, (g + 1) * TG)
        ones = self.cst[:, 384:512]
        xT = self.xT
        st = {}

        def p0():
            ps, psb = self.psum[6 + self.flip("norm")]
            st["ps"] = (ps, psb)
            for c in range(NCH):
                k = self.flip("sq")
                sq = self.SCRB[:, (2 + k) * 512:(3 + k) * 512]
                sqb = self.scrbb[2 + k]
                P.op("act", lambda e, c=c, sq=sq: e.activation(out=sq, in_=xT[:, c, tgs], func=AF.Square),
                     reads=[self.xb[c][g]], writes=[sqb])
                P.op("pe", lambda e, c=c, sq=sq: e.matmul(ps[:], lhsT=ones, rhs=sq, start=(c == 0), stop=(c == NCH - 1)),
                     reads=[sqb, self.cstb], writes=[psb])
        if scratch is None:
            rs, rsb = self.SCR[:, 5 * 512:6 * 512], self.scrb[5]
        else:
            rs, rsb = scratch

        def p1():
            ps, psb = st["ps"]
            P.op("act", lambda e: e.activation(out=rs, in_=ps[:], func=AF.Ln, scale=1.0 / D_MODEL, bias=RMS_EPS),
                 reads=[psb], writes=[rsb])
            P.op("act", lambda e: e.activation(out=rs, in_=rs, func=AF.Exp, scale=-0.5), reads=[rsb], writes=[rsb])

        def p2(c0):
            for c in range(c0, c0 + 2):
                out_ap, obufs = dst_fn(c)
                gcol = self.gains[:, nidx * 8 + c:nidx * 8 + c + 1]
                rd = [self.xb[c][g], rsb, self.gainb]
                P.op("dve", lambda e, c=c, out_ap=out_ap, gcol=gcol: e.scalar_tensor_tensor(
                    out=out_ap, in0=xT[:, c, tgs], scalar=gcol, in1=rs, op0=ALU.mult, op1=ALU.mult),
                    reads=rd, writes=obufs)
        return [p0, p1] + [(lambda c0=c0: p2(c0)) for c0 in range(0, NCH, 2)]

    def norm_group(self, nidx, g, dst_fn):
        for p in self.norm_parts(nidx, g, dst_fn):
            p()

    def ffn(self, widx, nidx):
        P = self.P
        xT = self.xT
        hn3 = self.HN[:].rearrange("p (c t) -> p c t", c=NCH)
        hnb = [[self.nb(f"hn{c}_{g}", "HN") for g in range(NG)] for c in range(NCH)]
        actT = self.R[:, 0:11 * SEQ].rearrange("p (j t) -> p j t", j=11)
        actb = [[self.nb(f"act{j}_{g}", "R") for g in range(NG)] for j in range(11)]
        for g in range(NG):
            tgs = slice(g * TG, (g + 1) * TG)
            self.norm_group(nidx, g, lambda c, tgs=tgs, g=g: (hn3[:, c, tgs], [hnb[c][g]]))
        for (j0, j1) in ((0, 11), (11, 22)):
            nj = j1 - j0
            for j in range(j0, j1):
                slot, sbuf_ = self.ws.acquire(("win", widx, j))
                wv = slot[:, 0:2048].rearrange("p (g c m) -> p g c m", g=2, c=NCH)
                jj = j - j0
                for g in range(NG):
                    tgs = slice(g * TG, (g + 1) * TG)
                    par = self.flip("gu")
                    gps, gpb = self.psum[par * 2]
                    ups, upb = self.psum[par * 2 + 1]
                    for gu, (ps, pb) in enumerate(((gps, gpb), (ups, upb))):
                        for c in range(NCH):
                            P.op("pe", lambda e, ps=ps, gu=gu, c=c, wv=wv, tgs=tgs: e.matmul(
                                ps[:], lhsT=wv[:, gu, c, :], rhs=hn3[:, c, tgs], start=(c == 0), stop=(c == NCH - 1)),
                                reads=[sbuf_, hnb[c][g]], writes=[pb])
                    sg = self.SCR[:, par * 512:(par + 1) * 512]
                    sgb = self.scrb[par]
                    P.op("act", lambda e, sg=sg, gps=gps: e.activation(out=sg, in_=gps[:], func=AF.Silu),
                         reads=[gpb], writes=[sgb])
                    P.op("dve", lambda e, sg=sg, ups=ups, jj=jj, tgs=tgs: e.tensor_tensor(
                        out=actT[:, jj, tgs], in0=sg, in1=ups[:], op=ALU.mult),
                        reads=[sgb, upb], writes=[actb[jj][g]])
            for c in range(NCH):
                slot, sbuf_ = self.ws.acquire(("wout", widx, j0, c))
                wv = slot[:, 0:nj * 128].rearrange("p (j m) -> p j m", m=128)
                for g in range(NG):
                    tgs = slice(g * TG, (g + 1) * TG)
                    ps, pb = self.psum[4 + self.flip("y")]
                    for jj in range(nj):
                        P.op("pe", lambda e, ps=ps, jj=jj, wv=wv, tgs=tgs: e.matmul(
                            ps[:], lhsT=wv[:, jj, :], rhs=actT[:, jj, tgs], start=(jj == 0), stop=(jj == nj - 1)),
                            reads=[sbuf_, actb[jj][g]], writes=[pb])
                    P.op("dve", lambda e, ps=ps, c=c, tgs=tgs: e.scalar_tensor_tensor(
                        out=xT[:, c, tgs], in0=ps[:], scalar=0.5, in1=xT[:, c, tgs], op0=ALU.mult, op1=ALU.add),
                        reads=[pb, self.xb[c][g]], writes=[self.xb[c][g]])

    def final_norm(self, nidx):
        xT = self.xT
        for g in range(NG):
            tgs = slice(g * TG, (g + 1) * TG)
            self.norm_group(nidx, g, lambda c, tgs=tgs, g=g: (xT[:, c, tgs], [self.xb[c][g]]))

    def proj_fm(self, slot, sbuf_, half, src3, srcb, evac):
        P = self.P
        wv = slot[:, 0:2048].rearrange("p (c m) -> p c m", c=NCH)
        ps, pb = self.psum[6 + self.flip("norm")]
        for c in range(NCH):
            P.op("pe", lambda e, ps=ps, c=c, wv=wv, half=half: e.matmul(
                ps[:], lhsT=wv[:, c, half * 128:(half + 1) * 128], rhs=src3[:, c, :],
                start=(c == 0), stop=(c == NCH - 1)),
                reads=[sbuf_, srcb[c]], writes=[pb])
        evac(ps, pb)

    def sb_mixer(self, nidx):
        P = self.P
        xT = self.xT
        kT = self.R[:, 0:NCH * SEQ].rearrange("p (m t) -> p m t", m=NCH)
        vT = self.R[:, NCH * SEQ:16 * SEQ].rearrange("p (i f) -> p i f", i=16)
        kb = [[self.nb(f"k{m}_{g}", "R") for g in range(NG)] for m in range(NCH)]
        vb = [[self.nb(f"v{i}_{nb}", "R") for nb in range(4)] for i in range(16)]
        hn_g = self.HN[:, 0:4096].rearrange("p (c t) -> p c t", c=NCH)
        q_m = [self.HN[:, 4096:8192].rearrange("p (c t) -> p c t", c=NCH),
               self.HN[:, 8192:12288].rearrange("p (c t) -> p c t", c=NCH)]
        o_g = self.HN[:, 12288:16384].rearrange("p (c t) -> p c t", c=NCH)
        hngb = [self.nb(f"hng{c}", "HN") for c in range(NCH)]
        qgb = [[self.nb(f"qg{hf}_{c}", "HN") for c in range(NCH)] for hf in range(2)]
        ogb = [self.nb(f"og{c}", "HN") for c in range(NCH)]
        P.op("dve", lambda e: e.memset(q_m[0][64:128, :, :], 0.0), writes=qgb[0])
        P.op("dve", lambda e: e.memset(q_m[1][0:64, :, :], 0.0), writes=qgb[1])
        ident = self.cst[:, 0:128]
        tinc = self.cst[:, 128:256]
        maskneg = self.cst[:, 256:384]
        ones = self.cst[:, 384:512]
        zeros = self.cst[:, 640:768]
        cstb = self.cstb
        NE = 4
        e_t = [(self.SCR[:, k * 512:(k + 1) * 512], self.scrb[k]) for k in range(4)]
        ecs_t = [(self.SCR[:, (4 + k) * 512:(5 + k) * 512], self.scrb[4 + k]) for k in range(2)]
        sp_t = [(self.SCRB[:, k * 512:(k + 1) * 512], self.scrbb[k]) for k in range(2)]
        aT_t = [(self.SCRC[:, k * 512:(k + 1) * 512], self.nb(f"aT{k}", "SCRC")) for k in range(2)]
        R_t = [(self.SCRC[:, 1024 + k * 512:1024 + (k + 1) * 512], self.nb(f"R{k}", "SCRC")) for k in range(2)]

        class T:
            pass

        def kv_timeline(g):
            tgs = slice(g * TG, (g + 1) * TG)
            st = {}
            tl = []
            nparts = self.norm_parts(nidx, g, lambda c: (hn_g[:, c, :], [hngb[c]]),
                                     scratch=(self.ropet[:, 0, :], self.ropeb))
            for j, p in enumerate(nparts):
                tl.append((2 * j if j < 2 else 3 + j, p))
            t0 = 3 + len(nparts) + 1
            units = []
            for nb in range(4):
                for half in range(2):
                    units.append(("k", nb, half))
            for nb in range(4):
                for tt in range(4):
                    units.append(("v", nb, tt))
            for ui, (kind, nb, sub) in enumerate(units):
                ctx = {}
                start = t0 + 3 * ui

                def mm(j, kind=kind, nb=nb, sub=sub, ctx=ctx):
                    if j == 0:
                        if sub == 0:
                            st[kind, nb] = self.ws.acquire(("sb", (4 if kind == "k" else 8) + nb, g))
                        ctx["ps"] = self.psum[6 + self.flip("norm")]
                    slot, sbuf_ = st[kind, nb]
                    wv = slot[:, 0:2048].rearrange("p (c m) -> p c m", c=NCH)
                    ps, pb = ctx["ps"]
                    for c in (2 * j, 2 * j + 1):
                        if kind == "k":
                            P.op("pe", lambda e, c=c: e.matmul(
                                ps[:], lhsT=wv[:, c, sub * 128:(sub + 1) * 128], rhs=hn_g[:, c, :],
                                start=(c == 0), stop=(c == NCH - 1)),
                                reads=[sbuf_, hngb[c]], writes=[pb])
                        else:
                            P.op("pe", lambda e, c=c: e.matmul(
                                ps[:, 0:256], lhsT=hn_g[:, c, sub * 128:(sub + 1) * 128], rhs=wv[:, c, :],
                                start=(c == 0), stop=(c == NCH - 1)),
                                reads=[sbuf_, hngb[c]], writes=[pb])

                def ev(kind=kind, nb=nb, sub=sub, ctx=ctx):
                    ps, pb = ctx["ps"]
                    if kind == "k":
                        m = nb * 2 + sub
                        P.op("dve", lambda e: e.tensor_copy(out=kT[:, m, tgs], in_=ps[:]),
                             reads=[pb], writes=[kb[m][g]])
                    else:
                        ti = g * 4 + sub
                        P.op("dve", lambda e: e.tensor_copy(out=vT[:, ti, nb * 256:(nb + 1) * 256], in_=ps[:, 0:256]),
                             reads=[pb], writes=[vb[ti][nb]])
                for j in range(4):
                    tl.append((start + j, (lambda j=j, mm=mm: mm(j))))
                tl.append((start + 5, ev))
            tl = [(off, i, fn) for i, (off, fn) in enumerate(tl)]
            tl.sort(key=lambda x: (x[0], x[1]))
            return tl

        for (_, _, fn) in kv_timeline(0):
            fn()
        for g in range(NG):
            tgs = slice(g * TG, (g + 1) * TG)
            for nb in range(4):
                slot, sbuf_ = self.ws.acquire(("sb", nb, g))
                for half in range(2):
                    m = nb * 2 + half

                    def evac(ps, pb, m=m):
                        for hf in range(2):
                            P.op("dve", lambda e, hf=hf: e.tensor_scalar(out=q_m[hf][hf * 64:(hf + 1) * 64, m, :],
                                                                         in0=ps[hf * 64:(hf + 1) * 64, :], scalar1=0.125, scalar2=None, op0=ALU.mult),
                                 reads=[pb], writes=[qgb[hf][m]])
                    self.proj_fm(slot, sbuf_, half, hn_g, hngb, evac)
            side = kv_timeline(g + 1) if g + 1 < NG else []
            tasks = []
            for m in range(NCH):
                for half in range(2):
                    tiles = list(range(4 * g + 3, -1, -1))
                    for idx, i in enumerate(tiles):
                        t = T()
                        t.m, t.half, t.h, t.p0, t.i, t.idx = m, half, 2 * m + half, 64 * half, i, idx
                        t.qoff = max(0, i - 4 * g) * 128
                        t.diag = i >= 4 * g
                        t.first = idx == 0
                        t.last = idx == len(tiles) - 1
                        t.nqoff = 0 if t.last else max(0, tiles[idx + 1] - 4 * g) * 128
                        t.pqoff = t.qoff if idx == 0 else max(0, tiles[idx - 1] - 4 * g) * 128
                        t.n = len(tasks)
                        tasks.append(t)
            o_cur = {}

            def stA(t):
                t.z_ps, t.zb = self.psum[t.n % 2]
                qo = t.qoff
                P.op("pe", lambda e: e.matmul(
                    t.z_ps[:, qo:512], lhsT=kT[:, t.m, t.i * 128:(t.i + 1) * 128],
                    rhs=q_m[t.half][:, t.m, qo:512], start=True, stop=(not t.diag)),
                    reads=[kb[t.m][t.i // 4], qgb[t.half][t.m]], writes=[t.zb])
                if t.diag:
                    P.op("pe", lambda e: e.matmul(
                        t.z_ps[:, qo:qo + 128], lhsT=ident, rhs=maskneg, start=False, stop=True),
                        reads=[cstb], writes=[t.zb])
                for _f in range(getattr(self, "fill", 0)):
                    fps, fpb = self.psum[6 + (_f % 2)]
                    P.op("pe", lambda e, fps=fps: e.matmul(fps[:], lhsT=ones, rhs=self.cst[:, 0:512], start=True, stop=True),
                         reads=[cstb], writes=[fpb])

            def stB1(t):
                t.et, t.eb = e_t[t.n % NE]
                qo = t.qoff
                P.op("act", lambda e: e.activation(out=t.et[:, qo:512], in_=t.z_ps[:, qo:512], func=AF.Exp),
                     reads=[t.zb], writes=[t.eb])

            def stB2(t):
                t.spt, t.spb = sp_t[t.n % 2]
                qo = t.qoff
                P.op("act", lambda e: e.activation(out=t.spt[:, qo:512], in_=t.et[:, qo:512], func=AF.Ln, bias=1.0),
                     reads=[t.eb], writes=[t.spb])

            def stC(t):
                t.cs_ps, t.csb = self.psum[2 + t.n % 2]
                qo = t.qoff
                P.op("pe", lambda e: e.matmul(
                    t.cs_ps[:, qo:512], lhsT=tinc, rhs=t.spt[:, qo:512], start=True, stop=t.first),
                    reads=[t.spb, cstb], writes=[t.csb])
                if not t.first:
                    rt, rb = t.rprev
                    P.op("pe", lambda e: e.matmul(
                        t.cs_ps[:, qo:512], lhsT=ones, rhs=rt[:, qo:512], start=False, stop=True),
                        reads=[rb, cstb], writes=[t.csb])
                if not t.last:
                    rn, rnb = R_t[t.n % 2]
                    if t.first:
                        P.op("dve", lambda e: e.tensor_copy(out=rn[:, qo:512], in_=t.spt[:, qo:512]),
                             reads=[t.spb], writes=[rnb])
                    else:
                        rt, rb = t.rprev
                        P.op("dve", lambda e: e.tensor_tensor(
                            out=rn[:, qo:512], in0=rt[:, qo:512], in1=t.spt[:, qo:512], op=ALU.add),
                            reads=[t.spb, rb], writes=[rnb])
                    if t.nqoff < qo:
                        P.op("dve", lambda e: e.memset(rn[:, t.nqoff:qo], 0.0), reads=[], writes=[rnb])
                    tasks[t.n + 1].rprev = (rn, rnb)

            def stD1(t):
                t.ect, t.ecb = ecs_t[t.n % 2]
                qo = t.qoff
                P.op("act", lambda e: e.activation(out=t.ect[:, qo:512], in_=t.cs_ps[:, qo:512], func=AF.Exp, scale=-1.0),
                     reads=[t.csb], writes=[t.ecb])

            def stD2(t):
                t.at, t.ab = aT_t[t.n % 2]
                qo = t.qoff
                P.op("dve", lambda e: e.tensor_tensor(
                    out=t.at[:, qo:512], in0=t.et[:, qo:512], in1=t.ect[:, qo:512], op=ALU.mult),
                    reads=[t.eb, t.ecb], writes=[t.ab])

            def stE(t):
                o_ps, opb = self.psum[4 + t.half]
                qo = t.qoff
                if t.first:
                    P.op("pe", lambda e: e.matmul(o_ps[:], lhsT=zeros, rhs=self.cst[:, 0:512], start=True, stop=False),
                         reads=[cstb], writes=[opb])
                P.op("pe", lambda e: e.matmul(
                    o_ps[:, qo:512], lhsT=vT[:, t.i, t.m * 128:(t.m + 1) * 128], rhs=t.at[:, qo:512],
                    start=False, stop=t.last),
                    reads=[vb[t.i][t.m // 2], t.ab], writes=[opb])

            def stF(t):
                if t.last:
                    o_ps, opb = self.psum[4 + t.half]
                    m, p0 = t.m, t.p0
                    P.op("dve", lambda e: e.tensor_copy(out=o_g[p0:p0 + 64, m, :], in_=o_ps[p0:p0 + 64, :]),
                         reads=[opb], writes=[ogb[m]])

            sched = self.sb_sched
            n = len(tasks)
            maxlag = max(l for (_, l) in sched)
            fns = {"A": stA, "B1": stB1, "B2": stB2, "C": stC, "D1": stD1, "D2": stD2, "E": stE, "F": stF}
            for s in range(n + maxlag):
                for (name, lag) in sched:
                    k = s - lag
                    if 0 <= k < n:
                        fns[name](tasks[k])
                while side and side[0][0] + 6 <= s:
                    side.pop(0)[2]()
            while side:
                side.pop(0)[2]()
            for nb in range(4):
                slot, sbuf_ = self.ws.acquire(("sb", 12 + nb, g))
                for half in range(2):
                    c = nb * 2 + half

                    def evac(ps, pb, c=c, tgs=tgs, g=g):
                        P.op("dve", lambda e: e.tensor_tensor(
                            out=xT[:, c, tgs], in0=ps[:], in1=xT[:, c, tgs], op=ALU.add),
                            reads=[pb, self.xb[c][g]], writes=[self.xb[c][g]])
                    self.proj_fm(slot, sbuf_, half, o_g, ogb, evac)

    DF_ORDER = list(range(16))

    def diff_mixer(self, nidx):
        P = self.P
        xT = self.xT
        kT = self.R[:, 0:NCH * SEQ].rearrange("p (m t) -> p m t", m=NCH)
        vT = self.R[:, NCH * SEQ:16 * SEQ].rearrange("p (i f) -> p i f", i=16)
        kb = [[self.nb(f"k{m}_{g}", "R") for g in range(NG)] for m in range(NCH)]
        vb = [[self.nb(f"v{i}_{nb}", "R") for nb in range(4)] for i in range(16)]
        hn_g = self.HN[:, 0:4096].rearrange("p (c t) -> p c t", c=NCH)
        q_m = [self.HN[:, 4096:8192].rearrange("p (c t) -> p c t", c=NCH),
               self.HN[:, 8192:12288].rearrange("p (c t) -> p c t", c=NCH)]
        o_g = self.HN[:, 12288:16384].rearrange("p (c t) -> p c t", c=NCH)
        hngb = [self.nb(f"hng{c}", "HN") for c in range(NCH)]
        qgb = [[self.nb(f"qg{hf}_{c}", "HN") for c in range(NCH)] for hf in range(2)]
        ogb = [self.nb(f"og{c}", "HN") for c in range(NCH)]
        P.op("dve", lambda e: e.memset(q_m[0][64:128, :, :], 0.0), writes=qgb[0])
        P.op("dve", lambda e: e.memset(q_m[1][0:64, :, :], 0.0), writes=qgb[1])
        ones = self.cst[:, 384:512]
        permm = self.cst[:, 512:640]
        cstb = self.cstb
        NP = 4
        p_t = [(self.SCRC[:, k * 512:(k + 1) * 512], self.nb(f"pT{k}", "SCRC")) for k in range(NP)]
        sqo, sqob = self.SCRB[:, 0:512], self.scrbb[0]
        scr = lambda k: (self.SCR[:, k * 512:(k + 1) * 512], self.scrb[k])
        small, smallb = self.small, self.smallb
        ropet, ropeb = self.ropet, self.ropeb

        tmp, tmpb = scr(0)
        P.op("sp", lambda e: e.dma_start(out=self.lamt[:], in_=self.d_lam), writes=[self.lamb], dsem="lam")
        P.op("sp", lambda e: e.dma_start(out=small[:, 6:7], in_=self.d_subln), writes=[smallb], dsem="subln")
        for k in range(2):
            P.op("dve", lambda e, k=k: e.tensor_tensor(out=tmp[:, k * 64:(k + 1) * 64], in0=self.lamt[:, k * 128:k * 128 + 64],
                                                       in1=self.lamt[:, k * 128 + 64:k * 128 + 128], op=ALU.mult),
                 reads=[self.lamb], writes=[tmpb])
            P.op("dve", lambda e, k=k: e.reduce_sum(out=small[:, k:k + 1], in_=tmp[:, k * 64:(k + 1) * 64], axis=mybir.AxisListType.X),
                 reads=[tmpb], writes=[smallb])
        P.op("act", lambda e: e.activation(out=small[:, 2:4], in_=small[:, 0:2], func=AF.Exp), reads=[smallb], writes=[smallb])
        P.op("dve", lambda e: e.tensor_tensor(out=small[:, 4:5], in0=small[:, 3:4], in1=small[:, 2:3], op=ALU.subtract),
             reads=[smallb], writes=[smallb])
        P.op("dve", lambda e: e.tensor_scalar(out=small[:, 4:5], in0=small[:, 4:5], scalar1=-LAMBDA_INIT, scalar2=None, op0=ALU.add),
             reads=[smallb], writes=[smallb])
        P.op("dve", lambda e: e.tensor_scalar(out=small[:, 5:6], in0=small[:, 6:7], scalar1=(1.0 - LAMBDA_INIT), scalar2=None, op0=ALU.mult),
             reads=[smallb], writes=[smallb])
        neglam = small[:, 4:5]
        subg = small[:, 5:6]

        class T:
            pass

        for g in range(NG):
            tgs = slice(g * TG, (g + 1) * TG)
            self.norm_group(nidx, g, lambda c: (hn_g[:, c, :], [hngb[c]]))
            P.op("sp", lambda e, tgs=tgs: e.dma_start(out=ropet[:], in_=self.d_rope[:, :, tgs]), writes=[ropeb], dsem="rope")
            chunks = []
            for kind in range(2):
                for nb in range(4):
                    for half in range(2):
                        chunks.append((kind, nb, half))
            slots = {}

            def rp_first(ci):
                kind, nb, half = chunks[ci]
                if half == 0:
                    slots[kind, nb] = self.ws.acquire(("df", kind * 4 + nb, g))
                slot, sbuf_ = slots[kind, nb]
                pair = ci % 2
                banks = (6, 7) if pair == 0 else (2, 3)
                wv = slot[:, 0:2048].rearrange("p (c m) -> p c m", c=NCH)
                ps1, pb1 = self.psum[banks[0]]
                for c in range(NCH):
                    P.op("pe", lambda e, c=c: e.matmul(
                        ps1[:], lhsT=wv[:, c, half * 128:(half + 1) * 128], rhs=hn_g[:, c, :],
                        start=(c == 0), stop=(c == NCH - 1)),
                        reads=[sbuf_, hngb[c]], writes=[pb1])
                qb, qbb = self.SCRB[:, pair * 512:(pair + 1) * 512], self.scrbb[pair]
                P.op("act", lambda e: e.activation(out=qb, in_=ps1[:], func=AF.Copy), reads=[pb1], writes=[qbb])

            def rp_second(ci):
                kind, nb, half = chunks[ci]
                m = nb * 2 + half
                pair = ci % 2
                banks = (6, 7) if pair == 0 else (2, 3)
                ps1, pb1 = self.psum[banks[0]]
                ps2, pb2 = self.psum[banks[1]]
                qb, qbb = self.SCRB[:, pair * 512:(pair + 1) * 512], self.scrbb[pair]
                P.op("pe", lambda e: e.matmul(ps2[:], lhsT=permm, rhs=qb, start=True, stop=True),
                     reads=[qbb, cstb], writes=[pb2])
                (t1, t1b), (t2, t2b) = scr(pair * 2), scr(pair * 2 + 1)
                P.op("dve", lambda e: e.tensor_tensor(out=t1, in0=ps1[:], in1=ropet[:, 0, :], op=ALU.mult),
                     reads=[pb1, ropeb, qbb], writes=[t1b])
                P.op("dve", lambda e: e.tensor_tensor(out=t2, in0=ps2[:], in1=ropet[:, 1, :], op=ALU.mult),
                     reads=[pb2, ropeb], writes=[t2b])
                if kind == 0:
                    for hf in range(2):
                        P.op("dve", lambda e, hf=hf: e.tensor_tensor(
                            out=q_m[hf][hf * 64:(hf + 1) * 64, m, :], in0=t1[hf * 64:(hf + 1) * 64, :],
                            in1=t2[hf * 64:(hf + 1) * 64, :], op=ALU.add),
                            reads=[t1b, t2b], writes=[qgb[hf][m]])
                else:
                    tg_l = tgs
                    P.op("dve", lambda e: e.tensor_tensor(out=kT[:, m, tg_l], in0=t1, in1=t2, op=ALU.add),
                         reads=[t1b, t2b], writes=[kb[m][g]])

            for ci in range(len(chunks) + 1):
                if ci < len(chunks):
                    rp_first(ci)
                if ci >= 1:
                    rp_second(ci - 1)
            for nb in range(4):
                slot, sbuf_ = self.ws.acquire(("df", 8 + nb, g))
                wv = slot[:, 0:2048].rearrange("p (c m) -> p c m", c=NCH)
                for tt in range(4):
                    ti = g * 4 + tt
                    ps, pb = self.psum[6 + self.flip("norm")]
                    for c in range(NCH):
                        P.op("pe", lambda e, ps=ps, c=c, wv=wv, tt=tt: e.matmul(
                            ps[:, 0:256], lhsT=hn_g[:, c, tt * 128:(tt + 1) * 128], rhs=wv[:, c, :],
                            start=(c == 0), stop=(c == NCH - 1)),
                            reads=[sbuf_, hngb[c]], writes=[pb])
                    P.op("dve", lambda e, ps=ps, ti=ti, nb=nb: e.tensor_copy(out=vT[:, ti, nb * 256:(nb + 1) * 256], in_=ps[:, 0:256]),
                         reads=[pb], writes=[vb[ti][nb]])
            tasks = []
            for h in range(NCH):
                for mp in range(2):
                    ntl = 4 * g + 4
                    for i in range(ntl):
                        t = T()
                        t.h, t.mp, t.p0, t.i = h, mp, 64 * mp, i
                        t.qoff = max(0, i - 4 * g) * 128
                        t.diag = i >= 4 * g
                        t.first = i == 0
                        t.last = i == ntl - 1
                        t.n = len(tasks)
                        t.par = (h * 2 + mp) % 2
                        tasks.append(t)

            def stA(t):
                t.s_ps, t.sb_ = self.psum[(0, 1, 6)[t.n % 3]]
                qo = t.qoff
                P.op("pe", lambda e: e.matmul(
                    t.s_ps[:, qo:512], lhsT=kT[:, t.h, t.i * 128:(t.i + 1) * 128],
                    rhs=q_m[t.mp][:, t.h, qo:512], start=True, stop=True),
                    reads=[kb[t.h][t.i // 4], qgb[t.mp][t.h]], writes=[t.sb_])

            def stB(t):
                t.pt, t.ptb = p_t[t.n % NP]
                qo = t.qoff
                P.op("act", lambda e: e.activation(out=t.pt[:, qo:512], in_=t.s_ps[:, qo:512], func=AF.Exp, scale=0.125),
                     reads=[t.sb_], writes=[t.ptb])
                if t.diag:
                    P.op("dve", lambda e: e.memset(t.pt[64:128, qo:qo + 64], 0.0), reads=[], writes=[t.ptb])

            def stC(t):
                o_ps, opb = self.psum[4 + t.par]
                den_ps, denb = self.psum[2 + t.par]
                qo = t.qoff
                P.op("pe", lambda e: e.matmul(den_ps[:, qo:512], lhsT=ones, rhs=t.pt[:, qo:512], start=t.first, stop=t.last),
                     reads=[t.ptb, cstb], writes=[denb])
                P.op("pe", lambda e: e.matmul(o_ps[:, qo:512], lhsT=vT[:, t.i, t.h * 128:(t.h + 1) * 128], rhs=t.pt[:, qo:512],
                                              start=t.first, stop=t.last),
                     reads=[t.ptb, vb[t.i][t.h // 2]], writes=[opb])

            def stF2(t):
                if not t.last:
                    return
                den_ps, denb = self.psum[2 + t.par]
                rden, rdenb = scr(2)
                P.op("act", lambda e: e.activation(out=rden, in_=den_ps[:], func=AF.Ln), reads=[denb], writes=[rdenb])
                P.op("act", lambda e: e.activation(out=rden, in_=rden, func=AF.Exp, scale=-1.0), reads=[rdenb], writes=[rdenb])

            def stF3(t):
                if not t.last:
                    return
                o_ps, opb = self.psum[4 + t.par]
                (rden, rdenb), (o1n, o1nb), (od, odb) = scr(2), scr(0), scr(1)
                if t.mp == 0:
                    P.op("dve", lambda e: e.tensor_tensor(out=o1n, in0=o_ps[:], in1=rden, op=ALU.mult),
                         reads=[opb, rdenb], writes=[o1nb])
                else:
                    P.op("dve", lambda e: e.tensor_tensor(out=od, in0=o_ps[:], in1=rden, op=ALU.mult),
                         reads=[opb, rdenb], writes=[odb])
                    P.op("dve", lambda e: e.scalar_tensor_tensor(out=od, in0=od, scalar=neglam, in1=o1n, op0=ALU.mult, op1=ALU.add),
                         reads=[odb, o1nb, smallb], writes=[odb])

            def stF4(t):
                if not (t.last and t.mp == 1):
                    return
                od, odb = scr(1)
                P.op("act", lambda e: e.activation(out=sqo, in_=od, func=AF.Square), reads=[odb], writes=[sqob])

            def stF4b(t):
                if not (t.last and t.mp == 1):
                    return
                t.ss = self.psum[7]
                P.op("pe", lambda e: e.matmul(t.ss[0][:], lhsT=ones, rhs=sqo, start=True, stop=True),
                     reads=[sqob, cstb], writes=[t.ss[1]])

            def stF5(t):
                if not (t.last and t.mp == 1):
                    return
                rs, rsb = scr(4)
                P.op("act", lambda e: e.activation(out=rs, in_=t.ss[0][:], func=AF.Ln, scale=1.0 / 128.0, bias=RMS_EPS),
                     reads=[t.ss[1]], writes=[rsb])
                P.op("act", lambda e: e.activation(out=rs, in_=rs, func=AF.Exp, scale=-0.5), reads=[rsb], writes=[rsb])

            def stF6(t):
                if not (t.last and t.mp == 1):
                    return
                (od, odb), (rs, rsb) = scr(1), scr(4)
                h = t.h
                P.op("dve", lambda e: e.scalar_tensor_tensor(out=o_g[:, h, :], in0=od, scalar=subg, in1=rs, op0=ALU.mult, op1=ALU.mult),
                     reads=[odb, rsb, smallb], writes=[ogb[h]])

            sched = self.df_sched
            n = len(tasks)
            maxlag = max(l for (_, l) in sched)
            fns = {"A": stA, "B": stB, "C": stC, "F2": stF2, "F3": stF3, "F4": stF4, "F4b": stF4b, "F5": stF5, "F6": stF6}
            for s in range(n + maxlag):
                for (name, lag) in sched:
                    k = s - lag
                    if 0 <= k < n:
                        fns[name](tasks[k])
            dbg = getattr(self, "dbg", None)
            if dbg:
                srcs = {"o": (o_g, ogb), "hn": (hn_g, hngb)}
                if dbg == "k":
                    for m in range(NCH):
                        P.op("dve", lambda e, m=m, tgs=tgs: e.tensor_copy(out=xT[:, m, tgs], in_=kT[:, m, tgs]),
                             reads=[kb[m][g], self.xb[m][g]], writes=[self.xb[m][g]])
                elif dbg == "v":
                    for m in range(NCH):
                        P.op("dve", lambda e, m=m, tgs=tgs: e.tensor_copy(out=xT[:, m, tgs], in_=vT[:, g * 4 + m // 2, (m % 2) * 512:(m % 2) * 512 + 512]),
                             reads=[vb[g * 4 + m // 2][nb] for nb in range(4)] + [self.xb[m][g]], writes=[self.xb[m][g]])
                else:
                    s3, sb3 = srcs[dbg]
                    for m in range(NCH):
                        P.op("dve", lambda e, m=m, tgs=tgs, s3=s3: e.tensor_copy(out=xT[:, m, tgs], in_=s3[:, m, :]),
                             reads=[sb3[m], self.xb[m][g]], writes=[self.xb[m][g]])
                for nb in range(4):
                    self.ws.acquire(("df", 12 + nb, g))
                continue
            for nb in range(4):
                slot, sbuf_ = self.ws.acquire(("df", 12 + nb, g))
                for half in range(2):
                    c = nb * 2 + half

                    def evac(ps, pb, c=c, tgs=tgs, g=g):
                        P.op("dve", lambda e: e.tensor_tensor(
                            out=xT[:, c, tgs], in0=ps[:], in1=xT[:, c, tgs], op=ALU.add),
                            reads=[pb, self.xb[c][g]], writes=[self.xb[c][g]])
                    self.proj_fm(slot, sbuf_, half, o_g, ogb, evac)


def _consts():
    c = np.zeros((128, 768), np.float32)
    c[:, 0:128] = np.eye(128, dtype=np.float32)
    j = np.arange(128)[:, None]
    k = np.arange(128)[None, :]
    c[:, 128:256] = (j >= k).astype(np.float32)
    c[:, 256:384] = np.where(j >= k, MASKNEG, 0.0)
    c[:, 384:512] = 1.0
    p = np.arange(128)
    c[(p // 64) * 64 + (p % 64 + 32) % 64, 512 + p] = 1.0
    return c


def _layout_weights(inputs):
    w_in = np.asarray(inputs["ffn_w_in"], np.float32).reshape(4, NCH, 128, 2, NF, 128)
    w_in = np.ascontiguousarray(w_in.transpose(0, 4, 2, 3, 1, 5)).reshape(4, NF, 128, 2048)
    w_out = np.asarray(inputs["ffn_w_out"], np.float32).reshape(4, NF, 128, NCH, 128)
    w_out = np.ascontiguousarray(w_out.transpose(0, 3, 2, 1, 4)).reshape(4, NCH, 128, D_FF)

    def colblocks(w, ncols):
        nb = ncols // 256
        a = w.reshape(NCH, 128, nb, 256).transpose(2, 1, 0, 3)
        return np.ascontiguousarray(a).reshape(nb, 128, 2048)
    sbq = np.asarray(inputs["sb_w_qkv"], np.float32)[0]
    sbo = np.asarray(inputs["sb_w_o"], np.float32)[0]
    w_sb = np.concatenate([colblocks(sbq, 3072), colblocks(sbo, 1024)], axis=0)
    dq = np.asarray(inputs["diff_w_qkv"], np.float32)[0]
    do = np.asarray(inputs["diff_w_o"], np.float32)[0]
    w_df = np.concatenate([colblocks(dq, 3072), colblocks(do, 1024)], axis=0)
    return w_in, w_out, w_sb, w_df


def _rope_tables():
    p = np.arange(128)
    i = p % 32
    inv = 10000.0 ** (-(i.astype(np.float32)) / 32.0)
    ang = np.arange(SEQ, dtype=np.float32)[None, :] * inv[:, None].astype(np.float32)
    cos = np.cos(ang).astype(np.float32)
    sin = np.sin(ang).astype(np.float32)
    sgn = np.where((p % 64) < 32, -1.0, 1.0).astype(np.float32)[:, None]
    return np.ascontiguousarray(np.stack([cos, sin * sgn], axis=1))


FULL_STAGES = [("ffn", 0, 0), ("sb", 1), ("ffn", 1, 2), ("ffn", 2, 3), ("diff", 4), ("ffn", 3, 5), ("final", 6)]
_CACHE = {}
_RUN_KW = {}


def _prepare(inputs):
    w_in, w_out, w_sb, w_df = _layout_weights(inputs)
    ng = np.asarray(inputs["norm_gains"], np.float32).reshape(6, NCH, 128)
    fg = np.asarray(inputs["final_gain"], np.float32).reshape(1, NCH, 128)
    gains = np.ascontiguousarray(np.concatenate([ng, fg], 0).transpose(2, 0, 1)).reshape(128, 56)
    lam = np.ascontiguousarray(np.broadcast_to(np.asarray(inputs["diff_lambda"], np.float32).reshape(1, 256), (128, 256)))
    subln = np.ascontiguousarray(np.asarray(inputs["diff_subln"], np.float32).reshape(128, 1))
    shared = {"gains": gains, "consts": _consts(), "w_in": w_in, "w_out": w_out, "w_sb": w_sb, "w_df": w_df,
              "rope": _rope_tables(), "lam": lam, "subln": subln}
    return shared


def _x_to_dev(xb):
    return np.ascontiguousarray(xb.T.reshape(NCH, 128, SEQ).transpose(1, 0, 2))


def _x_from_dev(y):
    return np.ascontiguousarray(y.transpose(1, 0, 2).reshape(D_MODEL, SEQ).T)


def run_stages(stages, xs, shared, core_ids=None):
    key = tuple(stages)
    if key not in _CACHE:
        _CACHE[key] = Builder(list(stages)).build()
    nc = _CACHE[key]
    n = len(xs)
    in_maps = [dict(shared, xT=_x_to_dev(np.asarray(x, np.float32))) for x in xs]
    res = run_bass_kernel_spmd(nc, in_maps, core_ids=list(range(n)), **_RUN_KW)
    if res.exec_time_ns is not None:
        print("exec_time_ns", res.exec_time_ns)
    return [_x_from_dev(r["outT"]) for r in res.results]


def kernel(**inputs):
    shared = _prepare(inputs)
    x = np.asarray(inputs["x"], np.float32)
    outs = run_stages(FULL_STAGES, [x[b] for b in range(x.shape[0])], shared)
    return np.stack(outs, axis=0).astype(np.float32)
```
